# Optimizing a Trainium2 kernel written in Bass

```python
import jax, jax.numpy as jnp
from jax import lax
import numpy as np

D_MODEL = 1024
BATCH = 2
SEQ = 8192
DEPTH = 2

D_MIX = D_MODEL
GROUP_WIDTH = D_MIX // 4
POOL_WINDOWS = (2, 4, 8, 16)
POOL_GROUP = GROUP_WIDTH // len(POOL_WINDOWS)
SCONV_K = 3
SSD_D_INNER = GROUP_WIDTH
SSD_HEAD_DIM = 64
SSD_HEADS = SSD_D_INNER // SSD_HEAD_DIM
SSD_GROUPS = 2
SSD_STATE = 128
SSD_CONV_K = 4
SSD_CHUNK = 128
SSD_CONV_DIM = SSD_D_INNER + 2 * SSD_GROUPS * SSD_STATE
S5_WIDTH = GROUP_WIDTH
S5_GROUP = 16
S5_GROUPS = S5_WIDTH // S5_GROUP
S5_STATE = 64
MLP_HIDDEN = 4 * D_MODEL
EPS = 1e-6

IN_SPLITS = (GROUP_WIDTH,
             GROUP_WIDTH, GROUP_WIDTH, GROUP_WIDTH,
             SSD_D_INNER, SSD_CONV_DIM, SSD_HEADS,
             S5_WIDTH)
D_IN_PROJ = sum(IN_SPLITS)
IN_SPLIT_IDX = tuple(int(v) for v in np.cumsum(IN_SPLITS)[:-1])

kernel_name = "hymba_style_pool_conv_ssd_s5_hybrid"

f32 = jnp.float32


def rmsnorm(x, w):
    xf = x.astype(f32)
    y = xf * lax.rsqrt(jnp.mean(xf * xf, axis=-1, keepdims=True) + EPS)
    return (y * w.astype(f32)).astype(x.dtype)


def causal_dwconv(x, w):
    k, ch = w.shape
    return lax.conv_general_dilated(
        x, w[:, None, :].astype(x.dtype), window_strides=(1,),
        padding=[(k - 1, 0)], dimension_numbers=("NWC", "WIO", "NWC"),
        feature_group_count=ch)


def pool_mixer(v, w_grp, scale):
    b, s, _ = v.shape
    vf = v.astype(f32)
    cs = jnp.pad(jnp.cumsum(vf, axis=1), ((0, 0), (1, 0), (0, 0)))
    t = jnp.arange(s)
    outs = []
    for g, win in enumerate(POOL_WINDOWS):
        csg = cs[..., g * POOL_GROUP:(g + 1) * POOL_GROUP]
        start = jnp.maximum(t + 1 - win, 0)
        wsum = csg[:, 1:] - csg[:, start]
        count = jnp.minimum(t + 1, win).astype(f32)[None, :, None]
        outs.append(wsum / count - vf[..., g * POOL_GROUP:(g + 1) * POOL_GROUP])
    p = jnp.stack(outs, axis=2)
    y = jnp.einsum('bsgc,gcd->bsgd', p, w_grp.astype(f32)).reshape(b, s, GROUP_WIDTH)
    return y * scale.astype(f32)


def short_conv_mixer(gate_b, gate_c, h, w):
    return gate_b * causal_dwconv(gate_c * h, w)


def ssd_mixer(z, xbc, dt_raw, conv_w, conv_b, dt_bias, a_log, d_skip):
    b, s, _ = z.shape
    nc = s // SSD_CHUNK
    rep = SSD_HEADS // SSD_GROUPS
    xbc = jax.nn.silu((causal_dwconv(xbc, conv_w) + conv_b).astype(f32))
    xs, bm, cm = jnp.split(xbc, [SSD_D_INNER, SSD_D_INNER + SSD_GROUPS * SSD_STATE], axis=-1)
    xs = xs.reshape(b, nc, SSD_CHUNK, SSD_HEADS, SSD_HEAD_DIM)
    bm = jnp.repeat(bm.reshape(b, nc, SSD_CHUNK, SSD_GROUPS, SSD_STATE), rep, axis=3)
    cm = jnp.repeat(cm.reshape(b, nc, SSD_CHUNK, SSD_GROUPS, SSD_STATE), rep, axis=3)
    dt = jax.nn.softplus(dt_raw.astype(f32) + dt_bias.astype(f32))
    dt = dt.reshape(b, nc, SSD_CHUNK, SSD_HEADS)
    a = -jnp.exp(a_log.astype(f32))
    a_dt = (dt * a).transpose(0, 3, 1, 2)
    x_dt = xs * dt[..., None]
    a_cs = jnp.cumsum(a_dt, axis=-1)
    diff = a_cs[..., :, None] - a_cs[..., None, :]
    causal = jnp.tril(jnp.ones((SSD_CHUNK, SSD_CHUNK), dtype=bool))
    decay = jnp.exp(jnp.where(causal, diff, -jnp.inf))
    scores = jnp.einsum('bclhn,bcshn->bhcls', cm, bm) * decay
    y_diag = jnp.einsum('bhcls,bcshp->bclhp', scores, x_dt)
    decay_to_end = jnp.exp(a_cs[..., -1:] - a_cs)
    states = jnp.einsum('bclhn,bhcl,bclhp->bchpn', bm, decay_to_end, x_dt)
    chunk_decay = jnp.exp(a_cs[..., -1])

    def step(carry, inp):
        st, dec = inp
        return carry * dec[..., None, None] + st, carry

    init = jnp.zeros((b, SSD_HEADS, SSD_HEAD_DIM, SSD_STATE), f32)
    _, prev = lax.scan(step, init, (states.transpose(1, 0, 2, 3, 4), chunk_decay.transpose(2, 0, 1)))
    prev = prev.transpose(1, 0, 2, 3, 4)
    y_off = jnp.einsum('bclhn,bchpn,bhcl->bclhp', cm, prev, jnp.exp(a_cs))
    y = y_diag + y_off + xs * d_skip.astype(f32)[:, None]
    y = y.reshape(b, s, SSD_D_INNER)
    return y * jax.nn.silu(z.astype(f32))


def _complex_affine_combine(e1, e2):
    a1r, a1i, b1r, b1i = e1
    a2r, a2i, b2r, b2i = e2
    return (a2r * a1r - a2i * a1i,
            a2r * a1i + a2i * a1r,
            a2r * b1r - a2i * b1i + b2r,
            a2r * b1i + a2i * b1r + b2i)


def s5_mixer(u, a_re, a_im, log_step, b_re, b_im, c_re, c_im, d_skip, glu_w, glu_b):
    bsz, s, _ = u.shape
    uf = u.astype(f32)
    ug = uf.reshape(bsz, s, S5_GROUPS, S5_GROUP)
    a_re = a_re.astype(f32); a_im = a_im.astype(f32)
    step = jnp.exp(log_step.astype(f32))[:, None]
    mag = jnp.exp(a_re * step)
    lam_re = mag * jnp.cos(a_im * step)
    lam_im = mag * jnp.sin(a_im * step)
    den = a_re * a_re + a_im * a_im
    nr = lam_re - 1.0
    f_re = (nr * a_re + lam_im * a_im) / den
    f_im = (lam_im * a_re - nr * a_im) / den
    b_re = b_re.astype(f32); b_im = b_im.astype(f32)
    bb_re = f_re[..., None] * b_re - f_im[..., None] * b_im
    bb_im = f_re[..., None] * b_im + f_im[..., None] * b_re
    bu_re = jnp.einsum('bsgh,gph->bsgp', ug, bb_re)
    bu_im = jnp.einsum('bsgh,gph->bsgp', ug, bb_im)
    lr = jnp.broadcast_to(lam_re, bu_re.shape)
    li = jnp.broadcast_to(lam_im, bu_re.shape)
    _, _, st_re, st_im = lax.associative_scan(_complex_affine_combine, (lr, li, bu_re, bu_im), axis=1)
    y = (jnp.einsum('bsgp,ghp->bsgh', st_re, c_re.astype(f32))
         - jnp.einsum('bsgp,ghp->bsgh', st_im, c_im.astype(f32)))
    y = y.reshape(bsz, s, S5_WIDTH) + d_skip.astype(f32) * uf
    g = jax.nn.gelu(y)
    return g * jax.nn.sigmoid(g @ glu_w.astype(f32) + glu_b.astype(f32))


def setup_inputs(seed: int = 0) -> dict:
    key = jax.random.key(seed)
    ks = jax.random.split(key, 32)
    L, D = DEPTH, D_MODEL
    nrm = lambda k, shape, sc: jax.random.normal(k, shape, f32) * sc
    gain = lambda k, shape: 1.0 + 0.01 * jax.random.normal(k, shape, f32)
    ssd_dt = jnp.exp(jax.random.uniform(ks[10], (L, SSD_HEADS), f32, np.log(1e-3), np.log(1e-1)))
    s5_a_im = (jnp.pi * jnp.arange(S5_STATE, dtype=f32))[None, None, :] + 0.01 * jax.random.normal(ks[14], (L, S5_GROUPS, S5_STATE), f32)
    return {
        "x": jax.random.normal(ks[0], (BATCH, SEQ, D), f32),
        "c": jax.random.normal(ks[1], (BATCH, D), f32),
        "norm_mix_w": gain(ks[2], (L, D)),
        "norm_mlp_w": gain(ks[3], (L, D)),
        "ada_w": nrm(ks[4], (L, D, 6 * D), 0.5 * D ** -0.5),
        "ada_b": nrm(ks[5], (L, 6 * D), 0.01),
        "w_in": nrm(ks[6], (L, D, D_IN_PROJ), D ** -0.5),
        "pool_w": nrm(ks[7], (L, len(POOL_WINDOWS), POOL_GROUP, POOL_GROUP), POOL_GROUP ** -0.5),
        "pool_scale": 1.0 + 0.1 * jax.random.normal(ks[8], (L, GROUP_WIDTH), f32),
        "sconv_w": nrm(ks[9], (L, SCONV_K, GROUP_WIDTH), SCONV_K ** -0.5),
        "ssd_conv_w": nrm(ks[11], (L, SSD_CONV_K, SSD_CONV_DIM), SSD_CONV_K ** -0.5),
        "ssd_conv_b": nrm(ks[12], (L, SSD_CONV_DIM), 0.01),
        "ssd_dt_bias": ssd_dt + jnp.log(-jnp.expm1(-ssd_dt)),
        "ssd_a_log": jnp.log(jax.random.uniform(ks[13], (L, SSD_HEADS), f32, 1.0, 16.0)),
        "ssd_d": gain(ks[15], (L, SSD_HEADS)),
        "s5_a_re": -0.5 + 0.01 * jax.random.normal(ks[16], (L, S5_GROUPS, S5_STATE), f32),
        "s5_a_im": s5_a_im,
        "s5_log_step": jax.random.uniform(ks[17], (L, S5_GROUPS), f32, np.log(1e-3), np.log(1e-1)),
        "s5_b_re": nrm(ks[18], (L, S5_GROUPS, S5_STATE, S5_GROUP), (2 * S5_GROUP) ** -0.5),
        "s5_b_im": nrm(ks[19], (L, S5_GROUPS, S5_STATE, S5_GROUP), (2 * S5_GROUP) ** -0.5),
        "s5_c_re": nrm(ks[20], (L, S5_GROUPS, S5_GROUP, S5_STATE), S5_STATE ** -0.5),
        "s5_c_im": nrm(ks[21], (L, S5_GROUPS, S5_GROUP, S5_STATE), S5_STATE ** -0.5),
        "s5_d": nrm(ks[22], (L, S5_WIDTH), 1.0),
        "s5_glu_w": nrm(ks[23], (L, S5_WIDTH, S5_WIDTH), S5_WIDTH ** -0.5),
        "s5_glu_b": nrm(ks[24], (L, S5_WIDTH), 0.01),
        "branch_norm_w": gain(ks[25], (L, D_MIX)),
        "w_out": nrm(ks[26], (L, D_MIX, D), D_MIX ** -0.5),
        "mlp_w1": nrm(ks[27], (L, D, MLP_HIDDEN), D ** -0.5),
        "mlp_w2": nrm(ks[28], (L, MLP_HIDDEN, D), MLP_HIDDEN ** -0.5),
        "final_norm_w": gain(ks[29], (D,)),
    }


def reference(x, c, norm_mix_w, norm_mlp_w, ada_w, ada_b, w_in, pool_w, pool_scale,
              sconv_w, ssd_conv_w, ssd_conv_b, ssd_dt_bias, ssd_a_log, ssd_d,
              s5_a_re, s5_a_im, s5_log_step, s5_b_re, s5_b_im, s5_c_re, s5_c_im,
              s5_d, s5_glu_w, s5_glu_b, branch_norm_w, w_out, mlp_w1, mlp_w2,
              final_norm_w):
    dtype = x.dtype
    b, s, _ = x.shape
    cond = jax.nn.silu(c)
    h = x
    for l in range(DEPTH):
        mod = (cond @ ada_w[l] + ada_b[l])[:, None, :]
        sh1, sc1, g1, sh2, sc2, g2 = jnp.split(mod, 6, axis=-1)
        u = rmsnorm(h, norm_mix_w[l]) * (1.0 + sc1) + sh1
        proj = u @ w_in[l]
        (p_pool, p_gb, p_gc, p_h, p_z, p_xbc, p_dt, p_s5) = jnp.split(proj, IN_SPLIT_IDX, axis=-1)
        y_a = pool_mixer(p_pool, pool_w[l], pool_scale[l])
        y_b = short_conv_mixer(p_gb, p_gc, p_h, sconv_w[l])
        y_c = ssd_mixer(p_z, p_xbc, p_dt, ssd_conv_w[l], ssd_conv_b[l],
                        ssd_dt_bias[l], ssd_a_log[l], ssd_d[l])
        y_d = s5_mixer(p_s5, s5_a_re[l], s5_a_im[l], s5_log_step[l], s5_b_re[l],
                       s5_b_im[l], s5_c_re[l], s5_c_im[l], s5_d[l], s5_glu_w[l], s5_glu_b[l])
        groups = jnp.stack([y_a.astype(dtype), y_b.astype(dtype),
                            y_c.astype(dtype), y_d.astype(dtype)], axis=2)
        groups = rmsnorm(groups, branch_norm_w[l].reshape(4, GROUP_WIDTH)).reshape(b, s, D_MIX)
        h = h + g1 * (groups @ w_out[l])
        v = rmsnorm(h, norm_mlp_w[l]) * (1.0 + sc2) + sh2
        h = h + g2 * (jnp.square(jax.nn.relu(v @ mlp_w1[l])) @ mlp_w2[l])
    return rmsnorm(h, final_norm_w)
```

```python
import numpy as np
import concourse.bass as bass
import concourse.mybir as mybir
from concourse.bass_utils import run_bass_kernel_spmd

F32, BF16 = mybir.dt.float32, mybir.dt.bfloat16
AF = mybir.ActivationFunctionType
ALU = mybir.AluOpType

T = 2048
TB = 512
NTB = 4
NCH = 16
DM = 1024
KC = 8
SEQ = 8192
DEPTH = 2
EPS = 1e-6
ENGS = ("pe", "act", "dve", "pool", "sp")
NDMA = 12

PCOL = {}
_off = 0
for _n, _w in [("nw1", 8), ("nw2", 8), ("bnw", 8), ("fnw", 8), ("adab", 48), ("pscale", 2), ("scw", 6),
               ("cvw", 24), ("cvb", 6), ("dtb", 4), ("alog", 4), ("dsk", 4), ("are", 8), ("aim", 8), ("lst", 8),
               ("s5d", 2), ("glub", 2), ("cond", 8),
               ("invc", 32), ("sel", 8)]:
    PCOL[_n] = (_off, _w)
    _off += _w
NPCOL = _off


class Prog:
    def __init__(self):
        self.q = {e: [] for e in ENGS}
        self.cnt = {e: 0 for e in ENGS}
        self.seen = {e: {} for e in ENGS}
        self.lastw = {}
        self.rd = {}
        self.dma_i = 0
        self.fam = {}
        import os as _os
        self.nosame = set(x for x in _os.environ.get("KNOSAME", "").split(",") if x)
        self.total = 0
        import os
        self.limit = int(os.environ.get("KLIMIT", "0")) or None

    def _skip(self):
        self.total += 1
        return self.limit is not None and self.total > self.limit

    def _deps(self, eng, reads, writes):
        need = {}

        def add(d):
            if d is None:
                return
            s, v = d
            if need.get(s, 0) < v:
                need[s] = v
        for k0 in reads:
            for k in self._rel(k0):
                add(self.lastw.get(k))
        for k0 in writes:
            for k in self._rel(k0):
                add(self.lastw.get(k))
                for s, v in self.rd.get(k, {}).items():
                    add((s, v))
        out = []
        for s, v in need.items():
            if s == eng and (eng == "pe" or eng in self.nosame):
                continue
            if self.seen[eng].get(s, 0) >= v:
                continue
            self.seen[eng][s] = v
            out.append((s, v))
        return out

    def _rel(self, k):
        if isinstance(k, tuple) and k[0] == "ps":
            fam = self.fam.setdefault(k[1], set())
            fam.add(k)
            if len(k) == 2:
                return list(fam)
            return [k, ("ps", k[1])]
        return [k]

    def _commit(self, reads, writes, tok):
        s, v = tok
        for k in reads:
            d = self.rd.setdefault(k, {})
            if d.get(s, 0) < v:
                d[s] = v
        for k in writes:
            self.lastw[k] = tok
            self.rd[k] = {}

    @staticmethod
    def _norm(reads, writes):
        r2, w2 = [], list(writes)
        for k in reads:
            if isinstance(k, tuple) and k[0] == "ps":
                w2.append(k)
            else:
                r2.append(k)
        w2 = [("ps", k[1]) if (isinstance(k, tuple) and k[0] == "ps") else k for k in w2]
        return r2, w2

    def op(self, eng, fn, reads=(), writes=()):
        if self._skip():
            return
        reads, writes = self._norm(reads, writes)
        waits = self._deps(eng, reads, writes)
        self.cnt[eng] += 1
        self.q[eng].append((waits, fn, (eng, 1)))
        self._commit(reads, writes, (eng, self.cnt[eng]))

    def dma(self, eng, fn, reads=(), writes=()):
        if self._skip():
            return None
        reads = list(reads)
        writes = list(writes)
        i = self.dma_i
        self.dma_i += 1
        sem = "dma%d" % (i % NDMA)
        val = 16 * (i // NDMA + 1)
        waits = self._deps(eng, reads, writes)
        if i >= NDMA:
            prev = 16 * (i // NDMA)
            if self.seen[eng].get(sem, 0) < prev:
                self.seen[eng][sem] = prev
                waits.append((sem, prev))
        self.q[eng].append((waits, fn, (sem, 16)))
        self._commit(reads, writes, (sem, val))
        return (sem, val)

    def cc(self, fn, sem, reads=(), writes=()):
        if self._skip():
            return
        reads, writes = self._norm(reads, writes)
        waits = self._deps("pool", reads, writes)
        self.q["pool"].append((waits, fn, (sem, 1)))
        self._commit(reads, writes, (sem, 1))

    def wait_all(self, eng):
        waits = []
        for e in ENGS:
            if e != eng and self.cnt[e] > self.seen[eng].get(e, 0):
                self.seen[eng][e] = self.cnt[e]
                waits.append((e, self.cnt[e]))
        for j in range(min(NDMA, self.dma_i)):
            sem = "dma%d" % j
            n = (self.dma_i - 1 - j) // NDMA + 1
            if self.seen[eng].get(sem, 0) < 16 * n:
                self.seen[eng][sem] = 16 * n
                waits.append((sem, 16 * n))
        self.q[eng].append((waits, None, None))


class Buf:
    def __init__(self, name, ap):
        self.name = name
        self.ap = ap

    def k(self, *idx):
        return (self.name,) + tuple(idx)


def build(depth=DEPTH, dbg=None, stop_after=None):
    nseg = 1
    nc = bass.Bass("TRN2", target_bir_lowering=False)
    P = Prog()
    dram = {}

    def din(name, shape):
        dram[name] = nc.dram_tensor(name, list(shape), F32, kind="ExternalInput").ap()
        return dram[name]

    xT = din("xT", [DM, T])
    pp = din("pp", [DEPTH, 128, NPCOL])
    cmat = din("cmat", [128, 4, 128])
    ada_w = din("ada_w", [DEPTH, DM, 1536])
    w_in = din("w_in", [DEPTH, DM, 2308])
    w_out = din("w_out", [DEPTH, DM, DM])
    w1 = din("mlp_w1", [DEPTH, DM, 4 * DM])
    w2 = din("mlp_w2", [DEPTH, 4 * DM, DM])
    poolw = din("poolw", [DEPTH, 128, 2, 128])
    gluw = din("gluw", [DEPTH, 128, 2, 256])
    s5b = din("s5b", [DEPTH, 128, 256])
    s5c = din("s5c", [DEPTH, 128, 512])
    yT = nc.dram_tensor("yT", [DM, T], F32, kind="ExternalOutput").ap()
    GRP = [[0, 1, 2, 3], [4, 5, 6, 7]]
    ccd = {}
    for l_ in range(DEPTH):
        for nm_, w_ in (("h", 64), ("x", 16), ("s", 272), ("m", 32)):
            ccd[(nm_, l_, "i")] = nc.dram_tensor("cc%s%di" % (nm_, l_), [128, w_], F32, kind="Internal").ap()
            ccd[(nm_, l_, "o")] = nc.dram_tensor("cc%s%do" % (nm_, l_), [4 * 128, w_], F32, kind="Internal").ap()
    dbg_out = {}
    if dbg:
        for name, shape in dbg.items():
            dbg_out[name] = nc.dram_tensor("dbg_" + name, list(shape), F32, kind="ExternalOutput").ap()

    import contextlib
    es = contextlib.ExitStack()
    with es:
        def sb(name, shape, dt):
            return es.enter_context(nc.sbuf_tensor(name, list(shape), dt))

        hT = sb("hT", [128, KC, T], F32)
        uT = sb("uT", [128, KC, T], BF16)
        Yr = sb("Yr", [128, KC, T], BF16)
        AR = sb("AR", [128, 15360], F32)
        cm = sb("cm", [128, 4, 128], BF16)
        ppS = sb("ppS", [128, DEPTH, NPCOL], F32)
        modT = sb("modT", [128, DEPTH, 48], F32)
        der = sb("der", [128, DEPTH, 64], F32)
        condb = sb("condb", [128, 8], BF16)
        epsT = sb("epsT", [128, 2], F32)
        poolwS = sb("poolwS", [128, DEPTH, 2, 128], BF16)
        gluwS = sb("gluwS", [128, DEPTH, 2, 256], BF16)
        cPool = sb("cPool", [128, DEPTH, 2, 16], F32)
        cSc = sb("cSc", [128, DEPTH, 2, 2], F32)
        cCv = sb("cCv", [128, DEPTH, 6, 3], F32)
        cS = sb("cS", [128, DEPTH, 256], F32)
        cX = sb("cX", [128, DEPTH, 8, 2], F32)
        ssdp = sb("ssdp", [128, DEPTH, 16], F32)
        s5pw = sb("s5pw", [128, DEPTH, 8, 3, 16], F32)
        s5lv = sb("s5lv", [128, DEPTH, 8, 3, 8], F32)
        bbS = sb("bbS", [128, DEPTH, 2, 8, 16], F32)
        tiny = sb("tiny", [128, 896], F32)
        tinyb = sb("tinyb", [128, 128], BF16)

        ps = [es.enter_context(nc.psum_tensor("ps%d" % i, [128, 512], F32)) for i in range(8)]
        sems = {}
        for e in ENGS:
            sems[e] = es.enter_context(nc.semaphore("s_" + e))
        for j in range(NDMA):
            sems["dma%d" % j] = es.enter_context(nc.semaphore("s_dma%d" % j))
        for l_ in range(DEPTH):
            for nm_ in ("h", "x", "s", "m"):
                sems["cc%s%d" % (nm_, l_)] = es.enter_context(nc.semaphore("s_cc%s%d" % (nm_, l_)))

        def allgather(nm_, l_, reads, writes):
            i_, o_ = ccd[(nm_, l_, "i")], ccd[(nm_, l_, "o")]
            P.cc(lambda e: e.collective_compute("AllGather", ALU.bypass, replica_groups=GRP, ins=[i_], outs=[o_]),
                 "cc%s%d" % (nm_, l_), reads, writes)

        ARb = AR[:, :].bitcast(BF16)

        def arf(off, n):
            return AR[:, off:off + n]

        def arb(off, n):
            return ARb[:, 2 * off:2 * (off + n)]

        def ark(off, n):
            return [("ar", b) for b in range(off // 256, (off + n + 255) // 256)]

        ident = cm[:, 0, :]
        ones = cm[:, 1, :]
        tri = cm[:, 2, :]
        negm = cm[:, 3, :]

        def pcol(l, name, a=0, b=None):
            o, w = PCOL[name]
            if b is None:
                b = w
            return ppS[:, l, o + a:o + b]

        def mm(out, lhsT, rhs, start, stop, reads, writes):
            P.op("pe", lambda e: e.matmul(out, lhsT=lhsT, rhs=rhs, start=start, stop=stop), reads, writes)

        def act(out, in_, func, reads, writes, bias=None, scale=None):
            kw = {}
            if bias is not None:
                kw["bias"] = bias
            if scale is not None:
                kw["scale"] = scale
            P.op("act", lambda e: e.activation(out=out, in_=in_, func=func, **kw), reads, writes)

        def tt(out, in0, in1, op, reads, writes, eng="dve"):
            P.op(eng, lambda e: e.tensor_tensor(out=out, in0=in0, in1=in1, op=op), reads, writes)

        def ts(out, in0, s1, s2, op0, op1, reads, writes, eng="dve"):
            if s2 is None:
                P.op(eng, lambda e: e.tensor_scalar(out=out, in0=in0, scalar1=s1, scalar2=None, op0=op0), reads, writes)
            else:
                P.op(eng, lambda e: e.tensor_scalar(out=out, in0=in0, scalar1=s1, scalar2=s2, op0=op0, op1=op1), reads, writes)

        def stt(out, in0, scalar, in1, op0, op1, reads, writes, eng="dve"):
            P.op(eng, lambda e: e.scalar_tensor_tensor(out=out, in0=in0, scalar=scalar, in1=in1, op0=op0, op1=op1), reads, writes)

        def cp(out, in_, reads, writes, eng="dve"):
            P.op(eng, lambda e: e.tensor_copy(out=out, in_=in_), reads, writes)

        def mset(ap, val, writes, eng="dve"):
            P.op(eng, lambda e: e.memset(ap, val), [], writes)

        def dma(eng, out, in_, reads, writes):
            return P.dma(eng, lambda e: e.dma_start(out=out, in_=in_), reads, writes)

        dma("pool", cm[:, :, :], cmat, [], ["cm"])
        dma("sp", ppS[:, :, :], pp.rearrange("l p c -> p l c"), [], ["pp"])
        dma("pool", poolwS[:, :, :, :], poolw.rearrange("l p c d -> p l c d"), [], ["poolw"])
        dma("pool", gluwS[:, :, :, :], gluw.rearrange("l p c d -> p l c d"), [], ["gluw"])
        mset(epsT[:, 0:1], EPS, ["eps"])
        mset(epsT[:, 1:2], 1.0, ["eps"])
        for t_, kk in ((cPool, "cPool"), (cSc, "cSc"), (cCv, "cCv"), (cS, "cS"), (cX, "cX")):
            mset(t_[:], 0.0, [kk])
        act(condb[:, :], pcol(0, "cond"), AF.Silu, ["pp"], ["condb"])

        WADA = 0
        bi = 0
        for l in range(DEPTH):
            for blk in range(3):
                slot = bi % 2
                bi += 1
                wsl = arb(WADA + slot * 2048, 2048).rearrange("p (k c) -> p k c", k=8)
                wk = ark(WADA + slot * 2048, 2048)
                dma("pool", wsl, ada_w[l, :, blk * 512:(blk + 1) * 512].rearrange("(k p) c -> p k c", p=128), [], wk)
                for jj in range(4):
                    j = l * 12 + blk * 4 + jj
                    for k in range(KC):
                        mm(ps[6][:, j:j + 1], wsl[:, k, jj * 128:(jj + 1) * 128], condb[:, k:k + 1], k == 0, k == KC - 1,
                           wk + ["condb"], [("ps", 6)])
        mpk = tiny[:, 0:24].rearrange("p (l c) -> p l c", l=2)
        tyk0 = [("tiny", "all")]
        for l in range(DEPTH):
            tt(mpk[:, l, :], ps[6][:, l * 12:(l + 1) * 12], pcol(l, "adab", 0, 12), ALU.add, [("ps", 6), "pp"], tyk0)
        dma("sp", ccd[("m", 0, "i")][:, 0:24], tiny[:, 0:24], tyk0, [("cci", "m", 0)])
        allgather("m", 0, [("cci", "m", 0)], [("cco", "m", 0)])
        mrb = tiny[:, 32:160].rearrange("p (r f) -> p r f", r=4)
        dma("sp", mrb, ccd[("m", 0, "o")].rearrange("(r p) f -> p r f", p=128), [("cco", "m", 0)], tyk0)
        for l in range(DEPTH):
            for sg_ in range(4):
                cp(modT[:, l, sg_ * 12:(sg_ + 1) * 12], mrb[:, sg_, l * 12:(l + 1) * 12], tyk0, [("mod", l)])
        for l in range(depth):
            stt(der[:, l, 0:8], modT[:, l, 8:16], 1.0, pcol(l, "nw1"), ALU.add, ALU.mult, [("mod", l), "pp"], [("der", l)])
            stt(der[:, l, 24:32], modT[:, l, 32:40], 1.0, pcol(l, "nw2"), ALU.add, ALU.mult, [("mod", l), "pp"], [("der", l)])
            cp(der[:, l, 8:16], modT[:, l, 0:8], [("mod", l)], [("der", l)])
            cp(der[:, l, 16:24], modT[:, l, 16:24], [("mod", l)], [("der", l)])
            cp(der[:, l, 32:40], modT[:, l, 24:32], [("mod", l)], [("der", l)])
            cp(der[:, l, 40:48], modT[:, l, 40:48], [("mod", l)], [("der", l)])

        def S1(l, k): return der[:, l, 0 + k:1 + k]
        def B1(l, k): return der[:, l, 8 + k:9 + k]
        def G1(l, k): return der[:, l, 16 + k:17 + k]
        def S2(l, k): return der[:, l, 24 + k:25 + k]
        def B2(l, k): return der[:, l, 32 + k:33 + k]
        def G2(l, k): return der[:, l, 40 + k:41 + k]

        for l in range(depth):
            act(ssdp[:, l, 0:4], pcol(l, "alog"), AF.Exp, ["pp"], [("ssdp", l)])
            ts(ssdp[:, l, 0:4], ssdp[:, l, 0:4], -1.0, None, ALU.mult, None, [("ssdp", l)], [("ssdp", l)])
            cp(ssdp[:, l, 4:8], pcol(l, "dtb"), ["pp"], [("ssdp", l)])
            cp(ssdp[:, l, 8:12], pcol(l, "dsk"), ["pp"], [("ssdp", l)])

        def tk(n): return [("tiny", n)]
        for l in range(depth):
            tyk = [("tiny", "all")]
            stp = tiny[:, 0:8]
            act(stp, pcol(l, "lst"), AF.Exp, ["pp"], tyk)
            mag = tiny[:, 8:16]
            tt(mag, pcol(l, "are"), stp, ALU.mult, ["pp"] + tyk, tyk)
            act(mag, mag, AF.Exp, tyk, tyk)
            th = tiny[:, 16:24]
            tt(th, pcol(l, "aim"), stp, ALU.mult, ["pp"] + tyk, tyk)
            sa = tiny[:, 24:32]
            ca = tiny[:, 32:40]
            act(sa, th, AF.Sin, tyk, tyk, scale=1.0 / 16.0)
            ts(ca, th, 1.0 / 16.0, float(np.pi / 2), ALU.mult, ALU.add, tyk, tyk)
            act(ca, ca, AF.Sin, tyk, tyk)
            for _ in range(4):
                t2a, t2b = tiny[:, 272:280], tiny[:, 280:288]
                tt(t2a, ca, ca, ALU.mult, tyk, tyk)
                tt(t2b, sa, sa, ALU.mult, tyk, tyk)
                tt(sa, sa, ca, ALU.mult, tyk, tyk)
                ts(sa, sa, 2.0, None, ALU.mult, None, tyk, tyk)
                tt(ca, t2a, t2b, ALU.subtract, tyk, tyk)
            lr = tiny[:, 40:48]
            li = tiny[:, 48:56]
            tt(lr, mag, ca, ALU.mult, tyk, tyk)
            tt(li, mag, sa, ALU.mult, tyk, tyk)
            den = tiny[:, 56:64]
            t0 = tiny[:, 64:72]
            tt(den, pcol(l, "are"), pcol(l, "are"), ALU.mult, ["pp"] + tyk, tyk)
            tt(t0, pcol(l, "aim"), pcol(l, "aim"), ALU.mult, ["pp"] + tyk, tyk)
            tt(den, den, t0, ALU.add, tyk, tyk)
            P.op("dve", lambda e, den=den: e.reciprocal(out=den, in_=den), tyk, tyk)
            nr = tiny[:, 72:80]
            ts(nr, lr, -1.0, None, ALU.add, None, tyk, tyk)
            fr = tiny[:, 80:88]
            fi = tiny[:, 88:96]
            t1 = tiny[:, 96:104]
            tt(fr, nr, pcol(l, "are"), ALU.mult, ["pp"] + tyk, tyk)
            tt(t1, li, pcol(l, "aim"), ALU.mult, ["pp"] + tyk, tyk)
            tt(fr, fr, t1, ALU.add, tyk, tyk)
            tt(fr, fr, den, ALU.mult, tyk, tyk)
            tt(fi, li, pcol(l, "are"), ALU.mult, ["pp"] + tyk, tyk)
            tt(t1, nr, pcol(l, "aim"), ALU.mult, ["pp"] + tyk, tyk)
            tt(fi, fi, t1, ALU.subtract, tyk, tyk)
            tt(fi, fi, den, ALU.mult, tyk, tyk)
            dma("sp", tiny[:, 512:768], s5b[l], [], tyk)
            bre = tiny[:, 512:640].rearrange("p (r h) -> p r h", r=8)
            bim = tiny[:, 640:768].rearrange("p (r h) -> p r h", r=8)
            frb = fr.unsqueeze(2).to_broadcast([128, 8, 16])
            fib = fi.unsqueeze(2).to_broadcast([128, 8, 16])
            tmpb = tiny[:, 128:256].rearrange("p (r h) -> p r h", r=8)
            tt(bbS[:, l, 0, :, :], bre, frb, ALU.mult, ["pp"] + tyk, [("bb", l)])
            tt(tmpb, bim, fib, ALU.mult, ["pp"] + tyk, tyk)
            tt(bbS[:, l, 0, :, :], bbS[:, l, 0, :, :], tmpb, ALU.subtract, tyk + [("bb", l)], [("bb", l)])
            tt(bbS[:, l, 1, :, :], bim, frb, ALU.mult, ["pp"] + tyk, [("bb", l)])
            tt(tmpb, bre, fib, ALU.mult, ["pp"] + tyk, tyk)
            tt(bbS[:, l, 1, :, :], bbS[:, l, 1, :, :], tmpb, ALU.add, tyk + [("bb", l)], [("bb", l)])
            ts(bbS[:, l, 1, :, :], bbS[:, l, 1, :, :], -1.0, None, ALU.mult, None, [("bb", l)], [("bb", l)])
            ts(li, li, -1.0, None, ALU.mult, None, tyk, tyk)
            pk = [("s5pw", l)]
            cp(s5pw[:, l, :, 0, 0], lr, tyk, pk)
            cp(s5pw[:, l, :, 1, 0], li, tyk, pk)
            for j in range(1, 16):
                pr_, pi_ = s5pw[:, l, :, 0, j - 1], s5pw[:, l, :, 1, j - 1]
                nr_, ni_ = s5pw[:, l, :, 0, j], s5pw[:, l, :, 1, j]
                ta, tb_ = tiny[:, 256:264], tiny[:, 264:272]
                tt(ta, pr_, lr, ALU.mult, tyk + pk, tyk)
                tt(tb_, pi_, li, ALU.mult, tyk + pk, tyk)
                tt(nr_, ta, tb_, ALU.subtract, tyk, pk)
                tt(ta, pr_, li, ALU.mult, tyk + pk, tyk)
                tt(tb_, pi_, lr, ALU.mult, tyk + pk, tyk)
                tt(ni_, ta, tb_, ALU.add, tyk, pk)
            ts(s5pw[:, l, :, 2, :], s5pw[:, l, :, 1, :], -1.0, None, ALU.mult, None, pk, pk)
            lk = [("s5lv", l)]
            cp(s5lv[:, l, :, 0, 0], s5pw[:, l, :, 0, 15], pk, lk)
            cp(s5lv[:, l, :, 1, 0], s5pw[:, l, :, 1, 15], pk, lk)
            for k in range(1, 8):
                pr_, pi_ = s5lv[:, l, :, 0, k - 1], s5lv[:, l, :, 1, k - 1]
                ta, tb_ = tiny[:, 256:264], tiny[:, 264:272]
                tt(ta, pr_, pr_, ALU.mult, lk + tyk, tyk)
                tt(tb_, pi_, pi_, ALU.mult, lk + tyk, tyk)
                tt(s5lv[:, l, :, 0, k], ta, tb_, ALU.subtract, tyk, lk)
                tt(ta, pr_, pi_, ALU.mult, lk + tyk, tyk)
                ts(s5lv[:, l, :, 1, k], ta, 2.0, None, ALU.mult, None, tyk, lk)
            ts(s5lv[:, l, :, 2, :], s5lv[:, l, :, 1, :], -1.0, None, ALU.mult, None, lk, lk)

        acc_i = [0]

        def next_acc():
            b = acc_i[0] % 4
            acc_i[0] += 1
            return b

        def rms_stats(src_fn, nchunks, denom, rstd_off, sq_off, src_keys_fn, tbs=None):
            for tb in (range(NTB) if tbs is None else tbs):
                bank = 4 + (tb % 2)
                for k in range(nchunks):
                    so = sq_off + ((tb * nchunks + k) % 4) * 256
                    sq = arb(so, 256)
                    src = src_fn(k, tb)
                    tt(sq, src, src, ALU.mult, src_keys_fn(k, tb), ark(so, 256), eng="pool")
                    mm(ps[bank][:, :], ones, sq, k == 0, k == nchunks - 1, ark(so, 256) + ["cm"], [("ps", bank)])
                r = arf(rstd_off + tb * TB, TB)
                rk = ark(rstd_off + tb * TB, TB)
                act(r, ps[bank][:, :], AF.Ln, [("ps", bank), "eps"], rk, bias=epsT[:, 0:1], scale=1.0 / denom)
                act(r, r, AF.Exp, rk, rk, scale=-0.5)

        def norm_mod(l, s_fn, b_fn):
            RSTD, SQ, TMP = 0, 2048, 3072
            for tb in range(NTB):
                rms_stats(lambda k, tb: hT[:, k, tb * TB:(tb + 1) * TB], KC, float(DM), RSTD, SQ,
                          lambda k, tb: [("hT", k, tb)], tbs=[tb])
                for k in range(KC):
                    to = TMP + ((k * NTB + tb) % 4) * TB
                    tmp = arf(to, TB)
                    stt(tmp, hT[:, k, tb * TB:(tb + 1) * TB], s_fn(l, k), arf(RSTD + tb * TB, TB), ALU.mult, ALU.mult,
                        [("hT", k, tb), ("der", l)] + ark(RSTD + tb * TB, TB), ark(to, TB))
                    act(uT[:, k, tb * TB:(tb + 1) * TB], tmp, AF.Identity, ark(to, TB) + [("der", l)], [("uT", k, tb)],
                        bias=b_fn(l, k))

        WIN = 14336
        win_i = [0]

        def load_win(l, c0, ncol):
            assert ncol <= 128
            slot = win_i[0] % 2
            win_i[0] += 1
            off = WIN + slot * 512
            w = arb(off, 512).rearrange("p (k c) -> p k c", k=8)
            dma("pool", w[:, :, 0:ncol], w_in[l, :, c0:c0 + ncol].rearrange("(k p) c -> p k c", p=128), [], ark(off, 512))
            return w, ark(off, 512)

        def proj_chunk(w, wk, cofs, ncol, evac):
            for tb in range(NTB):
                b = next_acc()
                for k in range(KC):
                    mm(ps[b][0:ncol, :], w[:, k, cofs:cofs + ncol], uT[:, k, tb * TB:(tb + 1) * TB], k == 0, k == KC - 1,
                       wk + [("uT", k, tb)], [("ps", b)])
                evac(tb, ps[b][0:ncol, :], ("ps", b))

        def dump(name, ap, keys):
            if dbg and name in dbg_out:
                dma("sp", dbg_out[name], ap, keys, [("dbg", name)])

        def all_keys_h():
            return [("hT", k, tb) for k in range(KC) for tb in range(NTB)]

        def all_keys(nm):
            return [(nm, k, tb) for k in range(KC) for tb in range(NTB)]

        def tail_prepass(l):
            PK = arf(0, 64)
            pkk = ark(0, 64)
            gct = tiny[:, 0:32].rearrange("p (c t) -> p c t", c=2)
            tyk = [("tiny", "all")]
            tail = slice(T - 16, T)

            def tproj(w, wk, cofs, evac):
                b = next_acc()
                for k in range(KC):
                    mm(ps[b][:, 0:16], w[:, k, cofs:cofs + 128], uT[:, k, tail], k == 0, k == KC - 1, wk + [("uT", k, 3)], [("ps", b)])
                evac(ps[b][:, 0:16], ("ps", b))
            for c in range(2):
                w, wk = load_win(l, c * 128, 128)
                tproj(w, wk, 0, lambda p_, pk, c=c: act(PK[:, c * 16:(c + 1) * 16], p_, AF.Copy, [pk], pkk))
            for c in range(2):
                w, wk = load_win(l, 512 + c * 128, 128)
                tproj(w, wk, 0, lambda p_, pk, c=c: act(gct[:, c, :], p_, AF.Copy, [pk], tyk))
            for c in range(2):
                w, wk = load_win(l, 768 + c * 128, 128)
                tproj(w, wk, 0, lambda p_, pk, c=c: tt(PK[:, 32 + 2 * c:34 + 2 * c], p_[:, 14:16], gct[:, c, 14:16], ALU.mult,
                                                       [pk] + tyk, pkk))
            for j in range(6):
                w, wk = load_win(l, 1280 + j * 128, 128)
                tproj(w, wk, 0, lambda p_, pk, j=j: act(PK[:, 36 + 3 * j:39 + 3 * j], p_[:, 13:16], AF.Copy, [pk], pkk))
            dma("sp", ccd[("h", l, "i")][:, 0:54], PK[:, 0:54], pkk, [("cci", "h", l)])
            allgather("h", l, [("cci", "h", l)], [("cco", "h", l)])

        def halo_apply(l):
            RB = arf(256, 256).rearrange("p (r f) -> p r f", r=4)
            rbk = ark(256, 256)
            tyk = [("tiny", "all")]
            dma("sp", RB, ccd[("h", l, "o")].rearrange("(r p) f -> p r f", p=128), [("cco", "h", l)], rbk)
            hal = tiny[:, 64:128]
            sel = pcol(l, "sel")
            ts(hal, RB[:, 0, :], sel[:, 0:1], None, ALU.mult, None, rbk + ["pp"], tyk)
            for j in range(1, 4):
                stt(hal, RB[:, j, :], sel[:, j:j + 1], hal, ALU.mult, ALU.add, rbk + ["pp"] + tyk, tyk)
            cp(cPool[:, l, :, :], hal[:, 0:32].rearrange("p (c t) -> p c t", c=2), tyk, [("cPool", l, 0), ("cPool", l, 1)])
            cp(cSc[:, l, :, :], hal[:, 32:36].rearrange("p (c t) -> p c t", c=2), tyk, [("cSc", l, 0), ("cSc", l, 1)])
            cp(cCv[:, l, :, :], hal[:, 36:54].rearrange("p (c t) -> p c t", c=6), tyk, [("cCv", l, j) for j in range(6)])

        def s5_combine(l):
            tyk = [("tiny", "all")]
            RB = tiny[:, 816:880].rearrange("p (r f) -> p r f", r=4)
            dma("sp", RB, ccd[("x", l, "o")].rearrange("(r p) f -> p r f", p=128), [("cco", "x", l)], tyk)
            sel = pcol(l, "sel")
            Lr, Li = s5lv[:, l, :, 0, 7], s5lv[:, l, :, 1, 7]
            lk = [("s5lv", l)]
            ar, ai, pr, pi, t1, t2 = (tiny[:, a:a + 8] for a in (768, 776, 784, 792, 880, 888))
            for t_ in (ar, ai, pr, pi):
                mset(t_, 0.0, tyk)
            for j in range(4):
                stt(ar, pr, sel[:, 4 + j:5 + j], ar, ALU.mult, ALU.add, tyk + ["pp"], tyk)
                stt(ai, pi, sel[:, 4 + j:5 + j], ai, ALU.mult, ALU.add, tyk + ["pp"], tyk)
                if j < 3:
                    F = RB[:, j, :].rearrange("p (r a) -> p r a", a=2)
                    tt(t1, pr, Lr, ALU.mult, tyk + lk, tyk)
                    tt(t2, pi, Li, ALU.mult, tyk + lk, tyk)
                    tt(t1, t1, t2, ALU.subtract, tyk, tyk)
                    tt(t2, pr, Li, ALU.mult, tyk + lk, tyk)
                    tt(pr, t1, F[:, :, 0], ALU.add, tyk, tyk)
                    tt(t1, pi, Lr, ALU.mult, tyk + lk, tyk)
                    tt(t1, t1, t2, ALU.add, tyk, tyk)
                    tt(pi, t1, F[:, :, 1], ALU.add, tyk, tyk)
            cp(cX[:, l, :, 0], ar, tyk, [("cX", l, r) for r in range(8)])
            cp(cX[:, l, :, 1], ai, tyk, [("cX", l, r) for r in range(8)])

        def mixer_pool(l, seg):
            V, SA, SB, PB = 0, 2304, 4608, 6912
            for c in range(2):
                w, wk = load_win(l, c * 128, 128)
                v = arf(V, 2064)
                vk = ark(V, 2064)
                cp(v[:, 0:16], cPool[:, l, c, :], [("cPool", l, c)], vk)
                proj_chunk(w, wk, 0, 128,
                           lambda tb, p_, pk: act(v[:, 16 + tb * TB:16 + (tb + 1) * TB], p_, AF.Copy, [pk], vk))
                cp(cPool[:, l, c, :], v[:, 2048:2064], vk, [("cPool", l, c)])
                sa, sbb = arf(SA, 2064), arf(SB, 2064)
                sak, sbk = ark(SA, 2064), ark(SB, 2064)
                tt(sa[:, 1:2064], v[:, 1:2064], v[:, 0:2063], ALU.add, vk, sak)
                tt(sbb[:, 3:2064], sa[:, 3:2064], sa[:, 1:2062], ALU.add, sak, sbk)
                if c == 0:
                    lo_src, hi_src, lo_w, hi_w = sa, sbb, 2, 4
                else:
                    tt(sa[:, 7:2064], sbb[:, 7:2064], sbb[:, 3:2060], ALU.add, sbk, sak)
                    tt(sbb[:, 15:2064], sa[:, 15:2064], sa[:, 7:2056], ALU.add, sak, sbk)
                    lo_src, hi_src, lo_w, hi_w = sa, sbb, 8, 16
                pb = arb(PB, 2048).rearrange("p (c t) -> p c t", c=2)
                pbk = ark(PB, 2048)
                stt(pb[0:64, c, :], lo_src[0:64, 16:2064], 1.0 / lo_w, v[0:64, 16:2064], ALU.mult, ALU.subtract, sak + sbk + vk, pbk)
                stt(pb[64:128, c, :], hi_src[64:128, 16:2064], 1.0 / hi_w, v[64:128, 16:2064], ALU.mult, ALU.subtract, sak + sbk + vk, pbk)
                if True:
                    ic = pcol(l, "invc").rearrange("p (c t) -> p c t", c=2)
                    tq = tiny[:, 512:528]
                    for (r0, r1, src) in ((0, 64, lo_src), (64, 128, hi_src)):
                        tt(tq[r0:r1, :], src[r0:r1, 16:32], ic[r0:r1, c, :], ALU.mult, sak + sbk + ["pp"], [("tiny", "all")])
                        tt(pb[r0:r1, c, 0:16], tq[r0:r1, :], v[r0:r1, 16:32], ALU.subtract, [("tiny", "all")] + vk, pbk)
            pb = arb(PB, 2048).rearrange("p (c t) -> p c t", c=2)
            pbk = ark(PB, 2048)
            for c in range(2):
                for tb in range(NTB):
                    b = next_acc()
                    mm(ps[b][:, :], poolwS[:, l, c, :], pb[:, c, tb * TB:(tb + 1) * TB], True, True, pbk + ["poolw"], [("ps", b)])
                    ts(Yr[:, c, tb * TB:(tb + 1) * TB], ps[b][:, :], pcol(l, "pscale", c, c + 1), None, ALU.mult, None,
                       [("ps", b), "pp"], [("Yr", c, tb)])

        def mixer_sconv(l, seg):
            GC, G, T1 = 0, 2048, 4352
            for c in range(2):
                gc = arf(GC, 2048)
                gck = ark(GC, 2048)
                wgc, wgck = load_win(l, 512 + c * 128, 128)
                proj_chunk(wgc, wgck, 0, 128,
                           lambda tb, p_, pk: act(gc[:, tb * TB:(tb + 1) * TB], p_, AF.Copy, [pk], gck))
                whh, whhk = load_win(l, 768 + c * 128, 128)
                g = arf(G, 2050)
                gk = ark(G, 2050)
                cp(g[:, 0:2], cSc[:, l, c, :], [("cSc", l, c)], gk)
                proj_chunk(whh, whhk, 0, 128,
                           lambda tb, p_, pk: tt(g[:, 2 + tb * TB:2 + (tb + 1) * TB], p_, gc[:, tb * TB:(tb + 1) * TB], ALU.mult,
                                                 [pk] + gck, gk))
                cp(cSc[:, l, c, :], g[:, 2048:2050], gk, [("cSc", l, c)])
                t1 = arf(T1, 2048)
                t1k = ark(T1, 2048)
                wv = pcol(l, "scw").rearrange("p (c k) -> p c k", c=2)
                ts(t1, g[:, 0:2048], wv[:, c, 0:1], None, ALU.mult, None, gk + ["pp"], t1k)
                stt(t1, g[:, 1:2049], wv[:, c, 1:2], t1, ALU.mult, ALU.add, gk + ["pp"] + t1k, t1k)
                stt(t1, g[:, 2:2050], wv[:, c, 2:3], t1, ALU.mult, ALU.add, gk + ["pp"] + t1k, t1k)
                wgb, wgbk = load_win(l, 256 + c * 128, 128)
                proj_chunk(wgb, wgbk, 0, 128,
                           lambda tb, p_, pk: tt(Yr[:, 2 + c, tb * TB:(tb + 1) * TB], p_, t1[:, tb * TB:(tb + 1) * TB], ALU.mult,
                                                 [pk] + t1k, [("Yr", 2 + c, tb)]))

        def mixer_ssd(l, seg):
            SZ, RAW, ACC, XBC = 0, 2048, 4352, 6400
            XTOK, BTOK = 2048, 4096
            sz = arb(SZ, 2048).rearrange("p (c t) -> p c t", c=2)
            szk = ark(SZ, 2048)
            for c in range(2):
                wz, wzk = load_win(l, 1024 + c * 128, 128)
                proj_chunk(wz, wzk, 0, 128,
                           lambda tb, p_, pk: act(sz[:, c, tb * TB:(tb + 1) * TB], p_, AF.Silu, [pk], szk))
            xbc = arb(XBC, 6144).rearrange("p (c t) -> p c t", c=6)
            kxb = [ark(XBC + j * 1024, 1024) for j in range(6)]
            cw = pcol(l, "cvw").rearrange("p (c k) -> p c k", c=6)
            for j in range(6):
                wx, wxk = load_win(l, 1280 + j * 128, 128)
                raw = arf(RAW, 2051)
                rawk = ark(RAW, 2051)
                cp(raw[:, 0:3], cCv[:, l, j, :], [("cCv", l, j)], rawk)
                proj_chunk(wx, wxk, 0, 128,
                           lambda tb, p_, pk: act(raw[:, 3 + tb * TB:3 + (tb + 1) * TB], p_, AF.Copy, [pk], rawk))
                cp(cCv[:, l, j, :], raw[:, 2048:2051], rawk, [("cCv", l, j)])
                acc = arf(ACC, 2048)
                acck = ark(ACC, 2048)
                ts(acc, raw[:, 0:2048], cw[:, j, 0:1], None, ALU.mult, None, rawk + ["pp"], acck)
                for kk in range(1, 4):
                    stt(acc, raw[:, kk:kk + 2048], cw[:, j, kk:kk + 1], acc, ALU.mult, ALU.add, rawk + acck + ["pp"], acck)
                act(xbc[:, j, :], acc, AF.Silu, acck + ["pp"], kxb[j], bias=pcol(l, "cvb", j, j + 1))
            print("  ssd: before dt", P.total)
            wd, wdk = load_win(l, 2048, 4)
            for c in range(NCH):
                for k in range(KC):
                    mm(ps[6][:, c * 4:(c + 1) * 4], uT[:, k, c * 128:(c + 1) * 128], wd[:, k, 0:4], k == 0, k == KC - 1,
                       wdk + [("uT", k, c // 4)], [("ps", 6)])
            print("  ssd: before small", P.total)
            tyk = [("tiny", "all")]
            def v3(a): return tiny[:, a:a + 64].rearrange("p (c h) -> p c h", h=4)
            dt_, adt, acs, tot, eacs, dte, cd, ddte = (v3(a) for a in (0, 64, 128, 192, 256, 320, 384, 448))
            xsp = v3(512)
            ex = v3(576)
            bc4 = lambda a, b: ssdp[:, l, a:b].unsqueeze(1).to_broadcast([128, NCH, 4])
            tt(xsp, ps[6][:, 0:64].rearrange("p (c h) -> p c h", h=4), bc4(4, 8), ALU.add, [("ps", 6), ("ssdp", l)], tyk)
            ts(ex, xsp, 30.0, None, ALU.min, None, tyk, tyk)
            act(ex, ex, AF.Exp, tyk, tyk)
            act(ex, ex, AF.Ln, tyk + ["eps"], tyk, bias=epsT[:, 1:2])
            tt(dt_, ex, xsp, ALU.max, tyk, tyk)
            tt(adt, dt_, bc4(0, 4), ALU.mult, tyk + [("ssdp", l)], tyk)
            ahi = tinyb[:, 0:64]
            alo = tinyb[:, 64:128]
            tbk = [("tinyb", "a")]
            adf = tiny[:, 64:128]
            cp(ahi, adf, tyk, tbk)
            tt(tiny[:, 640:704], adf, ahi, ALU.subtract, tyk + tbk, tyk)
            cp(alo, tiny[:, 640:704], tyk, tbk)
            cp(tiny[:, 768:832], ahi, tbk, tyk)
            mm(ps[6][:, 64:128], tri, ahi, True, False, tbk + ["cm"], [("ps", 6)])
            mm(ps[6][:, 64:128], tri, alo, False, True, tbk + ["cm"], [("ps", 6)])
            mm(ps[6][:, 128:192], ones, ahi, True, False, tbk + ["cm"], [("ps", 6)])
            mm(ps[6][:, 128:192], ones, alo, False, True, tbk + ["cm"], [("ps", 6)])
            cp(tiny[:, 128:256], ps[6][:, 64:192], [("ps", 6)], tyk)
            act(tiny[:, 256:320], tiny[:, 128:192], AF.Exp, tyk, tyk)
            tt(tiny[:, 320:384], tiny[:, 192:256], tiny[:, 128:192], ALU.subtract, tyk, tyk)
            act(tiny[:, 320:384], tiny[:, 320:384], AF.Exp, tyk, tyk)
            act(tiny[:, 384:448], tiny[:, 192:256], AF.Exp, tyk, tyk)
            tt(tiny[:, 448:512], tiny[:, 0:64], tiny[:, 320:384], ALU.mult, tyk, tyk)
            ts(tiny[:, 704:768], tiny[:, 128:192], -1.0, None, ALU.mult, None, tyk, tyk)
            nacs = v3(704)
            print("  ssd: before transposes", P.total)
            xtok = arb(XTOK, 2048).rearrange("p (c f) -> p c f", c=NCH)
            btok = arb(BTOK, 2048).rearrange("p (c f) -> p c f", c=NCH)
            xtk, btk = ark(XTOK, 2048), ark(BTOK, 2048)
            ti = 0
            for c in range(NCH):
                for j in range(4):
                    o = (ti % 4) * 128
                    ti += 1
                    bnk = 4 + (ti - 1) % 4
                    pk_ = ("ps", bnk)
                    mm(ps[bnk][:, 0:128], xbc[:, j, c * 128:(c + 1) * 128], ident, True, True, kxb[j] + ["cm"], [pk_])
                    dst = (xtok if j < 2 else btok)[:, c, (j % 2) * 128:(j % 2 + 1) * 128]
                    if ti % 2 == 0:
                        cp(dst, ps[bnk][:, 0:128], [pk_], xtk if j < 2 else btk)
                    else:
                        act(dst, ps[bnk][:, 0:128], AF.Copy, [pk_], xtk if j < 2 else btk)
            WT = XBC
            E_ = arb(WT, 256).rearrange("p (h s) -> p h s", h=4)
            MT = arb(WT + 256, 256).rearrange("p (h s) -> p h s", h=4)
            RH = arb(WT + 512, 512).rearrange("p (a h s) -> p a h s", a=2, h=4)
            XDT = arb(WT + 1024, 128)
            XDE = arb(WT + 1152, 128)
            YSB = arf(WT + 1280, 256)
            YTK = arb(WT + 1536, 128)
            SBF = arb(WT + 1664, 128)
            STMP = arf(WT + 1792, 256)
            kE, kMT, kRH, kXDT, kXDE, kYSB, kYTK, kSBF, kST = (ark(WT + a, n) for a, n in
                ((0, 256), (256, 256), (512, 512), (1024, 128), (1152, 128), (1280, 256), (1536, 128), (1664, 128), (1792, 256)))
            print("  ssd: before main loop", P.total)
            Sst = cS[:, l, :]
            kS = [("cS", l)]
            PKG, RBO, PTO = 12544, 12816, 13904
            pkg = arf(PKG, 272)
            pkgk = ark(PKG, 272)
            SL = pkg[:, 0:256]
            mset(SL, 0.0, pkgk)
            for c in range(NCH):
                x3 = xtok[:, c, :].rearrange("p (h d) -> p h d", h=4)
                tt(XDE.rearrange("p (h d) -> p h d", h=4), x3, ddte[:, c, :].unsqueeze(2).to_broadcast([128, 4, 64]), ALU.mult, xtk + tyk, kXDE)
                for g in range(2):
                    mm(ps[5][:, g * 128:(g + 1) * 128], btok[:, c, g * 128:(g + 1) * 128], XDE[:, g * 128:(g + 1) * 128], True, True,
                       btk + kXDE, [("ps", 5)])
                tt(STMP.rearrange("p (h d) -> p h d", h=4), SL.rearrange("p (h d) -> p h d", h=4),
                   cd[:, c, :].unsqueeze(2).to_broadcast([128, 4, 64]), ALU.mult, pkgk + tyk, kST)
                tt(SL, STMP, ps[5][:, 0:256], ALU.add, kST + [("ps", 5)], pkgk)
            tr_ = tiny[:, 832:864].rearrange("p (c h) -> p c h", h=4)
            tt(tr_, tot[:, 0:8, :], tot[:, 8:16, :], ALU.add, tyk, tyk)
            tt(tr_[:, 0:4, :], tr_[:, 0:4, :], tr_[:, 4:8, :], ALU.add, tyk, tyk)
            tt(tr_[:, 0:2, :], tr_[:, 0:2, :], tr_[:, 2:4, :], ALU.add, tyk, tyk)
            tt(tr_[:, 0:1, :], tr_[:, 0:1, :], tr_[:, 1:2, :], ALU.add, tyk, tyk)
            act(pkg[:, 256:260], tiny[:, 832:836], AF.Exp, tyk, pkgk)
            dma("sp", ccd[("s", l, "i")][:, 0:260], pkg[:, 0:260], pkgk, [("cci", "s", l)])
            allgather("s", l, [("cci", "s", l)], [("cco", "s", l)])
            RBs = arf(RBO, 1088).rearrange("p (r f) -> p r f", r=4)
            rbsk = ark(RBO, 1088)
            dma("sp", RBs, ccd[("s", l, "o")].rearrange("(r p) f -> p r f", p=128), [("cco", "s", l)], rbsk)
            PT = arf(PTO, 256)
            ptk = ark(PTO, 256)
            sel = pcol(l, "sel")
            mset(Sst, 0.0, kS)
            mset(PT, 0.0, ptk)
            for j in range(4):
                stt(Sst, PT, sel[:, 4 + j:5 + j], Sst, ALU.mult, ALU.add, ptk + kS + ["pp"], kS)
                if j < 3:
                    tt(PT.rearrange("p (h d) -> p h d", h=4), PT.rearrange("p (h d) -> p h d", h=4),
                       RBs[:, j, 256:260].unsqueeze(2).to_broadcast([128, 4, 64]), ALU.mult, ptk + rbsk, ptk)
                    tt(PT, PT, RBs[:, j, 0:256], ALU.add, ptk + rbsk, ptk)
            cp(SBF, Sst, kS, kSBF)
            for c in range(NCH):
                tsl = slice(c * 128, (c + 1) * 128)
                if c < 2:
                    print("  ssd: chunk", c, P.total)
                for h in range(4):
                    ts(RH[:, 0, h, :], tri, tiny[:, 768 + c * 4 + h:768 + c * 4 + h + 1], None, ALU.mult, None, ["cm"] + tyk, kRH)
                    ts(RH[:, 1, h, :], tri, tiny[:, 640 + c * 4 + h:640 + c * 4 + h + 1], None, ALU.mult, None, ["cm"] + tyk, kRH)
                for h in range(4):
                    o = ps[0][:, h * 128:(h + 1) * 128]
                    mm(o, ones, RH[:, 0, h, :], True, False, kRH + ["cm"], [("ps", 0)])
                    mm(o, ones, RH[:, 1, h, :], False, False, kRH + ["cm"], [("ps", 0)])
                    mm(o, ident, negm, False, True, ["cm"], [("ps", 0)])
                for h in range(4):
                    act(E_[:, h, :], ps[0][:, h * 128:(h + 1) * 128], AF.Exp, [("ps", 0)] + tyk, kE, bias=nacs[:, c, h:h + 1])
                for g in range(2):
                    mm(ps[1][:, g * 128:(g + 1) * 128], xbc[:, 2 + g, tsl], xbc[:, 4 + g, tsl], True, True,
                       kxb[2 + g] + kxb[4 + g], [("ps", 1)])
                for h in range(4):
                    g = h // 2
                    tt(MT[:, h, :], ps[1][:, g * 128:(g + 1) * 128], E_[:, h, :], ALU.mult, [("ps", 1)] + kE, kMT)
                x3 = xtok[:, c, :].rearrange("p (h d) -> p h d", h=4)
                tt(XDT.rearrange("p (h d) -> p h d", h=4), x3, dt_[:, c, :].unsqueeze(2).to_broadcast([128, 4, 64]), ALU.mult, xtk + tyk, kXDT)
                tt(XDE.rearrange("p (h d) -> p h d", h=4), x3, ddte[:, c, :].unsqueeze(2).to_broadcast([128, 4, 64]), ALU.mult, xtk + tyk, kXDE)
                for h in range(4):
                    o = ps[2][:, h * 64:(h + 1) * 64]
                    mm(o, MT[:, h, :], XDT[:, h * 64:(h + 1) * 64], True, True, kMT + kXDT, [("ps", 2, "y")])
                for h in range(4):
                    g = h // 2
                    mm(ps[3][:, h * 64:(h + 1) * 64], xbc[:, 4 + g, tsl], SBF[:, h * 64:(h + 1) * 64], True, True,
                       kxb[4 + g] + kSBF, [("ps", 3)])
                tt(YSB.rearrange("p (h d) -> p h d", h=4), x3, ssdp[:, l, 8:12].unsqueeze(2).to_broadcast([128, 4, 64]), ALU.mult,
                   xtk + [("ssdp", l)], kYSB)
                tt(YSB, YSB, ps[2][:, 0:256], ALU.add, kYSB + [("ps", 2, "y")], kYSB)
                tt(STMP.rearrange("p (h d) -> p h d", h=4), ps[3][:, 0:256].rearrange("p (h d) -> p h d", h=4),
                   eacs[:, c, :].unsqueeze(2).to_broadcast([128, 4, 64]), ALU.mult, [("ps", 3)] + tyk, kST)
                tt(YTK, STMP, YSB, ALU.add, kST + kYSB, kYTK)
                for j in range(2):
                    o = j * 128
                    mm(ps[4][:, o:o + 128], YTK[:, j * 128:(j + 1) * 128], ident, True, True, kYTK + ["cm"], [("ps", 4, o)])
                    tt(Yr[:, 4 + j, tsl], ps[4][:, o:o + 128], sz[:, j, tsl], ALU.mult, [("ps", 4, o)] + szk, [("Yr", 4 + j, c // 4)])
                for g in range(2):
                    mm(ps[5][:, g * 128:(g + 1) * 128], btok[:, c, g * 128:(g + 1) * 128], XDE[:, g * 128:(g + 1) * 128], True, True,
                       btk + kXDE, [("ps", 5)])
                tt(STMP.rearrange("p (h d) -> p h d", h=4), Sst.rearrange("p (h d) -> p h d", h=4),
                   cd[:, c, :].unsqueeze(2).to_broadcast([128, 4, 64]), ALU.mult, kS + tyk, kST)
                tt(Sst, STMP, ps[5][:, 0:256], ALU.add, kST + [("ps", 5)], kS)
                act(SBF, Sst, AF.Copy, kS, kSBF)

        def mixer_s5(l, seg):
            U, STG, BT, W, WB, PAT, INJ, TAB = 0, 2048, 4096, 6144, 8192, 10240, 11264, 13312
            tyk = [("tiny", "all")]
            pk = [("s5pw", l)]
            lk = [("s5lv", l)]
            u = arb(U, 2048).rearrange("p (c t) -> p c t", c=2)
            uk = ark(U, 2048)
            for c in range(2):
                wu, wuk = load_win(l, 2052 + c * 128, 128)
                proj_chunk(wu, wuk, 0, 128,
                           lambda tb, p_, pk_: act(u[:, c, tb * TB:(tb + 1) * TB], p_, AF.Copy, [pk_], uk))
            pat = arb(PAT, 1024)
            patk = ark(PAT, 1024)
            mset(pat, 1.0, patk)
            mset(pat.rearrange("p (c j) -> p c j", j=16)[:, :, 0:1], 0.0, patk)
            pw0 = tiny[:, 0:256].rearrange("p (r a j) -> p r a j", r=8, a=2)
            qq = tiny[:, 256:512].rearrange("p (r a j) -> p r a j", r=8, a=2)
            den = tiny[:, 512:640].rearrange("p (r j) -> p r j", r=8)
            tmp = tiny[:, 640:768].rearrange("p (r j) -> p r j", r=8)
            mset(pw0[:, :, 0, 0:1], 1.0, tyk)
            mset(pw0[:, :, 1, 0:1], 0.0, tyk)
            for a in range(2):
                cp(pw0[:, :, a, 1:16], s5pw[:, l, :, a, 0:15], pk, tyk)
            tt(den, pw0[:, :, 0, :], pw0[:, :, 0, :], ALU.mult, tyk, tyk)
            tt(tmp, pw0[:, :, 1, :], pw0[:, :, 1, :], ALU.mult, tyk, tyk)
            tt(den, den, tmp, ALU.add, tyk, tyk)
            P.op("dve", lambda e: e.reciprocal(out=den, in_=den), tyk, tyk)
            tt(qq[:, :, 0, :], pw0[:, :, 0, :], den, ALU.mult, tyk, tyk)
            tt(qq[:, :, 1, :], pw0[:, :, 1, :], den, ALU.mult, tyk, tyk)
            ts(qq[:, :, 1, :], qq[:, :, 1, :], -1.0, None, ALU.mult, None, tyk, tyk)
            bpad = arf(TAB, 512).rearrange("p (r a h) -> p r a h", r=8, a=2)
            cpad = arf(TAB + 512, 512).rearrange("p (r a h) -> p r a h", r=8, a=2)
            tabk = ark(TAB, 1024)
            mset(arf(TAB, 512), 0.0, tabk)
            for a in range(2):
                cp(bpad[0:64, :, a, 0:16], bbS[0:64, l, a, :, :], [("bb", l)], tabk)
                cp(bpad[64:128, :, a, 16:32], bbS[64:128, l, a, :, :], [("bb", l)], tabk)
            dma("sp", arf(TAB + 512, 512), s5c[l], [], tabk)
            Ef = Yr[:, 6:8, :].rearrange("p a t -> p (a t)").bitcast(F32)
            Eall = Ef.rearrange("p (r a c) -> p r a c", r=8, a=2)
            ek = [("Yr", 6 + a, tb) for a in range(2) for tb in range(NTB)]
            bufA = arf(W, 2048).rearrange("p (r a c) -> p r a c", r=8, a=2)
            bufB = arf(WB, 2048).rearrange("p (r a c) -> p r a c", r=8, a=2)
            kA, kB = ark(W, 2048), ark(WB, 2048)
            T1 = arf(STG, 2048)
            T2 = arf(BT, 2048)
            kT1, kT2 = ark(STG, 2048), ark(BT, 2048)

            def bc8(ap, n):
                return ap.unsqueeze(2).to_broadcast([128, 8, n])

            def lvl2(src, sk):
                cur, ck_ = src, sk
                nxt_list = [(bufA, kA), (bufB, kB)]
                if src is bufA:
                    nxt_list = [(bufB, kB), (bufA, kA)]
                for lev in range(7):
                    d = 1 << lev
                    n = 128 - d
                    dst, dk = nxt_list[lev % 2]
                    cr, ci = s5lv[:, l, :, 0, lev], s5lv[:, l, :, 1, lev]
                    t1 = T1[:, 0:8 * n].rearrange("p (r c) -> p r c", r=8)
                    t2 = T2[:, 0:8 * n].rearrange("p (r c) -> p r c", r=8)
                    cp(dst[:, :, :, 0:d], cur[:, :, :, 0:d], ck_, dk)
                    tt(t1, cur[:, :, 0, 0:n], bc8(cr, n), ALU.mult, ck_ + lk, kT1)
                    tt(t2, cur[:, :, 1, 0:n], bc8(ci, n), ALU.mult, ck_ + lk, kT2)
                    tt(t1, t1, t2, ALU.subtract, kT1 + kT2, kT1)
                    tt(dst[:, :, 0, d:128], cur[:, :, 0, d:128], t1, ALU.add, ck_ + kT1, dk)
                    tt(t1, cur[:, :, 1, 0:n], bc8(cr, n), ALU.mult, ck_ + lk, kT1)
                    tt(t2, cur[:, :, 0, 0:n], bc8(ci, n), ALU.mult, ck_ + lk, kT2)
                    tt(t1, t1, t2, ALU.add, kT1 + kT2, kT1)
                    tt(dst[:, :, 1, d:128], cur[:, :, 1, d:128], t1, ALU.add, ck_ + kT1, dk)
                    cur, ck_ = dst, dk
                return cur, ck_

            def run_pass(mode):
                inj = arf(INJ, 2048).rearrange("p (r a c) -> p r a c", r=8, a=2)
                injk = ark(INJ, 2048)
                if mode == "full":
                    cp(bufA, Eall, ek, kA)
                    mur, mui = s5lv[:, l, :, 0, 0], s5lv[:, l, :, 1, 0]
                    xr_, xi_ = cX[:, l, :, 0], cX[:, l, :, 1]
                    ckx = [("cX", l, r) for r in range(8)]
                    a1, a2 = tiny[:, 768:776], tiny[:, 776:784]
                    tt(a1, mur, xr_, ALU.mult, lk + ckx, tyk)
                    tt(a2, mui, xi_, ALU.mult, lk + ckx, tyk)
                    tt(a1, a1, a2, ALU.subtract, tyk, tyk)
                    tt(bufA[:, :, 0, 0], bufA[:, :, 0, 0], a1, ALU.add, kA + tyk, kA)
                    tt(a1, mur, xi_, ALU.mult, lk + ckx, tyk)
                    tt(a2, mui, xr_, ALU.mult, lk + ckx, tyk)
                    tt(a1, a1, a2, ALU.add, tyk, tyk)
                    tt(bufA[:, :, 1, 0], bufA[:, :, 1, 0], a1, ALU.add, kA + tyk, kA)
                    res, rk = lvl2(bufA, kA)
                    lr8, li8 = s5pw[:, l, :, 0, 0], s5pw[:, l, :, 1, 0]
                    n = 127
                    t1 = T1[:, 0:8 * n].rearrange("p (r c) -> p r c", r=8)
                    t2 = T2[:, 0:8 * n].rearrange("p (r c) -> p r c", r=8)
                    tt(t1, res[:, :, 0, 0:n], bc8(lr8, n), ALU.mult, rk + pk, kT1)
                    tt(t2, res[:, :, 1, 0:n], bc8(li8, n), ALU.mult, rk + pk, kT2)
                    tt(inj[:, :, 0, 1:128], t1, t2, ALU.subtract, kT1 + kT2, injk)
                    tt(t1, res[:, :, 1, 0:n], bc8(lr8, n), ALU.mult, rk + pk, kT1)
                    tt(t2, res[:, :, 0, 0:n], bc8(li8, n), ALU.mult, rk + pk, kT2)
                    tt(inj[:, :, 1, 1:128], t1, t2, ALU.add, kT1 + kT2, injk)
                    tt(a1, lr8, xr_, ALU.mult, pk + ckx, tyk)
                    tt(a2, li8, xi_, ALU.mult, pk + ckx, tyk)
                    tt(inj[:, :, 0, 0], a1, a2, ALU.subtract, tyk, injk)
                    tt(a1, lr8, xi_, ALU.mult, pk + ckx, tyk)
                    tt(a2, li8, xr_, ALU.mult, pk + ckx, tyk)
                    tt(inj[:, :, 1, 0], a1, a2, ALU.add, tyk, injk)

                stg = arb(STG, 2048).rearrange("p (a j c) -> p a j c", a=2, j=16)
                stgk = ark(STG, 2048)
                btv = arb(BT, 2048).rearrange("p (a j c) -> p a j c", a=2, j=16)
                btk_ = ark(BT, 2048)
                Wn = arf(W, 2048)
                wk_ = ark(W, 2048)
                Wv = Wn.rearrange("p (c j) -> p j c", j=16)
                wb = arb(WB, 2048).rearrange("p (a t) -> p a t", a=2)
                wbk = ark(WB, 2048)

                def b16(ap):
                    return ap.unsqueeze(2).to_broadcast([128, 16, 32])

                def h32(ap):
                    return ap.unsqueeze(1).to_broadcast([128, 16, 32])

                ct = Yr[:, 4:6, :].rearrange("p a (j c) -> p a j c", j=16)
                ctk = [("Yr", 4 + a_, t_) for a_ in range(2) for t_ in range(NTB)]
                if mode == "p1":
                    tmp_f, tmpk = arf(WB, 1024), ark(WB, 1024)
                    W1n, w1k_ = arf(INJ, 2048), ark(INJ, 2048)
                else:
                    tmp_f = Yr[:, 7, :].bitcast(F32)
                    tmpk = [("Yr", 7, t_) for t_ in range(NTB)]
                    W1n = Yr[:, 0:2, :].rearrange("p a t -> p (a t)").bitcast(F32)
                    w1k_ = [("Yr", a_, t_) for a_ in range(2) for t_ in range(NTB)]
                Wbufs = [(Wn, wk_), (W1n, w1k_)]
                t1 = tmp_f[:, 0:512].rearrange("p (j h) -> p j h", j=16)
                t2 = tmp_f[:, 512:1024].rearrange("p (j h) -> p j h", j=16)
                mset(arb(STG, 2048), 0.0, stgk)
                if mode == "full":
                    mset(Yr[:, 4:6, :], 0.0, ctk)

                def tables_b_dve(r):
                    q = r % 4
                    if r > 0:
                        pq = (r - 1) % 4
                        mset(stg[:, :, :, pq * 32:(pq + 1) * 32], 0.0, stgk)
                    Bre, Bim = h32(bpad[:, r, 0, :]), h32(bpad[:, r, 1, :])
                    qr_, qi_ = b16(qq[:, r, 0, :]), b16(qq[:, r, 1, :])
                    sv = stg[:, :, :, q * 32:(q + 1) * 32]
                    tt(t1, Bre, qr_, ALU.mult, tabk + tyk, tmpk)
                    tt(t2, Bim, qi_, ALU.mult, tabk + tyk, tmpk)
                    tt(sv[:, 0], t1, t2, ALU.subtract, tmpk, stgk)
                    tt(t1, Bim, qr_, ALU.mult, tabk + tyk, tmpk)
                    tt(t2, Bre, qi_, ALU.mult, tabk + tyk, tmpk)
                    tt(sv[:, 1], t1, t2, ALU.add, tmpk, stgk)

                def tables_b_pe(r):
                    for a in range(2):
                        for lg in range(4):
                            bnk = 4 + lg
                            for li_ in range(4):
                                mm(ps[bnk][:, li_ * 128:(li_ + 1) * 128], stg[:, a, lg * 4 + li_, :], ident, True, True,
                                   stgk + ["cm"], [("ps", bnk)])
                            act(btv[:, a, lg * 4:(lg + 1) * 4, :], ps[bnk][:, :].rearrange("p (j c) -> p j c", j=4), AF.Copy,
                                [("ps", bnk)], btk_)

                def ct_dve(r):
                    q = r % 4
                    if r > 0:
                        pq = (r - 1) % 4
                        mset(ct[:, :, :, pq * 32:(pq + 1) * 32], 0.0, ctk)
                    Cre, Cim = h32(cpad[:, r, 0, :]), h32(cpad[:, r, 1, :])
                    pr_, pi_ = b16(pw0[:, r, 0, :]), b16(pw0[:, r, 1, :])
                    cv = ct[:, :, :, q * 32:(q + 1) * 32]
                    tt(t1, Cre, pr_, ALU.mult, tabk + tyk, tmpk)
                    tt(t2, Cim, pi_, ALU.mult, tabk + tyk, tmpk)
                    tt(cv[:, 0], t1, t2, ALU.add, tmpk, ctk)
                    tt(t1, Cim, pr_, ALU.mult, tabk + tyk, tmpk)
                    tt(t2, Cre, pi_, ALU.mult, tabk + tyk, tmpk)
                    tt(cv[:, 1], t1, t2, ALU.subtract, tmpk, ctk)

                def bu_mm(r, a, wv_, wkk):
                    uv = u[:, r // 4, :].rearrange("p (c j) -> p j c", j=16)
                    for lg in range(4):
                        bnk = 4 + lg
                        for li_ in range(4):
                            j = lg * 4 + li_
                            mm(ps[bnk][:, li_ * 128:(li_ + 1) * 128], btv[:, a, j, :], uv[:, j, :], True, True, btk_ + uk, [("ps", bnk)])
                        act(wv_[:, lg * 4:(lg + 1) * 4, :], ps[bnk][:, :].rearrange("p (j c) -> p j c", j=4), AF.Copy, [("ps", bnk)], wkk)

                def scan(wn_, wkk):
                    P.op("dve", lambda e, wn_=wn_: e.tensor_tensor_scan(out=wn_, data0=pat, data1=wn_, initial=0.0, op0=ALU.mult, op1=ALU.add),
                         wkk + patk, wkk)

                tables_b_dve(0)
                tables_b_pe(0)
                for r in range(8):
                    oc = r // 4
                    q = r % 4
                    uv = u[:, oc, :].rearrange("p (c j) -> p j c", j=16)
                    wvs = [(wn_.rearrange("p (c j) -> p j c", j=16), wn_, kk_) for (wn_, kk_) in Wbufs]
                    if mode == "p1":
                        bu_mm(r, 0, wvs[0][0], wvs[0][2])
                        bu_mm(r, 1, wvs[1][0], wvs[1][2])
                        if r + 1 < 8:
                            tables_b_dve(r + 1)
                        for a in range(2):
                            scan(wvs[a][1], wvs[a][2])
                            cp(Eall[:, r, a, :], wvs[a][0][:, 15, :], wvs[a][2], ek)
                        if r + 1 < 8:
                            tables_b_pe(r + 1)
                        continue
                    bu_mm(r, 0, wvs[0][0], wvs[0][2])
                    bu_mm(r, 1, wvs[1][0], wvs[1][2])
                    ct_dve(r)
                    if r + 1 < 8:
                        tables_b_dve(r + 1)
                    for a in range(2):
                        wv_, wn_, wkk = wvs[a]
                        tt(wv_[:, 0, :], wv_[:, 0, :], inj[:, r, a, :], ALU.add, wkk + injk, wkk)
                        scan(wn_, wkk)
                        act(wb[:, a, :], wn_, AF.Copy, wkk, wbk)
                    if r + 1 < 8:
                        tables_b_pe(r + 1)
                    wbv = [wb[:, a, :].rearrange("p (c j) -> p j c", j=16) for a in range(2)]
                    for j in range(16):
                        o = ps[j // 4][:, (j % 4) * 128:(j % 4 + 1) * 128]
                        P.op("pe", lambda e, o=o, j=j, q=q, wbv=wbv: e.matmul(o, lhsT=ct[:, 0, j, :], rhs=wbv[0][:, j, :],
                                                                             start=(q == 0 and j % 4 == 0), stop=False, skip_group_check=True),
                             ctk + wbk, [("ps", j // 4)])
                        P.op("pe", lambda e, o=o, j=j, q=q, wbv=wbv: e.matmul(o, lhsT=ct[:, 1, j, :], rhs=wbv[1][:, j, :], start=False,
                                                                             stop=(q == 3), skip_group_check=True), ctk + wbk, [("ps", j // 4)])
                    if q == 3:
                        for tb in range(NTB):
                            yo = W + (tb % 2) * 512
                            yt = arf(yo, 512)
                            ytk = ark(yo, 512)
                            uv4 = uv[:, tb * 4:(tb + 1) * 4, :]
                            stt(yt.rearrange("p (j c) -> p j c", j=4), uv4, pcol(l, "s5d", oc, oc + 1),
                                ps[tb][:, :].rearrange("p (j c) -> p j c", j=4), ALU.mult, ALU.add, uk + ["pp", ("ps", tb)], ytk)
                            act(Yr[:, 6 + oc, :].rearrange("p (c j) -> p j c", j=16)[:, tb * 4:(tb + 1) * 4, :],
                                yt.rearrange("p (j c) -> p j c", j=4), AF.Gelu_apprx_tanh, ytk, [("Yr", 6 + oc, t_) for t_ in range(NTB)])
                if mode == "p1":
                    p15r, p15i = s5pw[:, l, :, 0, 14], s5pw[:, l, :, 1, 14]
                    n = 128
                    t1 = T1[:, 0:1024].rearrange("p (r c) -> p r c", r=8)
                    t2 = T1[:, 1024:2048].rearrange("p (r c) -> p r c", r=8)
                    t3 = T2[:, 0:1024].rearrange("p (r c) -> p r c", r=8)
                    t4 = T2[:, 1024:2048].rearrange("p (r c) -> p r c", r=8)
                    tt(t1, Eall[:, :, 0, :], bc8(p15r, n), ALU.mult, ek + pk, kT1)
                    tt(t2, Eall[:, :, 1, :], bc8(p15i, n), ALU.mult, ek + pk, kT1)
                    tt(t3, Eall[:, :, 1, :], bc8(p15r, n), ALU.mult, ek + pk, kT2)
                    tt(t4, Eall[:, :, 0, :], bc8(p15i, n), ALU.mult, ek + pk, kT2)
                    tt(Eall[:, :, 0, :], t1, t2, ALU.subtract, kT1, ek)
                    tt(Eall[:, :, 1, :], t3, t4, ALU.add, kT2, ek)
                    res, rk = lvl2(Eall, ek)
                    cp(tiny[:, 800:816].rearrange("p (r a) -> p r a", a=2), res[:, :, :, 127], rk, tyk)
                    dma("sp", ccd[("x", l, "i")], tiny[:, 800:816], tyk, [("cci", "x", l)])
                    allgather("x", l, [("cci", "x", l)], [("cco", "x", l)])
                    return

            run_pass("p1")
            s5_combine(l)
            run_pass("full")
            SG = W
            for tb in range(NTB):
                sgs = []
                for m in range(2):
                    b = next_acc()
                    for k in range(2):
                        mm(ps[b][:, :], gluwS[:, l, k, m * 128:(m + 1) * 128], Yr[:, 6 + k, tb * TB:(tb + 1) * TB], k == 0, k == 1,
                           [("Yr", 6 + k, tb), "gluw"], [("ps", b)])
                    so = SG + ((tb * 2 + m) % 4) * 256
                    sg = arb(so, 256)
                    act(sg, ps[b][:, :], AF.Sigmoid, [("ps", b), "pp"], ark(so, 256), bias=pcol(l, "glub", m, m + 1))
                    sgs.append((sg, ark(so, 256)))
                for m in range(2):
                    sg, sgk = sgs[m]
                    tt(Yr[:, 6 + m, tb * TB:(tb + 1) * TB], Yr[:, 6 + m, tb * TB:(tb + 1) * TB], sg, ALU.mult,
                       [("Yr", 6 + m, tb)] + sgk, [("Yr", 6 + m, tb)])

        W1O = [7168, 9216]
        W2O = [11264, 13312]
        NG = 8

        def mlp_load(l, g):
            sl = g % 2
            w1g = arb(W1O[sl], 2048).rearrange("p (k c) -> p k c", k=8)
            w2g = arb(W2O[sl], 2048).rearrange("p (k c) -> p k c", k=4)
            w1k, w2k = ark(W1O[sl], 2048), ark(W2O[sl], 2048)
            dma("pool", w1g, w1[l, :, g * 512:(g + 1) * 512].rearrange("(k p) c -> p k c", p=128), [], w1k)
            dma("pool", w2g, w2[l, g * 512:(g + 1) * 512, :].rearrange("(k p) c -> p k c", p=128), [], w2k)

        def out_proj(l):
            WO, RSTD, SQ = 0, 4096, 6144
            wo = arb(WO, 4096).rearrange("p (k c) -> p k c", k=8)
            wok = ark(WO, 4096)
            dma("pool", wo, w_out[l, :, :].rearrange("(k p) c -> p k c", p=128), [], wok)
            mlp_load(l, 0)
            mlp_load(l, 1)
            for g in range(4):
                rms_stats(lambda k, tb: Yr[:, 2 * g + k, tb * TB:(tb + 1) * TB], 2, 256.0, RSTD, SQ,
                          lambda k, tb: [("Yr", 2 * g + k, tb)])
                for k in range(2):
                    ch = 2 * g + k
                    for tb in range(NTB):
                        stt(uT[:, ch, tb * TB:(tb + 1) * TB], Yr[:, ch, tb * TB:(tb + 1) * TB], pcol(l, "bnw", ch, ch + 1),
                            arf(RSTD + tb * TB, TB), ALU.mult, ALU.mult, [("Yr", ch, tb), "pp"] + ark(RSTD + tb * TB, TB), [("uT", ch, tb)])
            for tb in range(NTB):
                for m in range(KC):
                    b = next_acc()
                    for k in range(KC):
                        mm(ps[b][:, :], wo[:, k, m * 128:(m + 1) * 128], uT[:, k, tb * TB:(tb + 1) * TB], k == 0, k == KC - 1,
                           wok + [("uT", k, tb)], [("ps", b)])
                    stt(hT[:, m, tb * TB:(tb + 1) * TB], ps[b][:, :], G1(l, m), hT[:, m, tb * TB:(tb + 1) * TB], ALU.mult, ALU.add,
                        [("ps", b), ("der", l), ("hT", m, tb)], [("hT", m, tb)])

        def mlp(l):
            norm_mod(l, S2, B2)
            RS = 5120

            def up(g):
                sl = g % 2
                w1g = arb(W1O[sl], 2048).rearrange("p (k c) -> p k c", k=8)
                w1k = ark(W1O[sl], 2048)
                for jc in range(4):
                    yc = sl * 4 + jc
                    for tb in range(NTB):
                        b = next_acc()
                        for k in range(KC):
                            mm(ps[b][:, :], w1g[:, k, jc * 128:(jc + 1) * 128], uT[:, k, tb * TB:(tb + 1) * TB], k == 0, k == KC - 1,
                               w1k + [("uT", k, tb)], [("ps", b)])
                        ro = RS + ((jc * NTB + tb) % 4) * 256
                        rs = arb(ro, 256)
                        act(rs, ps[b][:, :], AF.Relu, [("ps", b)], ark(ro, 256))
                        tt(Yr[:, yc, tb * TB:(tb + 1) * TB], rs, rs, ALU.mult, ark(ro, 256), [("Yr", yc, tb)], eng="pool")

            def down(g):
                sl = g % 2
                w2g = arb(W2O[sl], 2048).rearrange("p (k c) -> p k c", k=4)
                w2k = ark(W2O[sl], 2048)
                order = [(m, tb) for m in range(KC) for tb in range(NTB)] if g < NG - 1 else [(m, tb) for tb in range(NTB) for m in range(KC)]
                for (m, tb) in order:
                    if True:
                        b = next_acc()
                        for jc in range(4):
                            mm(ps[b][:, :], w2g[:, jc, m * 128:(m + 1) * 128], Yr[:, sl * 4 + jc, tb * TB:(tb + 1) * TB], jc == 0, jc == 3,
                               w2k + [("Yr", sl * 4 + jc, tb)], [("ps", b)])
                        stt(hT[:, m, tb * TB:(tb + 1) * TB], ps[b][:, :], G2(l, m), hT[:, m, tb * TB:(tb + 1) * TB], ALU.mult, ALU.add,
                            [("ps", b), ("der", l), ("hT", m, tb)], [("hT", m, tb)])

            up(0)
            for g in range(NG):
                if g + 1 < NG:
                    up(g + 1)
                down(g)
                if g + 2 < NG:
                    mlp_load(l, g + 2)

        def final_out(seg):
            RSTD, SQ, OUT = 0, 2048, 4096
            rms_stats(lambda k, tb: hT[:, k, tb * TB:(tb + 1) * TB], KC, float(DM), RSTD, SQ, lambda k, tb: [("hT", k, tb)])
            for k in range(KC):
                oo = OUT + (k % 2) * 2048
                o = arf(oo, 2048)
                for tb in range(NTB):
                    stt(o[:, tb * TB:(tb + 1) * TB], hT[:, k, tb * TB:(tb + 1) * TB], pcol(0, "fnw", k, k + 1), arf(RSTD + tb * TB, TB),
                        ALU.mult, ALU.mult, [("hT", k, tb), "pp"] + ark(RSTD + tb * TB, TB), ark(oo, 2048))
                dma("sp", yT[k * 128:(k + 1) * 128, :], o, ark(oo, 2048), [("yT", k, seg)])

        stopped = False
        for seg in range(nseg):
            if stopped:
                break
            for k in range(KC):
                dma("sp", hT[:, k, :], xT[k * 128:(k + 1) * 128, :], [], [("hT", k, tb) for tb in range(NTB)])
            for l in range(depth):
                norm_mod(l, S1, B1)
                if stop_after == (seg, l, "u"):
                    stopped = True
                    break
                print("ops before tail", P.total)
                tail_prepass(l)
                print("ops before s5 p1", P.total)
                mixer_s5(l, seg)
                print("ops before halo", P.total)
                halo_apply(l)
                mixer_pool(l, seg)
                mixer_sconv(l, seg)
                print("ops before ssd", P.total)
                mixer_ssd(l, seg)
                print("ops after ssd", P.total)
                if stop_after == (seg, l, "mix"):
                    stopped = True
                    break
                out_proj(l)
                if stop_after == (seg, l, "hmix"):
                    stopped = True
                    break
                mlp(l)
                if stop_after == (seg, l, "h"):
                    stopped = True
                    break
            if not stopped:
                final_out(seg)
        print("total ops recorded:", P.total)
        P.limit = None
        if dbg:
            if "uT" in dbg_out:
                cp(AR[:, 0:2048], uT[:, 0, :], all_keys("uT"), ark(0, 2048))
            for name in dbg_out:
                if name == "Yr":
                    for k in range(KC):
                        o = arf((k % 2) * 2048, 2048)
                        cp(o, Yr[:, k, :], [("Yr", k, tb) for tb in range(NTB)], ark((k % 2) * 2048, 2048))
                        dma("sp", dbg_out[name][k * 128:(k + 1) * 128, :], o, ark((k % 2) * 2048, 2048), [("dbg", name, k)])
                elif name == "uT":
                    for k in range(KC):
                        o = arf((k % 2) * 2048, 2048)
                        cp(o, uT[:, k, :], [("uT", k, tb) for tb in range(NTB)], ark((k % 2) * 2048, 2048))
                        dma("sp", dbg_out[name][k * 128:(k + 1) * 128, :], o, ark((k % 2) * 2048, 2048), [("dbg", name, k)])
                elif name == "hT":
                    for k in range(KC):
                        dma("sp", dbg_out[name][k * 128:(k + 1) * 128, :], hT[:, k, :], [("hT", k, tb) for tb in range(NTB)], [("dbg", name, k)])
                elif name == "mod":
                    dma("sp", dbg_out[name], modT[:, :, :].rearrange("p l c -> p (l c)"), [("mod", 0), ("mod", 1)], [("dbg", name)])
        P.wait_all("sp")

        with nc.Block() as block:
            def replay(e, name):
                for waits, fn, inc in P.q[name]:
                    for s, v in waits:
                        e.wait_ge(sems[s], v)
                    if fn is not None:
                        fn(e).then_inc(sems[inc[0]], inc[1])

            @block.tensor
            def _(e):
                replay(e, "pe")

            @block.scalar
            def _(e):
                replay(e, "act")

            @block.vector
            def _(e):
                replay(e, "dve")

            @block.gpsimd
            def _(e):
                replay(e, "pool")

            @block.sync
            def _(e):
                replay(e, "sp")
    return nc


def _fm(v):
    return np.ascontiguousarray(v.reshape(-1, 128).T)


def _pack_params(inp, b, sg):
    L = DEPTH
    pp = np.zeros((L, 128, NPCOL), np.float32)

    def put(l, name, arr):
        o, w = PCOL[name]
        arr = np.asarray(arr, np.float32).reshape(128, w)
        pp[l, :, o:o + w] = arr
    wins = (2, 4, 8, 16)
    for l in range(L):
        put(l, "nw1", _fm(inp["norm_mix_w"][l]))
        put(l, "nw2", _fm(inp["norm_mlp_w"][l]))
        put(l, "bnw", _fm(inp["branch_norm_w"][l]))
        put(l, "fnw", _fm(inp["final_norm_w"]))
        adab = np.zeros((128, 48), np.float32)
        adab[:, 0:12] = _fm(inp["ada_b"][l])[:, sg * 12:(sg + 1) * 12]
        put(l, "adab", adab)
        put(l, "pscale", _fm(inp["pool_scale"][l]))
        put(l, "scw", inp["sconv_w"][l].reshape(3, 2, 128).transpose(2, 1, 0))
        put(l, "cvw", inp["ssd_conv_w"][l].reshape(4, 6, 128).transpose(2, 1, 0))
        put(l, "cvb", _fm(inp["ssd_conv_b"][l]))
        put(l, "dtb", np.broadcast_to(inp["ssd_dt_bias"][l][None, :], (128, 4)))
        put(l, "alog", np.broadcast_to(inp["ssd_a_log"][l][None, :], (128, 4)))
        put(l, "dsk", np.broadcast_to(inp["ssd_d"][l][None, :], (128, 4)))
        def gp(a):
            return a.reshape(8, 2, 64).transpose(1, 2, 0).reshape(128, 8)
        put(l, "are", gp(inp["s5_a_re"][l]))
        put(l, "aim", gp(inp["s5_a_im"][l]))
        put(l, "lst", gp(np.broadcast_to(inp["s5_log_step"][l][:, None], (16, 64))))
        put(l, "s5d", _fm(inp["s5_d"][l]))
        put(l, "glub", _fm(inp["s5_glu_b"][l]))
        put(l, "cond", _fm(inp["c"][b]))
        invc = np.zeros((128, 2, 16), np.float32)
        for c in range(2):
            for half in range(2):
                win = wins[c * 2 + half]
                if sg == 0:
                    invc[half * 64:(half + 1) * 64, c, :] = 1.0 / np.minimum(np.arange(16) + 1, win)
                else:
                    invc[half * 64:(half + 1) * 64, c, :] = 1.0 / win
        put(l, "invc", invc)
        sel = np.zeros((128, 8), np.float32)
        if sg > 0:
            sel[:, sg - 1] = 1.0
        sel[:, 4 + sg] = 1.0
        put(l, "sel", sel)
    return pp


def _consts():
    cm = np.zeros((128, 4, 128), np.float32)
    i = np.arange(128)
    cm[:, 0, :] = (i[:, None] == i[None, :])
    cm[:, 1, :] = 1.0
    cm[:, 2, :] = (i[:, None] <= i[None, :])
    cm[:, 3, :] = np.where(i[None, :] < i[:, None], -30000.0, 0.0)
    return cm


def _host_inputs(inp, b, sg):
    f = lambda a: np.ascontiguousarray(np.asarray(a, np.float32))
    pw = np.zeros((DEPTH, 128, 2, 128), np.float32)
    for l in range(DEPTH):
        for g in range(4):
            c, half = g // 2, g % 2
            pw[l, half * 64:(half + 1) * 64, c, half * 64:(half + 1) * 64] = inp["pool_w"][l, g]
    gw = np.ascontiguousarray(np.asarray(inp["s5_glu_w"], np.float32).reshape(DEPTH, 2, 128, 256).transpose(0, 2, 1, 3))
    s5b = np.zeros((DEPTH, 128, 256), np.float32)
    s5c = np.zeros((DEPTH, 128, 8, 2, 32), np.float32)
    for l in range(DEPTH):
        s5b[l, :, 0:128] = np.asarray(inp["s5_b_re"][l]).reshape(8, 2, 64, 16).transpose(1, 2, 0, 3).reshape(128, 128)
        s5b[l, :, 128:256] = np.asarray(inp["s5_b_im"][l]).reshape(8, 2, 64, 16).transpose(1, 2, 0, 3).reshape(128, 128)
        for ri, nm in enumerate(("s5_c_re", "s5_c_im")):
            cc = np.asarray(inp[nm][l]).reshape(8, 2, 16, 64)
            for r in range(8):
                for gi in range(2):
                    s5c[l, gi * 64:(gi + 1) * 64, r, ri, gi * 16:gi * 16 + 16] = cc[r, gi].T
    return {
        "s5b": s5b, "s5c": s5c.reshape(DEPTH, 128, 512),
        "xT": f(np.asarray(inp["x"][b][sg * T:(sg + 1) * T]).T),
        "pp": _pack_params(inp, b, sg),
        "cmat": _consts(),
        "ada_w": f(np.asarray(inp["ada_w"])[:, :, sg * 1536:(sg + 1) * 1536]), "w_in": f(inp["w_in"]), "w_out": f(inp["w_out"]),
        "mlp_w1": f(inp["mlp_w1"]), "mlp_w2": f(inp["mlp_w2"]),
        "poolw": pw, "gluw": gw,
    }


_NC_CACHE = {}


def kernel(**inputs):
    inp = {k: np.asarray(v) for k, v in inputs.items()}
    if "full" not in _NC_CACHE:
        _NC_CACHE["full"] = build()
    nc = _NC_CACHE["full"]
    in_maps = [_host_inputs(inp, r // 4, r % 4) for r in range(8)]
    res = run_bass_kernel_spmd(nc, in_maps, core_ids=list(range(8)))
    out = np.empty((2, SEQ, DM), np.float32)
    for r in range(8):
        out[r // 4, (r % 4) * T:(r % 4 + 1) * T, :] = res.results[r]["yT"].T
    return out
```

```python
import numpy as np
import concourse.bass as bass
import concourse.mybir as mybir
from concourse.bass_utils import run_bass_kernel_spmd

F32, BF16 = mybir.dt.float32, mybir.dt.bfloat16
AF = mybir.ActivationFunctionType
ALU = mybir.AluOpType

T = 2048
TB = 512
NTB = 4
NCH = 16
DM = 1024
KC = 8
SEQ = 8192
DEPTH = 2
EPS = 1e-6
ENGS = ("pe", "act", "dve", "pool", "sp")
NDMA = 12

PCOL = {}
_off = 0
for _n, _w in [("nw1", 8), ("nw2", 8), ("bnw", 8), ("fnw", 8), ("adab", 48), ("pscale", 2), ("scw", 6),
               ("cvw", 24), ("cvb", 6), ("dtb", 4), ("alog", 4), ("dsk", 4), ("are", 8), ("aim", 8), ("lst", 8),
               ("s5d", 2), ("glub", 2), ("cond", 8),
               ("invc", 32), ("sel", 8)]:
    PCOL[_n] = (_off, _w)
    _off += _w
NPCOL = _off


class Prog:
    def __init__(self):
        self.q = {e: [] for e in ENGS}
        self.cnt = {e: 0 for e in ENGS}
        self.seen = {e: {} for e in ENGS}
        self.lastw = {}
        self.rd = {}
        self.dma_i = 0
        self.fam = {}
        import os as _os
        self.nosame = set(x for x in _os.environ.get("KNOSAME", "").split(",") if x)
        self.total = 0
        import os
        self.limit = int(os.environ.get("KLIMIT", "0")) or None

    def _skip(self):
        self.total += 1
        return self.limit is not None and self.total > self.limit

    def _deps(self, eng, reads, writes):
        need = {}

        def add(d):
            if d is None:
                return
            s, v = d
            if need.get(s, 0) < v:
                need[s] = v
        for k0 in reads:
            for k in self._rel(k0):
                add(self.lastw.get(k))
        for k0 in writes:
            for k in self._rel(k0):
                add(self.lastw.get(k))
                for s, v in self.rd.get(k, {}).items():
                    add((s, v))
        out = []
        for s, v in need.items():
            if s == eng and (eng == "pe" or eng in self.nosame):
                continue
            if self.seen[eng].get(s, 0) >= v:
                continue
            self.seen[eng][s] = v
            out.append((s, v))
        return out

    def _rel(self, k):
        if isinstance(k, tuple) and k[0] == "ps":
            fam = self.fam.setdefault(k[1], set())
            fam.add(k)
            if len(k) == 2:
                return list(fam)
            return [k, ("ps", k[1])]
        return [k]

    def _commit(self, reads, writes, tok):
        s, v = tok
        for k in reads:
            d = self.rd.setdefault(k, {})
            if d.get(s, 0) < v:
                d[s] = v
        for k in writes:
            self.lastw[k] = tok
            self.rd[k] = {}

    @staticmethod
    def _norm(reads, writes):
        r2, w2 = [], list(writes)
        for k in reads:
            if isinstance(k, tuple) and k[0] == "ps":
                w2.append(k)
            else:
                r2.append(k)
        w2 = [("ps", k[1]) if (isinstance(k, tuple) and k[0] == "ps") else k for k in w2]
        return r2, w2

    def op(self, eng, fn, reads=(), writes=()):
        if self._skip():
            return
        reads, writes = self._norm(reads, writes)
        waits = self._deps(eng, reads, writes)
        self.cnt[eng] += 1
        self.q[eng].append((waits, fn, (eng, 1)))
        self._commit(reads, writes, (eng, self.cnt[eng]))

    def dma(self, eng, fn, reads=(), writes=()):
        if self._skip():
            return None
        reads = list(reads)
        writes = list(writes)
        i = self.dma_i
        self.dma_i += 1
        sem = "dma%d" % (i % NDMA)
        val = 16 * (i // NDMA + 1)
        waits = self._deps(eng, reads, writes)
        if i >= NDMA:
            prev = 16 * (i // NDMA)
            if self.seen[eng].get(sem, 0) < prev:
                self.seen[eng][sem] = prev
                waits.append((sem, prev))
        self.q[eng].append((waits, fn, (sem, 16)))
        self._commit(reads, writes, (sem, val))
        return (sem, val)

    def cc(self, fn, sem, reads=(), writes=()):
        if self._skip():
            return
        reads, writes = self._norm(reads, writes)
        waits = self._deps("pool", reads, writes)
        self.q["pool"].append((waits, fn, (sem, 1)))
        self._commit(reads, writes, (sem, 1))

    def wait_all(self, eng):
        waits = []
        for e in ENGS:
            if e != eng and self.cnt[e] > self.seen[eng].get(e, 0):
                self.seen[eng][e] = self.cnt[e]
                waits.append((e, self.cnt[e]))
        for j in range(min(NDMA, self.dma_i)):
            sem = "dma%d" % j
            n = (self.dma_i - 1 - j) // NDMA + 1
            if self.seen[eng].get(sem, 0) < 16 * n:
                self.seen[eng][sem] = 16 * n
                waits.append((sem, 16 * n))
        self.q[eng].append((waits, None, None))


class Buf:
    def __init__(self, name, ap):
        self.name = name
        self.ap = ap

    def k(self, *idx):
        return (self.name,) + tuple(idx)


def build(depth=DEPTH, dbg=None, stop_after=None):
    nseg = 1
    nc = bass.Bass("TRN2", target_bir_lowering=False)
    P = Prog()
    dram = {}

    def din(name, shape):
        dram[name] = nc.dram_tensor(name, list(shape), F32, kind="ExternalInput").ap()
        return dram[name]

    xT = din("xT", [DM, T])
    pp = din("pp", [DEPTH, 128, NPCOL])
    cmat = din("cmat", [128, 4, 128])
    ada_w = din("ada_w", [DEPTH, DM, 1536])
    w_in = din("w_in", [DEPTH, DM, 2308])
    w_out = din("w_out", [DEPTH, DM, DM])
    w1 = din("mlp_w1", [DEPTH, DM, 4 * DM])
    w2 = din("mlp_w2", [DEPTH, 4 * DM, DM])
    poolw = din("poolw", [DEPTH, 128, 2, 128])
    gluw = din("gluw", [DEPTH, 128, 2, 256])
    s5b = din("s5b", [DEPTH, 128, 256])
    s5c = din("s5c", [DEPTH, 128, 512])
    yT = nc.dram_tensor("yT", [DM, T], F32, kind="ExternalOutput").ap()
    GRP = [[0, 1, 2, 3], [4, 5, 6, 7]]
    ccd = {}
    for l_ in range(DEPTH):
        for nm_, w_ in (("h", 64), ("x", 16), ("s", 272), ("m", 32)):
            ccd[(nm_, l_, "i")] = nc.dram_tensor("cc%s%di" % (nm_, l_), [128, w_], F32, kind="Internal").ap()
            ccd[(nm_, l_, "o")] = nc.dram_tensor("cc%s%do" % (nm_, l_), [4 * 128, w_], F32, kind="Internal").ap()
    dbg_out = {}
    if dbg:
        for name, shape in dbg.items():
            dbg_out[name] = nc.dram_tensor("dbg_" + name, list(shape), F32, kind="ExternalOutput").ap()

    import contextlib
    es = contextlib.ExitStack()
    with es:
        def sb(name, shape, dt):
            return es.enter_context(nc.sbuf_tensor(name, list(shape), dt))

        hT = sb("hT", [128, KC, T], F32)
        uT = sb("uT", [128, KC, T], BF16)
        Yr = sb("Yr", [128, KC, T], BF16)
        AR = sb("AR", [128, 15360], F32)
        cm = sb("cm", [128, 4, 128], BF16)
        ppS = sb("ppS", [128, DEPTH, NPCOL], F32)
        modT = sb("modT", [128, DEPTH, 48], F32)
        der = sb("der", [128, DEPTH, 64], F32)
        condb = sb("condb", [128, 8], BF16)
        epsT = sb("epsT", [128, 2], F32)
        poolwS = sb("poolwS", [128, DEPTH, 2, 128], BF16)
        gluwS = sb("gluwS", [128, DEPTH, 2, 256], BF16)
        cPool = sb("cPool", [128, DEPTH, 2, 16], F32)
        cSc = sb("cSc", [128, DEPTH, 2, 2], F32)
        cCv = sb("cCv", [128, DEPTH, 6, 3], F32)
        cS = sb("cS", [128, DEPTH, 256], F32)
        cX = sb("cX", [128, DEPTH, 8, 2], F32)
        ssdp = sb("ssdp", [128, DEPTH, 16], F32)
        s5pw = sb("s5pw", [128, DEPTH, 8, 3, 16], F32)
        s5lv = sb("s5lv", [128, DEPTH, 8, 3, 8], F32)
        bbS = sb("bbS", [128, DEPTH, 2, 8, 16], F32)
        tiny = sb("tiny", [128, 896], F32)
        tinyb = sb("tinyb", [128, 128], BF16)

        ps = [es.enter_context(nc.psum_tensor("ps%d" % i, [128, 512], F32)) for i in range(8)]
        sems = {}
        for e in ENGS:
            sems[e] = es.enter_context(nc.semaphore("s_" + e))
        for j in range(NDMA):
            sems["dma%d" % j] = es.enter_context(nc.semaphore("s_dma%d" % j))
        for l_ in range(DEPTH):
            for nm_ in ("h", "x", "s", "m"):
                sems["cc%s%d" % (nm_, l_)] = es.enter_context(nc.semaphore("s_cc%s%d" % (nm_, l_)))

        def allgather(nm_, l_, reads, writes):
            i_, o_ = ccd[(nm_, l_, "i")], ccd[(nm_, l_, "o")]
            P.cc(lambda e: e.collective_compute("AllGather", ALU.bypass, replica_groups=GRP, ins=[i_], outs=[o_]),
                 "cc%s%d" % (nm_, l_), reads, writes)

        ARb = AR[:, :].bitcast(BF16)

        def arf(off, n):
            return AR[:, off:off + n]

        def arb(off, n):
            return ARb[:, 2 * off:2 * (off + n)]

        def ark(off, n):
            return [("ar", b) for b in range(off // 256, (off + n + 255) // 256)]

        ident = cm[:, 0, :]
        ones = cm[:, 1, :]
        tri = cm[:, 2, :]
        negm = cm[:, 3, :]

        def pcol(l, name, a=0, b=None):
            o, w = PCOL[name]
            if b is None:
                b = w
            return ppS[:, l, o + a:o + b]

        def mm(out, lhsT, rhs, start, stop, reads, writes):
            P.op("pe", lambda e: e.matmul(out, lhsT=lhsT, rhs=rhs, start=start, stop=stop), reads, writes)

        def act(out, in_, func, reads, writes, bias=None, scale=None):
            kw = {}
            if bias is not None:
                kw["bias"] = bias
            if scale is not None:
                kw["scale"] = scale
            P.op("act", lambda e: e.activation(out=out, in_=in_, func=func, **kw), reads, writes)

        def tt(out, in0, in1, op, reads, writes, eng="dve"):
            P.op(eng, lambda e: e.tensor_tensor(out=out, in0=in0, in1=in1, op=op), reads, writes)

        def ts(out, in0, s1, s2, op0, op1, reads, writes, eng="dve"):
            if s2 is None:
                P.op(eng, lambda e: e.tensor_scalar(out=out, in0=in0, scalar1=s1, scalar2=None, op0=op0), reads, writes)
            else:
                P.op(eng, lambda e: e.tensor_scalar(out=out, in0=in0, scalar1=s1, scalar2=s2, op0=op0, op1=op1), reads, writes)

        def stt(out, in0, scalar, in1, op0, op1, reads, writes, eng="dve"):
            P.op(eng, lambda e: e.scalar_tensor_tensor(out=out, in0=in0, scalar=scalar, in1=in1, op0=op0, op1=op1), reads, writes)

        def cp(out, in_, reads, writes, eng="dve"):
            P.op(eng, lambda e: e.tensor_copy(out=out, in_=in_), reads, writes)

        def mset(ap, val, writes, eng="dve"):
            P.op(eng, lambda e: e.memset(ap, val), [], writes)

        def dma(eng, out, in_, reads, writes):
            return P.dma(eng, lambda e: e.dma_start(out=out, in_=in_), reads, writes)

        dma("pool", cm[:, :, :], cmat, [], ["cm"])
        dma("sp", ppS[:, :, :], pp.rearrange("l p c -> p l c"), [], ["pp"])
        dma("pool", poolwS[:, :, :, :], poolw.rearrange("l p c d -> p l c d"), [], ["poolw"])
        dma("pool", gluwS[:, :, :, :], gluw.rearrange("l p c d -> p l c d"), [], ["gluw"])
        mset(epsT[:, 0:1], EPS, ["eps"])
        mset(epsT[:, 1:2], 1.0, ["eps"])
        for t_, kk in ((cPool, "cPool"), (cSc, "cSc"), (cCv, "cCv"), (cS, "cS"), (cX, "cX")):
            mset(t_[:], 0.0, [kk])
        act(condb[:, :], pcol(0, "cond"), AF.Silu, ["pp"], ["condb"])

        WADA = 0
        bi = 0
        for l in range(DEPTH):
            for blk in range(3):
                slot = bi % 2
                bi += 1
                wsl = arb(WADA + slot * 2048, 2048).rearrange("p (k c) -> p k c", k=8)
                wk = ark(WADA + slot * 2048, 2048)
                dma("pool", wsl, ada_w[l, :, blk * 512:(blk + 1) * 512].rearrange("(k p) c -> p k c", p=128), [], wk)
                for jj in range(4):
                    j = l * 12 + blk * 4 + jj
                    for k in range(KC):
                        mm(ps[6][:, j:j + 1], wsl[:, k, jj * 128:(jj + 1) * 128], condb[:, k:k + 1], k == 0, k == KC - 1,
                           wk + ["condb"], [("ps", 6)])
        mpk = tiny[:, 0:24].rearrange("p (l c) -> p l c", l=2)
        tyk0 = [("tiny", "all")]
        for l in range(DEPTH):
            tt(mpk[:, l, :], ps[6][:, l * 12:(l + 1) * 12], pcol(l, "adab", 0, 12), ALU.add, [("ps", 6), "pp"], tyk0)
        dma("sp", ccd[("m", 0, "i")][:, 0:24], tiny[:, 0:24], tyk0, [("cci", "m", 0)])
        allgather("m", 0, [("cci", "m", 0)], [("cco", "m", 0)])
        mrb = tiny[:, 32:160].rearrange("p (r f) -> p r f", r=4)
        dma("sp", mrb, ccd[("m", 0, "o")].rearrange("(r p) f -> p r f", p=128), [("cco", "m", 0)], tyk0)
        for l in range(DEPTH):
            for sg_ in range(4):
                cp(modT[:, l, sg_ * 12:(sg_ + 1) * 12], mrb[:, sg_, l * 12:(l + 1) * 12], tyk0, [("mod", l)])
        for l in range(depth):
            stt(der[:, l, 0:8], modT[:, l, 8:16], 1.0, pcol(l, "nw1"), ALU.add, ALU.mult, [("mod", l), "pp"], [("der", l)])
            stt(der[:, l, 24:32], modT[:, l, 32:40], 1.0, pcol(l, "nw2"), ALU.add, ALU.mult, [("mod", l), "pp"], [("der", l)])
            cp(der[:, l, 8:16], modT[:, l, 0:8], [("mod", l)], [("der", l)])
            cp(der[:, l, 16:24], modT[:, l, 16:24], [("mod", l)], [("der", l)])
            cp(der[:, l, 32:40], modT[:, l, 24:32], [("mod", l)], [("der", l)])
            cp(der[:, l, 40:48], modT[:, l, 40:48], [("mod", l)], [("der", l)])

        def S1(l, k): return der[:, l, 0 + k:1 + k]
        def B1(l, k): return der[:, l, 8 + k:9 + k]
        def G1(l, k): return der[:, l, 16 + k:17 + k]
        def S2(l, k): return der[:, l, 24 + k:25 + k]
        def B2(l, k): return der[:, l, 32 + k:33 + k]
        def G2(l, k): return der[:, l, 40 + k:41 + k]

        def param_prologue():
            for l in range(depth):
                act(ssdp[:, l, 0:4], pcol(l, "alog"), AF.Exp, ["pp"], [("ssdp", l)])
                ts(ssdp[:, l, 0:4], ssdp[:, l, 0:4], -1.0, None, ALU.mult, None, [("ssdp", l)], [("ssdp", l)])
                cp(ssdp[:, l, 4:8], pcol(l, "dtb"), ["pp"], [("ssdp", l)])
                cp(ssdp[:, l, 8:12], pcol(l, "dsk"), ["pp"], [("ssdp", l)])

            def tk(n): return [("tiny", n)]
            for l in range(depth):
                tyk = [("tiny", "all")]
                stp = tiny[:, 0:8]
                act(stp, pcol(l, "lst"), AF.Exp, ["pp"], tyk)
                mag = tiny[:, 8:16]
                tt(mag, pcol(l, "are"), stp, ALU.mult, ["pp"] + tyk, tyk)
                act(mag, mag, AF.Exp, tyk, tyk)
                th = tiny[:, 16:24]
                tt(th, pcol(l, "aim"), stp, ALU.mult, ["pp"] + tyk, tyk)
                sa = tiny[:, 24:32]
                ca = tiny[:, 32:40]
                act(sa, th, AF.Sin, tyk, tyk, scale=1.0 / 16.0)
                ts(ca, th, 1.0 / 16.0, float(np.pi / 2), ALU.mult, ALU.add, tyk, tyk)
                act(ca, ca, AF.Sin, tyk, tyk)
                for _ in range(4):
                    t2a, t2b = tiny[:, 272:280], tiny[:, 280:288]
                    tt(t2a, ca, ca, ALU.mult, tyk, tyk)
                    tt(t2b, sa, sa, ALU.mult, tyk, tyk)
                    tt(sa, sa, ca, ALU.mult, tyk, tyk)
                    ts(sa, sa, 2.0, None, ALU.mult, None, tyk, tyk)
                    tt(ca, t2a, t2b, ALU.subtract, tyk, tyk)
                lr = tiny[:, 40:48]
                li = tiny[:, 48:56]
                tt(lr, mag, ca, ALU.mult, tyk, tyk)
                tt(li, mag, sa, ALU.mult, tyk, tyk)
                den = tiny[:, 56:64]
                t0 = tiny[:, 64:72]
                tt(den, pcol(l, "are"), pcol(l, "are"), ALU.mult, ["pp"] + tyk, tyk)
                tt(t0, pcol(l, "aim"), pcol(l, "aim"), ALU.mult, ["pp"] + tyk, tyk)
                tt(den, den, t0, ALU.add, tyk, tyk)
                P.op("dve", lambda e, den=den: e.reciprocal(out=den, in_=den), tyk, tyk)
                nr = tiny[:, 72:80]
                ts(nr, lr, -1.0, None, ALU.add, None, tyk, tyk)
                fr = tiny[:, 80:88]
                fi = tiny[:, 88:96]
                t1 = tiny[:, 96:104]
                tt(fr, nr, pcol(l, "are"), ALU.mult, ["pp"] + tyk, tyk)
                tt(t1, li, pcol(l, "aim"), ALU.mult, ["pp"] + tyk, tyk)
                tt(fr, fr, t1, ALU.add, tyk, tyk)
                tt(fr, fr, den, ALU.mult, tyk, tyk)
                tt(fi, li, pcol(l, "are"), ALU.mult, ["pp"] + tyk, tyk)
                tt(t1, nr, pcol(l, "aim"), ALU.mult, ["pp"] + tyk, tyk)
                tt(fi, fi, t1, ALU.subtract, tyk, tyk)
                tt(fi, fi, den, ALU.mult, tyk, tyk)
                dma("sp", tiny[:, 512:768], s5b[l], [], tyk)
                bre = tiny[:, 512:640].rearrange("p (r h) -> p r h", r=8)
                bim = tiny[:, 640:768].rearrange("p (r h) -> p r h", r=8)
                frb = fr.unsqueeze(2).to_broadcast([128, 8, 16])
                fib = fi.unsqueeze(2).to_broadcast([128, 8, 16])
                tmpb = tiny[:, 128:256].rearrange("p (r h) -> p r h", r=8)
                tt(bbS[:, l, 0, :, :], bre, frb, ALU.mult, ["pp"] + tyk, [("bb", l)])
                tt(tmpb, bim, fib, ALU.mult, ["pp"] + tyk, tyk)
                tt(bbS[:, l, 0, :, :], bbS[:, l, 0, :, :], tmpb, ALU.subtract, tyk + [("bb", l)], [("bb", l)])
                tt(bbS[:, l, 1, :, :], bim, frb, ALU.mult, ["pp"] + tyk, [("bb", l)])
                tt(tmpb, bre, fib, ALU.mult, ["pp"] + tyk, tyk)
                tt(bbS[:, l, 1, :, :], bbS[:, l, 1, :, :], tmpb, ALU.add, tyk + [("bb", l)], [("bb", l)])
                ts(bbS[:, l, 1, :, :], bbS[:, l, 1, :, :], -1.0, None, ALU.mult, None, [("bb", l)], [("bb", l)])
                ts(li, li, -1.0, None, ALU.mult, None, tyk, tyk)
                pk = [("s5pw", l)]
                cp(s5pw[:, l, :, 0, 0], lr, tyk, pk)
                cp(s5pw[:, l, :, 1, 0], li, tyk, pk)
                for j in range(1, 16):
                    pr_, pi_ = s5pw[:, l, :, 0, j - 1], s5pw[:, l, :, 1, j - 1]
                    nr_, ni_ = s5pw[:, l, :, 0, j], s5pw[:, l, :, 1, j]
                    ta, tb_ = tiny[:, 256:264], tiny[:, 264:272]
                    tt(ta, pr_, lr, ALU.mult, tyk + pk, tyk)
                    tt(tb_, pi_, li, ALU.mult, tyk + pk, tyk)
                    tt(nr_, ta, tb_, ALU.subtract, tyk, pk)
                    tt(ta, pr_, li, ALU.mult, tyk + pk, tyk)
                    tt(tb_, pi_, lr, ALU.mult, tyk + pk, tyk)
                    tt(ni_, ta, tb_, ALU.add, tyk, pk)
                ts(s5pw[:, l, :, 2, :], s5pw[:, l, :, 1, :], -1.0, None, ALU.mult, None, pk, pk)
                lk = [("s5lv", l)]
                cp(s5lv[:, l, :, 0, 0], s5pw[:, l, :, 0, 15], pk, lk)
                cp(s5lv[:, l, :, 1, 0], s5pw[:, l, :, 1, 15], pk, lk)
                for k in range(1, 8):
                    pr_, pi_ = s5lv[:, l, :, 0, k - 1], s5lv[:, l, :, 1, k - 1]
                    ta, tb_ = tiny[:, 256:264], tiny[:, 264:272]
                    tt(ta, pr_, pr_, ALU.mult, lk + tyk, tyk)
                    tt(tb_, pi_, pi_, ALU.mult, lk + tyk, tyk)
                    tt(s5lv[:, l, :, 0, k], ta, tb_, ALU.subtract, tyk, lk)
                    tt(ta, pr_, pi_, ALU.mult, lk + tyk, tyk)
                    ts(s5lv[:, l, :, 1, k], ta, 2.0, None, ALU.mult, None, tyk, lk)
                ts(s5lv[:, l, :, 2, :], s5lv[:, l, :, 1, :], -1.0, None, ALU.mult, None, lk, lk)

        acc_i = [0]

        def next_acc():
            b = acc_i[0] % 4
            acc_i[0] += 1
            return b

        def rms_stats(src_fn, nchunks, denom, rstd_off, sq_off, src_keys_fn):
            for tb in range(NTB):
                bank = 4 + (tb % 2)
                for k in range(nchunks):
                    so = sq_off + ((tb * nchunks + k) % 4) * 256
                    sq = arb(so, 256)
                    src = src_fn(k, tb)
                    tt(sq, src, src, ALU.mult, src_keys_fn(k, tb), ark(so, 256), eng="pool")
                    mm(ps[bank][:, :], ones, sq, k == 0, k == nchunks - 1, ark(so, 256) + ["cm"], [("ps", bank)])
                r = arf(rstd_off + tb * TB, TB)
                rk = ark(rstd_off + tb * TB, TB)
                act(r, ps[bank][:, :], AF.Ln, [("ps", bank), "eps"], rk, bias=epsT[:, 0:1], scale=1.0 / denom)
                act(r, r, AF.Exp, rk, rk, scale=-0.5)

        def norm_mod(l, s_fn, b_fn):
            RSTD, SQ, TMP = 0, 2048, 3072
            rms_stats(lambda k, tb: hT[:, k, tb * TB:(tb + 1) * TB], KC, float(DM), RSTD, SQ,
                      lambda k, tb: [("hT", k, tb)])
            for k in range(KC):
                for tb in range(NTB):
                    to = TMP + ((k * NTB + tb) % 4) * TB
                    tmp = arf(to, TB)
                    stt(tmp, hT[:, k, tb * TB:(tb + 1) * TB], s_fn(l, k), arf(RSTD + tb * TB, TB), ALU.mult, ALU.mult,
                        [("hT", k, tb), ("der", l)] + ark(RSTD + tb * TB, TB), ark(to, TB))
                    act(uT[:, k, tb * TB:(tb + 1) * TB], tmp, AF.Identity, ark(to, TB) + [("der", l)], [("uT", k, tb)],
                        bias=b_fn(l, k))

        WIN = 14336
        win_i = [0]

        def load_win(l, c0, ncol):
            assert ncol <= 128
            slot = win_i[0] % 2
            win_i[0] += 1
            off = WIN + slot * 512
            w = arb(off, 512).rearrange("p (k c) -> p k c", k=8)
            dma("pool", w[:, :, 0:ncol], w_in[l, :, c0:c0 + ncol].rearrange("(k p) c -> p k c", p=128), [], ark(off, 512))
            return w, ark(off, 512)

        def proj_chunk(w, wk, cofs, ncol, evac):
            for tb in range(NTB):
                b = next_acc()
                for k in range(KC):
                    mm(ps[b][0:ncol, :], w[:, k, cofs:cofs + ncol], uT[:, k, tb * TB:(tb + 1) * TB], k == 0, k == KC - 1,
                       wk + [("uT", k, tb)], [("ps", b)])
                evac(tb, ps[b][0:ncol, :], ("ps", b))

        def dump(name, ap, keys):
            if dbg and name in dbg_out:
                dma("sp", dbg_out[name], ap, keys, [("dbg", name)])

        def all_keys_h():
            return [("hT", k, tb) for k in range(KC) for tb in range(NTB)]

        def all_keys(nm):
            return [(nm, k, tb) for k in range(KC) for tb in range(NTB)]

        def tail_prepass(l):
            PK = arf(0, 64)
            pkk = ark(0, 64)
            gct = tiny[:, 0:32].rearrange("p (c t) -> p c t", c=2)
            tyk = [("tiny", "all")]
            tail = slice(T - 16, T)

            def tproj(w, wk, cofs, evac):
                b = next_acc()
                for k in range(KC):
                    mm(ps[b][:, 0:16], w[:, k, cofs:cofs + 128], uT[:, k, tail], k == 0, k == KC - 1, wk + [("uT", k, 3)], [("ps", b)])
                evac(ps[b][:, 0:16], ("ps", b))
            for c in range(2):
                w, wk = load_win(l, c * 128, 128)
                tproj(w, wk, 0, lambda p_, pk, c=c: act(PK[:, c * 16:(c + 1) * 16], p_, AF.Copy, [pk], pkk))
            for c in range(2):
                w, wk = load_win(l, 512 + c * 128, 128)
                tproj(w, wk, 0, lambda p_, pk, c=c: act(gct[:, c, :], p_, AF.Copy, [pk], tyk))
            for c in range(2):
                w, wk = load_win(l, 768 + c * 128, 128)
                tproj(w, wk, 0, lambda p_, pk, c=c: tt(PK[:, 32 + 2 * c:34 + 2 * c], p_[:, 14:16], gct[:, c, 14:16], ALU.mult,
                                                       [pk] + tyk, pkk))
            for j in range(6):
                w, wk = load_win(l, 1280 + j * 128, 128)
                tproj(w, wk, 0, lambda p_, pk, j=j: act(PK[:, 36 + 3 * j:39 + 3 * j], p_[:, 13:16], AF.Copy, [pk], pkk))
            dma("sp", ccd[("h", l, "i")][:, 0:54], PK[:, 0:54], pkk, [("cci", "h", l)])
            allgather("h", l, [("cci", "h", l)], [("cco", "h", l)])

        def halo_apply(l):
            RB = arf(256, 256).rearrange("p (r f) -> p r f", r=4)
            rbk = ark(256, 256)
            tyk = [("tiny", "all")]
            dma("sp", RB, ccd[("h", l, "o")].rearrange("(r p) f -> p r f", p=128), [("cco", "h", l)], rbk)
            hal = tiny[:, 64:128]
            sel = pcol(l, "sel")
            ts(hal, RB[:, 0, :], sel[:, 0:1], None, ALU.mult, None, rbk + ["pp"], tyk)
            for j in range(1, 4):
                stt(hal, RB[:, j, :], sel[:, j:j + 1], hal, ALU.mult, ALU.add, rbk + ["pp"] + tyk, tyk)
            cp(cPool[:, l, :, :], hal[:, 0:32].rearrange("p (c t) -> p c t", c=2), tyk, [("cPool", l, 0), ("cPool", l, 1)])
            cp(cSc[:, l, :, :], hal[:, 32:36].rearrange("p (c t) -> p c t", c=2), tyk, [("cSc", l, 0), ("cSc", l, 1)])
            cp(cCv[:, l, :, :], hal[:, 36:54].rearrange("p (c t) -> p c t", c=6), tyk, [("cCv", l, j) for j in range(6)])

        def s5_combine(l):
            tyk = [("tiny", "all")]
            RB = tiny[:, 816:880].rearrange("p (r f) -> p r f", r=4)
            dma("sp", RB, ccd[("x", l, "o")].rearrange("(r p) f -> p r f", p=128), [("cco", "x", l)], tyk)
            sel = pcol(l, "sel")
            Lr, Li = s5lv[:, l, :, 0, 7], s5lv[:, l, :, 1, 7]
            lk = [("s5lv", l)]
            ar, ai, pr, pi, t1, t2 = (tiny[:, a:a + 8] for a in (768, 776, 784, 792, 880, 888))
            for t_ in (ar, ai, pr, pi):
                mset(t_, 0.0, tyk)
            for j in range(4):
                stt(ar, pr, sel[:, 4 + j:5 + j], ar, ALU.mult, ALU.add, tyk + ["pp"], tyk)
                stt(ai, pi, sel[:, 4 + j:5 + j], ai, ALU.mult, ALU.add, tyk + ["pp"], tyk)
                if j < 3:
                    F = RB[:, j, :].rearrange("p (r a) -> p r a", a=2)
                    tt(t1, pr, Lr, ALU.mult, tyk + lk, tyk)
                    tt(t2, pi, Li, ALU.mult, tyk + lk, tyk)
                    tt(t1, t1, t2, ALU.subtract, tyk, tyk)
                    tt(t2, pr, Li, ALU.mult, tyk + lk, tyk)
                    tt(pr, t1, F[:, :, 0], ALU.add, tyk, tyk)
                    tt(t1, pi, Lr, ALU.mult, tyk + lk, tyk)
                    tt(t1, t1, t2, ALU.add, tyk, tyk)
                    tt(pi, t1, F[:, :, 1], ALU.add, tyk, tyk)
            cp(cX[:, l, :, 0], ar, tyk, [("cX", l, r) for r in range(8)])
            cp(cX[:, l, :, 1], ai, tyk, [("cX", l, r) for r in range(8)])

        def mixer_pool(l, seg):
            V, SA, SB, PB = 0, 2304, 4608, 6912
            for c in range(2):
                w, wk = load_win(l, c * 128, 128)
                v = arf(V, 2064)
                vk = ark(V, 2064)
                cp(v[:, 0:16], cPool[:, l, c, :], [("cPool", l, c)], vk)
                proj_chunk(w, wk, 0, 128,
                           lambda tb, p_, pk: act(v[:, 16 + tb * TB:16 + (tb + 1) * TB], p_, AF.Copy, [pk], vk))
                cp(cPool[:, l, c, :], v[:, 2048:2064], vk, [("cPool", l, c)])
                sa, sbb = arf(SA, 2064), arf(SB, 2064)
                sak, sbk = ark(SA, 2064), ark(SB, 2064)
                tt(sa[:, 1:2064], v[:, 1:2064], v[:, 0:2063], ALU.add, vk, sak)
                tt(sbb[:, 3:2064], sa[:, 3:2064], sa[:, 1:2062], ALU.add, sak, sbk)
                if c == 0:
                    lo_src, hi_src, lo_w, hi_w = sa, sbb, 2, 4
                else:
                    tt(sa[:, 7:2064], sbb[:, 7:2064], sbb[:, 3:2060], ALU.add, sbk, sak)
                    tt(sbb[:, 15:2064], sa[:, 15:2064], sa[:, 7:2056], ALU.add, sak, sbk)
                    lo_src, hi_src, lo_w, hi_w = sa, sbb, 8, 16
                pb = arb(PB, 2048).rearrange("p (c t) -> p c t", c=2)
                pbk = ark(PB, 2048)
                stt(pb[0:64, c, :], lo_src[0:64, 16:2064], 1.0 / lo_w, v[0:64, 16:2064], ALU.mult, ALU.subtract, sak + sbk + vk, pbk)
                stt(pb[64:128, c, :], hi_src[64:128, 16:2064], 1.0 / hi_w, v[64:128, 16:2064], ALU.mult, ALU.subtract, sak + sbk + vk, pbk)
                if True:
                    ic = pcol(l, "invc").rearrange("p (c t) -> p c t", c=2)
                    tq = tiny[:, 512:528]
                    for (r0, r1, src) in ((0, 64, lo_src), (64, 128, hi_src)):
                        tt(tq[r0:r1, :], src[r0:r1, 16:32], ic[r0:r1, c, :], ALU.mult, sak + sbk + ["pp"], [("tiny", "all")])
                        tt(pb[r0:r1, c, 0:16], tq[r0:r1, :], v[r0:r1, 16:32], ALU.subtract, [("tiny", "all")] + vk, pbk)
            pb = arb(PB, 2048).rearrange("p (c t) -> p c t", c=2)
            pbk = ark(PB, 2048)
            for c in range(2):
                for tb in range(NTB):
                    b = next_acc()
                    mm(ps[b][:, :], poolwS[:, l, c, :], pb[:, c, tb * TB:(tb + 1) * TB], True, True, pbk + ["poolw"], [("ps", b)])
                    ts(Yr[:, c, tb * TB:(tb + 1) * TB], ps[b][:, :], pcol(l, "pscale", c, c + 1), None, ALU.mult, None,
                       [("ps", b), "pp"], [("Yr", c, tb)])

        def mixer_sconv(l, seg):
            GC, G, T1 = 0, 2048, 4352
            for c in range(2):
                gc = arf(GC, 2048)
                gck = ark(GC, 2048)
                wgc, wgck = load_win(l, 512 + c * 128, 128)
                proj_chunk(wgc, wgck, 0, 128,
                           lambda tb, p_, pk: act(gc[:, tb * TB:(tb + 1) * TB], p_, AF.Copy, [pk], gck))
                whh, whhk = load_win(l, 768 + c * 128, 128)
                g = arf(G, 2050)
                gk = ark(G, 2050)
                cp(g[:, 0:2], cSc[:, l, c, :], [("cSc", l, c)], gk)
                proj_chunk(whh, whhk, 0, 128,
                           lambda tb, p_, pk: tt(g[:, 2 + tb * TB:2 + (tb + 1) * TB], p_, gc[:, tb * TB:(tb + 1) * TB], ALU.mult,
                                                 [pk] + gck, gk))
                cp(cSc[:, l, c, :], g[:, 2048:2050], gk, [("cSc", l, c)])
                t1 = arf(T1, 2048)
                t1k = ark(T1, 2048)
                wv = pcol(l, "scw").rearrange("p (c k) -> p c k", c=2)
                ts(t1, g[:, 0:2048], wv[:, c, 0:1], None, ALU.mult, None, gk + ["pp"], t1k)
                stt(t1, g[:, 1:2049], wv[:, c, 1:2], t1, ALU.mult, ALU.add, gk + ["pp"] + t1k, t1k)
                stt(t1, g[:, 2:2050], wv[:, c, 2:3], t1, ALU.mult, ALU.add, gk + ["pp"] + t1k, t1k)
                wgb, wgbk = load_win(l, 256 + c * 128, 128)
                proj_chunk(wgb, wgbk, 0, 128,
                           lambda tb, p_, pk: tt(Yr[:, 2 + c, tb * TB:(tb + 1) * TB], p_, t1[:, tb * TB:(tb + 1) * TB], ALU.mult,
                                                 [pk] + t1k, [("Yr", 2 + c, tb)]))

        def mixer_ssd(l, seg):
            SZ, RAW, ACC, XBC = 0, 2048, 4352, 6400
            XTOK, BTOK = 2048, 4096
            sz = arb(SZ, 2048).rearrange("p (c t) -> p c t", c=2)
            szk = ark(SZ, 2048)
            for c in range(2):
                wz, wzk = load_win(l, 1024 + c * 128, 128)
                proj_chunk(wz, wzk, 0, 128,
                           lambda tb, p_, pk: act(sz[:, c, tb * TB:(tb + 1) * TB], p_, AF.Silu, [pk], szk))
            xbc = arb(XBC, 6144).rearrange("p (c t) -> p c t", c=6)
            kxb = [ark(XBC + j * 1024, 1024) for j in range(6)]
            cw = pcol(l, "cvw").rearrange("p (c k) -> p c k", c=6)
            for j in range(6):
                wx, wxk = load_win(l, 1280 + j * 128, 128)
                raw = arf(RAW, 2051)
                rawk = ark(RAW, 2051)
                cp(raw[:, 0:3], cCv[:, l, j, :], [("cCv", l, j)], rawk)
                proj_chunk(wx, wxk, 0, 128,
                           lambda tb, p_, pk: act(raw[:, 3 + tb * TB:3 + (tb + 1) * TB], p_, AF.Copy, [pk], rawk))
                cp(cCv[:, l, j, :], raw[:, 2048:2051], rawk, [("cCv", l, j)])
                acc = arf(ACC, 2048)
                acck = ark(ACC, 2048)
                ts(acc, raw[:, 0:2048], cw[:, j, 0:1], None, ALU.mult, None, rawk + ["pp"], acck)
                for kk in range(1, 4):
                    stt(acc, raw[:, kk:kk + 2048], cw[:, j, kk:kk + 1], acc, ALU.mult, ALU.add, rawk + acck + ["pp"], acck)
                act(xbc[:, j, :], acc, AF.Silu, acck + ["pp"], kxb[j], bias=pcol(l, "cvb", j, j + 1))
            print("  ssd: before dt", P.total)
            wd, wdk = load_win(l, 2048, 4)
            for c in range(NCH):
                for k in range(KC):
                    mm(ps[6][:, c * 4:(c + 1) * 4], uT[:, k, c * 128:(c + 1) * 128], wd[:, k, 0:4], k == 0, k == KC - 1,
                       wdk + [("uT", k, c // 4)], [("ps", 6)])
            print("  ssd: before small", P.total)
            tyk = [("tiny", "all")]
            def v3(a): return tiny[:, a:a + 64].rearrange("p (c h) -> p c h", h=4)
            dt_, adt, acs, tot, eacs, dte, cd, ddte = (v3(a) for a in (0, 64, 128, 192, 256, 320, 384, 448))
            xsp = v3(512)
            ex = v3(576)
            bc4 = lambda a, b: ssdp[:, l, a:b].unsqueeze(1).to_broadcast([128, NCH, 4])
            tt(xsp, ps[6][:, 0:64].rearrange("p (c h) -> p c h", h=4), bc4(4, 8), ALU.add, [("ps", 6), ("ssdp", l)], tyk)
            ts(ex, xsp, 30.0, None, ALU.min, None, tyk, tyk)
            act(ex, ex, AF.Exp, tyk, tyk)
            act(ex, ex, AF.Ln, tyk + ["eps"], tyk, bias=epsT[:, 1:2])
            tt(dt_, ex, xsp, ALU.max, tyk, tyk)
            tt(adt, dt_, bc4(0, 4), ALU.mult, tyk + [("ssdp", l)], tyk)
            ahi = tinyb[:, 0:64]
            alo = tinyb[:, 64:128]
            tbk = [("tinyb", "a")]
            adf = tiny[:, 64:128]
            cp(ahi, adf, tyk, tbk)
            tt(tiny[:, 640:704], adf, ahi, ALU.subtract, tyk + tbk, tyk)
            cp(alo, tiny[:, 640:704], tyk, tbk)
            cp(tiny[:, 768:832], ahi, tbk, tyk)
            mm(ps[6][:, 64:128], tri, ahi, True, False, tbk + ["cm"], [("ps", 6)])
            mm(ps[6][:, 64:128], tri, alo, False, True, tbk + ["cm"], [("ps", 6)])
            mm(ps[6][:, 128:192], ones, ahi, True, False, tbk + ["cm"], [("ps", 6)])
            mm(ps[6][:, 128:192], ones, alo, False, True, tbk + ["cm"], [("ps", 6)])
            cp(tiny[:, 128:256], ps[6][:, 64:192], [("ps", 6)], tyk)
            act(tiny[:, 256:320], tiny[:, 128:192], AF.Exp, tyk, tyk)
            tt(tiny[:, 320:384], tiny[:, 192:256], tiny[:, 128:192], ALU.subtract, tyk, tyk)
            act(tiny[:, 320:384], tiny[:, 320:384], AF.Exp, tyk, tyk)
            act(tiny[:, 384:448], tiny[:, 192:256], AF.Exp, tyk, tyk)
            tt(tiny[:, 448:512], tiny[:, 0:64], tiny[:, 320:384], ALU.mult, tyk, tyk)
            ts(tiny[:, 704:768], tiny[:, 128:192], -1.0, None, ALU.mult, None, tyk, tyk)
            nacs = v3(704)
            print("  ssd: before transposes", P.total)
            xtok = arb(XTOK, 2048).rearrange("p (c f) -> p c f", c=NCH)
            btok = arb(BTOK, 2048).rearrange("p (c f) -> p c f", c=NCH)
            xtk, btk = ark(XTOK, 2048), ark(BTOK, 2048)
            ti = 0
            for c in range(NCH):
                for j in range(4):
                    o = (ti % 4) * 128
                    ti += 1
                    bnk = 4 + (ti - 1) % 4
                    pk_ = ("ps", bnk)
                    mm(ps[bnk][:, 0:128], xbc[:, j, c * 128:(c + 1) * 128], ident, True, True, kxb[j] + ["cm"], [pk_])
                    dst = (xtok if j < 2 else btok)[:, c, (j % 2) * 128:(j % 2 + 1) * 128]
                    if ti % 2 == 0:
                        cp(dst, ps[bnk][:, 0:128], [pk_], xtk if j < 2 else btk)
                    else:
                        act(dst, ps[bnk][:, 0:128], AF.Copy, [pk_], xtk if j < 2 else btk)
            WT = XBC
            E_ = arb(WT, 256).rearrange("p (h s) -> p h s", h=4)
            MT = arb(WT + 256, 256).rearrange("p (h s) -> p h s", h=4)
            RH = arb(WT + 512, 512).rearrange("p (a h s) -> p a h s", a=2, h=4)
            XDT = arb(WT + 1024, 128)
            XDE = arb(WT + 1152, 128)
            YSB = arf(WT + 1280, 256)
            YTK = arb(WT + 1536, 128)
            SBF = arb(WT + 1664, 128)
            STMP = arf(WT + 1792, 256)
            kE, kMT, kRH, kXDT, kXDE, kYSB, kYTK, kSBF, kST = (ark(WT + a, n) for a, n in
                ((0, 256), (256, 256), (512, 512), (1024, 128), (1152, 128), (1280, 256), (1536, 128), (1664, 128), (1792, 256)))
            print("  ssd: before main loop", P.total)
            Sst = cS[:, l, :]
            kS = [("cS", l)]
            PKG, RBO, PTO = 12544, 12816, 13904
            pkg = arf(PKG, 272)
            pkgk = ark(PKG, 272)
            SL = pkg[:, 0:256]
            mset(SL, 0.0, pkgk)
            for c in range(NCH):
                x3 = xtok[:, c, :].rearrange("p (h d) -> p h d", h=4)
                tt(XDE.rearrange("p (h d) -> p h d", h=4), x3, ddte[:, c, :].unsqueeze(2).to_broadcast([128, 4, 64]), ALU.mult, xtk + tyk, kXDE)
                for g in range(2):
                    mm(ps[5][:, g * 128:(g + 1) * 128], btok[:, c, g * 128:(g + 1) * 128], XDE[:, g * 128:(g + 1) * 128], True, True,
                       btk + kXDE, [("ps", 5)])
                tt(STMP.rearrange("p (h d) -> p h d", h=4), SL.rearrange("p (h d) -> p h d", h=4),
                   cd[:, c, :].unsqueeze(2).to_broadcast([128, 4, 64]), ALU.mult, pkgk + tyk, kST)
                tt(SL, STMP, ps[5][:, 0:256], ALU.add, kST + [("ps", 5)], pkgk)
            tr_ = tiny[:, 832:864].rearrange("p (c h) -> p c h", h=4)
            tt(tr_, tot[:, 0:8, :], tot[:, 8:16, :], ALU.add, tyk, tyk)
            tt(tr_[:, 0:4, :], tr_[:, 0:4, :], tr_[:, 4:8, :], ALU.add, tyk, tyk)
            tt(tr_[:, 0:2, :], tr_[:, 0:2, :], tr_[:, 2:4, :], ALU.add, tyk, tyk)
            tt(tr_[:, 0:1, :], tr_[:, 0:1, :], tr_[:, 1:2, :], ALU.add, tyk, tyk)
            act(pkg[:, 256:260], tiny[:, 832:836], AF.Exp, tyk, pkgk)
            dma("sp", ccd[("s", l, "i")][:, 0:260], pkg[:, 0:260], pkgk, [("cci", "s", l)])
            allgather("s", l, [("cci", "s", l)], [("cco", "s", l)])
            RBs = arf(RBO, 1088).rearrange("p (r f) -> p r f", r=4)
            rbsk = ark(RBO, 1088)
            dma("sp", RBs, ccd[("s", l, "o")].rearrange("(r p) f -> p r f", p=128), [("cco", "s", l)], rbsk)
            PT = arf(PTO, 256)
            ptk = ark(PTO, 256)
            sel = pcol(l, "sel")
            mset(Sst, 0.0, kS)
            mset(PT, 0.0, ptk)
            for j in range(4):
                stt(Sst, PT, sel[:, 4 + j:5 + j], Sst, ALU.mult, ALU.add, ptk + kS + ["pp"], kS)
                if j < 3:
                    tt(PT.rearrange("p (h d) -> p h d", h=4), PT.rearrange("p (h d) -> p h d", h=4),
                       RBs[:, j, 256:260].unsqueeze(2).to_broadcast([128, 4, 64]), ALU.mult, ptk + rbsk, ptk)
                    tt(PT, PT, RBs[:, j, 0:256], ALU.add, ptk + rbsk, ptk)
            cp(SBF, Sst, kS, kSBF)
            for c in range(NCH):
                tsl = slice(c * 128, (c + 1) * 128)
                if c < 2:
                    print("  ssd: chunk", c, P.total)
                for h in range(4):
                    ts(RH[:, 0, h, :], tri, tiny[:, 768 + c * 4 + h:768 + c * 4 + h + 1], None, ALU.mult, None, ["cm"] + tyk, kRH)
                    ts(RH[:, 1, h, :], tri, tiny[:, 640 + c * 4 + h:640 + c * 4 + h + 1], None, ALU.mult, None, ["cm"] + tyk, kRH)
                for h in range(4):
                    o = ps[0][:, h * 128:(h + 1) * 128]
                    mm(o, ones, RH[:, 0, h, :], True, False, kRH + ["cm"], [("ps", 0)])
                    mm(o, ones, RH[:, 1, h, :], False, False, kRH + ["cm"], [("ps", 0)])
                    mm(o, ident, negm, False, True, ["cm"], [("ps", 0)])
                for h in range(4):
                    act(E_[:, h, :], ps[0][:, h * 128:(h + 1) * 128], AF.Exp, [("ps", 0)] + tyk, kE, bias=nacs[:, c, h:h + 1])
                for g in range(2):
                    mm(ps[1][:, g * 128:(g + 1) * 128], xbc[:, 2 + g, tsl], xbc[:, 4 + g, tsl], True, True,
                       kxb[2 + g] + kxb[4 + g], [("ps", 1)])
                for h in range(4):
                    g = h // 2
                    tt(MT[:, h, :], ps[1][:, g * 128:(g + 1) * 128], E_[:, h, :], ALU.mult, [("ps", 1)] + kE, kMT)
                x3 = xtok[:, c, :].rearrange("p (h d) -> p h d", h=4)
                tt(XDT.rearrange("p (h d) -> p h d", h=4), x3, dt_[:, c, :].unsqueeze(2).to_broadcast([128, 4, 64]), ALU.mult, xtk + tyk, kXDT)
                tt(XDE.rearrange("p (h d) -> p h d", h=4), x3, ddte[:, c, :].unsqueeze(2).to_broadcast([128, 4, 64]), ALU.mult, xtk + tyk, kXDE)
                for h in range(4):
                    o = ps[2][:, h * 64:(h + 1) * 64]
                    mm(o, MT[:, h, :], XDT[:, h * 64:(h + 1) * 64], True, True, kMT + kXDT, [("ps", 2, "y")])
                for h in range(4):
                    g = h // 2
                    mm(ps[3][:, h * 64:(h + 1) * 64], xbc[:, 4 + g, tsl], SBF[:, h * 64:(h + 1) * 64], True, True,
                       kxb[4 + g] + kSBF, [("ps", 3)])
                tt(YSB.rearrange("p (h d) -> p h d", h=4), x3, ssdp[:, l, 8:12].unsqueeze(2).to_broadcast([128, 4, 64]), ALU.mult,
                   xtk + [("ssdp", l)], kYSB)
                tt(YSB, YSB, ps[2][:, 0:256], ALU.add, kYSB + [("ps", 2, "y")], kYSB)
                tt(STMP.rearrange("p (h d) -> p h d", h=4), ps[3][:, 0:256].rearrange("p (h d) -> p h d", h=4),
                   eacs[:, c, :].unsqueeze(2).to_broadcast([128, 4, 64]), ALU.mult, [("ps", 3)] + tyk, kST)
                tt(YTK, STMP, YSB, ALU.add, kST + kYSB, kYTK)
                for j in range(2):
                    o = j * 128
                    mm(ps[4][:, o:o + 128], YTK[:, j * 128:(j + 1) * 128], ident, True, True, kYTK + ["cm"], [("ps", 4, o)])
                    tt(Yr[:, 4 + j, tsl], ps[4][:, o:o + 128], sz[:, j, tsl], ALU.mult, [("ps", 4, o)] + szk, [("Yr", 4 + j, c // 4)])
                for g in range(2):
                    mm(ps[5][:, g * 128:(g + 1) * 128], btok[:, c, g * 128:(g + 1) * 128], XDE[:, g * 128:(g + 1) * 128], True, True,
                       btk + kXDE, [("ps", 5)])
                tt(STMP.rearrange("p (h d) -> p h d", h=4), Sst.rearrange("p (h d) -> p h d", h=4),
                   cd[:, c, :].unsqueeze(2).to_broadcast([128, 4, 64]), ALU.mult, kS + tyk, kST)
                tt(Sst, STMP, ps[5][:, 0:256], ALU.add, kST + [("ps", 5)], kS)
                act(SBF, Sst, AF.Copy, kS, kSBF)

        def mixer_s5(l, seg):
            U, STG, BT, W, WB, PAT, INJ, TAB = 0, 2048, 4096, 6144, 8192, 10240, 11264, 13312
            tyk = [("tiny", "all")]
            pk = [("s5pw", l)]
            lk = [("s5lv", l)]
            u = arb(U, 2048).rearrange("p (c t) -> p c t", c=2)
            uk = ark(U, 2048)
            for c in range(2):
                wu, wuk = load_win(l, 2052 + c * 128, 128)
                proj_chunk(wu, wuk, 0, 128,
                           lambda tb, p_, pk_: act(u[:, c, tb * TB:(tb + 1) * TB], p_, AF.Copy, [pk_], uk))
            pat = arb(PAT, 1024)
            patk = ark(PAT, 1024)
            mset(pat, 1.0, patk)
            mset(pat.rearrange("p (c j) -> p c j", j=16)[:, :, 0:1], 0.0, patk)
            pw0 = tiny[:, 0:256].rearrange("p (r a j) -> p r a j", r=8, a=2)
            qq = tiny[:, 256:512].rearrange("p (r a j) -> p r a j", r=8, a=2)
            den = tiny[:, 512:640].rearrange("p (r j) -> p r j", r=8)
            tmp = tiny[:, 640:768].rearrange("p (r j) -> p r j", r=8)
            mset(pw0[:, :, 0, 0:1], 1.0, tyk)
            mset(pw0[:, :, 1, 0:1], 0.0, tyk)
            for a in range(2):
                cp(pw0[:, :, a, 1:16], s5pw[:, l, :, a, 0:15], pk, tyk)
            tt(den, pw0[:, :, 0, :], pw0[:, :, 0, :], ALU.mult, tyk, tyk)
            tt(tmp, pw0[:, :, 1, :], pw0[:, :, 1, :], ALU.mult, tyk, tyk)
            tt(den, den, tmp, ALU.add, tyk, tyk)
            P.op("dve", lambda e: e.reciprocal(out=den, in_=den), tyk, tyk)
            tt(qq[:, :, 0, :], pw0[:, :, 0, :], den, ALU.mult, tyk, tyk)
            tt(qq[:, :, 1, :], pw0[:, :, 1, :], den, ALU.mult, tyk, tyk)
            ts(qq[:, :, 1, :], qq[:, :, 1, :], -1.0, None, ALU.mult, None, tyk, tyk)
            bpad = arf(TAB, 512).rearrange("p (r a h) -> p r a h", r=8, a=2)
            cpad = arf(TAB + 512, 512).rearrange("p (r a h) -> p r a h", r=8, a=2)
            tabk = ark(TAB, 1024)
            mset(arf(TAB, 512), 0.0, tabk)
            for a in range(2):
                cp(bpad[0:64, :, a, 0:16], bbS[0:64, l, a, :, :], [("bb", l)], tabk)
                cp(bpad[64:128, :, a, 16:32], bbS[64:128, l, a, :, :], [("bb", l)], tabk)
            dma("sp", arf(TAB + 512, 512), s5c[l], [], tabk)
            Ef = Yr[:, 6:8, :].rearrange("p a t -> p (a t)").bitcast(F32)
            Eall = Ef.rearrange("p (r a c) -> p r a c", r=8, a=2)
            ek = [("Yr", 6 + a, tb) for a in range(2) for tb in range(NTB)]
            bufA = arf(W, 2048).rearrange("p (r a c) -> p r a c", r=8, a=2)
            bufB = arf(WB, 2048).rearrange("p (r a c) -> p r a c", r=8, a=2)
            kA, kB = ark(W, 2048), ark(WB, 2048)
            T1 = arf(STG, 2048)
            T2 = arf(BT, 2048)
            kT1, kT2 = ark(STG, 2048), ark(BT, 2048)

            def bc8(ap, n):
                return ap.unsqueeze(2).to_broadcast([128, 8, n])

            def lvl2(src, sk):
                cur, ck_ = src, sk
                nxt_list = [(bufA, kA), (bufB, kB)]
                if src is bufA:
                    nxt_list = [(bufB, kB), (bufA, kA)]
                for lev in range(7):
                    d = 1 << lev
                    n = 128 - d
                    dst, dk = nxt_list[lev % 2]
                    cr, ci = s5lv[:, l, :, 0, lev], s5lv[:, l, :, 1, lev]
                    t1 = T1[:, 0:8 * n].rearrange("p (r c) -> p r c", r=8)
                    t2 = T2[:, 0:8 * n].rearrange("p (r c) -> p r c", r=8)
                    cp(dst[:, :, :, 0:d], cur[:, :, :, 0:d], ck_, dk)
                    tt(t1, cur[:, :, 0, 0:n], bc8(cr, n), ALU.mult, ck_ + lk, kT1)
                    tt(t2, cur[:, :, 1, 0:n], bc8(ci, n), ALU.mult, ck_ + lk, kT2)
                    tt(t1, t1, t2, ALU.subtract, kT1 + kT2, kT1)
                    tt(dst[:, :, 0, d:128], cur[:, :, 0, d:128], t1, ALU.add, ck_ + kT1, dk)
                    tt(t1, cur[:, :, 1, 0:n], bc8(cr, n), ALU.mult, ck_ + lk, kT1)
                    tt(t2, cur[:, :, 0, 0:n], bc8(ci, n), ALU.mult, ck_ + lk, kT2)
                    tt(t1, t1, t2, ALU.add, kT1 + kT2, kT1)
                    tt(dst[:, :, 1, d:128], cur[:, :, 1, d:128], t1, ALU.add, ck_ + kT1, dk)
                    cur, ck_ = dst, dk
                return cur, ck_

            def run_pass(mode):
                inj = arf(INJ, 2048).rearrange("p (r a c) -> p r a c", r=8, a=2)
                injk = ark(INJ, 2048)
                if mode == "full":
                    cp(bufA, Eall, ek, kA)
                    mur, mui = s5lv[:, l, :, 0, 0], s5lv[:, l, :, 1, 0]
                    xr_, xi_ = cX[:, l, :, 0], cX[:, l, :, 1]
                    ckx = [("cX", l, r) for r in range(8)]
                    a1, a2 = tiny[:, 768:776], tiny[:, 776:784]
                    tt(a1, mur, xr_, ALU.mult, lk + ckx, tyk)
                    tt(a2, mui, xi_, ALU.mult, lk + ckx, tyk)
                    tt(a1, a1, a2, ALU.subtract, tyk, tyk)
                    tt(bufA[:, :, 0, 0], bufA[:, :, 0, 0], a1, ALU.add, kA + tyk, kA)
                    tt(a1, mur, xi_, ALU.mult, lk + ckx, tyk)
                    tt(a2, mui, xr_, ALU.mult, lk + ckx, tyk)
                    tt(a1, a1, a2, ALU.add, tyk, tyk)
                    tt(bufA[:, :, 1, 0], bufA[:, :, 1, 0], a1, ALU.add, kA + tyk, kA)
                    res, rk = lvl2(bufA, kA)
                    lr8, li8 = s5pw[:, l, :, 0, 0], s5pw[:, l, :, 1, 0]
                    n = 127
                    t1 = T1[:, 0:8 * n].rearrange("p (r c) -> p r c", r=8)
                    t2 = T2[:, 0:8 * n].rearrange("p (r c) -> p r c", r=8)
                    tt(t1, res[:, :, 0, 0:n], bc8(lr8, n), ALU.mult, rk + pk, kT1)
                    tt(t2, res[:, :, 1, 0:n], bc8(li8, n), ALU.mult, rk + pk, kT2)
                    tt(inj[:, :, 0, 1:128], t1, t2, ALU.subtract, kT1 + kT2, injk)
                    tt(t1, res[:, :, 1, 0:n], bc8(lr8, n), ALU.mult, rk + pk, kT1)
                    tt(t2, res[:, :, 0, 0:n], bc8(li8, n), ALU.mult, rk + pk, kT2)
                    tt(inj[:, :, 1, 1:128], t1, t2, ALU.add, kT1 + kT2, injk)
                    tt(a1, lr8, xr_, ALU.mult, pk + ckx, tyk)
                    tt(a2, li8, xi_, ALU.mult, pk + ckx, tyk)
                    tt(inj[:, :, 0, 0], a1, a2, ALU.subtract, tyk, injk)
                    tt(a1, lr8, xi_, ALU.mult, pk + ckx, tyk)
                    tt(a2, li8, xr_, ALU.mult, pk + ckx, tyk)
                    tt(inj[:, :, 1, 0], a1, a2, ALU.add, tyk, injk)

                stg = arb(STG, 2048).rearrange("p (a j c) -> p a j c", a=2, j=16)
                stgk = ark(STG, 2048)
                btv = arb(BT, 2048).rearrange("p (a j c) -> p a j c", a=2, j=16)
                btk_ = ark(BT, 2048)
                Wn = arf(W, 2048)
                wk_ = ark(W, 2048)
                Wv = Wn.rearrange("p (c j) -> p j c", j=16)
                wb = arb(WB, 2048).rearrange("p (a t) -> p a t", a=2)
                wbk = ark(WB, 2048)

                def b16(ap):
                    return ap.unsqueeze(2).to_broadcast([128, 16, 32])

                def h32(ap):
                    return ap.unsqueeze(1).to_broadcast([128, 16, 32])

                ct = Yr[:, 4:6, :].rearrange("p a (j c) -> p a j c", j=16)
                ctk = [("Yr", 4 + a_, t_) for a_ in range(2) for t_ in range(NTB)]
                if mode == "p1":
                    tmp_f, tmpk = arf(WB, 1024), ark(WB, 1024)
                    W1n, w1k_ = arf(INJ, 2048), ark(INJ, 2048)
                else:
                    tmp_f = Yr[:, 7, :].bitcast(F32)
                    tmpk = [("Yr", 7, t_) for t_ in range(NTB)]
                    W1n = Yr[:, 0:2, :].rearrange("p a t -> p (a t)").bitcast(F32)
                    w1k_ = [("Yr", a_, t_) for a_ in range(2) for t_ in range(NTB)]
                Wbufs = [(Wn, wk_), (W1n, w1k_)]
                t1 = tmp_f[:, 0:512].rearrange("p (j h) -> p j h", j=16)
                t2 = tmp_f[:, 512:1024].rearrange("p (j h) -> p j h", j=16)
                mset(arb(STG, 2048), 0.0, stgk)
                if mode == "full":
                    mset(Yr[:, 4:6, :], 0.0, ctk)

                def tables_b_dve(r):
                    q = r % 4
                    if r > 0:
                        pq = (r - 1) % 4
                        mset(stg[:, :, :, pq * 32:(pq + 1) * 32], 0.0, stgk)
                    Bre, Bim = h32(bpad[:, r, 0, :]), h32(bpad[:, r, 1, :])
                    qr_, qi_ = b16(qq[:, r, 0, :]), b16(qq[:, r, 1, :])
                    sv = stg[:, :, :, q * 32:(q + 1) * 32]
                    tt(t1, Bre, qr_, ALU.mult, tabk + tyk, tmpk)
                    tt(t2, Bim, qi_, ALU.mult, tabk + tyk, tmpk)
                    tt(sv[:, 0], t1, t2, ALU.subtract, tmpk, stgk)
                    tt(t1, Bim, qr_, ALU.mult, tabk + tyk, tmpk)
                    tt(t2, Bre, qi_, ALU.mult, tabk + tyk, tmpk)
                    tt(sv[:, 1], t1, t2, ALU.add, tmpk, stgk)

                def tables_b_pe(r):
                    for a in range(2):
                        for lg in range(4):
                            bnk = 4 + lg
                            for li_ in range(4):
                                mm(ps[bnk][:, li_ * 128:(li_ + 1) * 128], stg[:, a, lg * 4 + li_, :], ident, True, True,
                                   stgk + ["cm"], [("ps", bnk)])
                            act(btv[:, a, lg * 4:(lg + 1) * 4, :], ps[bnk][:, :].rearrange("p (j c) -> p j c", j=4), AF.Copy,
                                [("ps", bnk)], btk_)

                def ct_dve(r):
                    q = r % 4
                    if r > 0:
                        pq = (r - 1) % 4
                        mset(ct[:, :, :, pq * 32:(pq + 1) * 32], 0.0, ctk)
                    Cre, Cim = h32(cpad[:, r, 0, :]), h32(cpad[:, r, 1, :])
                    pr_, pi_ = b16(pw0[:, r, 0, :]), b16(pw0[:, r, 1, :])
                    cv = ct[:, :, :, q * 32:(q + 1) * 32]
                    tt(t1, Cre, pr_, ALU.mult, tabk + tyk, tmpk)
                    tt(t2, Cim, pi_, ALU.mult, tabk + tyk, tmpk)
                    tt(cv[:, 0], t1, t2, ALU.add, tmpk, ctk)
                    tt(t1, Cim, pr_, ALU.mult, tabk + tyk, tmpk)
                    tt(t2, Cre, pi_, ALU.mult, tabk + tyk, tmpk)
                    tt(cv[:, 1], t1, t2, ALU.subtract, tmpk, ctk)

                def bu_mm(r, a, wv_, wkk):
                    uv = u[:, r // 4, :].rearrange("p (c j) -> p j c", j=16)
                    for lg in range(4):
                        bnk = 4 + lg
                        for li_ in range(4):
                            j = lg * 4 + li_
                            mm(ps[bnk][:, li_ * 128:(li_ + 1) * 128], btv[:, a, j, :], uv[:, j, :], True, True, btk_ + uk, [("ps", bnk)])
                        act(wv_[:, lg * 4:(lg + 1) * 4, :], ps[bnk][:, :].rearrange("p (j c) -> p j c", j=4), AF.Copy, [("ps", bnk)], wkk)

                def scan(wn_, wkk):
                    P.op("dve", lambda e, wn_=wn_: e.tensor_tensor_scan(out=wn_, data0=pat, data1=wn_, initial=0.0, op0=ALU.mult, op1=ALU.add),
                         wkk + patk, wkk)

                tables_b_dve(0)
                tables_b_pe(0)
                for r in range(8):
                    oc = r // 4
                    q = r % 4
                    uv = u[:, oc, :].rearrange("p (c j) -> p j c", j=16)
                    wvs = [(wn_.rearrange("p (c j) -> p j c", j=16), wn_, kk_) for (wn_, kk_) in Wbufs]
                    if mode == "p1":
                        bu_mm(r, 0, wvs[0][0], wvs[0][2])
                        bu_mm(r, 1, wvs[1][0], wvs[1][2])
                        if r + 1 < 8:
                            tables_b_dve(r + 1)
                        for a in range(2):
                            scan(wvs[a][1], wvs[a][2])
                            cp(Eall[:, r, a, :], wvs[a][0][:, 15, :], wvs[a][2], ek)
                        if r + 1 < 8:
                            tables_b_pe(r + 1)
                        continue
                    bu_mm(r, 0, wvs[0][0], wvs[0][2])
                    bu_mm(r, 1, wvs[1][0], wvs[1][2])
                    ct_dve(r)
                    if r + 1 < 8:
                        tables_b_dve(r + 1)
                    for a in range(2):
                        wv_, wn_, wkk = wvs[a]
                        tt(wv_[:, 0, :], wv_[:, 0, :], inj[:, r, a, :], ALU.add, wkk + injk, wkk)
                        scan(wn_, wkk)
                        act(wb[:, a, :], wn_, AF.Copy, wkk, wbk)
                    if r + 1 < 8:
                        tables_b_pe(r + 1)
                    wbv = [wb[:, a, :].rearrange("p (c j) -> p j c", j=16) for a in range(2)]
                    for j in range(16):
                        o = ps[j // 4][:, (j % 4) * 128:(j % 4 + 1) * 128]
                        P.op("pe", lambda e, o=o, j=j, q=q, wbv=wbv: e.matmul(o, lhsT=ct[:, 0, j, :], rhs=wbv[0][:, j, :],
                                                                             start=(q == 0 and j % 4 == 0), stop=False, skip_group_check=True),
                             ctk + wbk, [("ps", j // 4)])
                        P.op("pe", lambda e, o=o, j=j, q=q, wbv=wbv: e.matmul(o, lhsT=ct[:, 1, j, :], rhs=wbv[1][:, j, :], start=False,
                                                                             stop=(q == 3), skip_group_check=True), ctk + wbk, [("ps", j // 4)])
                    if q == 3:
                        for tb in range(NTB):
                            yo = W + (tb % 2) * 512
                            yt = arf(yo, 512)
                            ytk = ark(yo, 512)
                            uv4 = uv[:, tb * 4:(tb + 1) * 4, :]
                            stt(yt.rearrange("p (j c) -> p j c", j=4), uv4, pcol(l, "s5d", oc, oc + 1),
                                ps[tb][:, :].rearrange("p (j c) -> p j c", j=4), ALU.mult, ALU.add, uk + ["pp", ("ps", tb)], ytk)
                            act(Yr[:, 6 + oc, :].rearrange("p (c j) -> p j c", j=16)[:, tb * 4:(tb + 1) * 4, :],
                                yt.rearrange("p (j c) -> p j c", j=4), AF.Gelu_apprx_tanh, ytk, [("Yr", 6 + oc, t_) for t_ in range(NTB)])
                if mode == "p1":
                    p15r, p15i = s5pw[:, l, :, 0, 14], s5pw[:, l, :, 1, 14]
                    n = 128
                    t1 = T1[:, 0:1024].rearrange("p (r c) -> p r c", r=8)
                    t2 = T1[:, 1024:2048].rearrange("p (r c) -> p r c", r=8)
                    t3 = T2[:, 0:1024].rearrange("p (r c) -> p r c", r=8)
                    t4 = T2[:, 1024:2048].rearrange("p (r c) -> p r c", r=8)
                    tt(t1, Eall[:, :, 0, :], bc8(p15r, n), ALU.mult, ek + pk, kT1)
                    tt(t2, Eall[:, :, 1, :], bc8(p15i, n), ALU.mult, ek + pk, kT1)
                    tt(t3, Eall[:, :, 1, :], bc8(p15r, n), ALU.mult, ek + pk, kT2)
                    tt(t4, Eall[:, :, 0, :], bc8(p15i, n), ALU.mult, ek + pk, kT2)
                    tt(Eall[:, :, 0, :], t1, t2, ALU.subtract, kT1, ek)
                    tt(Eall[:, :, 1, :], t3, t4, ALU.add, kT2, ek)
                    res, rk = lvl2(Eall, ek)
                    cp(tiny[:, 800:816].rearrange("p (r a) -> p r a", a=2), res[:, :, :, 127], rk, tyk)
                    dma("sp", ccd[("x", l, "i")], tiny[:, 800:816], tyk, [("cci", "x", l)])
                    allgather("x", l, [("cci", "x", l)], [("cco", "x", l)])
                    return

            run_pass("p1")
            s5_combine(l)
            run_pass("full")
            SG = W
            for tb in range(NTB):
                sgs = []
                for m in range(2):
                    b = next_acc()
                    for k in range(2):
                        mm(ps[b][:, :], gluwS[:, l, k, m * 128:(m + 1) * 128], Yr[:, 6 + k, tb * TB:(tb + 1) * TB], k == 0, k == 1,
                           [("Yr", 6 + k, tb), "gluw"], [("ps", b)])
                    so = SG + ((tb * 2 + m) % 4) * 256
                    sg = arb(so, 256)
                    act(sg, ps[b][:, :], AF.Sigmoid, [("ps", b), "pp"], ark(so, 256), bias=pcol(l, "glub", m, m + 1))
                    sgs.append((sg, ark(so, 256)))
                for m in range(2):
                    sg, sgk = sgs[m]
                    tt(Yr[:, 6 + m, tb * TB:(tb + 1) * TB], Yr[:, 6 + m, tb * TB:(tb + 1) * TB], sg, ALU.mult,
                       [("Yr", 6 + m, tb)] + sgk, [("Yr", 6 + m, tb)])

        W1O = [7168, 9216]
        W2O = [11264, 13312]
        NG = 8

        def mlp_load(l, g):
            sl = g % 2
            w1g = arb(W1O[sl], 2048).rearrange("p (k c) -> p k c", k=8)
            w2g = arb(W2O[sl], 2048).rearrange("p (k c) -> p k c", k=4)
            w1k, w2k = ark(W1O[sl], 2048), ark(W2O[sl], 2048)
            dma("pool", w1g, w1[l, :, g * 512:(g + 1) * 512].rearrange("(k p) c -> p k c", p=128), [], w1k)
            dma("pool", w2g, w2[l, g * 512:(g + 1) * 512, :].rearrange("(k p) c -> p k c", p=128), [], w2k)

        def out_proj(l):
            WO, RSTD, SQ = 0, 4096, 6144
            wo = arb(WO, 4096).rearrange("p (k c) -> p k c", k=8)
            wok = ark(WO, 4096)
            dma("pool", wo, w_out[l, :, :].rearrange("(k p) c -> p k c", p=128), [], wok)
            mlp_load(l, 0)
            mlp_load(l, 1)
            for g in range(4):
                rms_stats(lambda k, tb: Yr[:, 2 * g + k, tb * TB:(tb + 1) * TB], 2, 256.0, RSTD, SQ,
                          lambda k, tb: [("Yr", 2 * g + k, tb)])
                for k in range(2):
                    ch = 2 * g + k
                    for tb in range(NTB):
                        stt(uT[:, ch, tb * TB:(tb + 1) * TB], Yr[:, ch, tb * TB:(tb + 1) * TB], pcol(l, "bnw", ch, ch + 1),
                            arf(RSTD + tb * TB, TB), ALU.mult, ALU.mult, [("Yr", ch, tb), "pp"] + ark(RSTD + tb * TB, TB), [("uT", ch, tb)])
            for m in range(KC):
                for tb in range(NTB):
                    b = next_acc()
                    for k in range(KC):
                        mm(ps[b][:, :], wo[:, k, m * 128:(m + 1) * 128], uT[:, k, tb * TB:(tb + 1) * TB], k == 0, k == KC - 1,
                           wok + [("uT", k, tb)], [("ps", b)])
                    stt(hT[:, m, tb * TB:(tb + 1) * TB], ps[b][:, :], G1(l, m), hT[:, m, tb * TB:(tb + 1) * TB], ALU.mult, ALU.add,
                        [("ps", b), ("der", l), ("hT", m, tb)], [("hT", m, tb)])

        def mlp(l):
            norm_mod(l, S2, B2)
            RS = 5120

            def up(g):
                sl = g % 2
                w1g = arb(W1O[sl], 2048).rearrange("p (k c) -> p k c", k=8)
                w1k = ark(W1O[sl], 2048)
                for jc in range(4):
                    yc = sl * 4 + jc
                    for tb in range(NTB):
                        b = next_acc()
                        for k in range(KC):
                            mm(ps[b][:, :], w1g[:, k, jc * 128:(jc + 1) * 128], uT[:, k, tb * TB:(tb + 1) * TB], k == 0, k == KC - 1,
                               w1k + [("uT", k, tb)], [("ps", b)])
                        ro = RS + ((jc * NTB + tb) % 4) * 256
                        rs = arb(ro, 256)
                        act(rs, ps[b][:, :], AF.Relu, [("ps", b)], ark(ro, 256))
                        tt(Yr[:, yc, tb * TB:(tb + 1) * TB], rs, rs, ALU.mult, ark(ro, 256), [("Yr", yc, tb)], eng="pool")

            def down(g):
                sl = g % 2
                w2g = arb(W2O[sl], 2048).rearrange("p (k c) -> p k c", k=4)
                w2k = ark(W2O[sl], 2048)
                for m in range(KC):
                    for tb in range(NTB):
                        b = next_acc()
                        for jc in range(4):
                            mm(ps[b][:, :], w2g[:, jc, m * 128:(m + 1) * 128], Yr[:, sl * 4 + jc, tb * TB:(tb + 1) * TB], jc == 0, jc == 3,
                               w2k + [("Yr", sl * 4 + jc, tb)], [("ps", b)])
                        stt(hT[:, m, tb * TB:(tb + 1) * TB], ps[b][:, :], G2(l, m), hT[:, m, tb * TB:(tb + 1) * TB], ALU.mult, ALU.add,
                            [("ps", b), ("der", l), ("hT", m, tb)], [("hT", m, tb)])

            up(0)
            for g in range(NG):
                if g + 1 < NG:
                    up(g + 1)
                down(g)
                if g + 2 < NG:
                    mlp_load(l, g + 2)

        def final_out(seg):
            RSTD, SQ, OUT = 0, 2048, 4096
            rms_stats(lambda k, tb: hT[:, k, tb * TB:(tb + 1) * TB], KC, float(DM), RSTD, SQ, lambda k, tb: [("hT", k, tb)])
            for k in range(KC):
                oo = OUT + (k % 2) * 2048
                o = arf(oo, 2048)
                for tb in range(NTB):
                    stt(o[:, tb * TB:(tb + 1) * TB], hT[:, k, tb * TB:(tb + 1) * TB], pcol(0, "fnw", k, k + 1), arf(RSTD + tb * TB, TB),
                        ALU.mult, ALU.mult, [("hT", k, tb), "pp"] + ark(RSTD + tb * TB, TB), ark(oo, 2048))
                dma("sp", yT[k * 128:(k + 1) * 128, :], o, ark(oo, 2048), [("yT", k, seg)])

        stopped = False
        for seg in range(nseg):
            if stopped:
                break
            for k in range(KC):
                dma("sp", hT[:, k, :], xT[k * 128:(k + 1) * 128, :], [], [("hT", k, tb) for tb in range(NTB)])
            for l in range(depth):
                norm_mod(l, S1, B1)
                if stop_after == (seg, l, "u"):
                    stopped = True
                    break
                print("ops before tail", P.total)
                tail_prepass(l)
                if l == 0:
                    param_prologue()
                print("ops before s5 p1", P.total)
                mixer_s5(l, seg)
                print("ops before halo", P.total)
                halo_apply(l)
                mixer_pool(l, seg)
                mixer_sconv(l, seg)
                print("ops before ssd", P.total)
                mixer_ssd(l, seg)
                print("ops after ssd", P.total)
                if stop_after == (seg, l, "mix"):
                    stopped = True
                    break
                out_proj(l)
                if stop_after == (seg, l, "hmix"):
                    stopped = True
                    break
                mlp(l)
                if stop_after == (seg, l, "h"):
                    stopped = True
                    break
            if not stopped:
                final_out(seg)
        print("total ops recorded:", P.total)
        P.limit = None
        if dbg:
            if "uT" in dbg_out:
                cp(AR[:, 0:2048], uT[:, 0, :], all_keys("uT"), ark(0, 2048))
            for name in dbg_out:
                if name == "Yr":
                    for k in range(KC):
                        o = arf((k % 2) * 2048, 2048)
                        cp(o, Yr[:, k, :], [("Yr", k, tb) for tb in range(NTB)], ark((k % 2) * 2048, 2048))
                        dma("sp", dbg_out[name][k * 128:(k + 1) * 128, :], o, ark((k % 2) * 2048, 2048), [("dbg", name, k)])
                elif name == "uT":
                    for k in range(KC):
                        o = arf((k % 2) * 2048, 2048)
                        cp(o, uT[:, k, :], [("uT", k, tb) for tb in range(NTB)], ark((k % 2) * 2048, 2048))
                        dma("sp", dbg_out[name][k * 128:(k + 1) * 128, :], o, ark((k % 2) * 2048, 2048), [("dbg", name, k)])
                elif name == "hT":
                    for k in range(KC):
                        dma("sp", dbg_out[name][k * 128:(k + 1) * 128, :], hT[:, k, :], [("hT", k, tb) for tb in range(NTB)], [("dbg", name, k)])
                elif name == "mod":
                    dma("sp", dbg_out[name], modT[:, :, :].rearrange("p l c -> p (l c)"), [("mod", 0), ("mod", 1)], [("dbg", name)])
        P.wait_all("sp")

        with nc.Block() as block:
            def replay(e, name):
                for waits, fn, inc in P.q[name]:
                    for s, v in waits:
                        e.wait_ge(sems[s], v)
                    if fn is not None:
                        fn(e).then_inc(sems[inc[0]], inc[1])

            @block.tensor
            def _(e):
                replay(e, "pe")

            @block.scalar
            def _(e):
                replay(e, "act")

            @block.vector
            def _(e):
                replay(e, "dve")

            @block.gpsimd
            def _(e):
                replay(e, "pool")

            @block.sync
            def _(e):
                replay(e, "sp")
    return nc


def _fm(v):
    return np.ascontiguousarray(v.reshape(-1, 128).T)


def _pack_params(inp, b, sg):
    L = DEPTH
    pp = np.zeros((L, 128, NPCOL), np.float32)

    def put(l, name, arr):
        o, w = PCOL[name]
        arr = np.asarray(arr, np.float32).reshape(128, w)
        pp[l, :, o:o + w] = arr
    wins = (2, 4, 8, 16)
    for l in range(L):
        put(l, "nw1", _fm(inp["norm_mix_w"][l]))
        put(l, "nw2", _fm(inp["norm_mlp_w"][l]))
        put(l, "bnw", _fm(inp["branch_norm_w"][l]))
        put(l, "fnw", _fm(inp["final_norm_w"]))
        adab = np.zeros((128, 48), np.float32)
        adab[:, 0:12] = _fm(inp["ada_b"][l])[:, sg * 12:(sg + 1) * 12]
        put(l, "adab", adab)
        put(l, "pscale", _fm(inp["pool_scale"][l]))
        put(l, "scw", inp["sconv_w"][l].reshape(3, 2, 128).transpose(2, 1, 0))
        put(l, "cvw", inp["ssd_conv_w"][l].reshape(4, 6, 128).transpose(2, 1, 0))
        put(l, "cvb", _fm(inp["ssd_conv_b"][l]))
        put(l, "dtb", np.broadcast_to(inp["ssd_dt_bias"][l][None, :], (128, 4)))
        put(l, "alog", np.broadcast_to(inp["ssd_a_log"][l][None, :], (128, 4)))
        put(l, "dsk", np.broadcast_to(inp["ssd_d"][l][None, :], (128, 4)))
        def gp(a):
            return a.reshape(8, 2, 64).transpose(1, 2, 0).reshape(128, 8)
        put(l, "are", gp(inp["s5_a_re"][l]))
        put(l, "aim", gp(inp["s5_a_im"][l]))
        put(l, "lst", gp(np.broadcast_to(inp["s5_log_step"][l][:, None], (16, 64))))
        put(l, "s5d", _fm(inp["s5_d"][l]))
        put(l, "glub", _fm(inp["s5_glu_b"][l]))
        put(l, "cond", _fm(inp["c"][b]))
        invc = np.zeros((128, 2, 16), np.float32)
        for c in range(2):
            for half in range(2):
                win = wins[c * 2 + half]
                if sg == 0:
                    invc[half * 64:(half + 1) * 64, c, :] = 1.0 / np.minimum(np.arange(16) + 1, win)
                else:
                    invc[half * 64:(half + 1) * 64, c, :] = 1.0 / win
        put(l, "invc", invc)
        sel = np.zeros((128, 8), np.float32)
        if sg > 0:
            sel[:, sg - 1] = 1.0
        sel[:, 4 + sg] = 1.0
        put(l, "sel", sel)
    return pp


def _consts():
    cm = np.zeros((128, 4, 128), np.float32)
    i = np.arange(128)
    cm[:, 0, :] = (i[:, None] == i[None, :])
    cm[:, 1, :] = 1.0
    cm[:, 2, :] = (i[:, None] <= i[None, :])
    cm[:, 3, :] = np.where(i[None, :] < i[:, None], -30000.0, 0.0)
    return cm


def _host_inputs(inp, b, sg):
    f = lambda a: np.ascontiguousarray(np.asarray(a, np.float32))
    pw = np.zeros((DEPTH, 128, 2, 128), np.float32)
    for l in range(DEPTH):
        for g in range(4):
            c, half = g // 2, g % 2
            pw[l, half * 64:(half + 1) * 64, c, half * 64:(half + 1) * 64] = inp["pool_w"][l, g]
    gw = np.ascontiguousarray(np.asarray(inp["s5_glu_w"], np.float32).reshape(DEPTH, 2, 128, 256).transpose(0, 2, 1, 3))
    s5b = np.zeros((DEPTH, 128, 256), np.float32)
    s5c = np.zeros((DEPTH, 128, 8, 2, 32), np.float32)
    for l in range(DEPTH):
        s5b[l, :, 0:128] = np.asarray(inp["s5_b_re"][l]).reshape(8, 2, 64, 16).transpose(1, 2, 0, 3).reshape(128, 128)
        s5b[l, :, 128:256] = np.asarray(inp["s5_b_im"][l]).reshape(8, 2, 64, 16).transpose(1, 2, 0, 3).reshape(128, 128)
        for ri, nm in enumerate(("s5_c_re", "s5_c_im")):
            cc = np.asarray(inp[nm][l]).reshape(8, 2, 16, 64)
            for r in range(8):
                for gi in range(2):
                    s5c[l, gi * 64:(gi + 1) * 64, r, ri, gi * 16:gi * 16 + 16] = cc[r, gi].T
    return {
        "s5b": s5b, "s5c": s5c.reshape(DEPTH, 128, 512),
        "xT": f(np.asarray(inp["x"][b][sg * T:(sg + 1) * T]).T),
        "pp": _pack_params(inp, b, sg),
        "cmat": _consts(),
        "ada_w": f(np.asarray(inp["ada_w"])[:, :, sg * 1536:(sg + 1) * 1536]), "w_in": f(inp["w_in"]), "w_out": f(inp["w_out"]),
        "mlp_w1": f(inp["mlp_w1"]), "mlp_w2": f(inp["mlp_w2"]),
        "poolw": pw, "gluw": gw,
    }


_NC_CACHE = {}


def kernel(**inputs):
    inp = {k: np.asarray(v) for k, v in inputs.items()}
    if "full" not in _NC_CACHE:
        _NC_CACHE["full"] = build()
    nc = _NC_CACHE["full"]
    in_maps = [_host_inputs(inp, r // 4, r % 4) for r in range(8)]
    res = run_bass_kernel_spmd(nc, in_maps, core_ids=list(range(8)))
    out = np.empty((2, SEQ, DM), np.float32)
    for r in range(8):
        out[r // 4, (r % 4) * T:(r % 4 + 1) * T, :] = res.results[r]["yT"].T
    return out
```

```python
import numpy as np
import concourse.bass as bass
import concourse.mybir as mybir
from concourse.bass_utils import run_bass_kernel_spmd

F32, BF16 = mybir.dt.float32, mybir.dt.bfloat16
AF = mybir.ActivationFunctionType
ALU = mybir.AluOpType

T = 2048
TB = 512
NTB = 4
NCH = 16
DM = 1024
KC = 8
SEQ = 8192
DEPTH = 2
EPS = 1e-6
ENGS = ("pe", "act", "dve", "pool", "sp")
NDMA = 12

PCOL = {}
_off = 0
for _n, _w in [("nw1", 8), ("nw2", 8), ("bnw", 8), ("fnw", 8), ("adab", 48), ("pscale", 2), ("scw", 6),
               ("cvw", 24), ("cvb", 6), ("dtb", 4), ("alog", 4), ("dsk", 4), ("are", 8), ("aim", 8), ("lst", 8),
               ("s5d", 2), ("glub", 2), ("cond", 8),
               ("invc", 32), ("sel", 8)]:
    PCOL[_n] = (_off, _w)
    _off += _w
NPCOL = _off


class Prog:
    def __init__(self):
        self.q = {e: [] for e in ENGS}
        self.cnt = {e: 0 for e in ENGS}
        self.seen = {e: {} for e in ENGS}
        self.lastw = {}
        self.rd = {}
        self.dma_i = 0
        self.fam = {}
        import os as _os
        self.nosame = set(x for x in _os.environ.get("KNOSAME", "").split(",") if x)
        self.total = 0
        import os
        self.limit = int(os.environ.get("KLIMIT", "0")) or None

    def _skip(self):
        self.total += 1
        return self.limit is not None and self.total > self.limit

    def _deps(self, eng, reads, writes):
        need = {}

        def add(d):
            if d is None:
                return
            s, v = d
            if need.get(s, 0) < v:
                need[s] = v
        for k0 in reads:
            for k in self._rel(k0):
                add(self.lastw.get(k))
        for k0 in writes:
            for k in self._rel(k0):
                add(self.lastw.get(k))
                for s, v in self.rd.get(k, {}).items():
                    add((s, v))
        out = []
        for s, v in need.items():
            if s == eng and (eng == "pe" or eng in self.nosame):
                continue
            if self.seen[eng].get(s, 0) >= v:
                continue
            self.seen[eng][s] = v
            out.append((s, v))
        return out

    def _rel(self, k):
        if isinstance(k, tuple) and k[0] == "ps":
            fam = self.fam.setdefault(k[1], set())
            fam.add(k)
            if len(k) == 2:
                return list(fam)
            return [k, ("ps", k[1])]
        return [k]

    def _commit(self, reads, writes, tok):
        s, v = tok
        for k in reads:
            d = self.rd.setdefault(k, {})
            if d.get(s, 0) < v:
                d[s] = v
        for k in writes:
            self.lastw[k] = tok
            self.rd[k] = {}

    @staticmethod
    def _norm(reads, writes):
        r2, w2 = [], list(writes)
        for k in reads:
            if isinstance(k, tuple) and k[0] == "ps":
                w2.append(k)
            else:
                r2.append(k)
        w2 = [("ps", k[1]) if (isinstance(k, tuple) and k[0] == "ps") else k for k in w2]
        return r2, w2

    def op(self, eng, fn, reads=(), writes=()):
        if self._skip():
            return
        reads, writes = self._norm(reads, writes)
        waits = self._deps(eng, reads, writes)
        self.cnt[eng] += 1
        self.q[eng].append((waits, fn, (eng, 1)))
        self._commit(reads, writes, (eng, self.cnt[eng]))

    def dma(self, eng, fn, reads=(), writes=()):
        if self._skip():
            return None
        reads = list(reads)
        writes = list(writes)
        i = self.dma_i
        self.dma_i += 1
        sem = "dma%d" % (i % NDMA)
        val = 16 * (i // NDMA + 1)
        waits = self._deps(eng, reads, writes)
        if i >= NDMA:
            prev = 16 * (i // NDMA)
            if self.seen[eng].get(sem, 0) < prev:
                self.seen[eng][sem] = prev
                waits.append((sem, prev))
        self.q[eng].append((waits, fn, (sem, 16)))
        self._commit(reads, writes, (sem, val))
        return (sem, val)

    def cc(self, fn, sem, reads=(), writes=()):
        if self._skip():
            return
        reads, writes = self._norm(reads, writes)
        waits = self._deps("pool", reads, writes)
        self.q["pool"].append((waits, fn, (sem, 1)))
        self._commit(reads, writes, (sem, 1))

    def wait_all(self, eng):
        waits = []
        for e in ENGS:
            if e != eng and self.cnt[e] > self.seen[eng].get(e, 0):
                self.seen[eng][e] = self.cnt[e]
                waits.append((e, self.cnt[e]))
        for j in range(min(NDMA, self.dma_i)):
            sem = "dma%d" % j
            n = (self.dma_i - 1 - j) // NDMA + 1
            if self.seen[eng].get(sem, 0) < 16 * n:
                self.seen[eng][sem] = 16 * n
                waits.append((sem, 16 * n))
        self.q[eng].append((waits, None, None))


class Buf:
    def __init__(self, name, ap):
        self.name = name
        self.ap = ap

    def k(self, *idx):
        return (self.name,) + tuple(idx)


def build(depth=DEPTH, dbg=None, stop_after=None):
    nseg = 1
    nc = bass.Bass("TRN2", target_bir_lowering=False)
    P = Prog()
    dram = {}

    def din(name, shape):
        dram[name] = nc.dram_tensor(name, list(shape), F32, kind="ExternalInput").ap()
        return dram[name]

    xT = din("xT", [DM, T])
    pp = din("pp", [DEPTH, 128, NPCOL])
    cmat = din("cmat", [128, 4, 128])
    ada_w = din("ada_w", [DEPTH, DM, 1536])
    w_in = din("w_in", [DEPTH, DM, 2308])
    w_out = din("w_out", [DEPTH, DM, DM])
    w1 = din("mlp_w1", [DEPTH, DM, 4 * DM])
    w2 = din("mlp_w2", [DEPTH, 4 * DM, DM])
    poolw = din("poolw", [DEPTH, 128, 2, 128])
    gluw = din("gluw", [DEPTH, 128, 2, 256])
    s5b = din("s5b", [DEPTH, 128, 256])
    s5c = din("s5c", [DEPTH, 128, 512])
    yT = nc.dram_tensor("yT", [DM, T], F32, kind="ExternalOutput").ap()
    GRP = [[0, 1, 2, 3], [4, 5, 6, 7]]
    ccd = {}
    for l_ in range(DEPTH):
        for nm_, w_ in (("h", 64), ("x", 16), ("s", 272), ("m", 32)):
            ccd[(nm_, l_, "i")] = nc.dram_tensor("cc%s%di" % (nm_, l_), [128, w_], F32, kind="Internal").ap()
            ccd[(nm_, l_, "o")] = nc.dram_tensor("cc%s%do" % (nm_, l_), [4 * 128, w_], F32, kind="Internal").ap()
    dbg_out = {}
    if dbg:
        for name, shape in dbg.items():
            dbg_out[name] = nc.dram_tensor("dbg_" + name, list(shape), F32, kind="ExternalOutput").ap()

    import contextlib
    es = contextlib.ExitStack()
    with es:
        def sb(name, shape, dt):
            return es.enter_context(nc.sbuf_tensor(name, list(shape), dt))

        hT = sb("hT", [128, KC, T], F32)
        uT = sb("uT", [128, KC, T], BF16)
        Yr = sb("Yr", [128, KC, T], BF16)
        AR = sb("AR", [128, 15360], F32)
        cm = sb("cm", [128, 4, 128], BF16)
        ppS = sb("ppS", [128, DEPTH, NPCOL], F32)
        modT = sb("modT", [128, DEPTH, 48], F32)
        der = sb("der", [128, DEPTH, 64], F32)
        condb = sb("condb", [128, 8], BF16)
        epsT = sb("epsT", [128, 2], F32)
        poolwS = sb("poolwS", [128, DEPTH, 2, 128], BF16)
        gluwS = sb("gluwS", [128, DEPTH, 2, 256], BF16)
        cPool = sb("cPool", [128, DEPTH, 2, 16], F32)
        cSc = sb("cSc", [128, DEPTH, 2, 2], F32)
        cCv = sb("cCv", [128, DEPTH, 6, 3], F32)
        cS = sb("cS", [128, DEPTH, 256], F32)
        cX = sb("cX", [128, DEPTH, 8, 2], F32)
        ssdp = sb("ssdp", [128, DEPTH, 16], F32)
        s5pw = sb("s5pw", [128, DEPTH, 8, 3, 16], F32)
        s5lv = sb("s5lv", [128, DEPTH, 8, 3, 8], F32)
        bbS = sb("bbS", [128, DEPTH, 2, 8, 16], F32)
        tiny = sb("tiny", [128, 896], F32)
        tinyb = sb("tinyb", [128, 128], BF16)

        ps = [es.enter_context(nc.psum_tensor("ps%d" % i, [128, 512], F32)) for i in range(8)]
        sems = {}
        for e in ENGS:
            sems[e] = es.enter_context(nc.semaphore("s_" + e))
        for j in range(NDMA):
            sems["dma%d" % j] = es.enter_context(nc.semaphore("s_dma%d" % j))
        for l_ in range(DEPTH):
            for nm_ in ("h", "x", "s", "m"):
                sems["cc%s%d" % (nm_, l_)] = es.enter_context(nc.semaphore("s_cc%s%d" % (nm_, l_)))

        def allgather(nm_, l_, reads, writes):
            i_, o_ = ccd[(nm_, l_, "i")], ccd[(nm_, l_, "o")]
            P.cc(lambda e: e.collective_compute("AllGather", ALU.bypass, replica_groups=GRP, ins=[i_], outs=[o_]),
                 "cc%s%d" % (nm_, l_), reads, writes)

        ARb = AR[:, :].bitcast(BF16)

        def arf(off, n):
            return AR[:, off:off + n]

        def arb(off, n):
            return ARb[:, 2 * off:2 * (off + n)]

        def ark(off, n):
            return [("ar", b) for b in range(off // 256, (off + n + 255) // 256)]

        ident = cm[:, 0, :]
        ones = cm[:, 1, :]
        tri = cm[:, 2, :]
        negm = cm[:, 3, :]

        def pcol(l, name, a=0, b=None):
            o, w = PCOL[name]
            if b is None:
                b = w
            return ppS[:, l, o + a:o + b]

        def mm(out, lhsT, rhs, start, stop, reads, writes):
            P.op("pe", lambda e: e.matmul(out, lhsT=lhsT, rhs=rhs, start=start, stop=stop), reads, writes)

        def act(out, in_, func, reads, writes, bias=None, scale=None):
            kw = {}
            if bias is not None:
                kw["bias"] = bias
            if scale is not None:
                kw["scale"] = scale
            P.op("act", lambda e: e.activation(out=out, in_=in_, func=func, **kw), reads, writes)

        def tt(out, in0, in1, op, reads, writes, eng="dve"):
            P.op(eng, lambda e: e.tensor_tensor(out=out, in0=in0, in1=in1, op=op), reads, writes)

        def ts(out, in0, s1, s2, op0, op1, reads, writes, eng="dve"):
            if s2 is None:
                P.op(eng, lambda e: e.tensor_scalar(out=out, in0=in0, scalar1=s1, scalar2=None, op0=op0), reads, writes)
            else:
                P.op(eng, lambda e: e.tensor_scalar(out=out, in0=in0, scalar1=s1, scalar2=s2, op0=op0, op1=op1), reads, writes)

        def stt(out, in0, scalar, in1, op0, op1, reads, writes, eng="dve"):
            P.op(eng, lambda e: e.scalar_tensor_tensor(out=out, in0=in0, scalar=scalar, in1=in1, op0=op0, op1=op1), reads, writes)

        def cp(out, in_, reads, writes, eng="dve"):
            P.op(eng, lambda e: e.tensor_copy(out=out, in_=in_), reads, writes)

        def mset(ap, val, writes, eng="dve"):
            P.op(eng, lambda e: e.memset(ap, val), [], writes)

        def dma(eng, out, in_, reads, writes):
            return P.dma(eng, lambda e: e.dma_start(out=out, in_=in_), reads, writes)

        dma("pool", cm[:, :, :], cmat, [], ["cm"])
        dma("sp", ppS[:, :, :], pp.rearrange("l p c -> p l c"), [], ["pp"])
        dma("pool", poolwS[:, :, :, :], poolw.rearrange("l p c d -> p l c d"), [], ["poolw"])
        dma("pool", gluwS[:, :, :, :], gluw.rearrange("l p c d -> p l c d"), [], ["gluw"])
        mset(epsT[:, 0:1], EPS, ["eps"])
        mset(epsT[:, 1:2], 1.0, ["eps"])
        for t_, kk in ((cPool, "cPool"), (cSc, "cSc"), (cCv, "cCv"), (cS, "cS"), (cX, "cX")):
            mset(t_[:], 0.0, [kk])
        act(condb[:, :], pcol(0, "cond"), AF.Silu, ["pp"], ["condb"])

        WADA = 0
        bi = 0
        for l in range(DEPTH):
            for blk in range(3):
                slot = bi % 2
                bi += 1
                wsl = arb(WADA + slot * 2048, 2048).rearrange("p (k c) -> p k c", k=8)
                wk = ark(WADA + slot * 2048, 2048)
                dma("pool", wsl, ada_w[l, :, blk * 512:(blk + 1) * 512].rearrange("(k p) c -> p k c", p=128), [], wk)
                for jj in range(4):
                    j = l * 12 + blk * 4 + jj
                    for k in range(KC):
                        mm(ps[6][:, j:j + 1], wsl[:, k, jj * 128:(jj + 1) * 128], condb[:, k:k + 1], k == 0, k == KC - 1,
                           wk + ["condb"], [("ps", 6)])
        mpk = tiny[:, 0:24].rearrange("p (l c) -> p l c", l=2)
        tyk0 = [("tiny", "all")]
        for l in range(DEPTH):
            tt(mpk[:, l, :], ps[6][:, l * 12:(l + 1) * 12], pcol(l, "adab", 0, 12), ALU.add, [("ps", 6), "pp"], tyk0)
        dma("sp", ccd[("m", 0, "i")][:, 0:24], tiny[:, 0:24], tyk0, [("cci", "m", 0)])
        allgather("m", 0, [("cci", "m", 0)], [("cco", "m", 0)])
        mrb = tiny[:, 32:160].rearrange("p (r f) -> p r f", r=4)
        dma("sp", mrb, ccd[("m", 0, "o")].rearrange("(r p) f -> p r f", p=128), [("cco", "m", 0)], tyk0)
        for l in range(DEPTH):
            for sg_ in range(4):
                cp(modT[:, l, sg_ * 12:(sg_ + 1) * 12], mrb[:, sg_, l * 12:(l + 1) * 12], tyk0, [("mod", l)])
        for l in range(depth):
            stt(der[:, l, 0:8], modT[:, l, 8:16], 1.0, pcol(l, "nw1"), ALU.add, ALU.mult, [("mod", l), "pp"], [("der", l)])
            stt(der[:, l, 24:32], modT[:, l, 32:40], 1.0, pcol(l, "nw2"), ALU.add, ALU.mult, [("mod", l), "pp"], [("der", l)])
            cp(der[:, l, 8:16], modT[:, l, 0:8], [("mod", l)], [("der", l)])
            cp(der[:, l, 16:24], modT[:, l, 16:24], [("mod", l)], [("der", l)])
            cp(der[:, l, 32:40], modT[:, l, 24:32], [("mod", l)], [("der", l)])
            cp(der[:, l, 40:48], modT[:, l, 40:48], [("mod", l)], [("der", l)])

        def S1(l, k): return der[:, l, 0 + k:1 + k]
        def B1(l, k): return der[:, l, 8 + k:9 + k]
        def G1(l, k): return der[:, l, 16 + k:17 + k]
        def S2(l, k): return der[:, l, 24 + k:25 + k]
        def B2(l, k): return der[:, l, 32 + k:33 + k]
        def G2(l, k): return der[:, l, 40 + k:41 + k]

        for l in range(depth):
            act(ssdp[:, l, 0:4], pcol(l, "alog"), AF.Exp, ["pp"], [("ssdp", l)])
            ts(ssdp[:, l, 0:4], ssdp[:, l, 0:4], -1.0, None, ALU.mult, None, [("ssdp", l)], [("ssdp", l)])
            cp(ssdp[:, l, 4:8], pcol(l, "dtb"), ["pp"], [("ssdp", l)])
            cp(ssdp[:, l, 8:12], pcol(l, "dsk"), ["pp"], [("ssdp", l)])

        def tk(n): return [("tiny", n)]
        for l in range(depth):
            tyk = [("tiny", "all")]
            stp = tiny[:, 0:8]
            act(stp, pcol(l, "lst"), AF.Exp, ["pp"], tyk)
            mag = tiny[:, 8:16]
            tt(mag, pcol(l, "are"), stp, ALU.mult, ["pp"] + tyk, tyk)
            act(mag, mag, AF.Exp, tyk, tyk)
            th = tiny[:, 16:24]
            tt(th, pcol(l, "aim"), stp, ALU.mult, ["pp"] + tyk, tyk)
            sa = tiny[:, 24:32]
            ca = tiny[:, 32:40]
            act(sa, th, AF.Sin, tyk, tyk, scale=1.0 / 16.0)
            ts(ca, th, 1.0 / 16.0, float(np.pi / 2), ALU.mult, ALU.add, tyk, tyk)
            act(ca, ca, AF.Sin, tyk, tyk)
            for _ in range(4):
                t2a, t2b = tiny[:, 272:280], tiny[:, 280:288]
                tt(t2a, ca, ca, ALU.mult, tyk, tyk)
                tt(t2b, sa, sa, ALU.mult, tyk, tyk)
                tt(sa, sa, ca, ALU.mult, tyk, tyk)
                ts(sa, sa, 2.0, None, ALU.mult, None, tyk, tyk)
                tt(ca, t2a, t2b, ALU.subtract, tyk, tyk)
            lr = tiny[:, 40:48]
            li = tiny[:, 48:56]
            tt(lr, mag, ca, ALU.mult, tyk, tyk)
            tt(li, mag, sa, ALU.mult, tyk, tyk)
            den = tiny[:, 56:64]
            t0 = tiny[:, 64:72]
            tt(den, pcol(l, "are"), pcol(l, "are"), ALU.mult, ["pp"] + tyk, tyk)
            tt(t0, pcol(l, "aim"), pcol(l, "aim"), ALU.mult, ["pp"] + tyk, tyk)
            tt(den, den, t0, ALU.add, tyk, tyk)
            P.op("dve", lambda e, den=den: e.reciprocal(out=den, in_=den), tyk, tyk)
            nr = tiny[:, 72:80]
            ts(nr, lr, -1.0, None, ALU.add, None, tyk, tyk)
            fr = tiny[:, 80:88]
            fi = tiny[:, 88:96]
            t1 = tiny[:, 96:104]
            tt(fr, nr, pcol(l, "are"), ALU.mult, ["pp"] + tyk, tyk)
            tt(t1, li, pcol(l, "aim"), ALU.mult, ["pp"] + tyk, tyk)
            tt(fr, fr, t1, ALU.add, tyk, tyk)
            tt(fr, fr, den, ALU.mult, tyk, tyk)
            tt(fi, li, pcol(l, "are"), ALU.mult, ["pp"] + tyk, tyk)
            tt(t1, nr, pcol(l, "aim"), ALU.mult, ["pp"] + tyk, tyk)
            tt(fi, fi, t1, ALU.subtract, tyk, tyk)
            tt(fi, fi, den, ALU.mult, tyk, tyk)
            dma("sp", tiny[:, 512:768], s5b[l], [], tyk)
            bre = tiny[:, 512:640].rearrange("p (r h) -> p r h", r=8)
            bim = tiny[:, 640:768].rearrange("p (r h) -> p r h", r=8)
            frb = fr.unsqueeze(2).to_broadcast([128, 8, 16])
            fib = fi.unsqueeze(2).to_broadcast([128, 8, 16])
            tmpb = tiny[:, 128:256].rearrange("p (r h) -> p r h", r=8)
            tt(bbS[:, l, 0, :, :], bre, frb, ALU.mult, ["pp"] + tyk, [("bb", l)])
            tt(tmpb, bim, fib, ALU.mult, ["pp"] + tyk, tyk)
            tt(bbS[:, l, 0, :, :], bbS[:, l, 0, :, :], tmpb, ALU.subtract, tyk + [("bb", l)], [("bb", l)])
            tt(bbS[:, l, 1, :, :], bim, frb, ALU.mult, ["pp"] + tyk, [("bb", l)])
            tt(tmpb, bre, fib, ALU.mult, ["pp"] + tyk, tyk)
            tt(bbS[:, l, 1, :, :], bbS[:, l, 1, :, :], tmpb, ALU.add, tyk + [("bb", l)], [("bb", l)])
            ts(bbS[:, l, 1, :, :], bbS[:, l, 1, :, :], -1.0, None, ALU.mult, None, [("bb", l)], [("bb", l)])
            ts(li, li, -1.0, None, ALU.mult, None, tyk, tyk)
            pk = [("s5pw", l)]
            cp(s5pw[:, l, :, 0, 0], lr, tyk, pk)
            cp(s5pw[:, l, :, 1, 0], li, tyk, pk)
            for j in range(1, 16):
                pr_, pi_ = s5pw[:, l, :, 0, j - 1], s5pw[:, l, :, 1, j - 1]
                nr_, ni_ = s5pw[:, l, :, 0, j], s5pw[:, l, :, 1, j]
                ta, tb_ = tiny[:, 256:264], tiny[:, 264:272]
                tt(ta, pr_, lr, ALU.mult, tyk + pk, tyk)
                tt(tb_, pi_, li, ALU.mult, tyk + pk, tyk)
                tt(nr_, ta, tb_, ALU.subtract, tyk, pk)
                tt(ta, pr_, li, ALU.mult, tyk + pk, tyk)
                tt(tb_, pi_, lr, ALU.mult, tyk + pk, tyk)
                tt(ni_, ta, tb_, ALU.add, tyk, pk)
            ts(s5pw[:, l, :, 2, :], s5pw[:, l, :, 1, :], -1.0, None, ALU.mult, None, pk, pk)
            lk = [("s5lv", l)]
            cp(s5lv[:, l, :, 0, 0], s5pw[:, l, :, 0, 15], pk, lk)
            cp(s5lv[:, l, :, 1, 0], s5pw[:, l, :, 1, 15], pk, lk)
            for k in range(1, 8):
                pr_, pi_ = s5lv[:, l, :, 0, k - 1], s5lv[:, l, :, 1, k - 1]
                ta, tb_ = tiny[:, 256:264], tiny[:, 264:272]
                tt(ta, pr_, pr_, ALU.mult, lk + tyk, tyk)
                tt(tb_, pi_, pi_, ALU.mult, lk + tyk, tyk)
                tt(s5lv[:, l, :, 0, k], ta, tb_, ALU.subtract, tyk, lk)
                tt(ta, pr_, pi_, ALU.mult, lk + tyk, tyk)
                ts(s5lv[:, l, :, 1, k], ta, 2.0, None, ALU.mult, None, tyk, lk)
            ts(s5lv[:, l, :, 2, :], s5lv[:, l, :, 1, :], -1.0, None, ALU.mult, None, lk, lk)

        acc_i = [0]

        def next_acc():
            b = acc_i[0] % 4
            acc_i[0] += 1
            return b

        def rms_stats(src_fn, nchunks, denom, rstd_off, sq_off, src_keys_fn):
            for tb in range(NTB):
                bank = 4 + (tb % 2)
                for k in range(nchunks):
                    so = sq_off + ((tb * nchunks + k) % 4) * 256
                    sq = arb(so, 256)
                    src = src_fn(k, tb)
                    tt(sq, src, src, ALU.mult, src_keys_fn(k, tb), ark(so, 256), eng="pool")
                    mm(ps[bank][:, :], ones, sq, k == 0, k == nchunks - 1, ark(so, 256) + ["cm"], [("ps", bank)])
                r = arf(rstd_off + tb * TB, TB)
                rk = ark(rstd_off + tb * TB, TB)
                act(r, ps[bank][:, :], AF.Ln, [("ps", bank), "eps"], rk, bias=epsT[:, 0:1], scale=1.0 / denom)
                act(r, r, AF.Exp, rk, rk, scale=-0.5)

        def norm_mod(l, s_fn, b_fn):
            RSTD, SQ, TMP = 0, 2048, 3072
            rms_stats(lambda k, tb: hT[:, k, tb * TB:(tb + 1) * TB], KC, float(DM), RSTD, SQ,
                      lambda k, tb: [("hT", k, tb)])
            for k in range(KC):
                for tb in range(NTB):
                    to = TMP + ((k * NTB + tb) % 4) * TB
                    tmp = arf(to, TB)
                    stt(tmp, hT[:, k, tb * TB:(tb + 1) * TB], s_fn(l, k), arf(RSTD + tb * TB, TB), ALU.mult, ALU.mult,
                        [("hT", k, tb), ("der", l)] + ark(RSTD + tb * TB, TB), ark(to, TB))
                    act(uT[:, k, tb * TB:(tb + 1) * TB], tmp, AF.Identity, ark(to, TB) + [("der", l)], [("uT", k, tb)],
                        bias=b_fn(l, k))

        WIN = 14336
        win_i = [0]

        def load_win(l, c0, ncol):
            assert ncol <= 128
            slot = win_i[0] % 2
            win_i[0] += 1
            off = WIN + slot * 512
            w = arb(off, 512).rearrange("p (k c) -> p k c", k=8)
            dma("pool", w[:, :, 0:ncol], w_in[l, :, c0:c0 + ncol].rearrange("(k p) c -> p k c", p=128), [], ark(off, 512))
            return w, ark(off, 512)

        def proj_chunk(w, wk, cofs, ncol, evac):
            for tb in range(NTB):
                b = next_acc()
                for k in range(KC):
                    mm(ps[b][0:ncol, :], w[:, k, cofs:cofs + ncol], uT[:, k, tb * TB:(tb + 1) * TB], k == 0, k == KC - 1,
                       wk + [("uT", k, tb)], [("ps", b)])
                evac(tb, ps[b][0:ncol, :], ("ps", b))

        def dump(name, ap, keys):
            if dbg and name in dbg_out:
                dma("sp", dbg_out[name], ap, keys, [("dbg", name)])

        def all_keys_h():
            return [("hT", k, tb) for k in range(KC) for tb in range(NTB)]

        def all_keys(nm):
            return [(nm, k, tb) for k in range(KC) for tb in range(NTB)]

        def tail_prepass(l):
            PK = arf(0, 64)
            pkk = ark(0, 64)
            gct = tiny[:, 0:32].rearrange("p (c t) -> p c t", c=2)
            tyk = [("tiny", "all")]
            tail = slice(T - 16, T)

            def tproj(w, wk, cofs, evac):
                b = next_acc()
                for k in range(KC):
                    mm(ps[b][:, 0:16], w[:, k, cofs:cofs + 128], uT[:, k, tail], k == 0, k == KC - 1, wk + [("uT", k, 3)], [("ps", b)])
                evac(ps[b][:, 0:16], ("ps", b))
            for c in range(2):
                w, wk = load_win(l, c * 128, 128)
                tproj(w, wk, 0, lambda p_, pk, c=c: act(PK[:, c * 16:(c + 1) * 16], p_, AF.Copy, [pk], pkk))
            for c in range(2):
                w, wk = load_win(l, 512 + c * 128, 128)
                tproj(w, wk, 0, lambda p_, pk, c=c: act(gct[:, c, :], p_, AF.Copy, [pk], tyk))
            for c in range(2):
                w, wk = load_win(l, 768 + c * 128, 128)
                tproj(w, wk, 0, lambda p_, pk, c=c: tt(PK[:, 32 + 2 * c:34 + 2 * c], p_[:, 14:16], gct[:, c, 14:16], ALU.mult,
                                                       [pk] + tyk, pkk))
            for j in range(6):
                w, wk = load_win(l, 1280 + j * 128, 128)
                tproj(w, wk, 0, lambda p_, pk, j=j: act(PK[:, 36 + 3 * j:39 + 3 * j], p_[:, 13:16], AF.Copy, [pk], pkk))
            dma("sp", ccd[("h", l, "i")][:, 0:54], PK[:, 0:54], pkk, [("cci", "h", l)])
            allgather("h", l, [("cci", "h", l)], [("cco", "h", l)])

        def halo_apply(l):
            RB = arf(256, 256).rearrange("p (r f) -> p r f", r=4)
            rbk = ark(256, 256)
            tyk = [("tiny", "all")]
            dma("sp", RB, ccd[("h", l, "o")].rearrange("(r p) f -> p r f", p=128), [("cco", "h", l)], rbk)
            hal = tiny[:, 64:128]
            sel = pcol(l, "sel")
            ts(hal, RB[:, 0, :], sel[:, 0:1], None, ALU.mult, None, rbk + ["pp"], tyk)
            for j in range(1, 4):
                stt(hal, RB[:, j, :], sel[:, j:j + 1], hal, ALU.mult, ALU.add, rbk + ["pp"] + tyk, tyk)
            cp(cPool[:, l, :, :], hal[:, 0:32].rearrange("p (c t) -> p c t", c=2), tyk, [("cPool", l, 0), ("cPool", l, 1)])
            cp(cSc[:, l, :, :], hal[:, 32:36].rearrange("p (c t) -> p c t", c=2), tyk, [("cSc", l, 0), ("cSc", l, 1)])
            cp(cCv[:, l, :, :], hal[:, 36:54].rearrange("p (c t) -> p c t", c=6), tyk, [("cCv", l, j) for j in range(6)])

        def s5_combine(l):
            tyk = [("tiny", "all")]
            RB = tiny[:, 816:880].rearrange("p (r f) -> p r f", r=4)
            dma("sp", RB, ccd[("x", l, "o")].rearrange("(r p) f -> p r f", p=128), [("cco", "x", l)], tyk)
            sel = pcol(l, "sel")
            Lr, Li = s5lv[:, l, :, 0, 7], s5lv[:, l, :, 1, 7]
            lk = [("s5lv", l)]
            ar, ai, pr, pi, t1, t2 = (tiny[:, a:a + 8] for a in (768, 776, 784, 792, 880, 888))
            for t_ in (ar, ai, pr, pi):
                mset(t_, 0.0, tyk)
            for j in range(4):
                stt(ar, pr, sel[:, 4 + j:5 + j], ar, ALU.mult, ALU.add, tyk + ["pp"], tyk)
                stt(ai, pi, sel[:, 4 + j:5 + j], ai, ALU.mult, ALU.add, tyk + ["pp"], tyk)
                if j < 3:
                    F = RB[:, j, :].rearrange("p (r a) -> p r a", a=2)
                    tt(t1, pr, Lr, ALU.mult, tyk + lk, tyk)
                    tt(t2, pi, Li, ALU.mult, tyk + lk, tyk)
                    tt(t1, t1, t2, ALU.subtract, tyk, tyk)
                    tt(t2, pr, Li, ALU.mult, tyk + lk, tyk)
                    tt(pr, t1, F[:, :, 0], ALU.add, tyk, tyk)
                    tt(t1, pi, Lr, ALU.mult, tyk + lk, tyk)
                    tt(t1, t1, t2, ALU.add, tyk, tyk)
                    tt(pi, t1, F[:, :, 1], ALU.add, tyk, tyk)
            cp(cX[:, l, :, 0], ar, tyk, [("cX", l, r) for r in range(8)])
            cp(cX[:, l, :, 1], ai, tyk, [("cX", l, r) for r in range(8)])

        def mixer_pool(l, seg):
            V, SA, SB, PB = 0, 2304, 4608, 6912
            for c in range(2):
                w, wk = load_win(l, c * 128, 128)
                v = arf(V, 2064)
                vk = ark(V, 2064)
                cp(v[:, 0:16], cPool[:, l, c, :], [("cPool", l, c)], vk)
                proj_chunk(w, wk, 0, 128,
                           lambda tb, p_, pk: act(v[:, 16 + tb * TB:16 + (tb + 1) * TB], p_, AF.Copy, [pk], vk))
                cp(cPool[:, l, c, :], v[:, 2048:2064], vk, [("cPool", l, c)])
                sa, sbb = arf(SA, 2064), arf(SB, 2064)
                sak, sbk = ark(SA, 2064), ark(SB, 2064)
                tt(sa[:, 1:2064], v[:, 1:2064], v[:, 0:2063], ALU.add, vk, sak)
                tt(sbb[:, 3:2064], sa[:, 3:2064], sa[:, 1:2062], ALU.add, sak, sbk)
                if c == 0:
                    lo_src, hi_src, lo_w, hi_w = sa, sbb, 2, 4
                else:
                    tt(sa[:, 7:2064], sbb[:, 7:2064], sbb[:, 3:2060], ALU.add, sbk, sak)
                    tt(sbb[:, 15:2064], sa[:, 15:2064], sa[:, 7:2056], ALU.add, sak, sbk)
                    lo_src, hi_src, lo_w, hi_w = sa, sbb, 8, 16
                pb = arb(PB, 2048).rearrange("p (c t) -> p c t", c=2)
                pbk = ark(PB, 2048)
                stt(pb[0:64, c, :], lo_src[0:64, 16:2064], 1.0 / lo_w, v[0:64, 16:2064], ALU.mult, ALU.subtract, sak + sbk + vk, pbk)
                stt(pb[64:128, c, :], hi_src[64:128, 16:2064], 1.0 / hi_w, v[64:128, 16:2064], ALU.mult, ALU.subtract, sak + sbk + vk, pbk)
                if True:
                    ic = pcol(l, "invc").rearrange("p (c t) -> p c t", c=2)
                    tq = tiny[:, 512:528]
                    for (r0, r1, src) in ((0, 64, lo_src), (64, 128, hi_src)):
                        tt(tq[r0:r1, :], src[r0:r1, 16:32], ic[r0:r1, c, :], ALU.mult, sak + sbk + ["pp"], [("tiny", "all")])
                        tt(pb[r0:r1, c, 0:16], tq[r0:r1, :], v[r0:r1, 16:32], ALU.subtract, [("tiny", "all")] + vk, pbk)
            pb = arb(PB, 2048).rearrange("p (c t) -> p c t", c=2)
            pbk = ark(PB, 2048)
            for c in range(2):
                for tb in range(NTB):
                    b = next_acc()
                    mm(ps[b][:, :], poolwS[:, l, c, :], pb[:, c, tb * TB:(tb + 1) * TB], True, True, pbk + ["poolw"], [("ps", b)])
                    ts(Yr[:, c, tb * TB:(tb + 1) * TB], ps[b][:, :], pcol(l, "pscale", c, c + 1), None, ALU.mult, None,
                       [("ps", b), "pp"], [("Yr", c, tb)])

        def mixer_sconv(l, seg):
            GC, G, T1 = 0, 2048, 4352
            for c in range(2):
                gc = arf(GC, 2048)
                gck = ark(GC, 2048)
                wgc, wgck = load_win(l, 512 + c * 128, 128)
                proj_chunk(wgc, wgck, 0, 128,
                           lambda tb, p_, pk: act(gc[:, tb * TB:(tb + 1) * TB], p_, AF.Copy, [pk], gck))
                whh, whhk = load_win(l, 768 + c * 128, 128)
                g = arf(G, 2050)
                gk = ark(G, 2050)
                cp(g[:, 0:2], cSc[:, l, c, :], [("cSc", l, c)], gk)
                proj_chunk(whh, whhk, 0, 128,
                           lambda tb, p_, pk: tt(g[:, 2 + tb * TB:2 + (tb + 1) * TB], p_, gc[:, tb * TB:(tb + 1) * TB], ALU.mult,
                                                 [pk] + gck, gk))
                cp(cSc[:, l, c, :], g[:, 2048:2050], gk, [("cSc", l, c)])
                t1 = arf(T1, 2048)
                t1k = ark(T1, 2048)
                wv = pcol(l, "scw").rearrange("p (c k) -> p c k", c=2)
                ts(t1, g[:, 0:2048], wv[:, c, 0:1], None, ALU.mult, None, gk + ["pp"], t1k)
                stt(t1, g[:, 1:2049], wv[:, c, 1:2], t1, ALU.mult, ALU.add, gk + ["pp"] + t1k, t1k)
                stt(t1, g[:, 2:2050], wv[:, c, 2:3], t1, ALU.mult, ALU.add, gk + ["pp"] + t1k, t1k)
                wgb, wgbk = load_win(l, 256 + c * 128, 128)
                proj_chunk(wgb, wgbk, 0, 128,
                           lambda tb, p_, pk: tt(Yr[:, 2 + c, tb * TB:(tb + 1) * TB], p_, t1[:, tb * TB:(tb + 1) * TB], ALU.mult,
                                                 [pk] + t1k, [("Yr", 2 + c, tb)]))

        def mixer_ssd(l, seg):
            SZ, RAW, ACC, XBC = 0, 2048, 4352, 6400
            XTOK, BTOK = 2048, 4096
            sz = arb(SZ, 2048).rearrange("p (c t) -> p c t", c=2)
            szk = ark(SZ, 2048)
            for c in range(2):
                wz, wzk = load_win(l, 1024 + c * 128, 128)
                proj_chunk(wz, wzk, 0, 128,
                           lambda tb, p_, pk: act(sz[:, c, tb * TB:(tb + 1) * TB], p_, AF.Silu, [pk], szk))
            xbc = arb(XBC, 6144).rearrange("p (c t) -> p c t", c=6)
            kxb = [ark(XBC + j * 1024, 1024) for j in range(6)]
            cw = pcol(l, "cvw").rearrange("p (c k) -> p c k", c=6)
            for j in range(6):
                wx, wxk = load_win(l, 1280 + j * 128, 128)
                raw = arf(RAW, 2051)
                rawk = ark(RAW, 2051)
                cp(raw[:, 0:3], cCv[:, l, j, :], [("cCv", l, j)], rawk)
                proj_chunk(wx, wxk, 0, 128,
                           lambda tb, p_, pk: act(raw[:, 3 + tb * TB:3 + (tb + 1) * TB], p_, AF.Copy, [pk], rawk))
                cp(cCv[:, l, j, :], raw[:, 2048:2051], rawk, [("cCv", l, j)])
                acc = arf(ACC, 2048)
                acck = ark(ACC, 2048)
                ts(acc, raw[:, 0:2048], cw[:, j, 0:1], None, ALU.mult, None, rawk + ["pp"], acck)
                for kk in range(1, 4):
                    stt(acc, raw[:, kk:kk + 2048], cw[:, j, kk:kk + 1], acc, ALU.mult, ALU.add, rawk + acck + ["pp"], acck)
                act(xbc[:, j, :], acc, AF.Silu, acck + ["pp"], kxb[j], bias=pcol(l, "cvb", j, j + 1))
            print("  ssd: before dt", P.total)
            wd, wdk = load_win(l, 2048, 4)
            for c in range(NCH):
                for k in range(KC):
                    mm(ps[6][:, c * 4:(c + 1) * 4], uT[:, k, c * 128:(c + 1) * 128], wd[:, k, 0:4], k == 0, k == KC - 1,
                       wdk + [("uT", k, c // 4)], [("ps", 6)])
            print("  ssd: before small", P.total)
            tyk = [("tiny", "all")]
            def v3(a): return tiny[:, a:a + 64].rearrange("p (c h) -> p c h", h=4)
            dt_, adt, acs, tot, eacs, dte, cd, ddte = (v3(a) for a in (0, 64, 128, 192, 256, 320, 384, 448))
            xsp = v3(512)
            ex = v3(576)
            bc4 = lambda a, b: ssdp[:, l, a:b].unsqueeze(1).to_broadcast([128, NCH, 4])
            tt(xsp, ps[6][:, 0:64].rearrange("p (c h) -> p c h", h=4), bc4(4, 8), ALU.add, [("ps", 6), ("ssdp", l)], tyk)
            ts(ex, xsp, 30.0, None, ALU.min, None, tyk, tyk)
            act(ex, ex, AF.Exp, tyk, tyk)
            act(ex, ex, AF.Ln, tyk + ["eps"], tyk, bias=epsT[:, 1:2])
            tt(dt_, ex, xsp, ALU.max, tyk, tyk)
            tt(adt, dt_, bc4(0, 4), ALU.mult, tyk + [("ssdp", l)], tyk)
            ahi = tinyb[:, 0:64]
            alo = tinyb[:, 64:128]
            tbk = [("tinyb", "a")]
            adf = tiny[:, 64:128]
            cp(ahi, adf, tyk, tbk)
            tt(tiny[:, 640:704], adf, ahi, ALU.subtract, tyk + tbk, tyk)
            cp(alo, tiny[:, 640:704], tyk, tbk)
            cp(tiny[:, 768:832], ahi, tbk, tyk)
            mm(ps[6][:, 64:128], tri, ahi, True, False, tbk + ["cm"], [("ps", 6)])
            mm(ps[6][:, 64:128], tri, alo, False, True, tbk + ["cm"], [("ps", 6)])
            mm(ps[6][:, 128:192], ones, ahi, True, False, tbk + ["cm"], [("ps", 6)])
            mm(ps[6][:, 128:192], ones, alo, False, True, tbk + ["cm"], [("ps", 6)])
            cp(tiny[:, 128:256], ps[6][:, 64:192], [("ps", 6)], tyk)
            act(tiny[:, 256:320], tiny[:, 128:192], AF.Exp, tyk, tyk)
            tt(tiny[:, 320:384], tiny[:, 192:256], tiny[:, 128:192], ALU.subtract, tyk, tyk)
            act(tiny[:, 320:384], tiny[:, 320:384], AF.Exp, tyk, tyk)
            act(tiny[:, 384:448], tiny[:, 192:256], AF.Exp, tyk, tyk)
            tt(tiny[:, 448:512], tiny[:, 0:64], tiny[:, 320:384], ALU.mult, tyk, tyk)
            ts(tiny[:, 704:768], tiny[:, 128:192], -1.0, None, ALU.mult, None, tyk, tyk)
            nacs = v3(704)
            print("  ssd: before transposes", P.total)
            xtok = arb(XTOK, 2048).rearrange("p (c f) -> p c f", c=NCH)
            btok = arb(BTOK, 2048).rearrange("p (c f) -> p c f", c=NCH)
            xtk, btk = ark(XTOK, 2048), ark(BTOK, 2048)
            ti = 0
            for c in range(NCH):
                for j in range(4):
                    o = (ti % 4) * 128
                    ti += 1
                    bnk = 4 + (ti - 1) % 4
                    pk_ = ("ps", bnk)
                    mm(ps[bnk][:, 0:128], xbc[:, j, c * 128:(c + 1) * 128], ident, True, True, kxb[j] + ["cm"], [pk_])
                    dst = (xtok if j < 2 else btok)[:, c, (j % 2) * 128:(j % 2 + 1) * 128]
                    if ti % 2 == 0:
                        cp(dst, ps[bnk][:, 0:128], [pk_], xtk if j < 2 else btk)
                    else:
                        act(dst, ps[bnk][:, 0:128], AF.Copy, [pk_], xtk if j < 2 else btk)
            WT = XBC
            E_ = arb(WT, 256).rearrange("p (h s) -> p h s", h=4)
            MT = arb(WT + 256, 256).rearrange("p (h s) -> p h s", h=4)
            RH = arb(WT + 512, 512).rearrange("p (a h s) -> p a h s", a=2, h=4)
            XDT = arb(WT + 1024, 128)
            XDE = arb(WT + 1152, 128)
            YSB = arf(WT + 1280, 256)
            YTK = arb(WT + 1536, 128)
            SBF = arb(WT + 1664, 128)
            STMP = arf(WT + 1792, 256)
            kE, kMT, kRH, kXDT, kXDE, kYSB, kYTK, kSBF, kST = (ark(WT + a, n) for a, n in
                ((0, 256), (256, 256), (512, 512), (1024, 128), (1152, 128), (1280, 256), (1536, 128), (1664, 128), (1792, 256)))
            print("  ssd: before main loop", P.total)
            Sst = cS[:, l, :]
            kS = [("cS", l)]
            PKG, RBO, PTO = 12544, 12816, 13904
            pkg = arf(PKG, 272)
            pkgk = ark(PKG, 272)
            SL = pkg[:, 0:256]
            mset(SL, 0.0, pkgk)
            for c in range(NCH):
                x3 = xtok[:, c, :].rearrange("p (h d) -> p h d", h=4)
                tt(XDE.rearrange("p (h d) -> p h d", h=4), x3, ddte[:, c, :].unsqueeze(2).to_broadcast([128, 4, 64]), ALU.mult, xtk + tyk, kXDE)
                for g in range(2):
                    mm(ps[5][:, g * 128:(g + 1) * 128], btok[:, c, g * 128:(g + 1) * 128], XDE[:, g * 128:(g + 1) * 128], True, True,
                       btk + kXDE, [("ps", 5)])
                tt(STMP.rearrange("p (h d) -> p h d", h=4), SL.rearrange("p (h d) -> p h d", h=4),
                   cd[:, c, :].unsqueeze(2).to_broadcast([128, 4, 64]), ALU.mult, pkgk + tyk, kST)
                tt(SL, STMP, ps[5][:, 0:256], ALU.add, kST + [("ps", 5)], pkgk)
            tr_ = tiny[:, 832:864].rearrange("p (c h) -> p c h", h=4)
            tt(tr_, tot[:, 0:8, :], tot[:, 8:16, :], ALU.add, tyk, tyk)
            tt(tr_[:, 0:4, :], tr_[:, 0:4, :], tr_[:, 4:8, :], ALU.add, tyk, tyk)
            tt(tr_[:, 0:2, :], tr_[:, 0:2, :], tr_[:, 2:4, :], ALU.add, tyk, tyk)
            tt(tr_[:, 0:1, :], tr_[:, 0:1, :], tr_[:, 1:2, :], ALU.add, tyk, tyk)
            act(pkg[:, 256:260], tiny[:, 832:836], AF.Exp, tyk, pkgk)
            dma("sp", ccd[("s", l, "i")][:, 0:260], pkg[:, 0:260], pkgk, [("cci", "s", l)])
            allgather("s", l, [("cci", "s", l)], [("cco", "s", l)])
            RBs = arf(RBO, 1088).rearrange("p (r f) -> p r f", r=4)
            rbsk = ark(RBO, 1088)
            dma("sp", RBs, ccd[("s", l, "o")].rearrange("(r p) f -> p r f", p=128), [("cco", "s", l)], rbsk)
            PT = arf(PTO, 256)
            ptk = ark(PTO, 256)
            sel = pcol(l, "sel")
            mset(Sst, 0.0, kS)
            mset(PT, 0.0, ptk)
            for j in range(4):
                stt(Sst, PT, sel[:, 4 + j:5 + j], Sst, ALU.mult, ALU.add, ptk + kS + ["pp"], kS)
                if j < 3:
                    tt(PT.rearrange("p (h d) -> p h d", h=4), PT.rearrange("p (h d) -> p h d", h=4),
                       RBs[:, j, 256:260].unsqueeze(2).to_broadcast([128, 4, 64]), ALU.mult, ptk + rbsk, ptk)
                    tt(PT, PT, RBs[:, j, 0:256], ALU.add, ptk + rbsk, ptk)
            cp(SBF, Sst, kS, kSBF)
            for c in range(NCH):
                tsl = slice(c * 128, (c + 1) * 128)
                if c < 2:
                    print("  ssd: chunk", c, P.total)
                for h in range(4):
                    o = ps[0][:, h * 128:(h + 1) * 128]
                    mm(o, tinyb[:, c * 4 + h:c * 4 + h + 1].to_broadcast([128, 128]), tri, True, False, tbk + ["cm"], [("ps", 0)])
                    mm(o, tinyb[:, 64 + c * 4 + h:64 + c * 4 + h + 1].to_broadcast([128, 128]), tri, False, False, tbk + ["cm"], [("ps", 0)])
                    mm(o, ident, negm, False, True, ["cm"], [("ps", 0)])
                for h in range(4):
                    act(E_[:, h, :], ps[0][:, h * 128:(h + 1) * 128], AF.Exp, [("ps", 0)] + tyk, kE, bias=nacs[:, c, h:h + 1])
                for g in range(2):
                    mm(ps[1][:, g * 128:(g + 1) * 128], xbc[:, 2 + g, tsl], xbc[:, 4 + g, tsl], True, True,
                       kxb[2 + g] + kxb[4 + g], [("ps", 1)])
                for h in range(4):
                    g = h // 2
                    tt(MT[:, h, :], ps[1][:, g * 128:(g + 1) * 128], E_[:, h, :], ALU.mult, [("ps", 1)] + kE, kMT)
                x3 = xtok[:, c, :].rearrange("p (h d) -> p h d", h=4)
                tt(XDT.rearrange("p (h d) -> p h d", h=4), x3, dt_[:, c, :].unsqueeze(2).to_broadcast([128, 4, 64]), ALU.mult, xtk + tyk, kXDT)
                tt(XDE.rearrange("p (h d) -> p h d", h=4), x3, ddte[:, c, :].unsqueeze(2).to_broadcast([128, 4, 64]), ALU.mult, xtk + tyk, kXDE)
                for h in range(4):
                    o = ps[2][:, h * 64:(h + 1) * 64]
                    mm(o, MT[:, h, :], XDT[:, h * 64:(h + 1) * 64], True, True, kMT + kXDT, [("ps", 2, "y")])
                for h in range(4):
                    g = h // 2
                    mm(ps[3][:, h * 64:(h + 1) * 64], xbc[:, 4 + g, tsl], SBF[:, h * 64:(h + 1) * 64], True, True,
                       kxb[4 + g] + kSBF, [("ps", 3)])
                tt(YSB.rearrange("p (h d) -> p h d", h=4), x3, ssdp[:, l, 8:12].unsqueeze(2).to_broadcast([128, 4, 64]), ALU.mult,
                   xtk + [("ssdp", l)], kYSB)
                tt(YSB, YSB, ps[2][:, 0:256], ALU.add, kYSB + [("ps", 2, "y")], kYSB)
                tt(STMP.rearrange("p (h d) -> p h d", h=4), ps[3][:, 0:256].rearrange("p (h d) -> p h d", h=4),
                   eacs[:, c, :].unsqueeze(2).to_broadcast([128, 4, 64]), ALU.mult, [("ps", 3)] + tyk, kST)
                tt(YTK, STMP, YSB, ALU.add, kST + kYSB, kYTK)
                for j in range(2):
                    o = j * 128
                    mm(ps[4][:, o:o + 128], YTK[:, j * 128:(j + 1) * 128], ident, True, True, kYTK + ["cm"], [("ps", 4, o)])
                    tt(Yr[:, 4 + j, tsl], ps[4][:, o:o + 128], sz[:, j, tsl], ALU.mult, [("ps", 4, o)] + szk, [("Yr", 4 + j, c // 4)])
                for g in range(2):
                    mm(ps[5][:, g * 128:(g + 1) * 128], btok[:, c, g * 128:(g + 1) * 128], XDE[:, g * 128:(g + 1) * 128], True, True,
                       btk + kXDE, [("ps", 5)])
                tt(STMP.rearrange("p (h d) -> p h d", h=4), Sst.rearrange("p (h d) -> p h d", h=4),
                   cd[:, c, :].unsqueeze(2).to_broadcast([128, 4, 64]), ALU.mult, kS + tyk, kST)
                tt(Sst, STMP, ps[5][:, 0:256], ALU.add, kST + [("ps", 5)], kS)
                act(SBF, Sst, AF.Copy, kS, kSBF)

        def mixer_s5(l, seg):
            U, STG, BT, W, WB, PAT, INJ, TAB = 0, 2048, 4096, 6144, 8192, 10240, 11264, 13312
            tyk = [("tiny", "all")]
            pk = [("s5pw", l)]
            lk = [("s5lv", l)]
            u = arb(U, 2048).rearrange("p (c t) -> p c t", c=2)
            uk = ark(U, 2048)
            for c in range(2):
                wu, wuk = load_win(l, 2052 + c * 128, 128)
                proj_chunk(wu, wuk, 0, 128,
                           lambda tb, p_, pk_: act(u[:, c, tb * TB:(tb + 1) * TB], p_, AF.Copy, [pk_], uk))
            pat = arb(PAT, 1024)
            patk = ark(PAT, 1024)
            mset(pat, 1.0, patk)
            mset(pat.rearrange("p (c j) -> p c j", j=16)[:, :, 0:1], 0.0, patk)
            pw0 = tiny[:, 0:256].rearrange("p (r a j) -> p r a j", r=8, a=2)
            qq = tiny[:, 256:512].rearrange("p (r a j) -> p r a j", r=8, a=2)
            den = tiny[:, 512:640].rearrange("p (r j) -> p r j", r=8)
            tmp = tiny[:, 640:768].rearrange("p (r j) -> p r j", r=8)
            mset(pw0[:, :, 0, 0:1], 1.0, tyk)
            mset(pw0[:, :, 1, 0:1], 0.0, tyk)
            for a in range(2):
                cp(pw0[:, :, a, 1:16], s5pw[:, l, :, a, 0:15], pk, tyk)
            tt(den, pw0[:, :, 0, :], pw0[:, :, 0, :], ALU.mult, tyk, tyk)
            tt(tmp, pw0[:, :, 1, :], pw0[:, :, 1, :], ALU.mult, tyk, tyk)
            tt(den, den, tmp, ALU.add, tyk, tyk)
            P.op("dve", lambda e: e.reciprocal(out=den, in_=den), tyk, tyk)
            tt(qq[:, :, 0, :], pw0[:, :, 0, :], den, ALU.mult, tyk, tyk)
            tt(qq[:, :, 1, :], pw0[:, :, 1, :], den, ALU.mult, tyk, tyk)
            ts(qq[:, :, 1, :], qq[:, :, 1, :], -1.0, None, ALU.mult, None, tyk, tyk)
            bpad = arf(TAB, 512).rearrange("p (r a h) -> p r a h", r=8, a=2)
            cpad = arf(TAB + 512, 512).rearrange("p (r a h) -> p r a h", r=8, a=2)
            tabk = ark(TAB, 1024)
            mset(arf(TAB, 512), 0.0, tabk)
            for a in range(2):
                cp(bpad[0:64, :, a, 0:16], bbS[0:64, l, a, :, :], [("bb", l)], tabk)
                cp(bpad[64:128, :, a, 16:32], bbS[64:128, l, a, :, :], [("bb", l)], tabk)
            dma("sp", arf(TAB + 512, 512), s5c[l], [], tabk)
            Ef = Yr[:, 6:8, :].rearrange("p a t -> p (a t)").bitcast(F32)
            Eall = Ef.rearrange("p (r a c) -> p r a c", r=8, a=2)
            ek = [("Yr", 6 + a, tb) for a in range(2) for tb in range(NTB)]
            bufA = arf(W, 2048).rearrange("p (r a c) -> p r a c", r=8, a=2)
            bufB = arf(WB, 2048).rearrange("p (r a c) -> p r a c", r=8, a=2)
            kA, kB = ark(W, 2048), ark(WB, 2048)
            T1 = arf(STG, 2048)
            T2 = arf(BT, 2048)
            kT1, kT2 = ark(STG, 2048), ark(BT, 2048)

            def bc8(ap, n):
                return ap.unsqueeze(2).to_broadcast([128, 8, n])

            def lvl2(src, sk):
                cur, ck_ = src, sk
                nxt_list = [(bufA, kA), (bufB, kB)]
                if src is bufA:
                    nxt_list = [(bufB, kB), (bufA, kA)]
                for lev in range(7):
                    d = 1 << lev
                    n = 128 - d
                    dst, dk = nxt_list[lev % 2]
                    cr, ci = s5lv[:, l, :, 0, lev], s5lv[:, l, :, 1, lev]
                    t1 = T1[:, 0:8 * n].rearrange("p (r c) -> p r c", r=8)
                    t2 = T2[:, 0:8 * n].rearrange("p (r c) -> p r c", r=8)
                    cp(dst[:, :, :, 0:d], cur[:, :, :, 0:d], ck_, dk)
                    tt(t1, cur[:, :, 0, 0:n], bc8(cr, n), ALU.mult, ck_ + lk, kT1)
                    tt(t2, cur[:, :, 1, 0:n], bc8(ci, n), ALU.mult, ck_ + lk, kT2)
                    tt(t1, t1, t2, ALU.subtract, kT1 + kT2, kT1)
                    tt(dst[:, :, 0, d:128], cur[:, :, 0, d:128], t1, ALU.add, ck_ + kT1, dk)
                    tt(t1, cur[:, :, 1, 0:n], bc8(cr, n), ALU.mult, ck_ + lk, kT1)
                    tt(t2, cur[:, :, 0, 0:n], bc8(ci, n), ALU.mult, ck_ + lk, kT2)
                    tt(t1, t1, t2, ALU.add, kT1 + kT2, kT1)
                    tt(dst[:, :, 1, d:128], cur[:, :, 1, d:128], t1, ALU.add, ck_ + kT1, dk)
                    cur, ck_ = dst, dk
                return cur, ck_

            def run_pass(mode):
                inj = arf(INJ, 2048).rearrange("p (r a c) -> p r a c", r=8, a=2)
                injk = ark(INJ, 2048)
                if mode == "full":
                    cp(bufA, Eall, ek, kA)
                    mur, mui = s5lv[:, l, :, 0, 0], s5lv[:, l, :, 1, 0]
                    xr_, xi_ = cX[:, l, :, 0], cX[:, l, :, 1]
                    ckx = [("cX", l, r) for r in range(8)]
                    a1, a2 = tiny[:, 768:776], tiny[:, 776:784]
                    tt(a1, mur, xr_, ALU.mult, lk + ckx, tyk)
                    tt(a2, mui, xi_, ALU.mult, lk + ckx, tyk)
                    tt(a1, a1, a2, ALU.subtract, tyk, tyk)
                    tt(bufA[:, :, 0, 0], bufA[:, :, 0, 0], a1, ALU.add, kA + tyk, kA)
                    tt(a1, mur, xi_, ALU.mult, lk + ckx, tyk)
                    tt(a2, mui, xr_, ALU.mult, lk + ckx, tyk)
                    tt(a1, a1, a2, ALU.add, tyk, tyk)
                    tt(bufA[:, :, 1, 0], bufA[:, :, 1, 0], a1, ALU.add, kA + tyk, kA)
                    res, rk = lvl2(bufA, kA)
                    lr8, li8 = s5pw[:, l, :, 0, 0], s5pw[:, l, :, 1, 0]
                    n = 127
                    t1 = T1[:, 0:8 * n].rearrange("p (r c) -> p r c", r=8)
                    t2 = T2[:, 0:8 * n].rearrange("p (r c) -> p r c", r=8)
                    tt(t1, res[:, :, 0, 0:n], bc8(lr8, n), ALU.mult, rk + pk, kT1)
                    tt(t2, res[:, :, 1, 0:n], bc8(li8, n), ALU.mult, rk + pk, kT2)
                    tt(inj[:, :, 0, 1:128], t1, t2, ALU.subtract, kT1 + kT2, injk)
                    tt(t1, res[:, :, 1, 0:n], bc8(lr8, n), ALU.mult, rk + pk, kT1)
                    tt(t2, res[:, :, 0, 0:n], bc8(li8, n), ALU.mult, rk + pk, kT2)
                    tt(inj[:, :, 1, 1:128], t1, t2, ALU.add, kT1 + kT2, injk)
                    tt(a1, lr8, xr_, ALU.mult, pk + ckx, tyk)
                    tt(a2, li8, xi_, ALU.mult, pk + ckx, tyk)
                    tt(inj[:, :, 0, 0], a1, a2, ALU.subtract, tyk, injk)
                    tt(a1, lr8, xi_, ALU.mult, pk + ckx, tyk)
                    tt(a2, li8, xr_, ALU.mult, pk + ckx, tyk)
                    tt(inj[:, :, 1, 0], a1, a2, ALU.add, tyk, injk)

                stg = arb(STG, 2048).rearrange("p (a j c) -> p a j c", a=2, j=16)
                stgk = ark(STG, 2048)
                btv = arb(BT, 2048).rearrange("p (a j c) -> p a j c", a=2, j=16)
                btk_ = ark(BT, 2048)
                Wn = arf(W, 2048)
                wk_ = ark(W, 2048)
                Wv = Wn.rearrange("p (c j) -> p j c", j=16)
                wb = arb(WB, 2048).rearrange("p (a t) -> p a t", a=2)
                wbk = ark(WB, 2048)

                def b16(ap):
                    return ap.unsqueeze(2).to_broadcast([128, 16, 32])

                def h32(ap):
                    return ap.unsqueeze(1).to_broadcast([128, 16, 32])

                ct = Yr[:, 4:6, :].rearrange("p a (j c) -> p a j c", j=16)
                ctk = [("Yr", 4 + a_, t_) for a_ in range(2) for t_ in range(NTB)]
                if mode == "p1":
                    tmp_f, tmpk = arf(WB, 1024), ark(WB, 1024)
                    W1n, w1k_ = arf(INJ, 2048), ark(INJ, 2048)
                else:
                    tmp_f = Yr[:, 7, :].bitcast(F32)
                    tmpk = [("Yr", 7, t_) for t_ in range(NTB)]
                    W1n = Yr[:, 0:2, :].rearrange("p a t -> p (a t)").bitcast(F32)
                    w1k_ = [("Yr", a_, t_) for a_ in range(2) for t_ in range(NTB)]
                Wbufs = [(Wn, wk_), (W1n, w1k_)]
                t1 = tmp_f[:, 0:512].rearrange("p (j h) -> p j h", j=16)
                t2 = tmp_f[:, 512:1024].rearrange("p (j h) -> p j h", j=16)
                mset(arb(STG, 2048), 0.0, stgk)
                if mode == "full":
                    mset(Yr[:, 4:6, :], 0.0, ctk)

                def tables_b_dve(r):
                    q = r % 4
                    if r > 0:
                        pq = (r - 1) % 4
                        mset(stg[:, :, :, pq * 32:(pq + 1) * 32], 0.0, stgk)
                    Bre, Bim = h32(bpad[:, r, 0, :]), h32(bpad[:, r, 1, :])
                    qr_, qi_ = b16(qq[:, r, 0, :]), b16(qq[:, r, 1, :])
                    sv = stg[:, :, :, q * 32:(q + 1) * 32]
                    tt(t1, Bre, qr_, ALU.mult, tabk + tyk, tmpk)
                    tt(t2, Bim, qi_, ALU.mult, tabk + tyk, tmpk)
                    tt(sv[:, 0], t1, t2, ALU.subtract, tmpk, stgk)
                    tt(t1, Bim, qr_, ALU.mult, tabk + tyk, tmpk)
                    tt(t2, Bre, qi_, ALU.mult, tabk + tyk, tmpk)
                    tt(sv[:, 1], t1, t2, ALU.add, tmpk, stgk)

                def tables_b_pe(r):
                    for a in range(2):
                        for lg in range(4):
                            bnk = 4 + lg
                            for li_ in range(4):
                                mm(ps[bnk][:, li_ * 128:(li_ + 1) * 128], stg[:, a, lg * 4 + li_, :], ident, True, True,
                                   stgk + ["cm"], [("ps", bnk)])
                            act(btv[:, a, lg * 4:(lg + 1) * 4, :], ps[bnk][:, :].rearrange("p (j c) -> p j c", j=4), AF.Copy,
                                [("ps", bnk)], btk_)

                def ct_dve(r):
                    q = r % 4
                    if r > 0:
                        pq = (r - 1) % 4
                        mset(ct[:, :, :, pq * 32:(pq + 1) * 32], 0.0, ctk)
                    Cre, Cim = h32(cpad[:, r, 0, :]), h32(cpad[:, r, 1, :])
                    pr_, pi_ = b16(pw0[:, r, 0, :]), b16(pw0[:, r, 1, :])
                    cv = ct[:, :, :, q * 32:(q + 1) * 32]
                    tt(t1, Cre, pr_, ALU.mult, tabk + tyk, tmpk)
                    tt(t2, Cim, pi_, ALU.mult, tabk + tyk, tmpk)
                    tt(cv[:, 0], t1, t2, ALU.add, tmpk, ctk)
                    tt(t1, Cim, pr_, ALU.mult, tabk + tyk, tmpk)
                    tt(t2, Cre, pi_, ALU.mult, tabk + tyk, tmpk)
                    tt(cv[:, 1], t1, t2, ALU.subtract, tmpk, ctk)

                def bu_mm(r, a, wv_, wkk):
                    uv = u[:, r // 4, :].rearrange("p (c j) -> p j c", j=16)
                    for lg in range(4):
                        bnk = 4 + lg
                        for li_ in range(4):
                            j = lg * 4 + li_
                            mm(ps[bnk][:, li_ * 128:(li_ + 1) * 128], btv[:, a, j, :], uv[:, j, :], True, True, btk_ + uk, [("ps", bnk)])
                        act(wv_[:, lg * 4:(lg + 1) * 4, :], ps[bnk][:, :].rearrange("p (j c) -> p j c", j=4), AF.Copy, [("ps", bnk)], wkk)

                def scan(wn_, wkk):
                    P.op("dve", lambda e, wn_=wn_: e.tensor_tensor_scan(out=wn_, data0=pat, data1=wn_, initial=0.0, op0=ALU.mult, op1=ALU.add),
                         wkk + patk, wkk)

                tables_b_dve(0)
                tables_b_pe(0)
                for r in range(8):
                    oc = r // 4
                    q = r % 4
                    uv = u[:, oc, :].rearrange("p (c j) -> p j c", j=16)
                    wvs = [(wn_.rearrange("p (c j) -> p j c", j=16), wn_, kk_) for (wn_, kk_) in Wbufs]
                    if mode == "p1":
                        bu_mm(r, 0, wvs[0][0], wvs[0][2])
                        bu_mm(r, 1, wvs[1][0], wvs[1][2])
                        if r + 1 < 8:
                            tables_b_dve(r + 1)
                        for a in range(2):
                            scan(wvs[a][1], wvs[a][2])
                            cp(Eall[:, r, a, :], wvs[a][0][:, 15, :], wvs[a][2], ek)
                        if r + 1 < 8:
                            tables_b_pe(r + 1)
                        continue
                    bu_mm(r, 0, wvs[0][0], wvs[0][2])
                    bu_mm(r, 1, wvs[1][0], wvs[1][2])
                    ct_dve(r)
                    if r + 1 < 8:
                        tables_b_dve(r + 1)
                    for a in range(2):
                        wv_, wn_, wkk = wvs[a]
                        tt(wv_[:, 0, :], wv_[:, 0, :], inj[:, r, a, :], ALU.add, wkk + injk, wkk)
                        scan(wn_, wkk)
                        act(wb[:, a, :], wn_, AF.Copy, wkk, wbk)
                    if r + 1 < 8:
                        tables_b_pe(r + 1)
                    wbv = [wb[:, a, :].rearrange("p (c j) -> p j c", j=16) for a in range(2)]
                    for j in range(16):
                        o = ps[j // 4][:, (j % 4) * 128:(j % 4 + 1) * 128]
                        P.op("pe", lambda e, o=o, j=j, q=q, wbv=wbv: e.matmul(o, lhsT=ct[:, 0, j, :], rhs=wbv[0][:, j, :],
                                                                             start=(q == 0 and j % 4 == 0), stop=False, skip_group_check=True),
                             ctk + wbk, [("ps", j // 4)])
                        P.op("pe", lambda e, o=o, j=j, q=q, wbv=wbv: e.matmul(o, lhsT=ct[:, 1, j, :], rhs=wbv[1][:, j, :], start=False,
                                                                             stop=(q == 3), skip_group_check=True), ctk + wbk, [("ps", j // 4)])
                    if q == 3:
                        for tb in range(NTB):
                            yo = W + (tb % 2) * 512
                            yt = arf(yo, 512)
                            ytk = ark(yo, 512)
                            uv4 = uv[:, tb * 4:(tb + 1) * 4, :]
                            stt(yt.rearrange("p (j c) -> p j c", j=4), uv4, pcol(l, "s5d", oc, oc + 1),
                                ps[tb][:, :].rearrange("p (j c) -> p j c", j=4), ALU.mult, ALU.add, uk + ["pp", ("ps", tb)], ytk)
                            act(Yr[:, 6 + oc, :].rearrange("p (c j) -> p j c", j=16)[:, tb * 4:(tb + 1) * 4, :],
                                yt.rearrange("p (j c) -> p j c", j=4), AF.Gelu_apprx_tanh, ytk, [("Yr", 6 + oc, t_) for t_ in range(NTB)])
                if mode == "p1":
                    p15r, p15i = s5pw[:, l, :, 0, 14], s5pw[:, l, :, 1, 14]
                    n = 128
                    t1 = T1[:, 0:1024].rearrange("p (r c) -> p r c", r=8)
                    t2 = T1[:, 1024:2048].rearrange("p (r c) -> p r c", r=8)
                    t3 = T2[:, 0:1024].rearrange("p (r c) -> p r c", r=8)
                    t4 = T2[:, 1024:2048].rearrange("p (r c) -> p r c", r=8)
                    tt(t1, Eall[:, :, 0, :], bc8(p15r, n), ALU.mult, ek + pk, kT1)
                    tt(t2, Eall[:, :, 1, :], bc8(p15i, n), ALU.mult, ek + pk, kT1)
                    tt(t3, Eall[:, :, 1, :], bc8(p15r, n), ALU.mult, ek + pk, kT2)
                    tt(t4, Eall[:, :, 0, :], bc8(p15i, n), ALU.mult, ek + pk, kT2)
                    tt(Eall[:, :, 0, :], t1, t2, ALU.subtract, kT1, ek)
                    tt(Eall[:, :, 1, :], t3, t4, ALU.add, kT2, ek)
                    res, rk = lvl2(Eall, ek)
                    cp(tiny[:, 800:816].rearrange("p (r a) -> p r a", a=2), res[:, :, :, 127], rk, tyk)
                    dma("sp", ccd[("x", l, "i")], tiny[:, 800:816], tyk, [("cci", "x", l)])
                    allgather("x", l, [("cci", "x", l)], [("cco", "x", l)])
                    return

            run_pass("p1")
            s5_combine(l)
            run_pass("full")
            SG = W
            for tb in range(NTB):
                sgs = []
                for m in range(2):
                    b = next_acc()
                    for k in range(2):
                        mm(ps[b][:, :], gluwS[:, l, k, m * 128:(m + 1) * 128], Yr[:, 6 + k, tb * TB:(tb + 1) * TB], k == 0, k == 1,
                           [("Yr", 6 + k, tb), "gluw"], [("ps", b)])
                    so = SG + ((tb * 2 + m) % 4) * 256
                    sg = arb(so, 256)
                    act(sg, ps[b][:, :], AF.Sigmoid, [("ps", b), "pp"], ark(so, 256), bias=pcol(l, "glub", m, m + 1))
                    sgs.append((sg, ark(so, 256)))
                for m in range(2):
                    sg, sgk = sgs[m]
                    tt(Yr[:, 6 + m, tb * TB:(tb + 1) * TB], Yr[:, 6 + m, tb * TB:(tb + 1) * TB], sg, ALU.mult,
                       [("Yr", 6 + m, tb)] + sgk, [("Yr", 6 + m, tb)])

        W1O = [7168, 9216]
        W2O = [11264, 13312]
        NG = 8

        def mlp_load(l, g):
            sl = g % 2
            w1g = arb(W1O[sl], 2048).rearrange("p (k c) -> p k c", k=8)
            w2g = arb(W2O[sl], 2048).rearrange("p (k c) -> p k c", k=4)
            w1k, w2k = ark(W1O[sl], 2048), ark(W2O[sl], 2048)
            dma("pool", w1g, w1[l, :, g * 512:(g + 1) * 512].rearrange("(k p) c -> p k c", p=128), [], w1k)
            dma("pool", w2g, w2[l, g * 512:(g + 1) * 512, :].rearrange("(k p) c -> p k c", p=128), [], w2k)

        def out_proj(l):
            WO, RSTD, SQ = 0, 4096, 6144
            wo = arb(WO, 4096).rearrange("p (k c) -> p k c", k=8)
            wok = ark(WO, 4096)
            dma("pool", wo, w_out[l, :, :].rearrange("(k p) c -> p k c", p=128), [], wok)
            mlp_load(l, 0)
            mlp_load(l, 1)
            for g in range(4):
                rms_stats(lambda k, tb: Yr[:, 2 * g + k, tb * TB:(tb + 1) * TB], 2, 256.0, RSTD, SQ,
                          lambda k, tb: [("Yr", 2 * g + k, tb)])
                for k in range(2):
                    ch = 2 * g + k
                    for tb in range(NTB):
                        stt(uT[:, ch, tb * TB:(tb + 1) * TB], Yr[:, ch, tb * TB:(tb + 1) * TB], pcol(l, "bnw", ch, ch + 1),
                            arf(RSTD + tb * TB, TB), ALU.mult, ALU.mult, [("Yr", ch, tb), "pp"] + ark(RSTD + tb * TB, TB), [("uT", ch, tb)])
            for m in range(KC):
                for tb in range(NTB):
                    b = next_acc()
                    for k in range(KC):
                        mm(ps[b][:, :], wo[:, k, m * 128:(m + 1) * 128], uT[:, k, tb * TB:(tb + 1) * TB], k == 0, k == KC - 1,
                           wok + [("uT", k, tb)], [("ps", b)])
                    stt(hT[:, m, tb * TB:(tb + 1) * TB], ps[b][:, :], G1(l, m), hT[:, m, tb * TB:(tb + 1) * TB], ALU.mult, ALU.add,
                        [("ps", b), ("der", l), ("hT", m, tb)], [("hT", m, tb)])

        def mlp(l):
            norm_mod(l, S2, B2)
            RS = 5120

            def up(g):
                sl = g % 2
                w1g = arb(W1O[sl], 2048).rearrange("p (k c) -> p k c", k=8)
                w1k = ark(W1O[sl], 2048)
                for jc in range(4):
                    yc = sl * 4 + jc
                    for tb in range(NTB):
                        b = next_acc()
                        for k in range(KC):
                            mm(ps[b][:, :], w1g[:, k, jc * 128:(jc + 1) * 128], uT[:, k, tb * TB:(tb + 1) * TB], k == 0, k == KC - 1,
                               w1k + [("uT", k, tb)], [("ps", b)])
                        ro = RS + ((jc * NTB + tb) % 4) * 256
                        rs = arb(ro, 256)
                        act(rs, ps[b][:, :], AF.Relu, [("ps", b)], ark(ro, 256))
                        tt(Yr[:, yc, tb * TB:(tb + 1) * TB], rs, rs, ALU.mult, ark(ro, 256), [("Yr", yc, tb)], eng="pool")

            def down(g):
                sl = g % 2
                w2g = arb(W2O[sl], 2048).rearrange("p (k c) -> p k c", k=4)
                w2k = ark(W2O[sl], 2048)
                for m in range(KC):
                    for tb in range(NTB):
                        b = next_acc()
                        for jc in range(4):
                            mm(ps[b][:, :], w2g[:, jc, m * 128:(m + 1) * 128], Yr[:, sl * 4 + jc, tb * TB:(tb + 1) * TB], jc == 0, jc == 3,
                               w2k + [("Yr", sl * 4 + jc, tb)], [("ps", b)])
                        stt(hT[:, m, tb * TB:(tb + 1) * TB], ps[b][:, :], G2(l, m), hT[:, m, tb * TB:(tb + 1) * TB], ALU.mult, ALU.add,
                            [("ps", b), ("der", l), ("hT", m, tb)], [("hT", m, tb)])

            up(0)
            for g in range(NG):
                if g + 1 < NG:
                    up(g + 1)
                down(g)
                if g + 2 < NG:
                    mlp_load(l, g + 2)

        def final_out(seg):
            RSTD, SQ, OUT = 0, 2048, 4096
            rms_stats(lambda k, tb: hT[:, k, tb * TB:(tb + 1) * TB], KC, float(DM), RSTD, SQ, lambda k, tb: [("hT", k, tb)])
            for k in range(KC):
                oo = OUT + (k % 2) * 2048
                o = arf(oo, 2048)
                for tb in range(NTB):
                    stt(o[:, tb * TB:(tb + 1) * TB], hT[:, k, tb * TB:(tb + 1) * TB], pcol(0, "fnw", k, k + 1), arf(RSTD + tb * TB, TB),
                        ALU.mult, ALU.mult, [("hT", k, tb), "pp"] + ark(RSTD + tb * TB, TB), ark(oo, 2048))
                dma("sp", yT[k * 128:(k + 1) * 128, :], o, ark(oo, 2048), [("yT", k, seg)])

        stopped = False
        for seg in range(nseg):
            if stopped:
                break
            for k in range(KC):
                dma("sp", hT[:, k, :], xT[k * 128:(k + 1) * 128, :], [], [("hT", k, tb) for tb in range(NTB)])
            for l in range(depth):
                norm_mod(l, S1, B1)
                if stop_after == (seg, l, "u"):
                    stopped = True
                    break
                print("ops before tail", P.total)
                tail_prepass(l)
                print("ops before s5 p1", P.total)
                mixer_s5(l, seg)
                print("ops before halo", P.total)
                halo_apply(l)
                mixer_pool(l, seg)
                mixer_sconv(l, seg)
                print("ops before ssd", P.total)
                mixer_ssd(l, seg)
                print("ops after ssd", P.total)
                if stop_after == (seg, l, "mix"):
                    stopped = True
                    break
                out_proj(l)
                if stop_after == (seg, l, "hmix"):
                    stopped = True
                    break
                mlp(l)
                if stop_after == (seg, l, "h"):
                    stopped = True
                    break
            if not stopped:
                final_out(seg)
        print("total ops recorded:", P.total)
        P.limit = None
        if dbg:
            if "uT" in dbg_out:
                cp(AR[:, 0:2048], uT[:, 0, :], all_keys("uT"), ark(0, 2048))
            for name in dbg_out:
                if name == "Yr":
                    for k in range(KC):
                        o = arf((k % 2) * 2048, 2048)
                        cp(o, Yr[:, k, :], [("Yr", k, tb) for tb in range(NTB)], ark((k % 2) * 2048, 2048))
                        dma("sp", dbg_out[name][k * 128:(k + 1) * 128, :], o, ark((k % 2) * 2048, 2048), [("dbg", name, k)])
                elif name == "uT":
                    for k in range(KC):
                        o = arf((k % 2) * 2048, 2048)
                        cp(o, uT[:, k, :], [("uT", k, tb) for tb in range(NTB)], ark((k % 2) * 2048, 2048))
                        dma("sp", dbg_out[name][k * 128:(k + 1) * 128, :], o, ark((k % 2) * 2048, 2048), [("dbg", name, k)])
                elif name == "hT":
                    for k in range(KC):
                        dma("sp", dbg_out[name][k * 128:(k + 1) * 128, :], hT[:, k, :], [("hT", k, tb) for tb in range(NTB)], [("dbg", name, k)])
                elif name == "mod":
                    dma("sp", dbg_out[name], modT[:, :, :].rearrange("p l c -> p (l c)"), [("mod", 0), ("mod", 1)], [("dbg", name)])
        P.wait_all("sp")

        with nc.Block() as block:
            def replay(e, name):
                for waits, fn, inc in P.q[name]:
                    for s, v in waits:
                        e.wait_ge(sems[s], v)
                    if fn is not None:
                        fn(e).then_inc(sems[inc[0]], inc[1])

            @block.tensor
            def _(e):
                replay(e, "pe")

            @block.scalar
            def _(e):
                replay(e, "act")

            @block.vector
            def _(e):
                replay(e, "dve")

            @block.gpsimd
            def _(e):
                replay(e, "pool")

            @block.sync
            def _(e):
                replay(e, "sp")
    return nc


def _fm(v):
    return np.ascontiguousarray(v.reshape(-1, 128).T)


def _pack_params(inp, b, sg):
    L = DEPTH
    pp = np.zeros((L, 128, NPCOL), np.float32)

    def put(l, name, arr):
        o, w = PCOL[name]
        arr = np.asarray(arr, np.float32).reshape(128, w)
        pp[l, :, o:o + w] = arr
    wins = (2, 4, 8, 16)
    for l in range(L):
        put(l, "nw1", _fm(inp["norm_mix_w"][l]))
        put(l, "nw2", _fm(inp["norm_mlp_w"][l]))
        put(l, "bnw", _fm(inp["branch_norm_w"][l]))
        put(l, "fnw", _fm(inp["final_norm_w"]))
        adab = np.zeros((128, 48), np.float32)
        adab[:, 0:12] = _fm(inp["ada_b"][l])[:, sg * 12:(sg + 1) * 12]
        put(l, "adab", adab)
        put(l, "pscale", _fm(inp["pool_scale"][l]))
        put(l, "scw", inp["sconv_w"][l].reshape(3, 2, 128).transpose(2, 1, 0))
        put(l, "cvw", inp["ssd_conv_w"][l].reshape(4, 6, 128).transpose(2, 1, 0))
        put(l, "cvb", _fm(inp["ssd_conv_b"][l]))
        put(l, "dtb", np.broadcast_to(inp["ssd_dt_bias"][l][None, :], (128, 4)))
        put(l, "alog", np.broadcast_to(inp["ssd_a_log"][l][None, :], (128, 4)))
        put(l, "dsk", np.broadcast_to(inp["ssd_d"][l][None, :], (128, 4)))
        def gp(a):
            return a.reshape(8, 2, 64).transpose(1, 2, 0).reshape(128, 8)
        put(l, "are", gp(inp["s5_a_re"][l]))
        put(l, "aim", gp(inp["s5_a_im"][l]))
        put(l, "lst", gp(np.broadcast_to(inp["s5_log_step"][l][:, None], (16, 64))))
        put(l, "s5d", _fm(inp["s5_d"][l]))
        put(l, "glub", _fm(inp["s5_glu_b"][l]))
        put(l, "cond", _fm(inp["c"][b]))
        invc = np.zeros((128, 2, 16), np.float32)
        for c in range(2):
            for half in range(2):
                win = wins[c * 2 + half]
                if sg == 0:
                    invc[half * 64:(half + 1) * 64, c, :] = 1.0 / np.minimum(np.arange(16) + 1, win)
                else:
                    invc[half * 64:(half + 1) * 64, c, :] = 1.0 / win
        put(l, "invc", invc)
        sel = np.zeros((128, 8), np.float32)
        if sg > 0:
            sel[:, sg - 1] = 1.0
        sel[:, 4 + sg] = 1.0
        put(l, "sel", sel)
    return pp


def _consts():
    cm = np.zeros((128, 4, 128), np.float32)
    i = np.arange(128)
    cm[:, 0, :] = (i[:, None] == i[None, :])
    cm[:, 1, :] = 1.0
    cm[:, 2, :] = (i[:, None] <= i[None, :])
    cm[:, 3, :] = np.where(i[None, :] < i[:, None], -30000.0, 0.0)
    return cm


def _host_inputs(inp, b, sg):
    f = lambda a: np.ascontiguousarray(np.asarray(a, np.float32))
    pw = np.zeros((DEPTH, 128, 2, 128), np.float32)
    for l in range(DEPTH):
        for g in range(4):
            c, half = g // 2, g % 2
            pw[l, half * 64:(half + 1) * 64, c, half * 64:(half + 1) * 64] = inp["pool_w"][l, g]
    gw = np.ascontiguousarray(np.asarray(inp["s5_glu_w"], np.float32).reshape(DEPTH, 2, 128, 256).transpose(0, 2, 1, 3))
    s5b = np.zeros((DEPTH, 128, 256), np.float32)
    s5c = np.zeros((DEPTH, 128, 8, 2, 32), np.float32)
    for l in range(DEPTH):
        s5b[l, :, 0:128] = np.asarray(inp["s5_b_re"][l]).reshape(8, 2, 64, 16).transpose(1, 2, 0, 3).reshape(128, 128)
        s5b[l, :, 128:256] = np.asarray(inp["s5_b_im"][l]).reshape(8, 2, 64, 16).transpose(1, 2, 0, 3).reshape(128, 128)
        for ri, nm in enumerate(("s5_c_re", "s5_c_im")):
            cc = np.asarray(inp[nm][l]).reshape(8, 2, 16, 64)
            for r in range(8):
                for gi in range(2):
                    s5c[l, gi * 64:(gi + 1) * 64, r, ri, gi * 16:gi * 16 + 16] = cc[r, gi].T
    return {
        "s5b": s5b, "s5c": s5c.reshape(DEPTH, 128, 512),
        "xT": f(np.asarray(inp["x"][b][sg * T:(sg + 1) * T]).T),
        "pp": _pack_params(inp, b, sg),
        "cmat": _consts(),
        "ada_w": f(np.asarray(inp["ada_w"])[:, :, sg * 1536:(sg + 1) * 1536]), "w_in": f(inp["w_in"]), "w_out": f(inp["w_out"]),
        "mlp_w1": f(inp["mlp_w1"]), "mlp_w2": f(inp["mlp_w2"]),
        "poolw": pw, "gluw": gw,
    }


_NC_CACHE = {}


def kernel(**inputs):
    inp = {k: np.asarray(v) for k, v in inputs.items()}
    if "full" not in _NC_CACHE:
        _NC_CACHE["full"] = build()
    nc = _NC_CACHE["full"]
    in_maps = [_host_inputs(inp, r // 4, r % 4) for r in range(8)]
    res = run_bass_kernel_spmd(nc, in_maps, core_ids=list(range(8)))
    out = np.empty((2, SEQ, DM), np.float32)
    for r in range(8):
        out[r // 4, (r % 4) * T:(r % 4 + 1) * T, :] = res.results[r]["yT"].T
    return out
```

```python
import numpy as np
import concourse.bass as bass
import concourse.mybir as mybir
from concourse.bass_utils import run_bass_kernel_spmd

F32, BF16 = mybir.dt.float32, mybir.dt.bfloat16
AF = mybir.ActivationFunctionType
ALU = mybir.AluOpType

T = 2048
TB = 512
NTB = 4
NCH = 16
DM = 1024
KC = 8
SEQ = 8192
DEPTH = 2
EPS = 1e-6
ENGS = ("pe", "act", "dve", "pool", "sp")
NDMA = 12

PCOL = {}
_off = 0
for _n, _w in [("nw1", 8), ("nw2", 8), ("bnw", 8), ("fnw", 8), ("adab", 48), ("pscale", 2), ("scw", 6),
               ("cvw", 24), ("cvb", 6), ("dtb", 4), ("alog", 4), ("dsk", 4), ("are", 8), ("aim", 8), ("lst", 8),
               ("s5d", 2), ("glub", 2), ("cond", 8),
               ("invc", 32), ("sel", 8)]:
    PCOL[_n] = (_off, _w)
    _off += _w
NPCOL = _off


class Prog:
    def __init__(self):
        self.q = {e: [] for e in ENGS}
        self.cnt = {e: 0 for e in ENGS}
        self.seen = {e: {} for e in ENGS}
        self.lastw = {}
        self.rd = {}
        self.dma_i = 0
        self.fam = {}
        import os as _os
        self.nosame = set(x for x in _os.environ.get("KNOSAME", "").split(",") if x)
        self.total = 0
        import os
        self.limit = int(os.environ.get("KLIMIT", "0")) or None

    def _skip(self):
        self.total += 1
        return self.limit is not None and self.total > self.limit

    def _deps(self, eng, reads, writes):
        need = {}

        def add(d):
            if d is None:
                return
            s, v = d
            if need.get(s, 0) < v:
                need[s] = v
        for k0 in reads:
            for k in self._rel(k0):
                add(self.lastw.get(k))
        for k0 in writes:
            for k in self._rel(k0):
                add(self.lastw.get(k))
                for s, v in self.rd.get(k, {}).items():
                    add((s, v))
        out = []
        for s, v in need.items():
            if s == eng and (eng == "pe" or eng in self.nosame):
                continue
            if self.seen[eng].get(s, 0) >= v:
                continue
            self.seen[eng][s] = v
            out.append((s, v))
        return out

    def _rel(self, k):
        if isinstance(k, tuple) and k[0] == "ps":
            fam = self.fam.setdefault(k[1], set())
            fam.add(k)
            if len(k) == 2:
                return list(fam)
            return [k, ("ps", k[1])]
        return [k]

    def _commit(self, reads, writes, tok):
        s, v = tok
        for k in reads:
            d = self.rd.setdefault(k, {})
            if d.get(s, 0) < v:
                d[s] = v
        for k in writes:
            self.lastw[k] = tok
            self.rd[k] = {}

    @staticmethod
    def _norm(reads, writes):
        r2, w2 = [], list(writes)
        for k in reads:
            if isinstance(k, tuple) and k[0] == "ps":
                w2.append(k)
            else:
                r2.append(k)
        w2 = [("ps", k[1]) if (isinstance(k, tuple) and k[0] == "ps") else k for k in w2]
        return r2, w2

    def op(self, eng, fn, reads=(), writes=()):
        if self._skip():
            return
        reads, writes = self._norm(reads, writes)
        waits = self._deps(eng, reads, writes)
        self.cnt[eng] += 1
        self.q[eng].append((waits, fn, (eng, 1)))
        self._commit(reads, writes, (eng, self.cnt[eng]))

    def dma(self, eng, fn, reads=(), writes=()):
        if self._skip():
            return None
        reads = list(reads)
        writes = list(writes)
        i = self.dma_i
        self.dma_i += 1
        sem = "dma%d" % (i % NDMA)
        val = 16 * (i // NDMA + 1)
        waits = self._deps(eng, reads, writes)
        if i >= NDMA:
            prev = 16 * (i // NDMA)
            if self.seen[eng].get(sem, 0) < prev:
                self.seen[eng][sem] = prev
                waits.append((sem, prev))
        self.q[eng].append((waits, fn, (sem, 16)))
        self._commit(reads, writes, (sem, val))
        return (sem, val)

    def cc(self, fn, sem, reads=(), writes=()):
        if self._skip():
            return
        reads, writes = self._norm(reads, writes)
        waits = self._deps("pool", reads, writes)
        self.q["pool"].append((waits, fn, (sem, 1)))
        self._commit(reads, writes, (sem, 1))

    def wait_all(self, eng):
        waits = []
        for e in ENGS:
            if e != eng and self.cnt[e] > self.seen[eng].get(e, 0):
                self.seen[eng][e] = self.cnt[e]
                waits.append((e, self.cnt[e]))
        for j in range(min(NDMA, self.dma_i)):
            sem = "dma%d" % j
            n = (self.dma_i - 1 - j) // NDMA + 1
            if self.seen[eng].get(sem, 0) < 16 * n:
                self.seen[eng][sem] = 16 * n
                waits.append((sem, 16 * n))
        self.q[eng].append((waits, None, None))


class Buf:
    def __init__(self, name, ap):
        self.name = name
        self.ap = ap

    def k(self, *idx):
        return (self.name,) + tuple(idx)


def build(depth=DEPTH, dbg=None, stop_after=None):
    nseg = 1
    nc = bass.Bass("TRN2", target_bir_lowering=False)
    P = Prog()
    dram = {}

    def din(name, shape):
        dram[name] = nc.dram_tensor(name, list(shape), F32, kind="ExternalInput").ap()
        return dram[name]

    xT = din("xT", [DM, T])
    pp = din("pp", [DEPTH, 128, NPCOL])
    cmat = din("cmat", [128, 4, 128])
    ada_w = din("ada_w", [DEPTH, DM, 1536])
    w_in = din("w_in", [DEPTH, DM, 2308])
    w_out = din("w_out", [DEPTH, DM, DM])
    w1 = din("mlp_w1", [DEPTH, DM, 4 * DM])
    w2 = din("mlp_w2", [DEPTH, 4 * DM, DM])
    poolw = din("poolw", [DEPTH, 128, 2, 128])
    gluw = din("gluw", [DEPTH, 128, 2, 256])
    s5b = din("s5b", [DEPTH, 128, 256])
    s5c = din("s5c", [DEPTH, 128, 512])
    yT = nc.dram_tensor("yT", [DM, T], F32, kind="ExternalOutput").ap()
    GRP = [[0, 1, 2, 3], [4, 5, 6, 7]]
    ccd = {}
    for l_ in range(DEPTH):
        for nm_, w_ in (("h", 64), ("x", 16), ("s", 272), ("m", 32)):
            ccd[(nm_, l_, "i")] = nc.dram_tensor("cc%s%di" % (nm_, l_), [128, w_], F32, kind="Internal").ap()
            ccd[(nm_, l_, "o")] = nc.dram_tensor("cc%s%do" % (nm_, l_), [4 * 128, w_], F32, kind="Internal").ap()
    dbg_out = {}
    if dbg:
        for name, shape in dbg.items():
            dbg_out[name] = nc.dram_tensor("dbg_" + name, list(shape), F32, kind="ExternalOutput").ap()

    import contextlib
    es = contextlib.ExitStack()
    with es:
        def sb(name, shape, dt):
            return es.enter_context(nc.sbuf_tensor(name, list(shape), dt))

        hT = sb("hT", [128, KC, T], F32)
        uT = sb("uT", [128, KC, T], BF16)
        Yr = sb("Yr", [128, KC, T], BF16)
        AR = sb("AR", [128, 15360], F32)
        cm = sb("cm", [128, 4, 128], BF16)
        ppS = sb("ppS", [128, DEPTH, NPCOL], F32)
        modT = sb("modT", [128, DEPTH, 48], F32)
        der = sb("der", [128, DEPTH, 64], F32)
        condb = sb("condb", [128, 8], BF16)
        epsT = sb("epsT", [128, 2], F32)
        poolwS = sb("poolwS", [128, DEPTH, 2, 128], BF16)
        gluwS = sb("gluwS", [128, DEPTH, 2, 256], BF16)
        cPool = sb("cPool", [128, DEPTH, 2, 16], F32)
        cSc = sb("cSc", [128, DEPTH, 2, 2], F32)
        cCv = sb("cCv", [128, DEPTH, 6, 3], F32)
        cS = sb("cS", [128, DEPTH, 256], F32)
        cX = sb("cX", [128, DEPTH, 8, 2], F32)
        ssdp = sb("ssdp", [128, DEPTH, 16], F32)
        s5pw = sb("s5pw", [128, DEPTH, 8, 3, 16], F32)
        s5lv = sb("s5lv", [128, DEPTH, 8, 3, 8], F32)
        bbS = sb("bbS", [128, DEPTH, 2, 8, 16], F32)
        tiny = sb("tiny", [128, 896], F32)
        tinyb = sb("tinyb", [128, 128], BF16)

        ps = [es.enter_context(nc.psum_tensor("ps%d" % i, [128, 512], F32)) for i in range(8)]
        sems = {}
        for e in ENGS:
            sems[e] = es.enter_context(nc.semaphore("s_" + e))
        for j in range(NDMA):
            sems["dma%d" % j] = es.enter_context(nc.semaphore("s_dma%d" % j))
        for l_ in range(DEPTH):
            for nm_ in ("h", "x", "s", "m"):
                sems["cc%s%d" % (nm_, l_)] = es.enter_context(nc.semaphore("s_cc%s%d" % (nm_, l_)))

        def allgather(nm_, l_, reads, writes):
            i_, o_ = ccd[(nm_, l_, "i")], ccd[(nm_, l_, "o")]
            P.cc(lambda e: e.collective_compute("AllGather", ALU.bypass, replica_groups=GRP, ins=[i_], outs=[o_]),
                 "cc%s%d" % (nm_, l_), reads, writes)

        ARb = AR[:, :].bitcast(BF16)

        def arf(off, n):
            return AR[:, off:off + n]

        def arb(off, n):
            return ARb[:, 2 * off:2 * (off + n)]

        def ark(off, n):
            return [("ar", b) for b in range(off // 256, (off + n + 255) // 256)]

        ident = cm[:, 0, :]
        ones = cm[:, 1, :]
        tri = cm[:, 2, :]
        negm = cm[:, 3, :]

        def pcol(l, name, a=0, b=None):
            o, w = PCOL[name]
            if b is None:
                b = w
            return ppS[:, l, o + a:o + b]

        def mm(out, lhsT, rhs, start, stop, reads, writes):
            P.op("pe", lambda e: e.matmul(out, lhsT=lhsT, rhs=rhs, start=start, stop=stop), reads, writes)

        def act(out, in_, func, reads, writes, bias=None, scale=None):
            kw = {}
            if bias is not None:
                kw["bias"] = bias
            if scale is not None:
                kw["scale"] = scale
            P.op("act", lambda e: e.activation(out=out, in_=in_, func=func, **kw), reads, writes)

        def tt(out, in0, in1, op, reads, writes, eng="dve"):
            P.op(eng, lambda e: e.tensor_tensor(out=out, in0=in0, in1=in1, op=op), reads, writes)

        def ts(out, in0, s1, s2, op0, op1, reads, writes, eng="dve"):
            if s2 is None:
                P.op(eng, lambda e: e.tensor_scalar(out=out, in0=in0, scalar1=s1, scalar2=None, op0=op0), reads, writes)
            else:
                P.op(eng, lambda e: e.tensor_scalar(out=out, in0=in0, scalar1=s1, scalar2=s2, op0=op0, op1=op1), reads, writes)

        def stt(out, in0, scalar, in1, op0, op1, reads, writes, eng="dve"):
            P.op(eng, lambda e: e.scalar_tensor_tensor(out=out, in0=in0, scalar=scalar, in1=in1, op0=op0, op1=op1), reads, writes)

        def cp(out, in_, reads, writes, eng="dve"):
            P.op(eng, lambda e: e.tensor_copy(out=out, in_=in_), reads, writes)

        def mset(ap, val, writes, eng="dve"):
            P.op(eng, lambda e: e.memset(ap, val), [], writes)

        def dma(eng, out, in_, reads, writes):
            return P.dma(eng, lambda e: e.dma_start(out=out, in_=in_), reads, writes)

        dma("pool", cm[:, :, :], cmat, [], ["cm"])
        dma("sp", ppS[:, :, :], pp.rearrange("l p c -> p l c"), [], ["pp"])
        dma("pool", poolwS[:, :, :, :], poolw.rearrange("l p c d -> p l c d"), [], ["poolw"])
        dma("pool", gluwS[:, :, :, :], gluw.rearrange("l p c d -> p l c d"), [], ["gluw"])
        mset(epsT[:, 0:1], EPS, ["eps"])
        mset(epsT[:, 1:2], 1.0, ["eps"])
        for t_, kk in ((cPool, "cPool"), (cSc, "cSc"), (cCv, "cCv"), (cS, "cS"), (cX, "cX")):
            mset(t_[:], 0.0, [kk])
        act(condb[:, :], pcol(0, "cond"), AF.Silu, ["pp"], ["condb"])

        WADA = 0
        bi = 0
        for l in range(DEPTH):
            for blk in range(3):
                slot = bi % 2
                bi += 1
                wsl = arb(WADA + slot * 2048, 2048).rearrange("p (k c) -> p k c", k=8)
                wk = ark(WADA + slot * 2048, 2048)
                dma("pool", wsl, ada_w[l, :, blk * 512:(blk + 1) * 512].rearrange("(k p) c -> p k c", p=128), [], wk)
                for jj in range(4):
                    j = l * 12 + blk * 4 + jj
                    for k in range(KC):
                        mm(ps[6][:, j:j + 1], wsl[:, k, jj * 128:(jj + 1) * 128], condb[:, k:k + 1], k == 0, k == KC - 1,
                           wk + ["condb"], [("ps", 6)])
        mpk = tiny[:, 0:24].rearrange("p (l c) -> p l c", l=2)
        tyk0 = [("tiny", "all")]
        for l in range(DEPTH):
            tt(mpk[:, l, :], ps[6][:, l * 12:(l + 1) * 12], pcol(l, "adab", 0, 12), ALU.add, [("ps", 6), "pp"], tyk0)
        dma("sp", ccd[("m", 0, "i")][:, 0:24], tiny[:, 0:24], tyk0, [("cci", "m", 0)])
        allgather("m", 0, [("cci", "m", 0)], [("cco", "m", 0)])
        mrb = tiny[:, 32:160].rearrange("p (r f) -> p r f", r=4)
        dma("sp", mrb, ccd[("m", 0, "o")].rearrange("(r p) f -> p r f", p=128), [("cco", "m", 0)], tyk0)
        for l in range(DEPTH):
            for sg_ in range(4):
                cp(modT[:, l, sg_ * 12:(sg_ + 1) * 12], mrb[:, sg_, l * 12:(l + 1) * 12], tyk0, [("mod", l)])
        for l in range(depth):
            stt(der[:, l, 0:8], modT[:, l, 8:16], 1.0, pcol(l, "nw1"), ALU.add, ALU.mult, [("mod", l), "pp"], [("der", l)])
            stt(der[:, l, 24:32], modT[:, l, 32:40], 1.0, pcol(l, "nw2"), ALU.add, ALU.mult, [("mod", l), "pp"], [("der", l)])
            cp(der[:, l, 8:16], modT[:, l, 0:8], [("mod", l)], [("der", l)])
            cp(der[:, l, 16:24], modT[:, l, 16:24], [("mod", l)], [("der", l)])
            cp(der[:, l, 32:40], modT[:, l, 24:32], [("mod", l)], [("der", l)])
            cp(der[:, l, 40:48], modT[:, l, 40:48], [("mod", l)], [("der", l)])

        def S1(l, k): return der[:, l, 0 + k:1 + k]
        def B1(l, k): return der[:, l, 8 + k:9 + k]
        def G1(l, k): return der[:, l, 16 + k:17 + k]
        def S2(l, k): return der[:, l, 24 + k:25 + k]
        def B2(l, k): return der[:, l, 32 + k:33 + k]
        def G2(l, k): return der[:, l, 40 + k:41 + k]

        for l in range(depth):
            act(ssdp[:, l, 0:4], pcol(l, "alog"), AF.Exp, ["pp"], [("ssdp", l)])
            ts(ssdp[:, l, 0:4], ssdp[:, l, 0:4], -1.0, None, ALU.mult, None, [("ssdp", l)], [("ssdp", l)])
            cp(ssdp[:, l, 4:8], pcol(l, "dtb"), ["pp"], [("ssdp", l)])
            cp(ssdp[:, l, 8:12], pcol(l, "dsk"), ["pp"], [("ssdp", l)])

        def tk(n): return [("tiny", n)]
        for l in range(depth):
            tyk = [("tiny", "all")]
            stp = tiny[:, 0:8]
            act(stp, pcol(l, "lst"), AF.Exp, ["pp"], tyk)
            mag = tiny[:, 8:16]
            tt(mag, pcol(l, "are"), stp, ALU.mult, ["pp"] + tyk, tyk)
            act(mag, mag, AF.Exp, tyk, tyk)
            th = tiny[:, 16:24]
            tt(th, pcol(l, "aim"), stp, ALU.mult, ["pp"] + tyk, tyk)
            sa = tiny[:, 24:32]
            ca = tiny[:, 32:40]
            act(sa, th, AF.Sin, tyk, tyk, scale=1.0 / 16.0)
            ts(ca, th, 1.0 / 16.0, float(np.pi / 2), ALU.mult, ALU.add, tyk, tyk)
            act(ca, ca, AF.Sin, tyk, tyk)
            for _ in range(4):
                t2a, t2b = tiny[:, 272:280], tiny[:, 280:288]
                tt(t2a, ca, ca, ALU.mult, tyk, tyk)
                tt(t2b, sa, sa, ALU.mult, tyk, tyk)
                tt(sa, sa, ca, ALU.mult, tyk, tyk)
                ts(sa, sa, 2.0, None, ALU.mult, None, tyk, tyk)
                tt(ca, t2a, t2b, ALU.subtract, tyk, tyk)
            lr = tiny[:, 40:48]
            li = tiny[:, 48:56]
            tt(lr, mag, ca, ALU.mult, tyk, tyk)
            tt(li, mag, sa, ALU.mult, tyk, tyk)
            den = tiny[:, 56:64]
            t0 = tiny[:, 64:72]
            tt(den, pcol(l, "are"), pcol(l, "are"), ALU.mult, ["pp"] + tyk, tyk)
            tt(t0, pcol(l, "aim"), pcol(l, "aim"), ALU.mult, ["pp"] + tyk, tyk)
            tt(den, den, t0, ALU.add, tyk, tyk)
            P.op("dve", lambda e, den=den: e.reciprocal(out=den, in_=den), tyk, tyk)
            nr = tiny[:, 72:80]
            ts(nr, lr, -1.0, None, ALU.add, None, tyk, tyk)
            fr = tiny[:, 80:88]
            fi = tiny[:, 88:96]
            t1 = tiny[:, 96:104]
            tt(fr, nr, pcol(l, "are"), ALU.mult, ["pp"] + tyk, tyk)
            tt(t1, li, pcol(l, "aim"), ALU.mult, ["pp"] + tyk, tyk)
            tt(fr, fr, t1, ALU.add, tyk, tyk)
            tt(fr, fr, den, ALU.mult, tyk, tyk)
            tt(fi, li, pcol(l, "are"), ALU.mult, ["pp"] + tyk, tyk)
            tt(t1, nr, pcol(l, "aim"), ALU.mult, ["pp"] + tyk, tyk)
            tt(fi, fi, t1, ALU.subtract, tyk, tyk)
            tt(fi, fi, den, ALU.mult, tyk, tyk)
            dma("sp", tiny[:, 512:768], s5b[l], [], tyk)
            bre = tiny[:, 512:640].rearrange("p (r h) -> p r h", r=8)
            bim = tiny[:, 640:768].rearrange("p (r h) -> p r h", r=8)
            frb = fr.unsqueeze(2).to_broadcast([128, 8, 16])
            fib = fi.unsqueeze(2).to_broadcast([128, 8, 16])
            tmpb = tiny[:, 128:256].rearrange("p (r h) -> p r h", r=8)
            tt(bbS[:, l, 0, :, :], bre, frb, ALU.mult, ["pp"] + tyk, [("bb", l)])
            tt(tmpb, bim, fib, ALU.mult, ["pp"] + tyk, tyk)
            tt(bbS[:, l, 0, :, :], bbS[:, l, 0, :, :], tmpb, ALU.subtract, tyk + [("bb", l)], [("bb", l)])
            tt(bbS[:, l, 1, :, :], bim, frb, ALU.mult, ["pp"] + tyk, [("bb", l)])
            tt(tmpb, bre, fib, ALU.mult, ["pp"] + tyk, tyk)
            tt(bbS[:, l, 1, :, :], bbS[:, l, 1, :, :], tmpb, ALU.add, tyk + [("bb", l)], [("bb", l)])
            ts(bbS[:, l, 1, :, :], bbS[:, l, 1, :, :], -1.0, None, ALU.mult, None, [("bb", l)], [("bb", l)])
            ts(li, li, -1.0, None, ALU.mult, None, tyk, tyk)
            pk = [("s5pw", l)]
            cp(s5pw[:, l, :, 0, 0], lr, tyk, pk)
            cp(s5pw[:, l, :, 1, 0], li, tyk, pk)
            for j in range(1, 16):
                pr_, pi_ = s5pw[:, l, :, 0, j - 1], s5pw[:, l, :, 1, j - 1]
                nr_, ni_ = s5pw[:, l, :, 0, j], s5pw[:, l, :, 1, j]
                ta, tb_ = tiny[:, 256:264], tiny[:, 264:272]
                tt(ta, pr_, lr, ALU.mult, tyk + pk, tyk)
                tt(tb_, pi_, li, ALU.mult, tyk + pk, tyk)
                tt(nr_, ta, tb_, ALU.subtract, tyk, pk)
                tt(ta, pr_, li, ALU.mult, tyk + pk, tyk)
                tt(tb_, pi_, lr, ALU.mult, tyk + pk, tyk)
                tt(ni_, ta, tb_, ALU.add, tyk, pk)
            ts(s5pw[:, l, :, 2, :], s5pw[:, l, :, 1, :], -1.0, None, ALU.mult, None, pk, pk)
            lk = [("s5lv", l)]
            cp(s5lv[:, l, :, 0, 0], s5pw[:, l, :, 0, 15], pk, lk)
            cp(s5lv[:, l, :, 1, 0], s5pw[:, l, :, 1, 15], pk, lk)
            for k in range(1, 8):
                pr_, pi_ = s5lv[:, l, :, 0, k - 1], s5lv[:, l, :, 1, k - 1]
                ta, tb_ = tiny[:, 256:264], tiny[:, 264:272]
                tt(ta, pr_, pr_, ALU.mult, lk + tyk, tyk)
                tt(tb_, pi_, pi_, ALU.mult, lk + tyk, tyk)
                tt(s5lv[:, l, :, 0, k], ta, tb_, ALU.subtract, tyk, lk)
                tt(ta, pr_, pi_, ALU.mult, lk + tyk, tyk)
                ts(s5lv[:, l, :, 1, k], ta, 2.0, None, ALU.mult, None, tyk, lk)
            ts(s5lv[:, l, :, 2, :], s5lv[:, l, :, 1, :], -1.0, None, ALU.mult, None, lk, lk)

        acc_i = [0]

        def next_acc():
            b = acc_i[0] % 4
            acc_i[0] += 1
            return b

        def rms_stats(src_fn, nchunks, denom, rstd_off, sq_off, src_keys_fn):
            for tb in range(NTB):
                bank = 4 + (tb % 2)
                for k in range(nchunks):
                    so = sq_off + ((tb * nchunks + k) % 4) * 256
                    sq = arb(so, 256)
                    src = src_fn(k, tb)
                    tt(sq, src, src, ALU.mult, src_keys_fn(k, tb), ark(so, 256), eng="pool")
                    mm(ps[bank][:, :], ones, sq, k == 0, k == nchunks - 1, ark(so, 256) + ["cm"], [("ps", bank)])
                r = arf(rstd_off + tb * TB, TB)
                rk = ark(rstd_off + tb * TB, TB)
                act(r, ps[bank][:, :], AF.Ln, [("ps", bank), "eps"], rk, bias=epsT[:, 0:1], scale=1.0 / denom)
                act(r, r, AF.Exp, rk, rk, scale=-0.5)

        def norm_mod(l, s_fn, b_fn):
            RSTD, SQ, TMP = 0, 2048, 3072
            rms_stats(lambda k, tb: hT[:, k, tb * TB:(tb + 1) * TB], KC, float(DM), RSTD, SQ,
                      lambda k, tb: [("hT", k, tb)])
            for k in range(KC):
                for tb in range(NTB):
                    to = TMP + ((k * NTB + tb) % 4) * TB
                    tmp = arf(to, TB)
                    stt(tmp, hT[:, k, tb * TB:(tb + 1) * TB], s_fn(l, k), arf(RSTD + tb * TB, TB), ALU.mult, ALU.mult,
                        [("hT", k, tb), ("der", l)] + ark(RSTD + tb * TB, TB), ark(to, TB))
                    act(uT[:, k, tb * TB:(tb + 1) * TB], tmp, AF.Identity, ark(to, TB) + [("der", l)], [("uT", k, tb)],
                        bias=b_fn(l, k))

        WIN = 14336
        win_i = [0]

        def load_win(l, c0, ncol):
            assert ncol <= 128
            slot = win_i[0] % 2
            win_i[0] += 1
            off = WIN + slot * 512
            w = arb(off, 512).rearrange("p (k c) -> p k c", k=8)
            dma("pool", w[:, :, 0:ncol], w_in[l, :, c0:c0 + ncol].rearrange("(k p) c -> p k c", p=128), [], ark(off, 512))
            return w, ark(off, 512)

        def proj_chunk(w, wk, cofs, ncol, evac):
            for tb in range(NTB):
                b = next_acc()
                for k in range(KC):
                    mm(ps[b][0:ncol, :], w[:, k, cofs:cofs + ncol], uT[:, k, tb * TB:(tb + 1) * TB], k == 0, k == KC - 1,
                       wk + [("uT", k, tb)], [("ps", b)])
                evac(tb, ps[b][0:ncol, :], ("ps", b))

        def dump(name, ap, keys):
            if dbg and name in dbg_out:
                dma("sp", dbg_out[name], ap, keys, [("dbg", name)])

        def all_keys_h():
            return [("hT", k, tb) for k in range(KC) for tb in range(NTB)]

        def all_keys(nm):
            return [(nm, k, tb) for k in range(KC) for tb in range(NTB)]

        def tail_prepass(l):
            PK = arf(0, 64)
            pkk = ark(0, 64)
            gct = tiny[:, 0:32].rearrange("p (c t) -> p c t", c=2)
            tyk = [("tiny", "all")]
            tail = slice(T - 16, T)

            def tproj(w, wk, cofs, evac):
                b = next_acc()
                for k in range(KC):
                    mm(ps[b][:, 0:16], w[:, k, cofs:cofs + 128], uT[:, k, tail], k == 0, k == KC - 1, wk + [("uT", k, 3)], [("ps", b)])
                evac(ps[b][:, 0:16], ("ps", b))
            for c in range(2):
                w, wk = load_win(l, c * 128, 128)
                tproj(w, wk, 0, lambda p_, pk, c=c: act(PK[:, c * 16:(c + 1) * 16], p_, AF.Copy, [pk], pkk))
            for c in range(2):
                w, wk = load_win(l, 512 + c * 128, 128)
                tproj(w, wk, 0, lambda p_, pk, c=c: act(gct[:, c, :], p_, AF.Copy, [pk], tyk))
            for c in range(2):
                w, wk = load_win(l, 768 + c * 128, 128)
                tproj(w, wk, 0, lambda p_, pk, c=c: tt(PK[:, 32 + 2 * c:34 + 2 * c], p_[:, 14:16], gct[:, c, 14:16], ALU.mult,
                                                       [pk] + tyk, pkk))
            for j in range(6):
                w, wk = load_win(l, 1280 + j * 128, 128)
                tproj(w, wk, 0, lambda p_, pk, j=j: act(PK[:, 36 + 3 * j:39 + 3 * j], p_[:, 13:16], AF.Copy, [pk], pkk))
            dma("sp", ccd[("h", l, "i")][:, 0:54], PK[:, 0:54], pkk, [("cci", "h", l)])
            allgather("h", l, [("cci", "h", l)], [("cco", "h", l)])

        def halo_apply(l):
            RB = arf(256, 256).rearrange("p (r f) -> p r f", r=4)
            rbk = ark(256, 256)
            tyk = [("tiny", "all")]
            dma("sp", RB, ccd[("h", l, "o")].rearrange("(r p) f -> p r f", p=128), [("cco", "h", l)], rbk)
            hal = tiny[:, 64:128]
            sel = pcol(l, "sel")
            ts(hal, RB[:, 0, :], sel[:, 0:1], None, ALU.mult, None, rbk + ["pp"], tyk)
            for j in range(1, 4):
                stt(hal, RB[:, j, :], sel[:, j:j + 1], hal, ALU.mult, ALU.add, rbk + ["pp"] + tyk, tyk)
            cp(cPool[:, l, :, :], hal[:, 0:32].rearrange("p (c t) -> p c t", c=2), tyk, [("cPool", l, 0), ("cPool", l, 1)])
            cp(cSc[:, l, :, :], hal[:, 32:36].rearrange("p (c t) -> p c t", c=2), tyk, [("cSc", l, 0), ("cSc", l, 1)])
            cp(cCv[:, l, :, :], hal[:, 36:54].rearrange("p (c t) -> p c t", c=6), tyk, [("cCv", l, j) for j in range(6)])

        def s5_combine(l):
            tyk = [("tiny", "all")]
            RB = tiny[:, 816:880].rearrange("p (r f) -> p r f", r=4)
            dma("sp", RB, ccd[("x", l, "o")].rearrange("(r p) f -> p r f", p=128), [("cco", "x", l)], tyk)
            sel = pcol(l, "sel")
            Lr, Li = s5lv[:, l, :, 0, 7], s5lv[:, l, :, 1, 7]
            lk = [("s5lv", l)]
            ar, ai, pr, pi, t1, t2 = (tiny[:, a:a + 8] for a in (768, 776, 784, 792, 880, 888))
            for t_ in (ar, ai, pr, pi):
                mset(t_, 0.0, tyk)
            for j in range(4):
                stt(ar, pr, sel[:, 4 + j:5 + j], ar, ALU.mult, ALU.add, tyk + ["pp"], tyk)
                stt(ai, pi, sel[:, 4 + j:5 + j], ai, ALU.mult, ALU.add, tyk + ["pp"], tyk)
                if j < 3:
                    F = RB[:, j, :].rearrange("p (r a) -> p r a", a=2)
                    tt(t1, pr, Lr, ALU.mult, tyk + lk, tyk)
                    tt(t2, pi, Li, ALU.mult, tyk + lk, tyk)
                    tt(t1, t1, t2, ALU.subtract, tyk, tyk)
                    tt(t2, pr, Li, ALU.mult, tyk + lk, tyk)
                    tt(pr, t1, F[:, :, 0], ALU.add, tyk, tyk)
                    tt(t1, pi, Lr, ALU.mult, tyk + lk, tyk)
                    tt(t1, t1, t2, ALU.add, tyk, tyk)
                    tt(pi, t1, F[:, :, 1], ALU.add, tyk, tyk)
            cp(cX[:, l, :, 0], ar, tyk, [("cX", l, r) for r in range(8)])
            cp(cX[:, l, :, 1], ai, tyk, [("cX", l, r) for r in range(8)])

        def mixer_pool(l, seg):
            V, SA, SB, PB = 0, 2304, 4608, 6912
            for c in range(2):
                w, wk = load_win(l, c * 128, 128)
                v = arf(V, 2064)
                vk = ark(V, 2064)
                cp(v[:, 0:16], cPool[:, l, c, :], [("cPool", l, c)], vk)
                proj_chunk(w, wk, 0, 128,
                           lambda tb, p_, pk: act(v[:, 16 + tb * TB:16 + (tb + 1) * TB], p_, AF.Copy, [pk], vk))
                cp(cPool[:, l, c, :], v[:, 2048:2064], vk, [("cPool", l, c)])
                sa, sbb = arf(SA, 2064), arf(SB, 2064)
                sak, sbk = ark(SA, 2064), ark(SB, 2064)
                tt(sa[:, 1:2064], v[:, 1:2064], v[:, 0:2063], ALU.add, vk, sak)
                tt(sbb[:, 3:2064], sa[:, 3:2064], sa[:, 1:2062], ALU.add, sak, sbk)
                if c == 0:
                    lo_src, hi_src, lo_w, hi_w = sa, sbb, 2, 4
                else:
                    tt(sa[:, 7:2064], sbb[:, 7:2064], sbb[:, 3:2060], ALU.add, sbk, sak)
                    tt(sbb[:, 15:2064], sa[:, 15:2064], sa[:, 7:2056], ALU.add, sak, sbk)
                    lo_src, hi_src, lo_w, hi_w = sa, sbb, 8, 16
                pb = arb(PB, 2048).rearrange("p (c t) -> p c t", c=2)
                pbk = ark(PB, 2048)
                stt(pb[0:64, c, :], lo_src[0:64, 16:2064], 1.0 / lo_w, v[0:64, 16:2064], ALU.mult, ALU.subtract, sak + sbk + vk, pbk)
                stt(pb[64:128, c, :], hi_src[64:128, 16:2064], 1.0 / hi_w, v[64:128, 16:2064], ALU.mult, ALU.subtract, sak + sbk + vk, pbk)
                if True:
                    ic = pcol(l, "invc").rearrange("p (c t) -> p c t", c=2)
                    tq = tiny[:, 512:528]
                    for (r0, r1, src) in ((0, 64, lo_src), (64, 128, hi_src)):
                        tt(tq[r0:r1, :], src[r0:r1, 16:32], ic[r0:r1, c, :], ALU.mult, sak + sbk + ["pp"], [("tiny", "all")])
                        tt(pb[r0:r1, c, 0:16], tq[r0:r1, :], v[r0:r1, 16:32], ALU.subtract, [("tiny", "all")] + vk, pbk)
            pb = arb(PB, 2048).rearrange("p (c t) -> p c t", c=2)
            pbk = ark(PB, 2048)
            for c in range(2):
                for tb in range(NTB):
                    b = next_acc()
                    mm(ps[b][:, :], poolwS[:, l, c, :], pb[:, c, tb * TB:(tb + 1) * TB], True, True, pbk + ["poolw"], [("ps", b)])
                    ts(Yr[:, c, tb * TB:(tb + 1) * TB], ps[b][:, :], pcol(l, "pscale", c, c + 1), None, ALU.mult, None,
                       [("ps", b), "pp"], [("Yr", c, tb)])

        def mixer_sconv(l, seg):
            GC, G, T1 = 0, 2048, 4352
            for c in range(2):
                gc = arf(GC, 2048)
                gck = ark(GC, 2048)
                wgc, wgck = load_win(l, 512 + c * 128, 128)
                proj_chunk(wgc, wgck, 0, 128,
                           lambda tb, p_, pk: act(gc[:, tb * TB:(tb + 1) * TB], p_, AF.Copy, [pk], gck))
                whh, whhk = load_win(l, 768 + c * 128, 128)
                g = arf(G, 2050)
                gk = ark(G, 2050)
                cp(g[:, 0:2], cSc[:, l, c, :], [("cSc", l, c)], gk)
                proj_chunk(whh, whhk, 0, 128,
                           lambda tb, p_, pk: tt(g[:, 2 + tb * TB:2 + (tb + 1) * TB], p_, gc[:, tb * TB:(tb + 1) * TB], ALU.mult,
                                                 [pk] + gck, gk))
                cp(cSc[:, l, c, :], g[:, 2048:2050], gk, [("cSc", l, c)])
                t1 = arf(T1, 2048)
                t1k = ark(T1, 2048)
                wv = pcol(l, "scw").rearrange("p (c k) -> p c k", c=2)
                ts(t1, g[:, 0:2048], wv[:, c, 0:1], None, ALU.mult, None, gk + ["pp"], t1k)
                stt(t1, g[:, 1:2049], wv[:, c, 1:2], t1, ALU.mult, ALU.add, gk + ["pp"] + t1k, t1k)
                stt(t1, g[:, 2:2050], wv[:, c, 2:3], t1, ALU.mult, ALU.add, gk + ["pp"] + t1k, t1k)
                wgb, wgbk = load_win(l, 256 + c * 128, 128)
                proj_chunk(wgb, wgbk, 0, 128,
                           lambda tb, p_, pk: tt(Yr[:, 2 + c, tb * TB:(tb + 1) * TB], p_, t1[:, tb * TB:(tb + 1) * TB], ALU.mult,
                                                 [pk] + t1k, [("Yr", 2 + c, tb)]))

        def mixer_ssd(l, seg):
            SZ, RAW, ACC, XBC = 0, 2048, 4352, 6400
            XTOK, BTOK = 2048, 4096
            sz = arb(SZ, 2048).rearrange("p (c t) -> p c t", c=2)
            szk = ark(SZ, 2048)
            for c in range(2):
                wz, wzk = load_win(l, 1024 + c * 128, 128)
                proj_chunk(wz, wzk, 0, 128,
                           lambda tb, p_, pk: act(sz[:, c, tb * TB:(tb + 1) * TB], p_, AF.Silu, [pk], szk))
            xbc = arb(XBC, 6144).rearrange("p (c t) -> p c t", c=6)
            kxb = [ark(XBC + j * 1024, 1024) for j in range(6)]
            cw = pcol(l, "cvw").rearrange("p (c k) -> p c k", c=6)
            for j in range(6):
                wx, wxk = load_win(l, 1280 + j * 128, 128)
                raw = arf(RAW, 2051)
                rawk = ark(RAW, 2051)
                cp(raw[:, 0:3], cCv[:, l, j, :], [("cCv", l, j)], rawk)
                proj_chunk(wx, wxk, 0, 128,
                           lambda tb, p_, pk: act(raw[:, 3 + tb * TB:3 + (tb + 1) * TB], p_, AF.Copy, [pk], rawk))
                cp(cCv[:, l, j, :], raw[:, 2048:2051], rawk, [("cCv", l, j)])
                acc = arf(ACC, 2048)
                acck = ark(ACC, 2048)
                ts(acc, raw[:, 0:2048], cw[:, j, 0:1], None, ALU.mult, None, rawk + ["pp"], acck)
                for kk in range(1, 4):
                    stt(acc, raw[:, kk:kk + 2048], cw[:, j, kk:kk + 1], acc, ALU.mult, ALU.add, rawk + acck + ["pp"], acck)
                act(xbc[:, j, :], acc, AF.Silu, acck + ["pp"], kxb[j], bias=pcol(l, "cvb", j, j + 1))
            print("  ssd: before dt", P.total)
            wd, wdk = load_win(l, 2048, 4)
            for c in range(NCH):
                for k in range(KC):
                    mm(ps[6][:, c * 4:(c + 1) * 4], uT[:, k, c * 128:(c + 1) * 128], wd[:, k, 0:4], k == 0, k == KC - 1,
                       wdk + [("uT", k, c // 4)], [("ps", 6)])
            print("  ssd: before small", P.total)
            tyk = [("tiny", "all")]
            def v3(a): return tiny[:, a:a + 64].rearrange("p (c h) -> p c h", h=4)
            dt_, adt, acs, tot, eacs, dte, cd, ddte = (v3(a) for a in (0, 64, 128, 192, 256, 320, 384, 448))
            xsp = v3(512)
            ex = v3(576)
            bc4 = lambda a, b: ssdp[:, l, a:b].unsqueeze(1).to_broadcast([128, NCH, 4])
            tt(xsp, ps[6][:, 0:64].rearrange("p (c h) -> p c h", h=4), bc4(4, 8), ALU.add, [("ps", 6), ("ssdp", l)], tyk)
            ts(ex, xsp, 30.0, None, ALU.min, None, tyk, tyk)
            act(ex, ex, AF.Exp, tyk, tyk)
            act(ex, ex, AF.Ln, tyk + ["eps"], tyk, bias=epsT[:, 1:2])
            tt(dt_, ex, xsp, ALU.max, tyk, tyk)
            tt(adt, dt_, bc4(0, 4), ALU.mult, tyk + [("ssdp", l)], tyk)
            ahi = tinyb[:, 0:64]
            alo = tinyb[:, 64:128]
            tbk = [("tinyb", "a")]
            adf = tiny[:, 64:128]
            cp(ahi, adf, tyk, tbk)
            tt(tiny[:, 640:704], adf, ahi, ALU.subtract, tyk + tbk, tyk)
            cp(alo, tiny[:, 640:704], tyk, tbk)
            cp(tiny[:, 768:832], ahi, tbk, tyk)
            mm(ps[6][:, 64:128], tri, ahi, True, False, tbk + ["cm"], [("ps", 6)])
            mm(ps[6][:, 64:128], tri, alo, False, True, tbk + ["cm"], [("ps", 6)])
            mm(ps[6][:, 128:192], ones, ahi, True, False, tbk + ["cm"], [("ps", 6)])
            mm(ps[6][:, 128:192], ones, alo, False, True, tbk + ["cm"], [("ps", 6)])
            cp(tiny[:, 128:256], ps[6][:, 64:192], [("ps", 6)], tyk)
            act(tiny[:, 256:320], tiny[:, 128:192], AF.Exp, tyk, tyk)
            tt(tiny[:, 320:384], tiny[:, 192:256], tiny[:, 128:192], ALU.subtract, tyk, tyk)
            act(tiny[:, 320:384], tiny[:, 320:384], AF.Exp, tyk, tyk)
            act(tiny[:, 384:448], tiny[:, 192:256], AF.Exp, tyk, tyk)
            tt(tiny[:, 448:512], tiny[:, 0:64], tiny[:, 320:384], ALU.mult, tyk, tyk)
            ts(tiny[:, 704:768], tiny[:, 128:192], -1.0, None, ALU.mult, None, tyk, tyk)
            nacs = v3(704)
            print("  ssd: before transposes", P.total)
            xtok = arb(XTOK, 2048).rearrange("p (c f) -> p c f", c=NCH)
            btok = arb(BTOK, 2048).rearrange("p (c f) -> p c f", c=NCH)
            xtk, btk = ark(XTOK, 2048), ark(BTOK, 2048)
            ti = 0
            for c in range(NCH):
                for j in range(4):
                    o = (ti % 4) * 128
                    ti += 1
                    bnk = 4 + (ti - 1) % 4
                    pk_ = ("ps", bnk)
                    mm(ps[bnk][:, 0:128], xbc[:, j, c * 128:(c + 1) * 128], ident, True, True, kxb[j] + ["cm"], [pk_])
                    dst = (xtok if j < 2 else btok)[:, c, (j % 2) * 128:(j % 2 + 1) * 128]
                    if ti % 2 == 0:
                        cp(dst, ps[bnk][:, 0:128], [pk_], xtk if j < 2 else btk)
                    else:
                        act(dst, ps[bnk][:, 0:128], AF.Copy, [pk_], xtk if j < 2 else btk)
            WT = XBC
            E_ = arb(WT, 256).rearrange("p (h s) -> p h s", h=4)
            MT = arb(WT + 256, 256).rearrange("p (h s) -> p h s", h=4)
            RH = arb(WT + 512, 512).rearrange("p (a h s) -> p a h s", a=2, h=4)
            XDT = arb(WT + 1024, 128)
            XDE = arb(WT + 1152, 128)
            YSB = arf(WT + 1280, 256)
            YTK = arb(WT + 1536, 128)
            SBF = arb(WT + 1664, 128)
            STMP = arf(WT + 1792, 256)
            kE, kMT, kRH, kXDT, kXDE, kYSB, kYTK, kSBF, kST = (ark(WT + a, n) for a, n in
                ((0, 256), (256, 256), (512, 512), (1024, 128), (1152, 128), (1280, 256), (1536, 128), (1664, 128), (1792, 256)))
            print("  ssd: before main loop", P.total)
            Sst = cS[:, l, :]
            kS = [("cS", l)]
            PKG, RBO, PTO = 12544, 12816, 13904
            pkg = arf(PKG, 272)
            pkgk = ark(PKG, 272)
            SL = pkg[:, 0:256]
            mset(SL, 0.0, pkgk)
            for c in range(NCH):
                x3 = xtok[:, c, :].rearrange("p (h d) -> p h d", h=4)
                tt(XDE.rearrange("p (h d) -> p h d", h=4), x3, ddte[:, c, :].unsqueeze(2).to_broadcast([128, 4, 64]), ALU.mult, xtk + tyk, kXDE)
                for g in range(2):
                    mm(ps[5][:, g * 128:(g + 1) * 128], btok[:, c, g * 128:(g + 1) * 128], XDE[:, g * 128:(g + 1) * 128], True, True,
                       btk + kXDE, [("ps", 5)])
                tt(STMP.rearrange("p (h d) -> p h d", h=4), SL.rearrange("p (h d) -> p h d", h=4),
                   cd[:, c, :].unsqueeze(2).to_broadcast([128, 4, 64]), ALU.mult, pkgk + tyk, kST)
                tt(SL, STMP, ps[5][:, 0:256], ALU.add, kST + [("ps", 5)], pkgk)
            tr_ = tiny[:, 832:864].rearrange("p (c h) -> p c h", h=4)
            tt(tr_, tot[:, 0:8, :], tot[:, 8:16, :], ALU.add, tyk, tyk)
            tt(tr_[:, 0:4, :], tr_[:, 0:4, :], tr_[:, 4:8, :], ALU.add, tyk, tyk)
            tt(tr_[:, 0:2, :], tr_[:, 0:2, :], tr_[:, 2:4, :], ALU.add, tyk, tyk)
            tt(tr_[:, 0:1, :], tr_[:, 0:1, :], tr_[:, 1:2, :], ALU.add, tyk, tyk)
            act(pkg[:, 256:260], tiny[:, 832:836], AF.Exp, tyk, pkgk)
            dma("sp", ccd[("s", l, "i")][:, 0:260], pkg[:, 0:260], pkgk, [("cci", "s", l)])
            allgather("s", l, [("cci", "s", l)], [("cco", "s", l)])
            RBs = arf(RBO, 1088).rearrange("p (r f) -> p r f", r=4)
            rbsk = ark(RBO, 1088)
            dma("sp", RBs, ccd[("s", l, "o")].rearrange("(r p) f -> p r f", p=128), [("cco", "s", l)], rbsk)
            PT = arf(PTO, 256)
            ptk = ark(PTO, 256)
            sel = pcol(l, "sel")
            mset(Sst, 0.0, kS)
            mset(PT, 0.0, ptk)
            for j in range(4):
                stt(Sst, PT, sel[:, 4 + j:5 + j], Sst, ALU.mult, ALU.add, ptk + kS + ["pp"], kS)
                if j < 3:
                    tt(PT.rearrange("p (h d) -> p h d", h=4), PT.rearrange("p (h d) -> p h d", h=4),
                       RBs[:, j, 256:260].unsqueeze(2).to_broadcast([128, 4, 64]), ALU.mult, ptk + rbsk, ptk)
                    tt(PT, PT, RBs[:, j, 0:256], ALU.add, ptk + rbsk, ptk)
            cp(SBF, Sst, kS, kSBF)
            for c in range(NCH):
                tsl = slice(c * 128, (c + 1) * 128)
                if c < 2:
                    print("  ssd: chunk", c, P.total)
                for h in range(4):
                    o = ps[0][:, h * 128:(h + 1) * 128]
                    mm(o, tinyb[:, c * 4 + h:c * 4 + h + 1].to_broadcast([128, 128]), tri, True, False, tbk + ["cm"], [("ps", 0)])
                    mm(o, tinyb[:, 64 + c * 4 + h:64 + c * 4 + h + 1].to_broadcast([128, 128]), tri, False, False, tbk + ["cm"], [("ps", 0)])
                    mm(o, ident, negm, False, True, ["cm"], [("ps", 0)])
                for h in range(4):
                    act(E_[:, h, :], ps[0][:, h * 128:(h + 1) * 128], AF.Exp, [("ps", 0)] + tyk, kE, bias=nacs[:, c, h:h + 1])
                for g in range(2):
                    mm(ps[1][:, g * 128:(g + 1) * 128], xbc[:, 2 + g, tsl], xbc[:, 4 + g, tsl], True, True,
                       kxb[2 + g] + kxb[4 + g], [("ps", 1)])
                for h in range(4):
                    g = h // 2
                    tt(MT[:, h, :], ps[1][:, g * 128:(g + 1) * 128], E_[:, h, :], ALU.mult, [("ps", 1)] + kE, kMT)
                x3 = xtok[:, c, :].rearrange("p (h d) -> p h d", h=4)
                tt(XDT.rearrange("p (h d) -> p h d", h=4), x3, dt_[:, c, :].unsqueeze(2).to_broadcast([128, 4, 64]), ALU.mult, xtk + tyk, kXDT)
                tt(XDE.rearrange("p (h d) -> p h d", h=4), x3, ddte[:, c, :].unsqueeze(2).to_broadcast([128, 4, 64]), ALU.mult, xtk + tyk, kXDE)
                for h in range(4):
                    o = ps[2][:, h * 64:(h + 1) * 64]
                    mm(o, MT[:, h, :], XDT[:, h * 64:(h + 1) * 64], True, True, kMT + kXDT, [("ps", 2, "y")])
                for h in range(4):
                    g = h // 2
                    mm(ps[3][:, h * 64:(h + 1) * 64], xbc[:, 4 + g, tsl], SBF[:, h * 64:(h + 1) * 64], True, True,
                       kxb[4 + g] + kSBF, [("ps", 3)])
                tt(YSB.rearrange("p (h d) -> p h d", h=4), x3, ssdp[:, l, 8:12].unsqueeze(2).to_broadcast([128, 4, 64]), ALU.mult,
                   xtk + [("ssdp", l)], kYSB)
                tt(YSB, YSB, ps[2][:, 0:256], ALU.add, kYSB + [("ps", 2, "y")], kYSB)
                tt(STMP.rearrange("p (h d) -> p h d", h=4), ps[3][:, 0:256].rearrange("p (h d) -> p h d", h=4),
                   eacs[:, c, :].unsqueeze(2).to_broadcast([128, 4, 64]), ALU.mult, [("ps", 3)] + tyk, kST)
                tt(YTK, STMP, YSB, ALU.add, kST + kYSB, kYTK)
                for j in range(2):
                    o = j * 128
                    mm(ps[4][:, o:o + 128], YTK[:, j * 128:(j + 1) * 128], ident, True, True, kYTK + ["cm"], [("ps", 4, o)])
                    tt(Yr[:, 4 + j, tsl], ps[4][:, o:o + 128], sz[:, j, tsl], ALU.mult, [("ps", 4, o)] + szk, [("Yr", 4 + j, c // 4)])
                for g in range(2):
                    mm(ps[5][:, g * 128:(g + 1) * 128], btok[:, c, g * 128:(g + 1) * 128], XDE[:, g * 128:(g + 1) * 128], True, True,
                       btk + kXDE, [("ps", 5)])
                tt(STMP.rearrange("p (h d) -> p h d", h=4), Sst.rearrange("p (h d) -> p h d", h=4),
                   cd[:, c, :].unsqueeze(2).to_broadcast([128, 4, 64]), ALU.mult, kS + tyk, kST)
                tt(Sst, STMP, ps[5][:, 0:256], ALU.add, kST + [("ps", 5)], kS)
                act(SBF, Sst, AF.Copy, kS, kSBF)

        def mixer_s5(l, seg):
            U, STG, BT, W, WB, PAT, INJ, TAB = 0, 2048, 4096, 6144, 8192, 10240, 11264, 13312
            tyk = [("tiny", "all")]
            pk = [("s5pw", l)]
            lk = [("s5lv", l)]
            u = arb(U, 2048).rearrange("p (c t) -> p c t", c=2)
            uk = ark(U, 2048)
            for c in range(2):
                wu, wuk = load_win(l, 2052 + c * 128, 128)
                proj_chunk(wu, wuk, 0, 128,
                           lambda tb, p_, pk_: act(u[:, c, tb * TB:(tb + 1) * TB], p_, AF.Copy, [pk_], uk))
            pat = arb(PAT, 1024)
            patk = ark(PAT, 1024)
            mset(pat, 1.0, patk)
            mset(pat.rearrange("p (c j) -> p c j", j=16)[:, :, 0:1], 0.0, patk)
            pw0 = tiny[:, 0:256].rearrange("p (r a j) -> p r a j", r=8, a=2)
            qq = tiny[:, 256:512].rearrange("p (r a j) -> p r a j", r=8, a=2)
            den = tiny[:, 512:640].rearrange("p (r j) -> p r j", r=8)
            tmp = tiny[:, 640:768].rearrange("p (r j) -> p r j", r=8)
            mset(pw0[:, :, 0, 0:1], 1.0, tyk)
            mset(pw0[:, :, 1, 0:1], 0.0, tyk)
            for a in range(2):
                cp(pw0[:, :, a, 1:16], s5pw[:, l, :, a, 0:15], pk, tyk)
            tt(den, pw0[:, :, 0, :], pw0[:, :, 0, :], ALU.mult, tyk, tyk)
            tt(tmp, pw0[:, :, 1, :], pw0[:, :, 1, :], ALU.mult, tyk, tyk)
            tt(den, den, tmp, ALU.add, tyk, tyk)
            P.op("dve", lambda e: e.reciprocal(out=den, in_=den), tyk, tyk)
            tt(qq[:, :, 0, :], pw0[:, :, 0, :], den, ALU.mult, tyk, tyk)
            tt(qq[:, :, 1, :], pw0[:, :, 1, :], den, ALU.mult, tyk, tyk)
            ts(qq[:, :, 1, :], qq[:, :, 1, :], -1.0, None, ALU.mult, None, tyk, tyk)
            bpad = arf(TAB, 512).rearrange("p (r a h) -> p r a h", r=8, a=2)
            cpad = arf(TAB + 512, 512).rearrange("p (r a h) -> p r a h", r=8, a=2)
            tabk = ark(TAB, 1024)
            mset(arf(TAB, 512), 0.0, tabk)
            for a in range(2):
                cp(bpad[0:64, :, a, 0:16], bbS[0:64, l, a, :, :], [("bb", l)], tabk)
                cp(bpad[64:128, :, a, 16:32], bbS[64:128, l, a, :, :], [("bb", l)], tabk)
            dma("sp", arf(TAB + 512, 512), s5c[l], [], tabk)
            Ef = Yr[:, 6:8, :].rearrange("p a t -> p (a t)").bitcast(F32)
            Eall = Ef.rearrange("p (r a c) -> p r a c", r=8, a=2)
            ek = [("Yr", 6 + a, tb) for a in range(2) for tb in range(NTB)]
            bufA = arf(W, 2048).rearrange("p (r a c) -> p r a c", r=8, a=2)
            bufB = arf(WB, 2048).rearrange("p (r a c) -> p r a c", r=8, a=2)
            kA, kB = ark(W, 2048), ark(WB, 2048)
            T1 = arf(STG, 2048)
            T2 = arf(BT, 2048)
            kT1, kT2 = ark(STG, 2048), ark(BT, 2048)

            def bc8(ap, n):
                return ap.unsqueeze(2).to_broadcast([128, 8, n])

            def lvl2(src, sk):
                cur, ck_ = src, sk
                nxt_list = [(bufA, kA), (bufB, kB)]
                if src is bufA:
                    nxt_list = [(bufB, kB), (bufA, kA)]
                for lev in range(7):
                    d = 1 << lev
                    n = 128 - d
                    dst, dk = nxt_list[lev % 2]
                    cr, ci = s5lv[:, l, :, 0, lev], s5lv[:, l, :, 1, lev]
                    t1 = T1[:, 0:8 * n].rearrange("p (r c) -> p r c", r=8)
                    t2 = T2[:, 0:8 * n].rearrange("p (r c) -> p r c", r=8)
                    cp(dst[:, :, :, 0:d], cur[:, :, :, 0:d], ck_, dk)
                    tt(t1, cur[:, :, 0, 0:n], bc8(cr, n), ALU.mult, ck_ + lk, kT1)
                    tt(t2, cur[:, :, 1, 0:n], bc8(ci, n), ALU.mult, ck_ + lk, kT2)
                    tt(t1, t1, t2, ALU.subtract, kT1 + kT2, kT1)
                    tt(dst[:, :, 0, d:128], cur[:, :, 0, d:128], t1, ALU.add, ck_ + kT1, dk)
                    tt(t1, cur[:, :, 1, 0:n], bc8(cr, n), ALU.mult, ck_ + lk, kT1)
                    tt(t2, cur[:, :, 0, 0:n], bc8(ci, n), ALU.mult, ck_ + lk, kT2)
                    tt(t1, t1, t2, ALU.add, kT1 + kT2, kT1)
                    tt(dst[:, :, 1, d:128], cur[:, :, 1, d:128], t1, ALU.add, ck_ + kT1, dk)
                    cur, ck_ = dst, dk
                return cur, ck_

            def run_pass(mode):
                inj = arf(INJ, 2048).rearrange("p (r a c) -> p r a c", r=8, a=2)
                injk = ark(INJ, 2048)
                if mode == "full":
                    cp(bufA, Eall, ek, kA)
                    mur, mui = s5lv[:, l, :, 0, 0], s5lv[:, l, :, 1, 0]
                    xr_, xi_ = cX[:, l, :, 0], cX[:, l, :, 1]
                    ckx = [("cX", l, r) for r in range(8)]
                    a1, a2 = tiny[:, 768:776], tiny[:, 776:784]
                    tt(a1, mur, xr_, ALU.mult, lk + ckx, tyk)
                    tt(a2, mui, xi_, ALU.mult, lk + ckx, tyk)
                    tt(a1, a1, a2, ALU.subtract, tyk, tyk)
                    tt(bufA[:, :, 0, 0], bufA[:, :, 0, 0], a1, ALU.add, kA + tyk, kA)
                    tt(a1, mur, xi_, ALU.mult, lk + ckx, tyk)
                    tt(a2, mui, xr_, ALU.mult, lk + ckx, tyk)
                    tt(a1, a1, a2, ALU.add, tyk, tyk)
                    tt(bufA[:, :, 1, 0], bufA[:, :, 1, 0], a1, ALU.add, kA + tyk, kA)
                    res, rk = lvl2(bufA, kA)
                    lr8, li8 = s5pw[:, l, :, 0, 0], s5pw[:, l, :, 1, 0]
                    n = 127
                    t1 = T1[:, 0:8 * n].rearrange("p (r c) -> p r c", r=8)
                    t2 = T2[:, 0:8 * n].rearrange("p (r c) -> p r c", r=8)
                    tt(t1, res[:, :, 0, 0:n], bc8(lr8, n), ALU.mult, rk + pk, kT1)
                    tt(t2, res[:, :, 1, 0:n], bc8(li8, n), ALU.mult, rk + pk, kT2)
                    tt(inj[:, :, 0, 1:128], t1, t2, ALU.subtract, kT1 + kT2, injk)
                    tt(t1, res[:, :, 1, 0:n], bc8(lr8, n), ALU.mult, rk + pk, kT1)
                    tt(t2, res[:, :, 0, 0:n], bc8(li8, n), ALU.mult, rk + pk, kT2)
                    tt(inj[:, :, 1, 1:128], t1, t2, ALU.add, kT1 + kT2, injk)
                    tt(a1, lr8, xr_, ALU.mult, pk + ckx, tyk)
                    tt(a2, li8, xi_, ALU.mult, pk + ckx, tyk)
                    tt(inj[:, :, 0, 0], a1, a2, ALU.subtract, tyk, injk)
                    tt(a1, lr8, xi_, ALU.mult, pk + ckx, tyk)
                    tt(a2, li8, xr_, ALU.mult, pk + ckx, tyk)
                    tt(inj[:, :, 1, 0], a1, a2, ALU.add, tyk, injk)

                stg = arb(STG, 2048).rearrange("p (a j c) -> p a j c", a=2, j=16)
                stgk = ark(STG, 2048)
                btv = arb(BT, 2048).rearrange("p (a j c) -> p a j c", a=2, j=16)
                btk_ = ark(BT, 2048)
                Wn = arf(W, 2048)
                wk_ = ark(W, 2048)
                Wv = Wn.rearrange("p (c j) -> p j c", j=16)
                wb = arb(WB, 2048).rearrange("p (a t) -> p a t", a=2)
                wbk = ark(WB, 2048)

                def b16(ap):
                    return ap.unsqueeze(2).to_broadcast([128, 16, 32])

                def h32(ap):
                    return ap.unsqueeze(1).to_broadcast([128, 16, 32])

                ct = Yr[:, 4:6, :].rearrange("p a (j c) -> p a j c", j=16)
                ctk = [("Yr", 4 + a_, t_) for a_ in range(2) for t_ in range(NTB)]
                if mode == "p1":
                    tmp_f, tmpk = arf(WB, 1024), ark(WB, 1024)
                    W1n, w1k_ = arf(INJ, 2048), ark(INJ, 2048)
                else:
                    tmp_f = Yr[:, 7, :].bitcast(F32)
                    tmpk = [("Yr", 7, t_) for t_ in range(NTB)]
                    W1n = Yr[:, 0:2, :].rearrange("p a t -> p (a t)").bitcast(F32)
                    w1k_ = [("Yr", a_, t_) for a_ in range(2) for t_ in range(NTB)]
                Wbufs = [(Wn, wk_), (W1n, w1k_)]
                t1 = tmp_f[:, 0:512].rearrange("p (j h) -> p j h", j=16)
                t2 = tmp_f[:, 512:1024].rearrange("p (j h) -> p j h", j=16)
                mset(arb(STG, 2048), 0.0, stgk)
                if mode == "full":
                    mset(Yr[:, 4:6, :], 0.0, ctk)

                ptf = Yr[:, 2, :].bitcast(F32)
                ptk = [("Yr", 2, t_) for t_ in range(NTB)]
                p1_ = ptf[:, 0:512].rearrange("p (j h) -> p j h", j=16)
                p2_ = ptf[:, 512:1024].rearrange("p (j h) -> p j h", j=16)

                def tables_b_dve(r):
                    q = r % 4
                    if r > 0:
                        pq = (r - 1) % 4
                        mset(stg[:, :, :, pq * 32:(pq + 1) * 32], 0.0, stgk, eng="pool")
                    Bre, Bim = h32(bpad[:, r, 0, :]), h32(bpad[:, r, 1, :])
                    qr_, qi_ = b16(qq[:, r, 0, :]), b16(qq[:, r, 1, :])
                    sv = stg[:, :, :, q * 32:(q + 1) * 32]
                    tt(p1_, Bre, qr_, ALU.mult, tabk + tyk, ptk, eng="pool")
                    tt(p2_, Bim, qi_, ALU.mult, tabk + tyk, ptk, eng="pool")
                    tt(sv[:, 0], p1_, p2_, ALU.subtract, ptk, stgk, eng="pool")
                    tt(p1_, Bim, qr_, ALU.mult, tabk + tyk, ptk, eng="pool")
                    tt(p2_, Bre, qi_, ALU.mult, tabk + tyk, ptk, eng="pool")
                    tt(sv[:, 1], p1_, p2_, ALU.add, ptk, stgk, eng="pool")

                def tables_b_pe(r):
                    for a in range(2):
                        for lg in range(4):
                            bnk = 4 + lg
                            for li_ in range(4):
                                mm(ps[bnk][:, li_ * 128:(li_ + 1) * 128], stg[:, a, lg * 4 + li_, :], ident, True, True,
                                   stgk + ["cm"], [("ps", bnk)])
                            act(btv[:, a, lg * 4:(lg + 1) * 4, :], ps[bnk][:, :].rearrange("p (j c) -> p j c", j=4), AF.Copy,
                                [("ps", bnk)], btk_)

                def ct_dve(r):
                    q = r % 4
                    if r > 0:
                        pq = (r - 1) % 4
                        mset(ct[:, :, :, pq * 32:(pq + 1) * 32], 0.0, ctk)
                    Cre, Cim = h32(cpad[:, r, 0, :]), h32(cpad[:, r, 1, :])
                    pr_, pi_ = b16(pw0[:, r, 0, :]), b16(pw0[:, r, 1, :])
                    cv = ct[:, :, :, q * 32:(q + 1) * 32]
                    tt(t1, Cre, pr_, ALU.mult, tabk + tyk, tmpk)
                    tt(t2, Cim, pi_, ALU.mult, tabk + tyk, tmpk)
                    tt(cv[:, 0], t1, t2, ALU.add, tmpk, ctk)
                    tt(t1, Cim, pr_, ALU.mult, tabk + tyk, tmpk)
                    tt(t2, Cre, pi_, ALU.mult, tabk + tyk, tmpk)
                    tt(cv[:, 1], t1, t2, ALU.subtract, tmpk, ctk)

                def bu_mm(r, a, wv_, wkk):
                    uv = u[:, r // 4, :].rearrange("p (c j) -> p j c", j=16)
                    for lg in range(4):
                        bnk = 4 + lg
                        for li_ in range(4):
                            j = lg * 4 + li_
                            mm(ps[bnk][:, li_ * 128:(li_ + 1) * 128], btv[:, a, j, :], uv[:, j, :], True, True, btk_ + uk, [("ps", bnk)])
                        act(wv_[:, lg * 4:(lg + 1) * 4, :], ps[bnk][:, :].rearrange("p (j c) -> p j c", j=4), AF.Copy, [("ps", bnk)], wkk)

                def scan(wn_, wkk):
                    P.op("dve", lambda e, wn_=wn_: e.tensor_tensor_scan(out=wn_, data0=pat, data1=wn_, initial=0.0, op0=ALU.mult, op1=ALU.add),
                         wkk + patk, wkk)

                tables_b_dve(0)
                tables_b_pe(0)
                for r in range(8):
                    oc = r // 4
                    q = r % 4
                    uv = u[:, oc, :].rearrange("p (c j) -> p j c", j=16)
                    wvs = [(wn_.rearrange("p (c j) -> p j c", j=16), wn_, kk_) for (wn_, kk_) in Wbufs]
                    if mode == "p1":
                        bu_mm(r, 0, wvs[0][0], wvs[0][2])
                        bu_mm(r, 1, wvs[1][0], wvs[1][2])
                        if r + 1 < 8:
                            tables_b_dve(r + 1)
                        for a in range(2):
                            scan(wvs[a][1], wvs[a][2])
                            cp(Eall[:, r, a, :], wvs[a][0][:, 15, :], wvs[a][2], ek)
                        if r + 1 < 8:
                            tables_b_pe(r + 1)
                        continue
                    bu_mm(r, 0, wvs[0][0], wvs[0][2])
                    bu_mm(r, 1, wvs[1][0], wvs[1][2])
                    ct_dve(r)
                    if r + 1 < 8:
                        tables_b_dve(r + 1)
                    for a in range(2):
                        wv_, wn_, wkk = wvs[a]
                        tt(wv_[:, 0, :], wv_[:, 0, :], inj[:, r, a, :], ALU.add, wkk + injk, wkk)
                        scan(wn_, wkk)
                        act(wb[:, a, :], wn_, AF.Copy, wkk, wbk)
                    if r + 1 < 8:
                        tables_b_pe(r + 1)
                    wbv = [wb[:, a, :].rearrange("p (c j) -> p j c", j=16) for a in range(2)]
                    for j in range(16):
                        o = ps[j // 4][:, (j % 4) * 128:(j % 4 + 1) * 128]
                        P.op("pe", lambda e, o=o, j=j, q=q, wbv=wbv: e.matmul(o, lhsT=ct[:, 0, j, :], rhs=wbv[0][:, j, :],
                                                                             start=(q == 0 and j % 4 == 0), stop=False, skip_group_check=True),
                             ctk + wbk, [("ps", j // 4)])
                        P.op("pe", lambda e, o=o, j=j, q=q, wbv=wbv: e.matmul(o, lhsT=ct[:, 1, j, :], rhs=wbv[1][:, j, :], start=False,
                                                                             stop=(q == 3), skip_group_check=True), ctk + wbk, [("ps", j // 4)])
                    if q == 3:
                        for tb in range(NTB):
                            yo = W + (tb % 2) * 512
                            yt = arf(yo, 512)
                            ytk = ark(yo, 512)
                            uv4 = uv[:, tb * 4:(tb + 1) * 4, :]
                            stt(yt.rearrange("p (j c) -> p j c", j=4), uv4, pcol(l, "s5d", oc, oc + 1),
                                ps[tb][:, :].rearrange("p (j c) -> p j c", j=4), ALU.mult, ALU.add, uk + ["pp", ("ps", tb)], ytk)
                            act(Yr[:, 6 + oc, :].rearrange("p (c j) -> p j c", j=16)[:, tb * 4:(tb + 1) * 4, :],
                                yt.rearrange("p (j c) -> p j c", j=4), AF.Gelu_apprx_tanh, ytk, [("Yr", 6 + oc, t_) for t_ in range(NTB)])
                if mode == "p1":
                    p15r, p15i = s5pw[:, l, :, 0, 14], s5pw[:, l, :, 1, 14]
                    n = 128
                    t1 = T1[:, 0:1024].rearrange("p (r c) -> p r c", r=8)
                    t2 = T1[:, 1024:2048].rearrange("p (r c) -> p r c", r=8)
                    t3 = T2[:, 0:1024].rearrange("p (r c) -> p r c", r=8)
                    t4 = T2[:, 1024:2048].rearrange("p (r c) -> p r c", r=8)
                    tt(t1, Eall[:, :, 0, :], bc8(p15r, n), ALU.mult, ek + pk, kT1)
                    tt(t2, Eall[:, :, 1, :], bc8(p15i, n), ALU.mult, ek + pk, kT1)
                    tt(t3, Eall[:, :, 1, :], bc8(p15r, n), ALU.mult, ek + pk, kT2)
                    tt(t4, Eall[:, :, 0, :], bc8(p15i, n), ALU.mult, ek + pk, kT2)
                    tt(Eall[:, :, 0, :], t1, t2, ALU.subtract, kT1, ek)
                    tt(Eall[:, :, 1, :], t3, t4, ALU.add, kT2, ek)
                    res, rk = lvl2(Eall, ek)
                    cp(tiny[:, 800:816].rearrange("p (r a) -> p r a", a=2), res[:, :, :, 127], rk, tyk)
                    dma("sp", ccd[("x", l, "i")], tiny[:, 800:816], tyk, [("cci", "x", l)])
                    allgather("x", l, [("cci", "x", l)], [("cco", "x", l)])
                    return

            run_pass("p1")
            s5_combine(l)
            run_pass("full")
            SG = W
            for tb in range(NTB):
                sgs = []
                for m in range(2):
                    b = next_acc()
                    for k in range(2):
                        mm(ps[b][:, :], gluwS[:, l, k, m * 128:(m + 1) * 128], Yr[:, 6 + k, tb * TB:(tb + 1) * TB], k == 0, k == 1,
                           [("Yr", 6 + k, tb), "gluw"], [("ps", b)])
                    so = SG + ((tb * 2 + m) % 4) * 256
                    sg = arb(so, 256)
                    act(sg, ps[b][:, :], AF.Sigmoid, [("ps", b), "pp"], ark(so, 256), bias=pcol(l, "glub", m, m + 1))
                    sgs.append((sg, ark(so, 256)))
                for m in range(2):
                    sg, sgk = sgs[m]
                    tt(Yr[:, 6 + m, tb * TB:(tb + 1) * TB], Yr[:, 6 + m, tb * TB:(tb + 1) * TB], sg, ALU.mult,
                       [("Yr", 6 + m, tb)] + sgk, [("Yr", 6 + m, tb)])

        W1O = [7168, 9216]
        W2O = [11264, 13312]
        NG = 8

        def mlp_load(l, g):
            sl = g % 2
            w1g = arb(W1O[sl], 2048).rearrange("p (k c) -> p k c", k=8)
            w2g = arb(W2O[sl], 2048).rearrange("p (k c) -> p k c", k=4)
            w1k, w2k = ark(W1O[sl], 2048), ark(W2O[sl], 2048)
            dma("pool", w1g, w1[l, :, g * 512:(g + 1) * 512].rearrange("(k p) c -> p k c", p=128), [], w1k)
            dma("pool", w2g, w2[l, g * 512:(g + 1) * 512, :].rearrange("(k p) c -> p k c", p=128), [], w2k)

        def out_proj(l):
            WO, RSTD, SQ = 0, 4096, 6144
            wo = arb(WO, 4096).rearrange("p (k c) -> p k c", k=8)
            wok = ark(WO, 4096)
            dma("pool", wo, w_out[l, :, :].rearrange("(k p) c -> p k c", p=128), [], wok)
            mlp_load(l, 0)
            mlp_load(l, 1)
            for g in range(4):
                rms_stats(lambda k, tb: Yr[:, 2 * g + k, tb * TB:(tb + 1) * TB], 2, 256.0, RSTD, SQ,
                          lambda k, tb: [("Yr", 2 * g + k, tb)])
                for k in range(2):
                    ch = 2 * g + k
                    for tb in range(NTB):
                        stt(uT[:, ch, tb * TB:(tb + 1) * TB], Yr[:, ch, tb * TB:(tb + 1) * TB], pcol(l, "bnw", ch, ch + 1),
                            arf(RSTD + tb * TB, TB), ALU.mult, ALU.mult, [("Yr", ch, tb), "pp"] + ark(RSTD + tb * TB, TB), [("uT", ch, tb)])
            for m in range(KC):
                for tb in range(NTB):
                    b = next_acc()
                    for k in range(KC):
                        mm(ps[b][:, :], wo[:, k, m * 128:(m + 1) * 128], uT[:, k, tb * TB:(tb + 1) * TB], k == 0, k == KC - 1,
                           wok + [("uT", k, tb)], [("ps", b)])
                    stt(hT[:, m, tb * TB:(tb + 1) * TB], ps[b][:, :], G1(l, m), hT[:, m, tb * TB:(tb + 1) * TB], ALU.mult, ALU.add,
                        [("ps", b), ("der", l), ("hT", m, tb)], [("hT", m, tb)])

        def mlp(l):
            norm_mod(l, S2, B2)
            RS = 5120

            def up(g):
                sl = g % 2
                w1g = arb(W1O[sl], 2048).rearrange("p (k c) -> p k c", k=8)
                w1k = ark(W1O[sl], 2048)
                for jc in range(4):
                    yc = sl * 4 + jc
                    for tb in range(NTB):
                        b = next_acc()
                        for k in range(KC):
                            mm(ps[b][:, :], w1g[:, k, jc * 128:(jc + 1) * 128], uT[:, k, tb * TB:(tb + 1) * TB], k == 0, k == KC - 1,
                               w1k + [("uT", k, tb)], [("ps", b)])
                        ro = RS + ((jc * NTB + tb) % 4) * 256
                        rs = arb(ro, 256)
                        act(rs, ps[b][:, :], AF.Relu, [("ps", b)], ark(ro, 256))
                        tt(Yr[:, yc, tb * TB:(tb + 1) * TB], rs, rs, ALU.mult, ark(ro, 256), [("Yr", yc, tb)], eng="pool")

            def down(g):
                sl = g % 2
                w2g = arb(W2O[sl], 2048).rearrange("p (k c) -> p k c", k=4)
                w2k = ark(W2O[sl], 2048)
                for m in range(KC):
                    for tb in range(NTB):
                        b = next_acc()
                        for jc in range(4):
                            mm(ps[b][:, :], w2g[:, jc, m * 128:(m + 1) * 128], Yr[:, sl * 4 + jc, tb * TB:(tb + 1) * TB], jc == 0, jc == 3,
                               w2k + [("Yr", sl * 4 + jc, tb)], [("ps", b)])
                        stt(hT[:, m, tb * TB:(tb + 1) * TB], ps[b][:, :], G2(l, m), hT[:, m, tb * TB:(tb + 1) * TB], ALU.mult, ALU.add,
                            [("ps", b), ("der", l), ("hT", m, tb)], [("hT", m, tb)])

            up(0)
            for g in range(NG):
                if g + 1 < NG:
                    up(g + 1)
                down(g)
                if g + 2 < NG:
                    mlp_load(l, g + 2)

        def final_out(seg):
            RSTD, SQ, OUT = 0, 2048, 4096
            rms_stats(lambda k, tb: hT[:, k, tb * TB:(tb + 1) * TB], KC, float(DM), RSTD, SQ, lambda k, tb: [("hT", k, tb)])
            for k in range(KC):
                oo = OUT + (k % 2) * 2048
                o = arf(oo, 2048)
                for tb in range(NTB):
                    stt(o[:, tb * TB:(tb + 1) * TB], hT[:, k, tb * TB:(tb + 1) * TB], pcol(0, "fnw", k, k + 1), arf(RSTD + tb * TB, TB),
                        ALU.mult, ALU.mult, [("hT", k, tb), "pp"] + ark(RSTD + tb * TB, TB), ark(oo, 2048))
                dma("sp", yT[k * 128:(k + 1) * 128, :], o, ark(oo, 2048), [("yT", k, seg)])

        stopped = False
        for seg in range(nseg):
            if stopped:
                break
            for k in range(KC):
                dma("sp", hT[:, k, :], xT[k * 128:(k + 1) * 128, :], [], [("hT", k, tb) for tb in range(NTB)])
            for l in range(depth):
                norm_mod(l, S1, B1)
                if stop_after == (seg, l, "u"):
                    stopped = True
                    break
                print("ops before tail", P.total)
                tail_prepass(l)
                print("ops before s5 p1", P.total)
                mixer_s5(l, seg)
                print("ops before halo", P.total)
                halo_apply(l)
                mixer_pool(l, seg)
                mixer_sconv(l, seg)
                print("ops before ssd", P.total)
                mixer_ssd(l, seg)
                print("ops after ssd", P.total)
                if stop_after == (seg, l, "mix"):
                    stopped = True
                    break
                out_proj(l)
                if stop_after == (seg, l, "hmix"):
                    stopped = True
                    break
                mlp(l)
                if stop_after == (seg, l, "h"):
                    stopped = True
                    break
            if not stopped:
                final_out(seg)
        print("total ops recorded:", P.total)
        P.limit = None
        if dbg:
            if "uT" in dbg_out:
                cp(AR[:, 0:2048], uT[:, 0, :], all_keys("uT"), ark(0, 2048))
            for name in dbg_out:
                if name == "Yr":
                    for k in range(KC):
                        o = arf((k % 2) * 2048, 2048)
                        cp(o, Yr[:, k, :], [("Yr", k, tb) for tb in range(NTB)], ark((k % 2) * 2048, 2048))
                        dma("sp", dbg_out[name][k * 128:(k + 1) * 128, :], o, ark((k % 2) * 2048, 2048), [("dbg", name, k)])
                elif name == "uT":
                    for k in range(KC):
                        o = arf((k % 2) * 2048, 2048)
                        cp(o, uT[:, k, :], [("uT", k, tb) for tb in range(NTB)], ark((k % 2) * 2048, 2048))
                        dma("sp", dbg_out[name][k * 128:(k + 1) * 128, :], o, ark((k % 2) * 2048, 2048), [("dbg", name, k)])
                elif name == "hT":
                    for k in range(KC):
                        dma("sp", dbg_out[name][k * 128:(k + 1) * 128, :], hT[:, k, :], [("hT", k, tb) for tb in range(NTB)], [("dbg", name, k)])
                elif name == "mod":
                    dma("sp", dbg_out[name], modT[:, :, :].rearrange("p l c -> p (l c)"), [("mod", 0), ("mod", 1)], [("dbg", name)])
        P.wait_all("sp")

        with nc.Block() as block:
            def replay(e, name):
                for waits, fn, inc in P.q[name]:
                    for s, v in waits:
                        e.wait_ge(sems[s], v)
                    if fn is not None:
                        fn(e).then_inc(sems[inc[0]], inc[1])

            @block.tensor
            def _(e):
                replay(e, "pe")

            @block.scalar
            def _(e):
                replay(e, "act")

            @block.vector
            def _(e):
                replay(e, "dve")

            @block.gpsimd
            def _(e):
                replay(e, "pool")

            @block.sync
            def _(e):
                replay(e, "sp")
    return nc


def _fm(v):
    return np.ascontiguousarray(v.reshape(-1, 128).T)


def _pack_params(inp, b, sg):
    L = DEPTH
    pp = np.zeros((L, 128, NPCOL), np.float32)

    def put(l, name, arr):
        o, w = PCOL[name]
        arr = np.asarray(arr, np.float32).reshape(128, w)
        pp[l, :, o:o + w] = arr
    wins = (2, 4, 8, 16)
    for l in range(L):
        put(l, "nw1", _fm(inp["norm_mix_w"][l]))
        put(l, "nw2", _fm(inp["norm_mlp_w"][l]))
        put(l, "bnw", _fm(inp["branch_norm_w"][l]))
        put(l, "fnw", _fm(inp["final_norm_w"]))
        adab = np.zeros((128, 48), np.float32)
        adab[:, 0:12] = _fm(inp["ada_b"][l])[:, sg * 12:(sg + 1) * 12]
        put(l, "adab", adab)
        put(l, "pscale", _fm(inp["pool_scale"][l]))
        put(l, "scw", inp["sconv_w"][l].reshape(3, 2, 128).transpose(2, 1, 0))
        put(l, "cvw", inp["ssd_conv_w"][l].reshape(4, 6, 128).transpose(2, 1, 0))
        put(l, "cvb", _fm(inp["ssd_conv_b"][l]))
        put(l, "dtb", np.broadcast_to(inp["ssd_dt_bias"][l][None, :], (128, 4)))
        put(l, "alog", np.broadcast_to(inp["ssd_a_log"][l][None, :], (128, 4)))
        put(l, "dsk", np.broadcast_to(inp["ssd_d"][l][None, :], (128, 4)))
        def gp(a):
            return a.reshape(8, 2, 64).transpose(1, 2, 0).reshape(128, 8)
        put(l, "are", gp(inp["s5_a_re"][l]))
        put(l, "aim", gp(inp["s5_a_im"][l]))
        put(l, "lst", gp(np.broadcast_to(inp["s5_log_step"][l][:, None], (16, 64))))
        put(l, "s5d", _fm(inp["s5_d"][l]))
        put(l, "glub", _fm(inp["s5_glu_b"][l]))
        put(l, "cond", _fm(inp["c"][b]))
        invc = np.zeros((128, 2, 16), np.float32)
        for c in range(2):
            for half in range(2):
                win = wins[c * 2 + half]
                if sg == 0:
                    invc[half * 64:(half + 1) * 64, c, :] = 1.0 / np.minimum(np.arange(16) + 1, win)
                else:
                    invc[half * 64:(half + 1) * 64, c, :] = 1.0 / win
        put(l, "invc", invc)
        sel = np.zeros((128, 8), np.float32)
        if sg > 0:
            sel[:, sg - 1] = 1.0
        sel[:, 4 + sg] = 1.0
        put(l, "sel", sel)
    return pp


def _consts():
    cm = np.zeros((128, 4, 128), np.float32)
    i = np.arange(128)
    cm[:, 0, :] = (i[:, None] == i[None, :])
    cm[:, 1, :] = 1.0
    cm[:, 2, :] = (i[:, None] <= i[None, :])
    cm[:, 3, :] = np.where(i[None, :] < i[:, None], -30000.0, 0.0)
    return cm


def _host_inputs(inp, b, sg):
    f = lambda a: np.ascontiguousarray(np.asarray(a, np.float32))
    pw = np.zeros((DEPTH, 128, 2, 128), np.float32)
    for l in range(DEPTH):
        for g in range(4):
            c, half = g // 2, g % 2
            pw[l, half * 64:(half + 1) * 64, c, half * 64:(half + 1) * 64] = inp["pool_w"][l, g]
    gw = np.ascontiguousarray(np.asarray(inp["s5_glu_w"], np.float32).reshape(DEPTH, 2, 128, 256).transpose(0, 2, 1, 3))
    s5b = np.zeros((DEPTH, 128, 256), np.float32)
    s5c = np.zeros((DEPTH, 128, 8, 2, 32), np.float32)
    for l in range(DEPTH):
        s5b[l, :, 0:128] = np.asarray(inp["s5_b_re"][l]).reshape(8, 2, 64, 16).transpose(1, 2, 0, 3).reshape(128, 128)
        s5b[l, :, 128:256] = np.asarray(inp["s5_b_im"][l]).reshape(8, 2, 64, 16).transpose(1, 2, 0, 3).reshape(128, 128)
        for ri, nm in enumerate(("s5_c_re", "s5_c_im")):
            cc = np.asarray(inp[nm][l]).reshape(8, 2, 16, 64)
            for r in range(8):
                for gi in range(2):
                    s5c[l, gi * 64:(gi + 1) * 64, r, ri, gi * 16:gi * 16 + 16] = cc[r, gi].T
    return {
        "s5b": s5b, "s5c": s5c.reshape(DEPTH, 128, 512),
        "xT": f(np.asarray(inp["x"][b][sg * T:(sg + 1) * T]).T),
        "pp": _pack_params(inp, b, sg),
        "cmat": _consts(),
        "ada_w": f(np.asarray(inp["ada_w"])[:, :, sg * 1536:(sg + 1) * 1536]), "w_in": f(inp["w_in"]), "w_out": f(inp["w_out"]),
        "mlp_w1": f(inp["mlp_w1"]), "mlp_w2": f(inp["mlp_w2"]),
        "poolw": pw, "gluw": gw,
    }


_NC_CACHE = {}


def kernel(**inputs):
    inp = {k: np.asarray(v) for k, v in inputs.items()}
    if "full" not in _NC_CACHE:
        _NC_CACHE["full"] = build()
    nc = _NC_CACHE["full"]
    in_maps = [_host_inputs(inp, r // 4, r % 4) for r in range(8)]
    res = run_bass_kernel_spmd(nc, in_maps, core_ids=list(range(8)))
    out = np.empty((2, SEQ, DM), np.float32)
    for r in range(8):
        out[r // 4, (r % 4) * T:(r % 4 + 1) * T, :] = res.results[r]["yT"].T
    return out
```

```python
import numpy as np
import concourse.bass as bass
import concourse.mybir as mybir
from concourse.bass_utils import run_bass_kernel_spmd

F32, BF16 = mybir.dt.float32, mybir.dt.bfloat16
AF = mybir.ActivationFunctionType
ALU = mybir.AluOpType

T = 2048
TB = 512
NTB = 4
NCH = 16
DM = 1024
KC = 8
SEQ = 8192
DEPTH = 2
EPS = 1e-6
ENGS = ("pe", "act", "dve", "pool", "sp")
NDMA = 12

PCOL = {}
_off = 0
for _n, _w in [("nw1", 8), ("nw2", 8), ("bnw", 8), ("fnw", 8), ("adab", 48), ("pscale", 2), ("scw", 6),
               ("cvw", 24), ("cvb", 6), ("dtb", 4), ("alog", 4), ("dsk", 4), ("are", 8), ("aim", 8), ("lst", 8),
               ("s5d", 2), ("glub", 2), ("cond", 8),
               ("invc", 32), ("sel", 8)]:
    PCOL[_n] = (_off, _w)
    _off += _w
NPCOL = _off


class Prog:
    def __init__(self):
        self.q = {e: [] for e in ENGS}
        self.cnt = {e: 0 for e in ENGS}
        self.seen = {e: {} for e in ENGS}
        self.lastw = {}
        self.rd = {}
        self.dma_i = 0
        self.fam = {}
        import os as _os
        self.nosame = set(x for x in _os.environ.get("KNOSAME", "").split(",") if x)
        self.total = 0
        import os
        self.limit = int(os.environ.get("KLIMIT", "0")) or None

    def _skip(self):
        self.total += 1
        return self.limit is not None and self.total > self.limit

    def _deps(self, eng, reads, writes):
        need = {}

        def add(d):
            if d is None:
                return
            s, v = d
            if need.get(s, 0) < v:
                need[s] = v
        for k0 in reads:
            for k in self._rel(k0):
                add(self.lastw.get(k))
        for k0 in writes:
            for k in self._rel(k0):
                add(self.lastw.get(k))
                for s, v in self.rd.get(k, {}).items():
                    add((s, v))
        out = []
        for s, v in need.items():
            if s == eng and (eng == "pe" or eng in self.nosame):
                continue
            if self.seen[eng].get(s, 0) >= v:
                continue
            self.seen[eng][s] = v
            out.append((s, v))
        return out

    def _rel(self, k):
        if isinstance(k, tuple) and k[0] == "ps":
            fam = self.fam.setdefault(k[1], set())
            fam.add(k)
            if len(k) == 2:
                return list(fam)
            return [k, ("ps", k[1])]
        return [k]

    def _commit(self, reads, writes, tok):
        s, v = tok
        for k in reads:
            d = self.rd.setdefault(k, {})
            if d.get(s, 0) < v:
                d[s] = v
        for k in writes:
            self.lastw[k] = tok
            self.rd[k] = {}

    @staticmethod
    def _norm(reads, writes):
        r2, w2 = [], list(writes)
        for k in reads:
            if isinstance(k, tuple) and k[0] == "ps":
                w2.append(k)
            else:
                r2.append(k)
        w2 = [("ps", k[1]) if (isinstance(k, tuple) and k[0] == "ps") else k for k in w2]
        return r2, w2

    def op(self, eng, fn, reads=(), writes=()):
        if self._skip():
            return
        reads, writes = self._norm(reads, writes)
        waits = self._deps(eng, reads, writes)
        self.cnt[eng] += 1
        self.q[eng].append((waits, fn, (eng, 1)))
        self._commit(reads, writes, (eng, self.cnt[eng]))

    def dma(self, eng, fn, reads=(), writes=()):
        if self._skip():
            return None
        reads = list(reads)
        writes = list(writes)
        i = self.dma_i
        self.dma_i += 1
        sem = "dma%d" % (i % NDMA)
        val = 16 * (i // NDMA + 1)
        waits = self._deps(eng, reads, writes)
        if i >= NDMA:
            prev = 16 * (i // NDMA)
            if self.seen[eng].get(sem, 0) < prev:
                self.seen[eng][sem] = prev
                waits.append((sem, prev))
        self.q[eng].append((waits, fn, (sem, 16)))
        self._commit(reads, writes, (sem, val))
        return (sem, val)

    def cc(self, fn, sem, reads=(), writes=()):
        if self._skip():
            return
        reads, writes = self._norm(reads, writes)
        waits = self._deps("pool", reads, writes)
        self.q["pool"].append((waits, fn, (sem, 1)))
        self._commit(reads, writes, (sem, 1))

    def wait_all(self, eng):
        waits = []
        for e in ENGS:
            if e != eng and self.cnt[e] > self.seen[eng].get(e, 0):
                self.seen[eng][e] = self.cnt[e]
                waits.append((e, self.cnt[e]))
        for j in range(min(NDMA, self.dma_i)):
            sem = "dma%d" % j
            n = (self.dma_i - 1 - j) // NDMA + 1
            if self.seen[eng].get(sem, 0) < 16 * n:
                self.seen[eng][sem] = 16 * n
                waits.append((sem, 16 * n))
        self.q[eng].append((waits, None, None))


class Buf:
    def __init__(self, name, ap):
        self.name = name
        self.ap = ap

    def k(self, *idx):
        return (self.name,) + tuple(idx)


def build(depth=DEPTH, dbg=None, stop_after=None):
    nseg = 1
    nc = bass.Bass("TRN2", target_bir_lowering=False)
    P = Prog()
    dram = {}

    def din(name, shape):
        dram[name] = nc.dram_tensor(name, list(shape), F32, kind="ExternalInput").ap()
        return dram[name]

    xT = din("xT", [DM, T])
    pp = din("pp", [DEPTH, 128, NPCOL])
    cmat = din("cmat", [128, 4, 128])
    ada_w = din("ada_w", [DEPTH, DM, 1536])
    w_in = din("w_in", [DEPTH, DM, 2308])
    w_out = din("w_out", [DEPTH, DM, DM])
    w1 = din("mlp_w1", [DEPTH, DM, 4 * DM])
    w2 = din("mlp_w2", [DEPTH, 4 * DM, DM])
    poolw = din("poolw", [DEPTH, 128, 2, 128])
    gluw = din("gluw", [DEPTH, 128, 2, 256])
    s5b = din("s5b", [DEPTH, 128, 256])
    s5c = din("s5c", [DEPTH, 128, 512])
    yT = nc.dram_tensor("yT", [DM, T], F32, kind="ExternalOutput").ap()
    GRP = [[0, 1, 2, 3], [4, 5, 6, 7]]
    ccd = {}
    for l_ in range(DEPTH):
        for nm_, w_ in (("h", 64), ("x", 16), ("s", 272), ("m", 32)):
            ccd[(nm_, l_, "i")] = nc.dram_tensor("cc%s%di" % (nm_, l_), [128, w_], F32, kind="Internal").ap()
            ccd[(nm_, l_, "o")] = nc.dram_tensor("cc%s%do" % (nm_, l_), [4 * 128, w_], F32, kind="Internal").ap()
    dbg_out = {}
    if dbg:
        for name, shape in dbg.items():
            dbg_out[name] = nc.dram_tensor("dbg_" + name, list(shape), F32, kind="ExternalOutput").ap()

    import contextlib
    es = contextlib.ExitStack()
    with es:
        def sb(name, shape, dt):
            return es.enter_context(nc.sbuf_tensor(name, list(shape), dt))

        hT = sb("hT", [128, KC, T], F32)
        uT = sb("uT", [128, KC, T], BF16)
        Yr = sb("Yr", [128, KC, T], BF16)
        AR = sb("AR", [128, 15360], F32)
        cm = sb("cm", [128, 4, 128], BF16)
        ppS = sb("ppS", [128, DEPTH, NPCOL], F32)
        modT = sb("modT", [128, DEPTH, 48], F32)
        der = sb("der", [128, DEPTH, 64], F32)
        condb = sb("condb", [128, 8], BF16)
        epsT = sb("epsT", [128, 2], F32)
        poolwS = sb("poolwS", [128, DEPTH, 2, 128], BF16)
        gluwS = sb("gluwS", [128, DEPTH, 2, 256], BF16)
        cPool = sb("cPool", [128, DEPTH, 2, 16], F32)
        cSc = sb("cSc", [128, DEPTH, 2, 2], F32)
        cCv = sb("cCv", [128, DEPTH, 6, 3], F32)
        cS = sb("cS", [128, DEPTH, 256], F32)
        cX = sb("cX", [128, DEPTH, 8, 2], F32)
        ssdp = sb("ssdp", [128, DEPTH, 16], F32)
        s5pw = sb("s5pw", [128, DEPTH, 8, 3, 16], F32)
        s5lv = sb("s5lv", [128, DEPTH, 8, 3, 8], F32)
        bbS = sb("bbS", [128, DEPTH, 2, 8, 16], F32)
        tiny = sb("tiny", [128, 896], F32)
        tinyb = sb("tinyb", [128, 128], BF16)

        ps = [es.enter_context(nc.psum_tensor("ps%d" % i, [128, 512], F32)) for i in range(8)]
        sems = {}
        for e in ENGS:
            sems[e] = es.enter_context(nc.semaphore("s_" + e))
        for j in range(NDMA):
            sems["dma%d" % j] = es.enter_context(nc.semaphore("s_dma%d" % j))
        for l_ in range(DEPTH):
            for nm_ in ("h", "x", "s", "m"):
                sems["cc%s%d" % (nm_, l_)] = es.enter_context(nc.semaphore("s_cc%s%d" % (nm_, l_)))

        def allgather(nm_, l_, reads, writes):
            i_, o_ = ccd[(nm_, l_, "i")], ccd[(nm_, l_, "o")]
            P.cc(lambda e: e.collective_compute("AllGather", ALU.bypass, replica_groups=GRP, ins=[i_], outs=[o_]),
                 "cc%s%d" % (nm_, l_), reads, writes)

        ARb = AR[:, :].bitcast(BF16)

        def arf(off, n):
            return AR[:, off:off + n]

        def arb(off, n):
            return ARb[:, 2 * off:2 * (off + n)]

        def ark(off, n):
            return [("ar", b) for b in range(off // 256, (off + n + 255) // 256)]

        ident = cm[:, 0, :]
        ones = cm[:, 1, :]
        tri = cm[:, 2, :]
        negm = cm[:, 3, :]

        def pcol(l, name, a=0, b=None):
            o, w = PCOL[name]
            if b is None:
                b = w
            return ppS[:, l, o + a:o + b]

        def mm(out, lhsT, rhs, start, stop, reads, writes):
            P.op("pe", lambda e: e.matmul(out, lhsT=lhsT, rhs=rhs, start=start, stop=stop), reads, writes)

        def act(out, in_, func, reads, writes, bias=None, scale=None):
            kw = {}
            if bias is not None:
                kw["bias"] = bias
            if scale is not None:
                kw["scale"] = scale
            P.op("act", lambda e: e.activation(out=out, in_=in_, func=func, **kw), reads, writes)

        def tt(out, in0, in1, op, reads, writes, eng="dve"):
            P.op(eng, lambda e: e.tensor_tensor(out=out, in0=in0, in1=in1, op=op), reads, writes)

        def ts(out, in0, s1, s2, op0, op1, reads, writes, eng="dve"):
            if s2 is None:
                P.op(eng, lambda e: e.tensor_scalar(out=out, in0=in0, scalar1=s1, scalar2=None, op0=op0), reads, writes)
            else:
                P.op(eng, lambda e: e.tensor_scalar(out=out, in0=in0, scalar1=s1, scalar2=s2, op0=op0, op1=op1), reads, writes)

        def stt(out, in0, scalar, in1, op0, op1, reads, writes, eng="dve"):
            P.op(eng, lambda e: e.scalar_tensor_tensor(out=out, in0=in0, scalar=scalar, in1=in1, op0=op0, op1=op1), reads, writes)

        def cp(out, in_, reads, writes, eng="dve"):
            P.op(eng, lambda e: e.tensor_copy(out=out, in_=in_), reads, writes)

        def mset(ap, val, writes, eng="dve"):
            P.op(eng, lambda e: e.memset(ap, val), [], writes)

        def dma(eng, out, in_, reads, writes):
            return P.dma(eng, lambda e: e.dma_start(out=out, in_=in_), reads, writes)

        dma("pool", cm[:, :, :], cmat, [], ["cm"])
        dma("sp", ppS[:, :, :], pp.rearrange("l p c -> p l c"), [], ["pp"])
        dma("pool", poolwS[:, :, :, :], poolw.rearrange("l p c d -> p l c d"), [], ["poolw"])
        dma("pool", gluwS[:, :, :, :], gluw.rearrange("l p c d -> p l c d"), [], ["gluw"])
        mset(epsT[:, 0:1], EPS, ["eps"])
        mset(epsT[:, 1:2], 1.0, ["eps"])
        for t_, kk in ((cPool, "cPool"), (cSc, "cSc"), (cCv, "cCv"), (cS, "cS"), (cX, "cX")):
            mset(t_[:], 0.0, [kk])
        act(condb[:, :], pcol(0, "cond"), AF.Silu, ["pp"], ["condb"])

        WADA = 0
        bi = 0
        for l in range(DEPTH):
            for blk in range(3):
                slot = bi % 2
                bi += 1
                wsl = arb(WADA + slot * 2048, 2048).rearrange("p (k c) -> p k c", k=8)
                wk = ark(WADA + slot * 2048, 2048)
                dma("pool", wsl, ada_w[l, :, blk * 512:(blk + 1) * 512].rearrange("(k p) c -> p k c", p=128), [], wk)
                for jj in range(4):
                    j = l * 12 + blk * 4 + jj
                    for k in range(KC):
                        mm(ps[6][:, j:j + 1], wsl[:, k, jj * 128:(jj + 1) * 128], condb[:, k:k + 1], k == 0, k == KC - 1,
                           wk + ["condb"], [("ps", 6)])
        mpk = tiny[:, 0:24].rearrange("p (l c) -> p l c", l=2)
        tyk0 = [("tiny", "all")]
        for l in range(DEPTH):
            tt(mpk[:, l, :], ps[6][:, l * 12:(l + 1) * 12], pcol(l, "adab", 0, 12), ALU.add, [("ps", 6), "pp"], tyk0)
        dma("sp", ccd[("m", 0, "i")][:, 0:24], tiny[:, 0:24], tyk0, [("cci", "m", 0)])
        allgather("m", 0, [("cci", "m", 0)], [("cco", "m", 0)])
        mrb = tiny[:, 32:160].rearrange("p (r f) -> p r f", r=4)
        dma("sp", mrb, ccd[("m", 0, "o")].rearrange("(r p) f -> p r f", p=128), [("cco", "m", 0)], tyk0)
        for l in range(DEPTH):
            for sg_ in range(4):
                cp(modT[:, l, sg_ * 12:(sg_ + 1) * 12], mrb[:, sg_, l * 12:(l + 1) * 12], tyk0, [("mod", l)])
        for l in range(depth):
            stt(der[:, l, 0:8], modT[:, l, 8:16], 1.0, pcol(l, "nw1"), ALU.add, ALU.mult, [("mod", l), "pp"], [("der", l)])
            stt(der[:, l, 24:32], modT[:, l, 32:40], 1.0, pcol(l, "nw2"), ALU.add, ALU.mult, [("mod", l), "pp"], [("der", l)])
            cp(der[:, l, 8:16], modT[:, l, 0:8], [("mod", l)], [("der", l)])
            cp(der[:, l, 16:24], modT[:, l, 16:24], [("mod", l)], [("der", l)])
            cp(der[:, l, 32:40], modT[:, l, 24:32], [("mod", l)], [("der", l)])
            cp(der[:, l, 40:48], modT[:, l, 40:48], [("mod", l)], [("der", l)])

        def S1(l, k): return der[:, l, 0 + k:1 + k]
        def B1(l, k): return der[:, l, 8 + k:9 + k]
        def G1(l, k): return der[:, l, 16 + k:17 + k]
        def S2(l, k): return der[:, l, 24 + k:25 + k]
        def B2(l, k): return der[:, l, 32 + k:33 + k]
        def G2(l, k): return der[:, l, 40 + k:41 + k]

        for l in range(depth):
            act(ssdp[:, l, 0:4], pcol(l, "alog"), AF.Exp, ["pp"], [("ssdp", l)])
            ts(ssdp[:, l, 0:4], ssdp[:, l, 0:4], -1.0, None, ALU.mult, None, [("ssdp", l)], [("ssdp", l)])
            cp(ssdp[:, l, 4:8], pcol(l, "dtb"), ["pp"], [("ssdp", l)])
            cp(ssdp[:, l, 8:12], pcol(l, "dsk"), ["pp"], [("ssdp", l)])

        def tk(n): return [("tiny", n)]
        for l in range(depth):
            tyk = [("tiny", "all")]
            stp = tiny[:, 0:8]
            act(stp, pcol(l, "lst"), AF.Exp, ["pp"], tyk)
            mag = tiny[:, 8:16]
            tt(mag, pcol(l, "are"), stp, ALU.mult, ["pp"] + tyk, tyk)
            act(mag, mag, AF.Exp, tyk, tyk)
            th = tiny[:, 16:24]
            tt(th, pcol(l, "aim"), stp, ALU.mult, ["pp"] + tyk, tyk)
            sa = tiny[:, 24:32]
            ca = tiny[:, 32:40]
            act(sa, th, AF.Sin, tyk, tyk, scale=1.0 / 16.0)
            ts(ca, th, 1.0 / 16.0, float(np.pi / 2), ALU.mult, ALU.add, tyk, tyk)
            act(ca, ca, AF.Sin, tyk, tyk)
            for _ in range(4):
                t2a, t2b = tiny[:, 272:280], tiny[:, 280:288]
                tt(t2a, ca, ca, ALU.mult, tyk, tyk)
                tt(t2b, sa, sa, ALU.mult, tyk, tyk)
                tt(sa, sa, ca, ALU.mult, tyk, tyk)
                ts(sa, sa, 2.0, None, ALU.mult, None, tyk, tyk)
                tt(ca, t2a, t2b, ALU.subtract, tyk, tyk)
            lr = tiny[:, 40:48]
            li = tiny[:, 48:56]
            tt(lr, mag, ca, ALU.mult, tyk, tyk)
            tt(li, mag, sa, ALU.mult, tyk, tyk)
            den = tiny[:, 56:64]
            t0 = tiny[:, 64:72]
            tt(den, pcol(l, "are"), pcol(l, "are"), ALU.mult, ["pp"] + tyk, tyk)
            tt(t0, pcol(l, "aim"), pcol(l, "aim"), ALU.mult, ["pp"] + tyk, tyk)
            tt(den, den, t0, ALU.add, tyk, tyk)
            P.op("dve", lambda e, den=den: e.reciprocal(out=den, in_=den), tyk, tyk)
            nr = tiny[:, 72:80]
            ts(nr, lr, -1.0, None, ALU.add, None, tyk, tyk)
            fr = tiny[:, 80:88]
            fi = tiny[:, 88:96]
            t1 = tiny[:, 96:104]
            tt(fr, nr, pcol(l, "are"), ALU.mult, ["pp"] + tyk, tyk)
            tt(t1, li, pcol(l, "aim"), ALU.mult, ["pp"] + tyk, tyk)
            tt(fr, fr, t1, ALU.add, tyk, tyk)
            tt(fr, fr, den, ALU.mult, tyk, tyk)
            tt(fi, li, pcol(l, "are"), ALU.mult, ["pp"] + tyk, tyk)
            tt(t1, nr, pcol(l, "aim"), ALU.mult, ["pp"] + tyk, tyk)
            tt(fi, fi, t1, ALU.subtract, tyk, tyk)
            tt(fi, fi, den, ALU.mult, tyk, tyk)
            dma("sp", tiny[:, 512:768], s5b[l], [], tyk)
            bre = tiny[:, 512:640].rearrange("p (r h) -> p r h", r=8)
            bim = tiny[:, 640:768].rearrange("p (r h) -> p r h", r=8)
            frb = fr.unsqueeze(2).to_broadcast([128, 8, 16])
            fib = fi.unsqueeze(2).to_broadcast([128, 8, 16])
            tmpb = tiny[:, 128:256].rearrange("p (r h) -> p r h", r=8)
            tt(bbS[:, l, 0, :, :], bre, frb, ALU.mult, ["pp"] + tyk, [("bb", l)])
            tt(tmpb, bim, fib, ALU.mult, ["pp"] + tyk, tyk)
            tt(bbS[:, l, 0, :, :], bbS[:, l, 0, :, :], tmpb, ALU.subtract, tyk + [("bb", l)], [("bb", l)])
            tt(bbS[:, l, 1, :, :], bim, frb, ALU.mult, ["pp"] + tyk, [("bb", l)])
            tt(tmpb, bre, fib, ALU.mult, ["pp"] + tyk, tyk)
            tt(bbS[:, l, 1, :, :], bbS[:, l, 1, :, :], tmpb, ALU.add, tyk + [("bb", l)], [("bb", l)])
            ts(bbS[:, l, 1, :, :], bbS[:, l, 1, :, :], -1.0, None, ALU.mult, None, [("bb", l)], [("bb", l)])
            ts(li, li, -1.0, None, ALU.mult, None, tyk, tyk)
            pk = [("s5pw", l)]
            cp(s5pw[:, l, :, 0, 0], lr, tyk, pk)
            cp(s5pw[:, l, :, 1, 0], li, tyk, pk)
            for j in range(1, 16):
                pr_, pi_ = s5pw[:, l, :, 0, j - 1], s5pw[:, l, :, 1, j - 1]
                nr_, ni_ = s5pw[:, l, :, 0, j], s5pw[:, l, :, 1, j]
                ta, tb_ = tiny[:, 256:264], tiny[:, 264:272]
                tt(ta, pr_, lr, ALU.mult, tyk + pk, tyk)
                tt(tb_, pi_, li, ALU.mult, tyk + pk, tyk)
                tt(nr_, ta, tb_, ALU.subtract, tyk, pk)
                tt(ta, pr_, li, ALU.mult, tyk + pk, tyk)
                tt(tb_, pi_, lr, ALU.mult, tyk + pk, tyk)
                tt(ni_, ta, tb_, ALU.add, tyk, pk)
            ts(s5pw[:, l, :, 2, :], s5pw[:, l, :, 1, :], -1.0, None, ALU.mult, None, pk, pk)
            lk = [("s5lv", l)]
            cp(s5lv[:, l, :, 0, 0], s5pw[:, l, :, 0, 15], pk, lk)
            cp(s5lv[:, l, :, 1, 0], s5pw[:, l, :, 1, 15], pk, lk)
            for k in range(1, 8):
                pr_, pi_ = s5lv[:, l, :, 0, k - 1], s5lv[:, l, :, 1, k - 1]
                ta, tb_ = tiny[:, 256:264], tiny[:, 264:272]
                tt(ta, pr_, pr_, ALU.mult, lk + tyk, tyk)
                tt(tb_, pi_, pi_, ALU.mult, lk + tyk, tyk)
                tt(s5lv[:, l, :, 0, k], ta, tb_, ALU.subtract, tyk, lk)
                tt(ta, pr_, pi_, ALU.mult, lk + tyk, tyk)
                ts(s5lv[:, l, :, 1, k], ta, 2.0, None, ALU.mult, None, tyk, lk)
            ts(s5lv[:, l, :, 2, :], s5lv[:, l, :, 1, :], -1.0, None, ALU.mult, None, lk, lk)

        acc_i = [0]

        def next_acc():
            b = acc_i[0] % 4
            acc_i[0] += 1
            return b

        def rms_stats(src_fn, nchunks, denom, rstd_off, sq_off, src_keys_fn):
            for tb in range(NTB):
                bank = 4 + (tb % 2)
                for k in range(nchunks):
                    so = sq_off + ((tb * nchunks + k) % 4) * 256
                    sq = arb(so, 256)
                    src = src_fn(k, tb)
                    tt(sq, src, src, ALU.mult, src_keys_fn(k, tb), ark(so, 256), eng="pool")
                    mm(ps[bank][:, :], ones, sq, k == 0, k == nchunks - 1, ark(so, 256) + ["cm"], [("ps", bank)])
                r = arf(rstd_off + tb * TB, TB)
                rk = ark(rstd_off + tb * TB, TB)
                act(r, ps[bank][:, :], AF.Ln, [("ps", bank), "eps"], rk, bias=epsT[:, 0:1], scale=1.0 / denom)
                act(r, r, AF.Exp, rk, rk, scale=-0.5)

        def norm_mod(l, s_fn, b_fn):
            RSTD, SQ, TMP = 0, 2048, 3072
            rms_stats(lambda k, tb: hT[:, k, tb * TB:(tb + 1) * TB], KC, float(DM), RSTD, SQ,
                      lambda k, tb: [("hT", k, tb)])
            for k in range(KC):
                for tb in range(NTB):
                    to = TMP + ((k * NTB + tb) % 4) * TB
                    tmp = arf(to, TB)
                    stt(tmp, hT[:, k, tb * TB:(tb + 1) * TB], s_fn(l, k), arf(RSTD + tb * TB, TB), ALU.mult, ALU.mult,
                        [("hT", k, tb), ("der", l)] + ark(RSTD + tb * TB, TB), ark(to, TB))
                    act(uT[:, k, tb * TB:(tb + 1) * TB], tmp, AF.Identity, ark(to, TB) + [("der", l)], [("uT", k, tb)],
                        bias=b_fn(l, k))

        WIN = 14336
        win_i = [0]

        def load_win(l, c0, ncol):
            assert ncol <= 128
            slot = win_i[0] % 2
            win_i[0] += 1
            off = WIN + slot * 512
            w = arb(off, 512).rearrange("p (k c) -> p k c", k=8)
            dma("pool", w[:, :, 0:ncol], w_in[l, :, c0:c0 + ncol].rearrange("(k p) c -> p k c", p=128), [], ark(off, 512))
            return w, ark(off, 512)

        def proj_chunk(w, wk, cofs, ncol, evac):
            for tb in range(NTB):
                b = next_acc()
                for k in range(KC):
                    mm(ps[b][0:ncol, :], w[:, k, cofs:cofs + ncol], uT[:, k, tb * TB:(tb + 1) * TB], k == 0, k == KC - 1,
                       wk + [("uT", k, tb)], [("ps", b)])
                evac(tb, ps[b][0:ncol, :], ("ps", b))

        def dump(name, ap, keys):
            if dbg and name in dbg_out:
                dma("sp", dbg_out[name], ap, keys, [("dbg", name)])

        def all_keys_h():
            return [("hT", k, tb) for k in range(KC) for tb in range(NTB)]

        def all_keys(nm):
            return [(nm, k, tb) for k in range(KC) for tb in range(NTB)]

        def tail_prepass(l):
            PK = arf(0, 64)
            pkk = ark(0, 64)
            gct = tiny[:, 0:32].rearrange("p (c t) -> p c t", c=2)
            tyk = [("tiny", "all")]
            tail = slice(T - 16, T)

            def tproj(w, wk, cofs, evac):
                b = next_acc()
                for k in range(KC):
                    mm(ps[b][:, 0:16], w[:, k, cofs:cofs + 128], uT[:, k, tail], k == 0, k == KC - 1, wk + [("uT", k, 3)], [("ps", b)])
                evac(ps[b][:, 0:16], ("ps", b))
            for c in range(2):
                w, wk = load_win(l, c * 128, 128)
                tproj(w, wk, 0, lambda p_, pk, c=c: act(PK[:, c * 16:(c + 1) * 16], p_, AF.Copy, [pk], pkk))
            for c in range(2):
                w, wk = load_win(l, 512 + c * 128, 128)
                tproj(w, wk, 0, lambda p_, pk, c=c: act(gct[:, c, :], p_, AF.Copy, [pk], tyk))
            for c in range(2):
                w, wk = load_win(l, 768 + c * 128, 128)
                tproj(w, wk, 0, lambda p_, pk, c=c: tt(PK[:, 32 + 2 * c:34 + 2 * c], p_[:, 14:16], gct[:, c, 14:16], ALU.mult,
                                                       [pk] + tyk, pkk))
            for j in range(6):
                w, wk = load_win(l, 1280 + j * 128, 128)
                tproj(w, wk, 0, lambda p_, pk, j=j: act(PK[:, 36 + 3 * j:39 + 3 * j], p_[:, 13:16], AF.Copy, [pk], pkk))
            dma("sp", ccd[("h", l, "i")][:, 0:54], PK[:, 0:54], pkk, [("cci", "h", l)])
            allgather("h", l, [("cci", "h", l)], [("cco", "h", l)])

        def halo_apply(l):
            RB = arf(256, 256).rearrange("p (r f) -> p r f", r=4)
            rbk = ark(256, 256)
            tyk = [("tiny", "all")]
            dma("sp", RB, ccd[("h", l, "o")].rearrange("(r p) f -> p r f", p=128), [("cco", "h", l)], rbk)
            hal = tiny[:, 64:128]
            sel = pcol(l, "sel")
            ts(hal, RB[:, 0, :], sel[:, 0:1], None, ALU.mult, None, rbk + ["pp"], tyk)
            for j in range(1, 4):
                stt(hal, RB[:, j, :], sel[:, j:j + 1], hal, ALU.mult, ALU.add, rbk + ["pp"] + tyk, tyk)
            cp(cPool[:, l, :, :], hal[:, 0:32].rearrange("p (c t) -> p c t", c=2), tyk, [("cPool", l, 0), ("cPool", l, 1)])
            cp(cSc[:, l, :, :], hal[:, 32:36].rearrange("p (c t) -> p c t", c=2), tyk, [("cSc", l, 0), ("cSc", l, 1)])
            cp(cCv[:, l, :, :], hal[:, 36:54].rearrange("p (c t) -> p c t", c=6), tyk, [("cCv", l, j) for j in range(6)])

        def s5_combine(l):
            tyk = [("tiny", "all")]
            RB = tiny[:, 816:880].rearrange("p (r f) -> p r f", r=4)
            dma("sp", RB, ccd[("x", l, "o")].rearrange("(r p) f -> p r f", p=128), [("cco", "x", l)], tyk)
            sel = pcol(l, "sel")
            Lr, Li = s5lv[:, l, :, 0, 7], s5lv[:, l, :, 1, 7]
            lk = [("s5lv", l)]
            ar, ai, pr, pi, t1, t2 = (tiny[:, a:a + 8] for a in (768, 776, 784, 792, 880, 888))
            for t_ in (ar, ai, pr, pi):
                mset(t_, 0.0, tyk)
            for j in range(4):
                stt(ar, pr, sel[:, 4 + j:5 + j], ar, ALU.mult, ALU.add, tyk + ["pp"], tyk)
                stt(ai, pi, sel[:, 4 + j:5 + j], ai, ALU.mult, ALU.add, tyk + ["pp"], tyk)
                if j < 3:
                    F = RB[:, j, :].rearrange("p (r a) -> p r a", a=2)
                    tt(t1, pr, Lr, ALU.mult, tyk + lk, tyk)
                    tt(t2, pi, Li, ALU.mult, tyk + lk, tyk)
                    tt(t1, t1, t2, ALU.subtract, tyk, tyk)
                    tt(t2, pr, Li, ALU.mult, tyk + lk, tyk)
                    tt(pr, t1, F[:, :, 0], ALU.add, tyk, tyk)
                    tt(t1, pi, Lr, ALU.mult, tyk + lk, tyk)
                    tt(t1, t1, t2, ALU.add, tyk, tyk)
                    tt(pi, t1, F[:, :, 1], ALU.add, tyk, tyk)
            cp(cX[:, l, :, 0], ar, tyk, [("cX", l, r) for r in range(8)])
            cp(cX[:, l, :, 1], ai, tyk, [("cX", l, r) for r in range(8)])

        def mixer_pool(l, seg):
            pbv = Yr[:, 4:6, :]
            for c in range(2):
                V, SA, SB = c * 6912, c * 6912 + 2304, c * 6912 + 4608
                w, wk = load_win(l, c * 128, 128)
                v = arf(V, 2064)
                vk = ark(V, 2064)
                cp(v[:, 0:16], cPool[:, l, c, :], [("cPool", l, c)], vk)
                proj_chunk(w, wk, 0, 128,
                           lambda tb, p_, pk: act(v[:, 16 + tb * TB:16 + (tb + 1) * TB], p_, AF.Copy, [pk], vk))
                cp(cPool[:, l, c, :], v[:, 2048:2064], vk, [("cPool", l, c)])
                sa, sbb = arf(SA, 2064), arf(SB, 2064)
                sak, sbk = ark(SA, 2064), ark(SB, 2064)
                tt(sa[:, 1:2064], v[:, 1:2064], v[:, 0:2063], ALU.add, vk, sak)
                tt(sbb[:, 3:2064], sa[:, 3:2064], sa[:, 1:2062], ALU.add, sak, sbk)
                if c == 0:
                    lo_src, hi_src, lo_w, hi_w = sa, sbb, 2, 4
                else:
                    tt(sa[:, 7:2064], sbb[:, 7:2064], sbb[:, 3:2060], ALU.add, sbk, sak)
                    tt(sbb[:, 15:2064], sa[:, 15:2064], sa[:, 7:2056], ALU.add, sak, sbk)
                    lo_src, hi_src, lo_w, hi_w = sa, sbb, 8, 16
                pb = pbv
                pbk = [("Yr", 4 + c, t_) for t_ in range(NTB)]
                stt(pb[0:64, c, :], lo_src[0:64, 16:2064], 1.0 / lo_w, v[0:64, 16:2064], ALU.mult, ALU.subtract, sak + sbk + vk, pbk)
                stt(pb[64:128, c, :], hi_src[64:128, 16:2064], 1.0 / hi_w, v[64:128, 16:2064], ALU.mult, ALU.subtract, sak + sbk + vk, pbk)
                if True:
                    ic = pcol(l, "invc").rearrange("p (c t) -> p c t", c=2)
                    tq = tiny[:, 512:528]
                    for (r0, r1, src) in ((0, 64, lo_src), (64, 128, hi_src)):
                        tt(tq[r0:r1, :], src[r0:r1, 16:32], ic[r0:r1, c, :], ALU.mult, sak + sbk + ["pp"], [("tiny", "all")])
                        tt(pb[r0:r1, c, 0:16], tq[r0:r1, :], v[r0:r1, 16:32], ALU.subtract, [("tiny", "all")] + vk, pbk)
            pb = pbv
            for c in range(2):
                for tb in range(NTB):
                    b = next_acc()
                    mm(ps[b][:, :], poolwS[:, l, c, :], pb[:, c, tb * TB:(tb + 1) * TB], True, True, [("Yr", 4 + c, tb), "poolw"], [("ps", b)])
                    ts(Yr[:, c, tb * TB:(tb + 1) * TB], ps[b][:, :], pcol(l, "pscale", c, c + 1), None, ALU.mult, None,
                       [("ps", b), "pp"], [("Yr", c, tb)])

        def mixer_sconv(l, seg):
            for c in range(2):
                GC, G, T1 = c * 6400, c * 6400 + 2048, c * 6400 + 4352
                gc = arf(GC, 2048)
                gck = ark(GC, 2048)
                wgc, wgck = load_win(l, 512 + c * 128, 128)
                proj_chunk(wgc, wgck, 0, 128,
                           lambda tb, p_, pk: act(gc[:, tb * TB:(tb + 1) * TB], p_, AF.Copy, [pk], gck))
                whh, whhk = load_win(l, 768 + c * 128, 128)
                g = arf(G, 2050)
                gk = ark(G, 2050)
                cp(g[:, 0:2], cSc[:, l, c, :], [("cSc", l, c)], gk)
                proj_chunk(whh, whhk, 0, 128,
                           lambda tb, p_, pk: tt(g[:, 2 + tb * TB:2 + (tb + 1) * TB], p_, gc[:, tb * TB:(tb + 1) * TB], ALU.mult,
                                                 [pk] + gck, gk))
                cp(cSc[:, l, c, :], g[:, 2048:2050], gk, [("cSc", l, c)])
                t1 = arf(T1, 2048)
                t1k = ark(T1, 2048)
                wv = pcol(l, "scw").rearrange("p (c k) -> p c k", c=2)
                ts(t1, g[:, 0:2048], wv[:, c, 0:1], None, ALU.mult, None, gk + ["pp"], t1k)
                stt(t1, g[:, 1:2049], wv[:, c, 1:2], t1, ALU.mult, ALU.add, gk + ["pp"] + t1k, t1k)
                stt(t1, g[:, 2:2050], wv[:, c, 2:3], t1, ALU.mult, ALU.add, gk + ["pp"] + t1k, t1k)
                wgb, wgbk = load_win(l, 256 + c * 128, 128)
                proj_chunk(wgb, wgbk, 0, 128,
                           lambda tb, p_, pk: tt(Yr[:, 2 + c, tb * TB:(tb + 1) * TB], p_, t1[:, tb * TB:(tb + 1) * TB], ALU.mult,
                                                 [pk] + t1k, [("Yr", 2 + c, tb)]))

        def mixer_ssd(l, seg):
            SZ, RAW, ACC, XBC = 0, 2048, 4352, 6400
            XTOK, BTOK = 2048, 4096
            sz = arb(SZ, 2048).rearrange("p (c t) -> p c t", c=2)
            szk = ark(SZ, 2048)
            for c in range(2):
                wz, wzk = load_win(l, 1024 + c * 128, 128)
                proj_chunk(wz, wzk, 0, 128,
                           lambda tb, p_, pk: act(sz[:, c, tb * TB:(tb + 1) * TB], p_, AF.Silu, [pk], szk))
            xbc = arb(XBC, 6144).rearrange("p (c t) -> p c t", c=6)
            kxb = [ark(XBC + j * 1024, 1024) for j in range(6)]
            cw = pcol(l, "cvw").rearrange("p (c k) -> p c k", c=6)
            for j in range(6):
                wx, wxk = load_win(l, 1280 + j * 128, 128)
                raw = arf(RAW, 2051)
                rawk = ark(RAW, 2051)
                cp(raw[:, 0:3], cCv[:, l, j, :], [("cCv", l, j)], rawk)
                proj_chunk(wx, wxk, 0, 128,
                           lambda tb, p_, pk: act(raw[:, 3 + tb * TB:3 + (tb + 1) * TB], p_, AF.Copy, [pk], rawk))
                cp(cCv[:, l, j, :], raw[:, 2048:2051], rawk, [("cCv", l, j)])
                acc = arf(ACC, 2048)
                acck = ark(ACC, 2048)
                ts(acc, raw[:, 0:2048], cw[:, j, 0:1], None, ALU.mult, None, rawk + ["pp"], acck)
                for kk in range(1, 4):
                    stt(acc, raw[:, kk:kk + 2048], cw[:, j, kk:kk + 1], acc, ALU.mult, ALU.add, rawk + acck + ["pp"], acck)
                act(xbc[:, j, :], acc, AF.Silu, acck + ["pp"], kxb[j], bias=pcol(l, "cvb", j, j + 1))
            print("  ssd: before dt", P.total)
            wd, wdk = load_win(l, 2048, 4)
            for c in range(NCH):
                for k in range(KC):
                    mm(ps[6][:, c * 4:(c + 1) * 4], uT[:, k, c * 128:(c + 1) * 128], wd[:, k, 0:4], k == 0, k == KC - 1,
                       wdk + [("uT", k, c // 4)], [("ps", 6)])
            print("  ssd: before small", P.total)
            tyk = [("tiny", "all")]
            def v3(a): return tiny[:, a:a + 64].rearrange("p (c h) -> p c h", h=4)
            dt_, adt, acs, tot, eacs, dte, cd, ddte = (v3(a) for a in (0, 64, 128, 192, 256, 320, 384, 448))
            xsp = v3(512)
            ex = v3(576)
            bc4 = lambda a, b: ssdp[:, l, a:b].unsqueeze(1).to_broadcast([128, NCH, 4])
            tt(xsp, ps[6][:, 0:64].rearrange("p (c h) -> p c h", h=4), bc4(4, 8), ALU.add, [("ps", 6), ("ssdp", l)], tyk)
            ts(ex, xsp, 30.0, None, ALU.min, None, tyk, tyk)
            act(ex, ex, AF.Exp, tyk, tyk)
            act(ex, ex, AF.Ln, tyk + ["eps"], tyk, bias=epsT[:, 1:2])
            tt(dt_, ex, xsp, ALU.max, tyk, tyk)
            tt(adt, dt_, bc4(0, 4), ALU.mult, tyk + [("ssdp", l)], tyk)
            ahi = tinyb[:, 0:64]
            alo = tinyb[:, 64:128]
            tbk = [("tinyb", "a")]
            adf = tiny[:, 64:128]
            cp(ahi, adf, tyk, tbk)
            tt(tiny[:, 640:704], adf, ahi, ALU.subtract, tyk + tbk, tyk)
            cp(alo, tiny[:, 640:704], tyk, tbk)
            cp(tiny[:, 768:832], ahi, tbk, tyk)
            mm(ps[6][:, 64:128], tri, ahi, True, False, tbk + ["cm"], [("ps", 6)])
            mm(ps[6][:, 64:128], tri, alo, False, True, tbk + ["cm"], [("ps", 6)])
            mm(ps[6][:, 128:192], ones, ahi, True, False, tbk + ["cm"], [("ps", 6)])
            mm(ps[6][:, 128:192], ones, alo, False, True, tbk + ["cm"], [("ps", 6)])
            cp(tiny[:, 128:256], ps[6][:, 64:192], [("ps", 6)], tyk)
            act(tiny[:, 256:320], tiny[:, 128:192], AF.Exp, tyk, tyk)
            tt(tiny[:, 320:384], tiny[:, 192:256], tiny[:, 128:192], ALU.subtract, tyk, tyk)
            act(tiny[:, 320:384], tiny[:, 320:384], AF.Exp, tyk, tyk)
            act(tiny[:, 384:448], tiny[:, 192:256], AF.Exp, tyk, tyk)
            tt(tiny[:, 448:512], tiny[:, 0:64], tiny[:, 320:384], ALU.mult, tyk, tyk)
            ts(tiny[:, 704:768], tiny[:, 128:192], -1.0, None, ALU.mult, None, tyk, tyk)
            nacs = v3(704)
            print("  ssd: before transposes", P.total)
            xtok = arb(XTOK, 2048).rearrange("p (c f) -> p c f", c=NCH)
            btok = arb(BTOK, 2048).rearrange("p (c f) -> p c f", c=NCH)
            xtk, btk = ark(XTOK, 2048), ark(BTOK, 2048)
            ti = 0
            for c in range(NCH):
                for j in range(4):
                    o = (ti % 4) * 128
                    ti += 1
                    bnk = 4 + (ti - 1) % 4
                    pk_ = ("ps", bnk)
                    mm(ps[bnk][:, 0:128], xbc[:, j, c * 128:(c + 1) * 128], ident, True, True, kxb[j] + ["cm"], [pk_])
                    dst = (xtok if j < 2 else btok)[:, c, (j % 2) * 128:(j % 2 + 1) * 128]
                    if ti % 2 == 0:
                        cp(dst, ps[bnk][:, 0:128], [pk_], xtk if j < 2 else btk)
                    else:
                        act(dst, ps[bnk][:, 0:128], AF.Copy, [pk_], xtk if j < 2 else btk)
            WT = XBC
            E_ = arb(WT, 256).rearrange("p (h s) -> p h s", h=4)
            MT = arb(WT + 256, 256).rearrange("p (h s) -> p h s", h=4)
            RH = arb(WT + 512, 512).rearrange("p (a h s) -> p a h s", a=2, h=4)
            XDT = arb(WT + 1024, 128)
            XDE = arb(WT + 1152, 128)
            YSB = arf(WT + 1280, 256)
            YTK = arb(WT + 1536, 128)
            SBF = arb(WT + 1664, 128)
            STMP = arf(WT + 1792, 256)
            kE, kMT, kRH, kXDT, kXDE, kYSB, kYTK, kSBF, kST = (ark(WT + a, n) for a, n in
                ((0, 256), (256, 256), (512, 512), (1024, 128), (1152, 128), (1280, 256), (1536, 128), (1664, 128), (1792, 256)))
            print("  ssd: before main loop", P.total)
            Sst = cS[:, l, :]
            kS = [("cS", l)]
            PKG, RBO, PTO = 12544, 12816, 13904
            pkg = arf(PKG, 272)
            pkgk = ark(PKG, 272)
            SL = pkg[:, 0:256]
            mset(SL, 0.0, pkgk)
            for c in range(NCH):
                x3 = xtok[:, c, :].rearrange("p (h d) -> p h d", h=4)
                tt(XDE.rearrange("p (h d) -> p h d", h=4), x3, ddte[:, c, :].unsqueeze(2).to_broadcast([128, 4, 64]), ALU.mult, xtk + tyk, kXDE)
                for g in range(2):
                    mm(ps[5][:, g * 128:(g + 1) * 128], btok[:, c, g * 128:(g + 1) * 128], XDE[:, g * 128:(g + 1) * 128], True, True,
                       btk + kXDE, [("ps", 5)])
                tt(STMP.rearrange("p (h d) -> p h d", h=4), SL.rearrange("p (h d) -> p h d", h=4),
                   cd[:, c, :].unsqueeze(2).to_broadcast([128, 4, 64]), ALU.mult, pkgk + tyk, kST)
                tt(SL, STMP, ps[5][:, 0:256], ALU.add, kST + [("ps", 5)], pkgk)
            tr_ = tiny[:, 832:864].rearrange("p (c h) -> p c h", h=4)
            tt(tr_, tot[:, 0:8, :], tot[:, 8:16, :], ALU.add, tyk, tyk)
            tt(tr_[:, 0:4, :], tr_[:, 0:4, :], tr_[:, 4:8, :], ALU.add, tyk, tyk)
            tt(tr_[:, 0:2, :], tr_[:, 0:2, :], tr_[:, 2:4, :], ALU.add, tyk, tyk)
            tt(tr_[:, 0:1, :], tr_[:, 0:1, :], tr_[:, 1:2, :], ALU.add, tyk, tyk)
            act(pkg[:, 256:260], tiny[:, 832:836], AF.Exp, tyk, pkgk)
            dma("sp", ccd[("s", l, "i")][:, 0:260], pkg[:, 0:260], pkgk, [("cci", "s", l)])
            allgather("s", l, [("cci", "s", l)], [("cco", "s", l)])
            RBs = arf(RBO, 1088).rearrange("p (r f) -> p r f", r=4)
            rbsk = ark(RBO, 1088)
            dma("sp", RBs, ccd[("s", l, "o")].rearrange("(r p) f -> p r f", p=128), [("cco", "s", l)], rbsk)
            PT = arf(PTO, 256)
            ptk = ark(PTO, 256)
            sel = pcol(l, "sel")
            mset(Sst, 0.0, kS)
            mset(PT, 0.0, ptk)
            for j in range(4):
                stt(Sst, PT, sel[:, 4 + j:5 + j], Sst, ALU.mult, ALU.add, ptk + kS + ["pp"], kS)
                if j < 3:
                    tt(PT.rearrange("p (h d) -> p h d", h=4), PT.rearrange("p (h d) -> p h d", h=4),
                       RBs[:, j, 256:260].unsqueeze(2).to_broadcast([128, 4, 64]), ALU.mult, ptk + rbsk, ptk)
                    tt(PT, PT, RBs[:, j, 0:256], ALU.add, ptk + rbsk, ptk)
            cp(SBF, Sst, kS, kSBF)
            for c in range(NCH):
                tsl = slice(c * 128, (c + 1) * 128)
                if c < 2:
                    print("  ssd: chunk", c, P.total)
                for h in range(4):
                    o = ps[0][:, h * 128:(h + 1) * 128]
                    mm(o, tinyb[:, c * 4 + h:c * 4 + h + 1].to_broadcast([128, 128]), tri, True, False, tbk + ["cm"], [("ps", 0)])
                    mm(o, tinyb[:, 64 + c * 4 + h:64 + c * 4 + h + 1].to_broadcast([128, 128]), tri, False, False, tbk + ["cm"], [("ps", 0)])
                    mm(o, ident, negm, False, True, ["cm"], [("ps", 0)])
                for h in range(4):
                    act(E_[:, h, :], ps[0][:, h * 128:(h + 1) * 128], AF.Exp, [("ps", 0)] + tyk, kE, bias=nacs[:, c, h:h + 1])
                for g in range(2):
                    mm(ps[1][:, g * 128:(g + 1) * 128], xbc[:, 2 + g, tsl], xbc[:, 4 + g, tsl], True, True,
                       kxb[2 + g] + kxb[4 + g], [("ps", 1)])
                for h in range(4):
                    g = h // 2
                    tt(MT[:, h, :], ps[1][:, g * 128:(g + 1) * 128], E_[:, h, :], ALU.mult, [("ps", 1)] + kE, kMT)
                x3 = xtok[:, c, :].rearrange("p (h d) -> p h d", h=4)
                tt(XDT.rearrange("p (h d) -> p h d", h=4), x3, dt_[:, c, :].unsqueeze(2).to_broadcast([128, 4, 64]), ALU.mult, xtk + tyk, kXDT, eng="pool")
                tt(XDE.rearrange("p (h d) -> p h d", h=4), x3, ddte[:, c, :].unsqueeze(2).to_broadcast([128, 4, 64]), ALU.mult, xtk + tyk, kXDE, eng="pool")
                for h in range(4):
                    o = ps[2][:, h * 64:(h + 1) * 64]
                    mm(o, MT[:, h, :], XDT[:, h * 64:(h + 1) * 64], True, True, kMT + kXDT, [("ps", 2, "y")])
                for h in range(4):
                    g = h // 2
                    mm(ps[3][:, h * 64:(h + 1) * 64], xbc[:, 4 + g, tsl], SBF[:, h * 64:(h + 1) * 64], True, True,
                       kxb[4 + g] + kSBF, [("ps", 3)])
                tt(YSB.rearrange("p (h d) -> p h d", h=4), x3, ssdp[:, l, 8:12].unsqueeze(2).to_broadcast([128, 4, 64]), ALU.mult,
                   xtk + [("ssdp", l)], kYSB)
                tt(YSB, YSB, ps[2][:, 0:256], ALU.add, kYSB + [("ps", 2, "y")], kYSB)
                tt(STMP.rearrange("p (h d) -> p h d", h=4), ps[3][:, 0:256].rearrange("p (h d) -> p h d", h=4),
                   eacs[:, c, :].unsqueeze(2).to_broadcast([128, 4, 64]), ALU.mult, [("ps", 3)] + tyk, kST)
                tt(YTK, STMP, YSB, ALU.add, kST + kYSB, kYTK)
                for j in range(2):
                    o = j * 128
                    mm(ps[4][:, o:o + 128], YTK[:, j * 128:(j + 1) * 128], ident, True, True, kYTK + ["cm"], [("ps", 4, o)])
                    tt(Yr[:, 4 + j, tsl], ps[4][:, o:o + 128], sz[:, j, tsl], ALU.mult, [("ps", 4, o)] + szk, [("Yr", 4 + j, c // 4)])
                for g in range(2):
                    mm(ps[5][:, g * 128:(g + 1) * 128], btok[:, c, g * 128:(g + 1) * 128], XDE[:, g * 128:(g + 1) * 128], True, True,
                       btk + kXDE, [("ps", 5)])
                tt(STMP.rearrange("p (h d) -> p h d", h=4), Sst.rearrange("p (h d) -> p h d", h=4),
                   cd[:, c, :].unsqueeze(2).to_broadcast([128, 4, 64]), ALU.mult, kS + tyk, kST)
                tt(Sst, STMP, ps[5][:, 0:256], ALU.add, kST + [("ps", 5)], kS)
                act(SBF, Sst, AF.Copy, kS, kSBF)

        def mixer_s5(l, seg):
            U, STG, BT, W, WB, PAT, INJ, TAB = 0, 2048, 4096, 6144, 8192, 10240, 11264, 13312
            tyk = [("tiny", "all")]
            pk = [("s5pw", l)]
            lk = [("s5lv", l)]
            u = arb(U, 2048).rearrange("p (c t) -> p c t", c=2)
            uk = ark(U, 2048)
            for c in range(2):
                wu, wuk = load_win(l, 2052 + c * 128, 128)
                proj_chunk(wu, wuk, 0, 128,
                           lambda tb, p_, pk_: act(u[:, c, tb * TB:(tb + 1) * TB], p_, AF.Copy, [pk_], uk))
            pat = arb(PAT, 1024)
            patk = ark(PAT, 1024)
            mset(pat, 1.0, patk)
            mset(pat.rearrange("p (c j) -> p c j", j=16)[:, :, 0:1], 0.0, patk)
            pw0 = tiny[:, 0:256].rearrange("p (r a j) -> p r a j", r=8, a=2)
            qq = tiny[:, 256:512].rearrange("p (r a j) -> p r a j", r=8, a=2)
            den = tiny[:, 512:640].rearrange("p (r j) -> p r j", r=8)
            tmp = tiny[:, 640:768].rearrange("p (r j) -> p r j", r=8)
            mset(pw0[:, :, 0, 0:1], 1.0, tyk)
            mset(pw0[:, :, 1, 0:1], 0.0, tyk)
            for a in range(2):
                cp(pw0[:, :, a, 1:16], s5pw[:, l, :, a, 0:15], pk, tyk)
            tt(den, pw0[:, :, 0, :], pw0[:, :, 0, :], ALU.mult, tyk, tyk)
            tt(tmp, pw0[:, :, 1, :], pw0[:, :, 1, :], ALU.mult, tyk, tyk)
            tt(den, den, tmp, ALU.add, tyk, tyk)
            P.op("dve", lambda e: e.reciprocal(out=den, in_=den), tyk, tyk)
            tt(qq[:, :, 0, :], pw0[:, :, 0, :], den, ALU.mult, tyk, tyk)
            tt(qq[:, :, 1, :], pw0[:, :, 1, :], den, ALU.mult, tyk, tyk)
            ts(qq[:, :, 1, :], qq[:, :, 1, :], -1.0, None, ALU.mult, None, tyk, tyk)
            bpad = arf(TAB, 512).rearrange("p (r a h) -> p r a h", r=8, a=2)
            cpad = arf(TAB + 512, 512).rearrange("p (r a h) -> p r a h", r=8, a=2)
            tabk = ark(TAB, 1024)
            mset(arf(TAB, 512), 0.0, tabk)
            for a in range(2):
                cp(bpad[0:64, :, a, 0:16], bbS[0:64, l, a, :, :], [("bb", l)], tabk)
                cp(bpad[64:128, :, a, 16:32], bbS[64:128, l, a, :, :], [("bb", l)], tabk)
            dma("sp", arf(TAB + 512, 512), s5c[l], [], tabk)
            Ef = Yr[:, 6:8, :].rearrange("p a t -> p (a t)").bitcast(F32)
            Eall = Ef.rearrange("p (r a c) -> p r a c", r=8, a=2)
            ek = [("Yr", 6 + a, tb) for a in range(2) for tb in range(NTB)]
            bufA = arf(W, 2048).rearrange("p (r a c) -> p r a c", r=8, a=2)
            bufB = arf(WB, 2048).rearrange("p (r a c) -> p r a c", r=8, a=2)
            kA, kB = ark(W, 2048), ark(WB, 2048)
            T1 = arf(STG, 2048)
            T2 = arf(BT, 2048)
            kT1, kT2 = ark(STG, 2048), ark(BT, 2048)

            def bc8(ap, n):
                return ap.unsqueeze(2).to_broadcast([128, 8, n])

            def lvl2(src, sk):
                cur, ck_ = src, sk
                nxt_list = [(bufA, kA), (bufB, kB)]
                if src is bufA:
                    nxt_list = [(bufB, kB), (bufA, kA)]
                for lev in range(7):
                    d = 1 << lev
                    n = 128 - d
                    dst, dk = nxt_list[lev % 2]
                    cr, ci = s5lv[:, l, :, 0, lev], s5lv[:, l, :, 1, lev]
                    t1 = T1[:, 0:8 * n].rearrange("p (r c) -> p r c", r=8)
                    t2 = T2[:, 0:8 * n].rearrange("p (r c) -> p r c", r=8)
                    cp(dst[:, :, :, 0:d], cur[:, :, :, 0:d], ck_, dk)
                    tt(t1, cur[:, :, 0, 0:n], bc8(cr, n), ALU.mult, ck_ + lk, kT1)
                    tt(t2, cur[:, :, 1, 0:n], bc8(ci, n), ALU.mult, ck_ + lk, kT2)
                    tt(t1, t1, t2, ALU.subtract, kT1 + kT2, kT1)
                    tt(dst[:, :, 0, d:128], cur[:, :, 0, d:128], t1, ALU.add, ck_ + kT1, dk)
                    tt(t1, cur[:, :, 1, 0:n], bc8(cr, n), ALU.mult, ck_ + lk, kT1)
                    tt(t2, cur[:, :, 0, 0:n], bc8(ci, n), ALU.mult, ck_ + lk, kT2)
                    tt(t1, t1, t2, ALU.add, kT1 + kT2, kT1)
                    tt(dst[:, :, 1, d:128], cur[:, :, 1, d:128], t1, ALU.add, ck_ + kT1, dk)
                    cur, ck_ = dst, dk
                return cur, ck_

            def run_pass(mode):
                inj = arf(INJ, 2048).rearrange("p (r a c) -> p r a c", r=8, a=2)
                injk = ark(INJ, 2048)
                if mode == "full":
                    cp(bufA, Eall, ek, kA)
                    mur, mui = s5lv[:, l, :, 0, 0], s5lv[:, l, :, 1, 0]
                    xr_, xi_ = cX[:, l, :, 0], cX[:, l, :, 1]
                    ckx = [("cX", l, r) for r in range(8)]
                    a1, a2 = tiny[:, 768:776], tiny[:, 776:784]
                    tt(a1, mur, xr_, ALU.mult, lk + ckx, tyk)
                    tt(a2, mui, xi_, ALU.mult, lk + ckx, tyk)
                    tt(a1, a1, a2, ALU.subtract, tyk, tyk)
                    tt(bufA[:, :, 0, 0], bufA[:, :, 0, 0], a1, ALU.add, kA + tyk, kA)
                    tt(a1, mur, xi_, ALU.mult, lk + ckx, tyk)
                    tt(a2, mui, xr_, ALU.mult, lk + ckx, tyk)
                    tt(a1, a1, a2, ALU.add, tyk, tyk)
                    tt(bufA[:, :, 1, 0], bufA[:, :, 1, 0], a1, ALU.add, kA + tyk, kA)
                    res, rk = lvl2(bufA, kA)
                    lr8, li8 = s5pw[:, l, :, 0, 0], s5pw[:, l, :, 1, 0]
                    n = 127
                    t1 = T1[:, 0:8 * n].rearrange("p (r c) -> p r c", r=8)
                    t2 = T2[:, 0:8 * n].rearrange("p (r c) -> p r c", r=8)
                    tt(t1, res[:, :, 0, 0:n], bc8(lr8, n), ALU.mult, rk + pk, kT1)
                    tt(t2, res[:, :, 1, 0:n], bc8(li8, n), ALU.mult, rk + pk, kT2)
                    tt(inj[:, :, 0, 1:128], t1, t2, ALU.subtract, kT1 + kT2, injk)
                    tt(t1, res[:, :, 1, 0:n], bc8(lr8, n), ALU.mult, rk + pk, kT1)
                    tt(t2, res[:, :, 0, 0:n], bc8(li8, n), ALU.mult, rk + pk, kT2)
                    tt(inj[:, :, 1, 1:128], t1, t2, ALU.add, kT1 + kT2, injk)
                    tt(a1, lr8, xr_, ALU.mult, pk + ckx, tyk)
                    tt(a2, li8, xi_, ALU.mult, pk + ckx, tyk)
                    tt(inj[:, :, 0, 0], a1, a2, ALU.subtract, tyk, injk)
                    tt(a1, lr8, xi_, ALU.mult, pk + ckx, tyk)
                    tt(a2, li8, xr_, ALU.mult, pk + ckx, tyk)
                    tt(inj[:, :, 1, 0], a1, a2, ALU.add, tyk, injk)

                stg = arb(STG, 2048).rearrange("p (a j c) -> p a j c", a=2, j=16)
                stgk = ark(STG, 2048)
                btv = arb(BT, 2048).rearrange("p (a j c) -> p a j c", a=2, j=16)
                btk_ = ark(BT, 2048)
                Wn = arf(W, 2048)
                wk_ = ark(W, 2048)
                Wv = Wn.rearrange("p (c j) -> p j c", j=16)
                wb = arb(WB, 2048).rearrange("p (a t) -> p a t", a=2)
                wbk = ark(WB, 2048)

                def b16(ap):
                    return ap.unsqueeze(2).to_broadcast([128, 16, 32])

                def h32(ap):
                    return ap.unsqueeze(1).to_broadcast([128, 16, 32])

                ct = Yr[:, 4:6, :].rearrange("p a (j c) -> p a j c", j=16)
                ctk = [("Yr", 4 + a_, t_) for a_ in range(2) for t_ in range(NTB)]
                if mode == "p1":
                    tmp_f, tmpk = arf(WB, 1024), ark(WB, 1024)
                    W1n, w1k_ = arf(INJ, 2048), ark(INJ, 2048)
                else:
                    tmp_f = Yr[:, 7, :].bitcast(F32)
                    tmpk = [("Yr", 7, t_) for t_ in range(NTB)]
                    W1n = Yr[:, 0:2, :].rearrange("p a t -> p (a t)").bitcast(F32)
                    w1k_ = [("Yr", a_, t_) for a_ in range(2) for t_ in range(NTB)]
                Wbufs = [(Wn, wk_), (W1n, w1k_)]
                t1 = tmp_f[:, 0:512].rearrange("p (j h) -> p j h", j=16)
                t2 = tmp_f[:, 512:1024].rearrange("p (j h) -> p j h", j=16)
                mset(arb(STG, 2048), 0.0, stgk)
                if mode == "full":
                    mset(Yr[:, 4:6, :], 0.0, ctk)

                ptf = Yr[:, 2, :].bitcast(F32)
                ptk = [("Yr", 2, t_) for t_ in range(NTB)]
                p1_ = ptf[:, 0:512].rearrange("p (j h) -> p j h", j=16)
                p2_ = ptf[:, 512:1024].rearrange("p (j h) -> p j h", j=16)

                def tables_b_dve(r):
                    q = r % 4
                    if r > 0:
                        pq = (r - 1) % 4
                        mset(stg[:, :, :, pq * 32:(pq + 1) * 32], 0.0, stgk, eng="pool")
                    Bre, Bim = h32(bpad[:, r, 0, :]), h32(bpad[:, r, 1, :])
                    qr_, qi_ = b16(qq[:, r, 0, :]), b16(qq[:, r, 1, :])
                    sv = stg[:, :, :, q * 32:(q + 1) * 32]
                    tt(p1_, Bre, qr_, ALU.mult, tabk + tyk, ptk, eng="pool")
                    tt(p2_, Bim, qi_, ALU.mult, tabk + tyk, ptk, eng="pool")
                    tt(sv[:, 0], p1_, p2_, ALU.subtract, ptk, stgk, eng="pool")
                    tt(p1_, Bim, qr_, ALU.mult, tabk + tyk, ptk, eng="pool")
                    tt(p2_, Bre, qi_, ALU.mult, tabk + tyk, ptk, eng="pool")
                    tt(sv[:, 1], p1_, p2_, ALU.add, ptk, stgk, eng="pool")

                def tables_b_pe(r):
                    for a in range(2):
                        for lg in range(4):
                            bnk = 4 + lg
                            for li_ in range(4):
                                mm(ps[bnk][:, li_ * 128:(li_ + 1) * 128], stg[:, a, lg * 4 + li_, :], ident, True, True,
                                   stgk + ["cm"], [("ps", bnk)])
                            act(btv[:, a, lg * 4:(lg + 1) * 4, :], ps[bnk][:, :].rearrange("p (j c) -> p j c", j=4), AF.Copy,
                                [("ps", bnk)], btk_)

                def ct_dve(r):
                    q = r % 4
                    if r > 0:
                        pq = (r - 1) % 4
                        mset(ct[:, :, :, pq * 32:(pq + 1) * 32], 0.0, ctk)
                    Cre, Cim = h32(cpad[:, r, 0, :]), h32(cpad[:, r, 1, :])
                    pr_, pi_ = b16(pw0[:, r, 0, :]), b16(pw0[:, r, 1, :])
                    cv = ct[:, :, :, q * 32:(q + 1) * 32]
                    tt(t1, Cre, pr_, ALU.mult, tabk + tyk, tmpk)
                    tt(t2, Cim, pi_, ALU.mult, tabk + tyk, tmpk)
                    tt(cv[:, 0], t1, t2, ALU.add, tmpk, ctk)
                    tt(t1, Cim, pr_, ALU.mult, tabk + tyk, tmpk)
                    tt(t2, Cre, pi_, ALU.mult, tabk + tyk, tmpk)
                    tt(cv[:, 1], t1, t2, ALU.subtract, tmpk, ctk)

                def bu_mm(r, a, wv_, wkk):
                    uv = u[:, r // 4, :].rearrange("p (c j) -> p j c", j=16)
                    for lg in range(4):
                        bnk = 4 + lg
                        for li_ in range(4):
                            j = lg * 4 + li_
                            mm(ps[bnk][:, li_ * 128:(li_ + 1) * 128], btv[:, a, j, :], uv[:, j, :], True, True, btk_ + uk, [("ps", bnk)])
                        act(wv_[:, lg * 4:(lg + 1) * 4, :], ps[bnk][:, :].rearrange("p (j c) -> p j c", j=4), AF.Copy, [("ps", bnk)], wkk)

                def scan(wn_, wkk):
                    P.op("dve", lambda e, wn_=wn_: e.tensor_tensor_scan(out=wn_, data0=pat, data1=wn_, initial=0.0, op0=ALU.mult, op1=ALU.add),
                         wkk + patk, wkk)

                tables_b_dve(0)
                tables_b_pe(0)
                for r in range(8):
                    oc = r // 4
                    q = r % 4
                    uv = u[:, oc, :].rearrange("p (c j) -> p j c", j=16)
                    wvs = [(wn_.rearrange("p (c j) -> p j c", j=16), wn_, kk_) for (wn_, kk_) in Wbufs]
                    if mode == "p1":
                        bu_mm(r, 0, wvs[0][0], wvs[0][2])
                        bu_mm(r, 1, wvs[1][0], wvs[1][2])
                        if r + 1 < 8:
                            tables_b_dve(r + 1)
                        for a in range(2):
                            scan(wvs[a][1], wvs[a][2])
                            cp(Eall[:, r, a, :], wvs[a][0][:, 15, :], wvs[a][2], ek)
                        if r + 1 < 8:
                            tables_b_pe(r + 1)
                        continue
                    bu_mm(r, 0, wvs[0][0], wvs[0][2])
                    bu_mm(r, 1, wvs[1][0], wvs[1][2])
                    ct_dve(r)
                    if r + 1 < 8:
                        tables_b_dve(r + 1)
                    for a in range(2):
                        wv_, wn_, wkk = wvs[a]
                        tt(wv_[:, 0, :], wv_[:, 0, :], inj[:, r, a, :], ALU.add, wkk + injk, wkk)
                        scan(wn_, wkk)
                        act(wb[:, a, :], wn_, AF.Copy, wkk, wbk)
                    if r + 1 < 8:
                        tables_b_pe(r + 1)
                    wbv = [wb[:, a, :].rearrange("p (c j) -> p j c", j=16) for a in range(2)]
                    for j in range(16):
                        o = ps[j // 4][:, (j % 4) * 128:(j % 4 + 1) * 128]
                        P.op("pe", lambda e, o=o, j=j, q=q, wbv=wbv: e.matmul(o, lhsT=ct[:, 0, j, :], rhs=wbv[0][:, j, :],
                                                                             start=(q == 0 and j % 4 == 0), stop=False, skip_group_check=True),
                             ctk + wbk, [("ps", j // 4)])
                        P.op("pe", lambda e, o=o, j=j, q=q, wbv=wbv: e.matmul(o, lhsT=ct[:, 1, j, :], rhs=wbv[1][:, j, :], start=False,
                                                                             stop=(q == 3), skip_group_check=True), ctk + wbk, [("ps", j // 4)])
                    if q == 3:
                        for tb in range(NTB):
                            yo = W + (tb % 2) * 512
                            yt = arf(yo, 512)
                            ytk = ark(yo, 512)
                            uv4 = uv[:, tb * 4:(tb + 1) * 4, :]
                            stt(yt.rearrange("p (j c) -> p j c", j=4), uv4, pcol(l, "s5d", oc, oc + 1),
                                ps[tb][:, :].rearrange("p (j c) -> p j c", j=4), ALU.mult, ALU.add, uk + ["pp", ("ps", tb)], ytk)
                            act(Yr[:, 6 + oc, :].rearrange("p (c j) -> p j c", j=16)[:, tb * 4:(tb + 1) * 4, :],
                                yt.rearrange("p (j c) -> p j c", j=4), AF.Gelu_apprx_tanh, ytk, [("Yr", 6 + oc, t_) for t_ in range(NTB)])
                if mode == "p1":
                    p15r, p15i = s5pw[:, l, :, 0, 14], s5pw[:, l, :, 1, 14]
                    n = 128
                    t1 = T1[:, 0:1024].rearrange("p (r c) -> p r c", r=8)
                    t2 = T1[:, 1024:2048].rearrange("p (r c) -> p r c", r=8)
                    t3 = T2[:, 0:1024].rearrange("p (r c) -> p r c", r=8)
                    t4 = T2[:, 1024:2048].rearrange("p (r c) -> p r c", r=8)
                    tt(t1, Eall[:, :, 0, :], bc8(p15r, n), ALU.mult, ek + pk, kT1)
                    tt(t2, Eall[:, :, 1, :], bc8(p15i, n), ALU.mult, ek + pk, kT1)
                    tt(t3, Eall[:, :, 1, :], bc8(p15r, n), ALU.mult, ek + pk, kT2)
                    tt(t4, Eall[:, :, 0, :], bc8(p15i, n), ALU.mult, ek + pk, kT2)
                    tt(Eall[:, :, 0, :], t1, t2, ALU.subtract, kT1, ek)
                    tt(Eall[:, :, 1, :], t3, t4, ALU.add, kT2, ek)
                    res, rk = lvl2(Eall, ek)
                    cp(tiny[:, 800:816].rearrange("p (r a) -> p r a", a=2), res[:, :, :, 127], rk, tyk)
                    dma("sp", ccd[("x", l, "i")], tiny[:, 800:816], tyk, [("cci", "x", l)])
                    allgather("x", l, [("cci", "x", l)], [("cco", "x", l)])
                    return

            run_pass("p1")
            s5_combine(l)
            run_pass("full")
            SG = W
            for tb in range(NTB):
                sgs = []
                for m in range(2):
                    b = next_acc()
                    for k in range(2):
                        mm(ps[b][:, :], gluwS[:, l, k, m * 128:(m + 1) * 128], Yr[:, 6 + k, tb * TB:(tb + 1) * TB], k == 0, k == 1,
                           [("Yr", 6 + k, tb), "gluw"], [("ps", b)])
                    so = SG + ((tb * 2 + m) % 4) * 256
                    sg = arb(so, 256)
                    act(sg, ps[b][:, :], AF.Sigmoid, [("ps", b), "pp"], ark(so, 256), bias=pcol(l, "glub", m, m + 1))
                    sgs.append((sg, ark(so, 256)))
                for m in range(2):
                    sg, sgk = sgs[m]
                    tt(Yr[:, 6 + m, tb * TB:(tb + 1) * TB], Yr[:, 6 + m, tb * TB:(tb + 1) * TB], sg, ALU.mult,
                       [("Yr", 6 + m, tb)] + sgk, [("Yr", 6 + m, tb)])

        W1O = [7168, 9216]
        W2O = [11264, 13312]
        NG = 8

        def mlp_load(l, g):
            sl = g % 2
            w1g = arb(W1O[sl], 2048).rearrange("p (k c) -> p k c", k=8)
            w2g = arb(W2O[sl], 2048).rearrange("p (k c) -> p k c", k=4)
            w1k, w2k = ark(W1O[sl], 2048), ark(W2O[sl], 2048)
            dma("pool", w1g, w1[l, :, g * 512:(g + 1) * 512].rearrange("(k p) c -> p k c", p=128), [], w1k)
            dma("pool", w2g, w2[l, g * 512:(g + 1) * 512, :].rearrange("(k p) c -> p k c", p=128), [], w2k)

        def out_proj(l):
            WO, RSTD, SQ = 0, 4096, 6144
            wo = arb(WO, 4096).rearrange("p (k c) -> p k c", k=8)
            wok = ark(WO, 4096)
            dma("pool", wo, w_out[l, :, :].rearrange("(k p) c -> p k c", p=128), [], wok)
            mlp_load(l, 0)
            mlp_load(l, 1)
            for g in range(4):
                rms_stats(lambda k, tb: Yr[:, 2 * g + k, tb * TB:(tb + 1) * TB], 2, 256.0, RSTD, SQ,
                          lambda k, tb: [("Yr", 2 * g + k, tb)])
                for k in range(2):
                    ch = 2 * g + k
                    for tb in range(NTB):
                        stt(uT[:, ch, tb * TB:(tb + 1) * TB], Yr[:, ch, tb * TB:(tb + 1) * TB], pcol(l, "bnw", ch, ch + 1),
                            arf(RSTD + tb * TB, TB), ALU.mult, ALU.mult, [("Yr", ch, tb), "pp"] + ark(RSTD + tb * TB, TB), [("uT", ch, tb)])
            for m in range(KC):
                for tb in range(NTB):
                    b = next_acc()
                    for k in range(KC):
                        mm(ps[b][:, :], wo[:, k, m * 128:(m + 1) * 128], uT[:, k, tb * TB:(tb + 1) * TB], k == 0, k == KC - 1,
                           wok + [("uT", k, tb)], [("ps", b)])
                    stt(hT[:, m, tb * TB:(tb + 1) * TB], ps[b][:, :], G1(l, m), hT[:, m, tb * TB:(tb + 1) * TB], ALU.mult, ALU.add,
                        [("ps", b), ("der", l), ("hT", m, tb)], [("hT", m, tb)])

        def mlp(l):
            norm_mod(l, S2, B2)
            RS = 5120

            def up(g):
                sl = g % 2
                w1g = arb(W1O[sl], 2048).rearrange("p (k c) -> p k c", k=8)
                w1k = ark(W1O[sl], 2048)
                for jc in range(4):
                    yc = sl * 4 + jc
                    for tb in range(NTB):
                        b = next_acc()
                        for k in range(KC):
                            mm(ps[b][:, :], w1g[:, k, jc * 128:(jc + 1) * 128], uT[:, k, tb * TB:(tb + 1) * TB], k == 0, k == KC - 1,
                               w1k + [("uT", k, tb)], [("ps", b)])
                        ro = RS + ((jc * NTB + tb) % 4) * 256
                        rs = arb(ro, 256)
                        act(rs, ps[b][:, :], AF.Relu, [("ps", b)], ark(ro, 256))
                        tt(Yr[:, yc, tb * TB:(tb + 1) * TB], rs, rs, ALU.mult, ark(ro, 256), [("Yr", yc, tb)], eng="pool")

            def down(g):
                sl = g % 2
                w2g = arb(W2O[sl], 2048).rearrange("p (k c) -> p k c", k=4)
                w2k = ark(W2O[sl], 2048)
                for m in range(KC):
                    for tb in range(NTB):
                        b = next_acc()
                        for jc in range(4):
                            mm(ps[b][:, :], w2g[:, jc, m * 128:(m + 1) * 128], Yr[:, sl * 4 + jc, tb * TB:(tb + 1) * TB], jc == 0, jc == 3,
                               w2k + [("Yr", sl * 4 + jc, tb)], [("ps", b)])
                        stt(hT[:, m, tb * TB:(tb + 1) * TB], ps[b][:, :], G2(l, m), hT[:, m, tb * TB:(tb + 1) * TB], ALU.mult, ALU.add,
                            [("ps", b), ("der", l), ("hT", m, tb)], [("hT", m, tb)])

            up(0)
            for g in range(NG):
                if g + 1 < NG:
                    up(g + 1)
                down(g)
                if g + 2 < NG:
                    mlp_load(l, g + 2)

        def final_out(seg):
            RSTD, SQ, OUT = 0, 2048, 4096
            rms_stats(lambda k, tb: hT[:, k, tb * TB:(tb + 1) * TB], KC, float(DM), RSTD, SQ, lambda k, tb: [("hT", k, tb)])
            for k in range(KC):
                oo = OUT + (k % 2) * 2048
                o = arf(oo, 2048)
                for tb in range(NTB):
                    stt(o[:, tb * TB:(tb + 1) * TB], hT[:, k, tb * TB:(tb + 1) * TB], pcol(0, "fnw", k, k + 1), arf(RSTD + tb * TB, TB),
                        ALU.mult, ALU.mult, [("hT", k, tb), "pp"] + ark(RSTD + tb * TB, TB), ark(oo, 2048))
                dma("sp", yT[k * 128:(k + 1) * 128, :], o, ark(oo, 2048), [("yT", k, seg)])

        stopped = False
        for seg in range(nseg):
            if stopped:
                break
            for k in range(KC):
                dma("sp", hT[:, k, :], xT[k * 128:(k + 1) * 128, :], [], [("hT", k, tb) for tb in range(NTB)])
            for l in range(depth):
                norm_mod(l, S1, B1)
                if stop_after == (seg, l, "u"):
                    stopped = True
                    break
                print("ops before tail", P.total)
                tail_prepass(l)
                print("ops before s5 p1", P.total)
                mixer_s5(l, seg)
                print("ops before halo", P.total)
                halo_apply(l)
                mixer_pool(l, seg)
                mixer_sconv(l, seg)
                print("ops before ssd", P.total)
                mixer_ssd(l, seg)
                print("ops after ssd", P.total)
                if stop_after == (seg, l, "mix"):
                    stopped = True
                    break
                out_proj(l)
                if stop_after == (seg, l, "hmix"):
                    stopped = True
                    break
                mlp(l)
                if stop_after == (seg, l, "h"):
                    stopped = True
                    break
            if not stopped:
                final_out(seg)
        print("total ops recorded:", P.total)
        P.limit = None
        if dbg:
            if "uT" in dbg_out:
                cp(AR[:, 0:2048], uT[:, 0, :], all_keys("uT"), ark(0, 2048))
            for name in dbg_out:
                if name == "Yr":
                    for k in range(KC):
                        o = arf((k % 2) * 2048, 2048)
                        cp(o, Yr[:, k, :], [("Yr", k, tb) for tb in range(NTB)], ark((k % 2) * 2048, 2048))
                        dma("sp", dbg_out[name][k * 128:(k + 1) * 128, :], o, ark((k % 2) * 2048, 2048), [("dbg", name, k)])
                elif name == "uT":
                    for k in range(KC):
                        o = arf((k % 2) * 2048, 2048)
                        cp(o, uT[:, k, :], [("uT", k, tb) for tb in range(NTB)], ark((k % 2) * 2048, 2048))
                        dma("sp", dbg_out[name][k * 128:(k + 1) * 128, :], o, ark((k % 2) * 2048, 2048), [("dbg", name, k)])
                elif name == "hT":
                    for k in range(KC):
                        dma("sp", dbg_out[name][k * 128:(k + 1) * 128, :], hT[:, k, :], [("hT", k, tb) for tb in range(NTB)], [("dbg", name, k)])
                elif name == "mod":
                    dma("sp", dbg_out[name], modT[:, :, :].rearrange("p l c -> p (l c)"), [("mod", 0), ("mod", 1)], [("dbg", name)])
        P.wait_all("sp")

        with nc.Block() as block:
            def replay(e, name):
                for waits, fn, inc in P.q[name]:
                    for s, v in waits:
                        e.wait_ge(sems[s], v)
                    if fn is not None:
                        fn(e).then_inc(sems[inc[0]], inc[1])

            @block.tensor
            def _(e):
                replay(e, "pe")

            @block.scalar
            def _(e):
                replay(e, "act")

            @block.vector
            def _(e):
                replay(e, "dve")

            @block.gpsimd
            def _(e):
                replay(e, "pool")

            @block.sync
            def _(e):
                replay(e, "sp")
    return nc


def _fm(v):
    return np.ascontiguousarray(v.reshape(-1, 128).T)


def _pack_params(inp, b, sg):
    L = DEPTH
    pp = np.zeros((L, 128, NPCOL), np.float32)

    def put(l, name, arr):
        o, w = PCOL[name]
        arr = np.asarray(arr, np.float32).reshape(128, w)
        pp[l, :, o:o + w] = arr
    wins = (2, 4, 8, 16)
    for l in range(L):
        put(l, "nw1", _fm(inp["norm_mix_w"][l]))
        put(l, "nw2", _fm(inp["norm_mlp_w"][l]))
        put(l, "bnw", _fm(inp["branch_norm_w"][l]))
        put(l, "fnw", _fm(inp["final_norm_w"]))
        adab = np.zeros((128, 48), np.float32)
        adab[:, 0:12] = _fm(inp["ada_b"][l])[:, sg * 12:(sg + 1) * 12]
        put(l, "adab", adab)
        put(l, "pscale", _fm(inp["pool_scale"][l]))
        put(l, "scw", inp["sconv_w"][l].reshape(3, 2, 128).transpose(2, 1, 0))
        put(l, "cvw", inp["ssd_conv_w"][l].reshape(4, 6, 128).transpose(2, 1, 0))
        put(l, "cvb", _fm(inp["ssd_conv_b"][l]))
        put(l, "dtb", np.broadcast_to(inp["ssd_dt_bias"][l][None, :], (128, 4)))
        put(l, "alog", np.broadcast_to(inp["ssd_a_log"][l][None, :], (128, 4)))
        put(l, "dsk", np.broadcast_to(inp["ssd_d"][l][None, :], (128, 4)))
        def gp(a):
            return a.reshape(8, 2, 64).transpose(1, 2, 0).reshape(128, 8)
        put(l, "are", gp(inp["s5_a_re"][l]))
        put(l, "aim", gp(inp["s5_a_im"][l]))
        put(l, "lst", gp(np.broadcast_to(inp["s5_log_step"][l][:, None], (16, 64))))
        put(l, "s5d", _fm(inp["s5_d"][l]))
        put(l, "glub", _fm(inp["s5_glu_b"][l]))
        put(l, "cond", _fm(inp["c"][b]))
        invc = np.zeros((128, 2, 16), np.float32)
        for c in range(2):
            for half in range(2):
                win = wins[c * 2 + half]
                if sg == 0:
                    invc[half * 64:(half + 1) * 64, c, :] = 1.0 / np.minimum(np.arange(16) + 1, win)
                else:
                    invc[half * 64:(half + 1) * 64, c, :] = 1.0 / win
        put(l, "invc", invc)
        sel = np.zeros((128, 8), np.float32)
        if sg > 0:
            sel[:, sg - 1] = 1.0
        sel[:, 4 + sg] = 1.0
        put(l, "sel", sel)
    return pp


def _consts():
    cm = np.zeros((128, 4, 128), np.float32)
    i = np.arange(128)
    cm[:, 0, :] = (i[:, None] == i[None, :])
    cm[:, 1, :] = 1.0
    cm[:, 2, :] = (i[:, None] <= i[None, :])
    cm[:, 3, :] = np.where(i[None, :] < i[:, None], -30000.0, 0.0)
    return cm


def _host_inputs(inp, b, sg):
    f = lambda a: np.ascontiguousarray(np.asarray(a, np.float32))
    pw = np.zeros((DEPTH, 128, 2, 128), np.float32)
    for l in range(DEPTH):
        for g in range(4):
            c, half = g // 2, g % 2
            pw[l, half * 64:(half + 1) * 64, c, half * 64:(half + 1) * 64] = inp["pool_w"][l, g]
    gw = np.ascontiguousarray(np.asarray(inp["s5_glu_w"], np.float32).reshape(DEPTH, 2, 128, 256).transpose(0, 2, 1, 3))
    s5b = np.zeros((DEPTH, 128, 256), np.float32)
    s5c = np.zeros((DEPTH, 128, 8, 2, 32), np.float32)
    for l in range(DEPTH):
        s5b[l, :, 0:128] = np.asarray(inp["s5_b_re"][l]).reshape(8, 2, 64, 16).transpose(1, 2, 0, 3).reshape(128, 128)
        s5b[l, :, 128:256] = np.asarray(inp["s5_b_im"][l]).reshape(8, 2, 64, 16).transpose(1, 2, 0, 3).reshape(128, 128)
        for ri, nm in enumerate(("s5_c_re", "s5_c_im")):
            cc = np.asarray(inp[nm][l]).reshape(8, 2, 16, 64)
            for r in range(8):
                for gi in range(2):
                    s5c[l, gi * 64:(gi + 1) * 64, r, ri, gi * 16:gi * 16 + 16] = cc[r, gi].T
    return {
        "s5b": s5b, "s5c": s5c.reshape(DEPTH, 128, 512),
        "xT": f(np.asarray(inp["x"][b][sg * T:(sg + 1) * T]).T),
        "pp": _pack_params(inp, b, sg),
        "cmat": _consts(),
        "ada_w": f(np.asarray(inp["ada_w"])[:, :, sg * 1536:(sg + 1) * 1536]), "w_in": f(inp["w_in"]), "w_out": f(inp["w_out"]),
        "mlp_w1": f(inp["mlp_w1"]), "mlp_w2": f(inp["mlp_w2"]),
        "poolw": pw, "gluw": gw,
    }


_NC_CACHE = {}


def kernel(**inputs):
    inp = {k: np.asarray(v) for k, v in inputs.items()}
    if "full" not in _NC_CACHE:
        _NC_CACHE["full"] = build()
    nc = _NC_CACHE["full"]
    in_maps = [_host_inputs(inp, r // 4, r % 4) for r in range(8)]
    res = run_bass_kernel_spmd(nc, in_maps, core_ids=list(range(8)))
    out = np.empty((2, SEQ, DM), np.float32)
    for r in range(8):
        out[r // 4, (r % 4) * T:(r % 4 + 1) * T, :] = res.results[r]["yT"].T
    return out
```

```python
import numpy as np
import concourse.bass as bass
import concourse.mybir as mybir
from concourse.bass_utils import run_bass_kernel_spmd

F32, BF16 = mybir.dt.float32, mybir.dt.bfloat16
AF = mybir.ActivationFunctionType
ALU = mybir.AluOpType

T = 2048
TB = 512
NTB = 4
NCH = 16
DM = 1024
KC = 8
SEQ = 8192
DEPTH = 2
EPS = 1e-6
ENGS = ("pe", "act", "dve", "pool", "sp")
NDMA = 12

PCOL = {}
_off = 0
for _n, _w in [("nw1", 8), ("nw2", 8), ("bnw", 8), ("fnw", 8), ("adab", 48), ("pscale", 2), ("scw", 6),
               ("cvw", 24), ("cvb", 6), ("dtb", 4), ("alog", 4), ("dsk", 4), ("are", 8), ("aim", 8), ("lst", 8),
               ("s5d", 2), ("glub", 2), ("cond", 8),
               ("invc", 32), ("sel", 8)]:
    PCOL[_n] = (_off, _w)
    _off += _w
NPCOL = _off


class Prog:
    def __init__(self):
        self.q = {e: [] for e in ENGS}
        self.cnt = {e: 0 for e in ENGS}
        self.seen = {e: {} for e in ENGS}
        self.lastw = {}
        self.rd = {}
        self.dma_i = 0
        self.fam = {}
        import os as _os
        self.nosame = set(x for x in _os.environ.get("KNOSAME", "").split(",") if x)
        self.total = 0
        import os
        self.limit = int(os.environ.get("KLIMIT", "0")) or None

    def _skip(self):
        self.total += 1
        return self.limit is not None and self.total > self.limit

    def _deps(self, eng, reads, writes):
        need = {}

        def add(d):
            if d is None:
                return
            s, v = d
            if need.get(s, 0) < v:
                need[s] = v
        for k0 in reads:
            for k in self._rel(k0):
                add(self.lastw.get(k))
        for k0 in writes:
            for k in self._rel(k0):
                add(self.lastw.get(k))
                for s, v in self.rd.get(k, {}).items():
                    add((s, v))
        out = []
        for s, v in need.items():
            if s == eng and (eng == "pe" or eng in self.nosame):
                continue
            if self.seen[eng].get(s, 0) >= v:
                continue
            self.seen[eng][s] = v
            out.append((s, v))
        return out

    def _rel(self, k):
        if isinstance(k, tuple) and k[0] == "ps":
            fam = self.fam.setdefault(k[1], set())
            fam.add(k)
            if len(k) == 2:
                return list(fam)
            return [k, ("ps", k[1])]
        return [k]

    def _commit(self, reads, writes, tok):
        s, v = tok
        for k in reads:
            d = self.rd.setdefault(k, {})
            if d.get(s, 0) < v:
                d[s] = v
        for k in writes:
            self.lastw[k] = tok
            self.rd[k] = {}

    @staticmethod
    def _norm(reads, writes):
        r2, w2 = [], list(writes)
        for k in reads:
            if isinstance(k, tuple) and k[0] == "ps":
                w2.append(k)
            else:
                r2.append(k)
        w2 = [("ps", k[1]) if (isinstance(k, tuple) and k[0] == "ps") else k for k in w2]
        return r2, w2

    def op(self, eng, fn, reads=(), writes=()):
        if self._skip():
            return
        reads, writes = self._norm(reads, writes)
        waits = self._deps(eng, reads, writes)
        self.cnt[eng] += 1
        self.q[eng].append((waits, fn, (eng, 1)))
        self._commit(reads, writes, (eng, self.cnt[eng]))

    def dma(self, eng, fn, reads=(), writes=()):
        if self._skip():
            return None
        reads = list(reads)
        writes = list(writes)
        i = self.dma_i
        self.dma_i += 1
        sem = "dma%d" % (i % NDMA)
        val = 16 * (i // NDMA + 1)
        waits = self._deps(eng, reads, writes)
        if i >= NDMA:
            prev = 16 * (i // NDMA)
            if self.seen[eng].get(sem, 0) < prev:
                self.seen[eng][sem] = prev
                waits.append((sem, prev))
        self.q[eng].append((waits, fn, (sem, 16)))
        self._commit(reads, writes, (sem, val))
        return (sem, val)

    def cc(self, fn, sem, reads=(), writes=()):
        if self._skip():
            return
        reads, writes = self._norm(reads, writes)
        waits = self._deps("pool", reads, writes)
        self.q["pool"].append((waits, fn, (sem, 1)))
        self._commit(reads, writes, (sem, 1))

    def wait_all(self, eng):
        waits = []
        for e in ENGS:
            if e != eng and self.cnt[e] > self.seen[eng].get(e, 0):
                self.seen[eng][e] = self.cnt[e]
                waits.append((e, self.cnt[e]))
        for j in range(min(NDMA, self.dma_i)):
            sem = "dma%d" % j
            n = (self.dma_i - 1 - j) // NDMA + 1
            if self.seen[eng].get(sem, 0) < 16 * n:
                self.seen[eng][sem] = 16 * n
                waits.append((sem, 16 * n))
        self.q[eng].append((waits, None, None))


class Buf:
    def __init__(self, name, ap):
        self.name = name
        self.ap = ap

    def k(self, *idx):
        return (self.name,) + tuple(idx)


def build(depth=DEPTH, dbg=None, stop_after=None):
    nseg = 1
    nc = bass.Bass("TRN2", target_bir_lowering=False)
    P = Prog()
    dram = {}

    def din(name, shape):
        dram[name] = nc.dram_tensor(name, list(shape), F32, kind="ExternalInput").ap()
        return dram[name]

    xT = din("xT", [DM, T])
    pp = din("pp", [DEPTH, 128, NPCOL])
    cmat = din("cmat", [128, 4, 128])
    ada_w = din("ada_w", [DEPTH, DM, 1536])
    w_in = din("w_in", [DEPTH, DM, 2308])
    w_out = din("w_out", [DEPTH, DM, DM])
    w1 = din("mlp_w1", [DEPTH, DM, 4 * DM])
    w2 = din("mlp_w2", [DEPTH, 4 * DM, DM])
    poolw = din("poolw", [DEPTH, 128, 2, 128])
    gluw = din("gluw", [DEPTH, 128, 2, 256])
    s5b = din("s5b", [DEPTH, 128, 256])
    s5c = din("s5c", [DEPTH, 128, 512])
    yT = nc.dram_tensor("yT", [DM, T], F32, kind="ExternalOutput").ap()
    GRP = [[0, 1, 2, 3], [4, 5, 6, 7]]
    ccd = {}
    for l_ in range(DEPTH):
        for nm_, w_ in (("h", 64), ("x", 16), ("s", 272), ("m", 32)):
            ccd[(nm_, l_, "i")] = nc.dram_tensor("cc%s%di" % (nm_, l_), [128, w_], F32, kind="Internal").ap()
            ccd[(nm_, l_, "o")] = nc.dram_tensor("cc%s%do" % (nm_, l_), [4 * 128, w_], F32, kind="Internal").ap()
    dbg_out = {}
    if dbg:
        for name, shape in dbg.items():
            dbg_out[name] = nc.dram_tensor("dbg_" + name, list(shape), F32, kind="ExternalOutput").ap()

    import contextlib
    es = contextlib.ExitStack()
    with es:
        def sb(name, shape, dt):
            return es.enter_context(nc.sbuf_tensor(name, list(shape), dt))

        hT = sb("hT", [128, KC, T], F32)
        uT = sb("uT", [128, KC, T], BF16)
        Yr = sb("Yr", [128, KC, T], BF16)
        AR = sb("AR", [128, 15360], F32)
        cm = sb("cm", [128, 4, 128], BF16)
        ppS = sb("ppS", [128, DEPTH, NPCOL], F32)
        modT = sb("modT", [128, DEPTH, 48], F32)
        der = sb("der", [128, DEPTH, 64], F32)
        condb = sb("condb", [128, 8], BF16)
        epsT = sb("epsT", [128, 2], F32)
        poolwS = sb("poolwS", [128, DEPTH, 2, 128], BF16)
        gluwS = sb("gluwS", [128, DEPTH, 2, 256], BF16)
        cPool = sb("cPool", [128, DEPTH, 2, 16], F32)
        cSc = sb("cSc", [128, DEPTH, 2, 2], F32)
        cCv = sb("cCv", [128, DEPTH, 6, 3], F32)
        cS = sb("cS", [128, DEPTH, 256], F32)
        cX = sb("cX", [128, DEPTH, 8, 2], F32)
        ssdp = sb("ssdp", [128, DEPTH, 16], F32)
        s5pw = sb("s5pw", [128, DEPTH, 8, 3, 16], F32)
        s5lv = sb("s5lv", [128, DEPTH, 8, 3, 8], F32)
        bbS = sb("bbS", [128, DEPTH, 2, 8, 16], F32)
        tiny = sb("tiny", [128, 896], F32)
        tinyb = sb("tinyb", [128, 128], BF16)

        ps = [es.enter_context(nc.psum_tensor("ps%d" % i, [128, 512], F32)) for i in range(8)]
        sems = {}
        for e in ENGS:
            sems[e] = es.enter_context(nc.semaphore("s_" + e))
        for j in range(NDMA):
            sems["dma%d" % j] = es.enter_context(nc.semaphore("s_dma%d" % j))
        for l_ in range(DEPTH):
            for nm_ in ("h", "x", "s", "m"):
                sems["cc%s%d" % (nm_, l_)] = es.enter_context(nc.semaphore("s_cc%s%d" % (nm_, l_)))

        def allgather(nm_, l_, reads, writes):
            i_, o_ = ccd[(nm_, l_, "i")], ccd[(nm_, l_, "o")]
            P.cc(lambda e: e.collective_compute("AllGather", ALU.bypass, replica_groups=GRP, ins=[i_], outs=[o_]),
                 "cc%s%d" % (nm_, l_), reads, writes)

        ARb = AR[:, :].bitcast(BF16)

        def arf(off, n):
            return AR[:, off:off + n]

        def arb(off, n):
            return ARb[:, 2 * off:2 * (off + n)]

        def ark(off, n):
            return [("ar", b) for b in range(off // 256, (off + n + 255) // 256)]

        ident = cm[:, 0, :]
        ones = cm[:, 1, :]
        tri = cm[:, 2, :]
        negm = cm[:, 3, :]

        def pcol(l, name, a=0, b=None):
            o, w = PCOL[name]
            if b is None:
                b = w
            return ppS[:, l, o + a:o + b]

        def mm(out, lhsT, rhs, start, stop, reads, writes):
            P.op("pe", lambda e: e.matmul(out, lhsT=lhsT, rhs=rhs, start=start, stop=stop), reads, writes)

        def act(out, in_, func, reads, writes, bias=None, scale=None):
            kw = {}
            if bias is not None:
                kw["bias"] = bias
            if scale is not None:
                kw["scale"] = scale
            P.op("act", lambda e: e.activation(out=out, in_=in_, func=func, **kw), reads, writes)

        def tt(out, in0, in1, op, reads, writes, eng="dve"):
            P.op(eng, lambda e: e.tensor_tensor(out=out, in0=in0, in1=in1, op=op), reads, writes)

        def ts(out, in0, s1, s2, op0, op1, reads, writes, eng="dve"):
            if s2 is None:
                P.op(eng, lambda e: e.tensor_scalar(out=out, in0=in0, scalar1=s1, scalar2=None, op0=op0), reads, writes)
            else:
                P.op(eng, lambda e: e.tensor_scalar(out=out, in0=in0, scalar1=s1, scalar2=s2, op0=op0, op1=op1), reads, writes)

        def stt(out, in0, scalar, in1, op0, op1, reads, writes, eng="dve"):
            P.op(eng, lambda e: e.scalar_tensor_tensor(out=out, in0=in0, scalar=scalar, in1=in1, op0=op0, op1=op1), reads, writes)

        def cp(out, in_, reads, writes, eng="dve"):
            P.op(eng, lambda e: e.tensor_copy(out=out, in_=in_), reads, writes)

        def mset(ap, val, writes, eng="dve"):
            P.op(eng, lambda e: e.memset(ap, val), [], writes)

        def dma(eng, out, in_, reads, writes):
            return P.dma(eng, lambda e: e.dma_start(out=out, in_=in_), reads, writes)

        dma("pool", cm[:, :, :], cmat, [], ["cm"])
        dma("sp", ppS[:, :, :], pp.rearrange("l p c -> p l c"), [], ["pp"])
        dma("pool", poolwS[:, :, :, :], poolw.rearrange("l p c d -> p l c d"), [], ["poolw"])
        dma("pool", gluwS[:, :, :, :], gluw.rearrange("l p c d -> p l c d"), [], ["gluw"])
        mset(epsT[:, 0:1], EPS, ["eps"])
        mset(epsT[:, 1:2], 1.0, ["eps"])
        for t_, kk in ((cPool, "cPool"), (cSc, "cSc"), (cCv, "cCv"), (cS, "cS"), (cX, "cX")):
            mset(t_[:], 0.0, [kk])
        act(condb[:, :], pcol(0, "cond"), AF.Silu, ["pp"], ["condb"])

        WADA = 0
        bi = 0
        for l in range(DEPTH):
            for blk in range(3):
                slot = bi % 2
                bi += 1
                wsl = arb(WADA + slot * 2048, 2048).rearrange("p (k c) -> p k c", k=8)
                wk = ark(WADA + slot * 2048, 2048)
                dma("pool", wsl, ada_w[l, :, blk * 512:(blk + 1) * 512].rearrange("(k p) c -> p k c", p=128), [], wk)
                for jj in range(4):
                    j = l * 12 + blk * 4 + jj
                    for k in range(KC):
                        mm(ps[6][:, j:j + 1], wsl[:, k, jj * 128:(jj + 1) * 128], condb[:, k:k + 1], k == 0, k == KC - 1,
                           wk + ["condb"], [("ps", 6)])
        mpk = tiny[:, 0:24].rearrange("p (l c) -> p l c", l=2)
        tyk0 = [("tiny", "all")]
        for l in range(DEPTH):
            tt(mpk[:, l, :], ps[6][:, l * 12:(l + 1) * 12], pcol(l, "adab", 0, 12), ALU.add, [("ps", 6), "pp"], tyk0)
        dma("sp", ccd[("m", 0, "i")][:, 0:24], tiny[:, 0:24], tyk0, [("cci", "m", 0)])
        allgather("m", 0, [("cci", "m", 0)], [("cco", "m", 0)])
        mrb = tiny[:, 32:160].rearrange("p (r f) -> p r f", r=4)
        dma("sp", mrb, ccd[("m", 0, "o")].rearrange("(r p) f -> p r f", p=128), [("cco", "m", 0)], tyk0)
        for l in range(DEPTH):
            for sg_ in range(4):
                cp(modT[:, l, sg_ * 12:(sg_ + 1) * 12], mrb[:, sg_, l * 12:(l + 1) * 12], tyk0, [("mod", l)])
        for l in range(depth):
            stt(der[:, l, 0:8], modT[:, l, 8:16], 1.0, pcol(l, "nw1"), ALU.add, ALU.mult, [("mod", l), "pp"], [("der", l)])
            stt(der[:, l, 24:32], modT[:, l, 32:40], 1.0, pcol(l, "nw2"), ALU.add, ALU.mult, [("mod", l), "pp"], [("der", l)])
            cp(der[:, l, 8:16], modT[:, l, 0:8], [("mod", l)], [("der", l)])
            cp(der[:, l, 16:24], modT[:, l, 16:24], [("mod", l)], [("der", l)])
            cp(der[:, l, 32:40], modT[:, l, 24:32], [("mod", l)], [("der", l)])
            cp(der[:, l, 40:48], modT[:, l, 40:48], [("mod", l)], [("der", l)])

        def S1(l, k): return der[:, l, 0 + k:1 + k]
        def B1(l, k): return der[:, l, 8 + k:9 + k]
        def G1(l, k): return der[:, l, 16 + k:17 + k]
        def S2(l, k): return der[:, l, 24 + k:25 + k]
        def B2(l, k): return der[:, l, 32 + k:33 + k]
        def G2(l, k): return der[:, l, 40 + k:41 + k]

        for l in range(depth):
            act(ssdp[:, l, 0:4], pcol(l, "alog"), AF.Exp, ["pp"], [("ssdp", l)])
            ts(ssdp[:, l, 0:4], ssdp[:, l, 0:4], -1.0, None, ALU.mult, None, [("ssdp", l)], [("ssdp", l)])
            cp(ssdp[:, l, 4:8], pcol(l, "dtb"), ["pp"], [("ssdp", l)])
            cp(ssdp[:, l, 8:12], pcol(l, "dsk"), ["pp"], [("ssdp", l)])

        def tk(n): return [("tiny", n)]
        for l in range(depth):
            tyk = [("tiny", "all")]
            stp = tiny[:, 0:8]
            act(stp, pcol(l, "lst"), AF.Exp, ["pp"], tyk)
            mag = tiny[:, 8:16]
            tt(mag, pcol(l, "are"), stp, ALU.mult, ["pp"] + tyk, tyk)
            act(mag, mag, AF.Exp, tyk, tyk)
            th = tiny[:, 16:24]
            tt(th, pcol(l, "aim"), stp, ALU.mult, ["pp"] + tyk, tyk)
            sa = tiny[:, 24:32]
            ca = tiny[:, 32:40]
            act(sa, th, AF.Sin, tyk, tyk, scale=1.0 / 16.0)
            ts(ca, th, 1.0 / 16.0, float(np.pi / 2), ALU.mult, ALU.add, tyk, tyk)
            act(ca, ca, AF.Sin, tyk, tyk)
            for _ in range(4):
                t2a, t2b = tiny[:, 272:280], tiny[:, 280:288]
                tt(t2a, ca, ca, ALU.mult, tyk, tyk)
                tt(t2b, sa, sa, ALU.mult, tyk, tyk)
                tt(sa, sa, ca, ALU.mult, tyk, tyk)
                ts(sa, sa, 2.0, None, ALU.mult, None, tyk, tyk)
                tt(ca, t2a, t2b, ALU.subtract, tyk, tyk)
            lr = tiny[:, 40:48]
            li = tiny[:, 48:56]
            tt(lr, mag, ca, ALU.mult, tyk, tyk)
            tt(li, mag, sa, ALU.mult, tyk, tyk)
            den = tiny[:, 56:64]
            t0 = tiny[:, 64:72]
            tt(den, pcol(l, "are"), pcol(l, "are"), ALU.mult, ["pp"] + tyk, tyk)
            tt(t0, pcol(l, "aim"), pcol(l, "aim"), ALU.mult, ["pp"] + tyk, tyk)
            tt(den, den, t0, ALU.add, tyk, tyk)
            P.op("dve", lambda e, den=den: e.reciprocal(out=den, in_=den), tyk, tyk)
            nr = tiny[:, 72:80]
            ts(nr, lr, -1.0, None, ALU.add, None, tyk, tyk)
            fr = tiny[:, 80:88]
            fi = tiny[:, 88:96]
            t1 = tiny[:, 96:104]
            tt(fr, nr, pcol(l, "are"), ALU.mult, ["pp"] + tyk, tyk)
            tt(t1, li, pcol(l, "aim"), ALU.mult, ["pp"] + tyk, tyk)
            tt(fr, fr, t1, ALU.add, tyk, tyk)
            tt(fr, fr, den, ALU.mult, tyk, tyk)
            tt(fi, li, pcol(l, "are"), ALU.mult, ["pp"] + tyk, tyk)
            tt(t1, nr, pcol(l, "aim"), ALU.mult, ["pp"] + tyk, tyk)
            tt(fi, fi, t1, ALU.subtract, tyk, tyk)
            tt(fi, fi, den, ALU.mult, tyk, tyk)
            dma("sp", tiny[:, 512:768], s5b[l], [], tyk)
            bre = tiny[:, 512:640].rearrange("p (r h) -> p r h", r=8)
            bim = tiny[:, 640:768].rearrange("p (r h) -> p r h", r=8)
            frb = fr.unsqueeze(2).to_broadcast([128, 8, 16])
            fib = fi.unsqueeze(2).to_broadcast([128, 8, 16])
            tmpb = tiny[:, 128:256].rearrange("p (r h) -> p r h", r=8)
            tt(bbS[:, l, 0, :, :], bre, frb, ALU.mult, ["pp"] + tyk, [("bb", l)])
            tt(tmpb, bim, fib, ALU.mult, ["pp"] + tyk, tyk)
            tt(bbS[:, l, 0, :, :], bbS[:, l, 0, :, :], tmpb, ALU.subtract, tyk + [("bb", l)], [("bb", l)])
            tt(bbS[:, l, 1, :, :], bim, frb, ALU.mult, ["pp"] + tyk, [("bb", l)])
            tt(tmpb, bre, fib, ALU.mult, ["pp"] + tyk, tyk)
            tt(bbS[:, l, 1, :, :], bbS[:, l, 1, :, :], tmpb, ALU.add, tyk + [("bb", l)], [("bb", l)])
            ts(bbS[:, l, 1, :, :], bbS[:, l, 1, :, :], -1.0, None, ALU.mult, None, [("bb", l)], [("bb", l)])
            ts(li, li, -1.0, None, ALU.mult, None, tyk, tyk)
            pk = [("s5pw", l)]
            cp(s5pw[:, l, :, 0, 0], lr, tyk, pk)
            cp(s5pw[:, l, :, 1, 0], li, tyk, pk)
            for j in range(1, 16):
                pr_, pi_ = s5pw[:, l, :, 0, j - 1], s5pw[:, l, :, 1, j - 1]
                nr_, ni_ = s5pw[:, l, :, 0, j], s5pw[:, l, :, 1, j]
                ta, tb_ = tiny[:, 256:264], tiny[:, 264:272]
                tt(ta, pr_, lr, ALU.mult, tyk + pk, tyk)
                tt(tb_, pi_, li, ALU.mult, tyk + pk, tyk)
                tt(nr_, ta, tb_, ALU.subtract, tyk, pk)
                tt(ta, pr_, li, ALU.mult, tyk + pk, tyk)
                tt(tb_, pi_, lr, ALU.mult, tyk + pk, tyk)
                tt(ni_, ta, tb_, ALU.add, tyk, pk)
            ts(s5pw[:, l, :, 2, :], s5pw[:, l, :, 1, :], -1.0, None, ALU.mult, None, pk, pk)
            lk = [("s5lv", l)]
            cp(s5lv[:, l, :, 0, 0], s5pw[:, l, :, 0, 15], pk, lk)
            cp(s5lv[:, l, :, 1, 0], s5pw[:, l, :, 1, 15], pk, lk)
            for k in range(1, 8):
                pr_, pi_ = s5lv[:, l, :, 0, k - 1], s5lv[:, l, :, 1, k - 1]
                ta, tb_ = tiny[:, 256:264], tiny[:, 264:272]
                tt(ta, pr_, pr_, ALU.mult, lk + tyk, tyk)
                tt(tb_, pi_, pi_, ALU.mult, lk + tyk, tyk)
                tt(s5lv[:, l, :, 0, k], ta, tb_, ALU.subtract, tyk, lk)
                tt(ta, pr_, pi_, ALU.mult, lk + tyk, tyk)
                ts(s5lv[:, l, :, 1, k], ta, 2.0, None, ALU.mult, None, tyk, lk)
            ts(s5lv[:, l, :, 2, :], s5lv[:, l, :, 1, :], -1.0, None, ALU.mult, None, lk, lk)

        acc_i = [0]

        def next_acc():
            b = acc_i[0] % 4
            acc_i[0] += 1
            return b

        def rms_stats(src_fn, nchunks, denom, rstd_off, sq_off, src_keys_fn):
            for tb in range(NTB):
                bank = 4 + (tb % 2)
                for k in range(nchunks):
                    so = sq_off + ((tb * nchunks + k) % 4) * 256
                    sq = arb(so, 256)
                    src = src_fn(k, tb)
                    tt(sq, src, src, ALU.mult, src_keys_fn(k, tb), ark(so, 256), eng="pool")
                    mm(ps[bank][:, :], ones, sq, k == 0, k == nchunks - 1, ark(so, 256) + ["cm"], [("ps", bank)])
                r = arf(rstd_off + tb * TB, TB)
                rk = ark(rstd_off + tb * TB, TB)
                act(r, ps[bank][:, :], AF.Ln, [("ps", bank), "eps"], rk, bias=epsT[:, 0:1], scale=1.0 / denom)
                act(r, r, AF.Exp, rk, rk, scale=-0.5)

        def norm_mod(l, s_fn, b_fn):
            RSTD, SQ, TMP = 0, 2048, 3072
            rms_stats(lambda k, tb: hT[:, k, tb * TB:(tb + 1) * TB], KC, float(DM), RSTD, SQ,
                      lambda k, tb: [("hT", k, tb)])
            for k in range(KC):
                for tb in range(NTB):
                    to = TMP + ((k * NTB + tb) % 4) * TB
                    tmp = arf(to, TB)
                    stt(tmp, hT[:, k, tb * TB:(tb + 1) * TB], s_fn(l, k), arf(RSTD + tb * TB, TB), ALU.mult, ALU.mult,
                        [("hT", k, tb), ("der", l)] + ark(RSTD + tb * TB, TB), ark(to, TB))
                    act(uT[:, k, tb * TB:(tb + 1) * TB], tmp, AF.Identity, ark(to, TB) + [("der", l)], [("uT", k, tb)],
                        bias=b_fn(l, k))

        WIN = 14336
        win_i = [0]

        def load_win(l, c0, ncol):
            assert ncol <= 128
            slot = win_i[0] % 2
            win_i[0] += 1
            off = WIN + slot * 512
            w = arb(off, 512).rearrange("p (k c) -> p k c", k=8)
            dma("pool", w[:, :, 0:ncol], w_in[l, :, c0:c0 + ncol].rearrange("(k p) c -> p k c", p=128), [], ark(off, 512))
            return w, ark(off, 512)

        def proj_chunk(w, wk, cofs, ncol, evac):
            for tb in range(NTB):
                b = next_acc()
                for k in range(KC):
                    mm(ps[b][0:ncol, :], w[:, k, cofs:cofs + ncol], uT[:, k, tb * TB:(tb + 1) * TB], k == 0, k == KC - 1,
                       wk + [("uT", k, tb)], [("ps", b)])
                evac(tb, ps[b][0:ncol, :], ("ps", b))

        def dump(name, ap, keys):
            if dbg and name in dbg_out:
                dma("sp", dbg_out[name], ap, keys, [("dbg", name)])

        def all_keys_h():
            return [("hT", k, tb) for k in range(KC) for tb in range(NTB)]

        def all_keys(nm):
            return [(nm, k, tb) for k in range(KC) for tb in range(NTB)]

        def tail_prepass(l):
            PK = arf(0, 64)
            pkk = ark(0, 64)
            gct = tiny[:, 0:32].rearrange("p (c t) -> p c t", c=2)
            tyk = [("tiny", "all")]
            tail = slice(T - 16, T)

            def tproj(w, wk, cofs, evac):
                b = next_acc()
                for k in range(KC):
                    mm(ps[b][:, 0:16], w[:, k, cofs:cofs + 128], uT[:, k, tail], k == 0, k == KC - 1, wk + [("uT", k, 3)], [("ps", b)])
                evac(ps[b][:, 0:16], ("ps", b))
            for c in range(2):
                w, wk = load_win(l, c * 128, 128)
                tproj(w, wk, 0, lambda p_, pk, c=c: act(PK[:, c * 16:(c + 1) * 16], p_, AF.Copy, [pk], pkk))
            for c in range(2):
                w, wk = load_win(l, 512 + c * 128, 128)
                tproj(w, wk, 0, lambda p_, pk, c=c: act(gct[:, c, :], p_, AF.Copy, [pk], tyk))
            for c in range(2):
                w, wk = load_win(l, 768 + c * 128, 128)
                tproj(w, wk, 0, lambda p_, pk, c=c: tt(PK[:, 32 + 2 * c:34 + 2 * c], p_[:, 14:16], gct[:, c, 14:16], ALU.mult,
                                                       [pk] + tyk, pkk))
            for j in range(6):
                w, wk = load_win(l, 1280 + j * 128, 128)
                tproj(w, wk, 0, lambda p_, pk, j=j: act(PK[:, 36 + 3 * j:39 + 3 * j], p_[:, 13:16], AF.Copy, [pk], pkk))
            dma("sp", ccd[("h", l, "i")][:, 0:54], PK[:, 0:54], pkk, [("cci", "h", l)])
            allgather("h", l, [("cci", "h", l)], [("cco", "h", l)])

        def halo_apply(l):
            RB = arf(256, 256).rearrange("p (r f) -> p r f", r=4)
            rbk = ark(256, 256)
            tyk = [("tiny", "all")]
            dma("sp", RB, ccd[("h", l, "o")].rearrange("(r p) f -> p r f", p=128), [("cco", "h", l)], rbk)
            hal = tiny[:, 64:128]
            sel = pcol(l, "sel")
            ts(hal, RB[:, 0, :], sel[:, 0:1], None, ALU.mult, None, rbk + ["pp"], tyk)
            for j in range(1, 4):
                stt(hal, RB[:, j, :], sel[:, j:j + 1], hal, ALU.mult, ALU.add, rbk + ["pp"] + tyk, tyk)
            cp(cPool[:, l, :, :], hal[:, 0:32].rearrange("p (c t) -> p c t", c=2), tyk, [("cPool", l, 0), ("cPool", l, 1)])
            cp(cSc[:, l, :, :], hal[:, 32:36].rearrange("p (c t) -> p c t", c=2), tyk, [("cSc", l, 0), ("cSc", l, 1)])
            cp(cCv[:, l, :, :], hal[:, 36:54].rearrange("p (c t) -> p c t", c=6), tyk, [("cCv", l, j) for j in range(6)])

        def s5_combine(l):
            tyk = [("tiny", "all")]
            RB = tiny[:, 816:880].rearrange("p (r f) -> p r f", r=4)
            dma("sp", RB, ccd[("x", l, "o")].rearrange("(r p) f -> p r f", p=128), [("cco", "x", l)], tyk)
            sel = pcol(l, "sel")
            Lr, Li = s5lv[:, l, :, 0, 7], s5lv[:, l, :, 1, 7]
            lk = [("s5lv", l)]
            ar, ai, pr, pi, t1, t2 = (tiny[:, a:a + 8] for a in (768, 776, 784, 792, 880, 888))
            for t_ in (ar, ai, pr, pi):
                mset(t_, 0.0, tyk)
            for j in range(4):
                stt(ar, pr, sel[:, 4 + j:5 + j], ar, ALU.mult, ALU.add, tyk + ["pp"], tyk)
                stt(ai, pi, sel[:, 4 + j:5 + j], ai, ALU.mult, ALU.add, tyk + ["pp"], tyk)
                if j < 3:
                    F = RB[:, j, :].rearrange("p (r a) -> p r a", a=2)
                    tt(t1, pr, Lr, ALU.mult, tyk + lk, tyk)
                    tt(t2, pi, Li, ALU.mult, tyk + lk, tyk)
                    tt(t1, t1, t2, ALU.subtract, tyk, tyk)
                    tt(t2, pr, Li, ALU.mult, tyk + lk, tyk)
                    tt(pr, t1, F[:, :, 0], ALU.add, tyk, tyk)
                    tt(t1, pi, Lr, ALU.mult, tyk + lk, tyk)
                    tt(t1, t1, t2, ALU.add, tyk, tyk)
                    tt(pi, t1, F[:, :, 1], ALU.add, tyk, tyk)
            cp(cX[:, l, :, 0], ar, tyk, [("cX", l, r) for r in range(8)])
            cp(cX[:, l, :, 1], ai, tyk, [("cX", l, r) for r in range(8)])

        def mixer_pool(l, seg):
            pbv = Yr[:, 4:6, :]
            for c in range(2):
                V, SA, SB = c * 6912, c * 6912 + 2304, c * 6912 + 4608
                w, wk = load_win(l, c * 128, 128)
                v = arf(V, 2064)
                vk = ark(V, 2064)
                cp(v[:, 0:16], cPool[:, l, c, :], [("cPool", l, c)], vk)
                proj_chunk(w, wk, 0, 128,
                           lambda tb, p_, pk: act(v[:, 16 + tb * TB:16 + (tb + 1) * TB], p_, AF.Copy, [pk], vk))
                cp(cPool[:, l, c, :], v[:, 2048:2064], vk, [("cPool", l, c)])
                sa, sbb = arf(SA, 2064), arf(SB, 2064)
                sak, sbk = ark(SA, 2064), ark(SB, 2064)
                tt(sa[:, 1:2064], v[:, 1:2064], v[:, 0:2063], ALU.add, vk, sak)
                tt(sbb[:, 3:2064], sa[:, 3:2064], sa[:, 1:2062], ALU.add, sak, sbk)
                if c == 0:
                    lo_src, hi_src, lo_w, hi_w = sa, sbb, 2, 4
                else:
                    tt(sa[:, 7:2064], sbb[:, 7:2064], sbb[:, 3:2060], ALU.add, sbk, sak)
                    tt(sbb[:, 15:2064], sa[:, 15:2064], sa[:, 7:2056], ALU.add, sak, sbk)
                    lo_src, hi_src, lo_w, hi_w = sa, sbb, 8, 16
                pb = pbv
                pbk = [("Yr", 4 + c, t_) for t_ in range(NTB)]
                stt(pb[0:64, c, :], lo_src[0:64, 16:2064], 1.0 / lo_w, v[0:64, 16:2064], ALU.mult, ALU.subtract, sak + sbk + vk, pbk)
                stt(pb[64:128, c, :], hi_src[64:128, 16:2064], 1.0 / hi_w, v[64:128, 16:2064], ALU.mult, ALU.subtract, sak + sbk + vk, pbk)
                if True:
                    ic = pcol(l, "invc").rearrange("p (c t) -> p c t", c=2)
                    tq = tiny[:, 512:528]
                    for (r0, r1, src) in ((0, 64, lo_src), (64, 128, hi_src)):
                        tt(tq[r0:r1, :], src[r0:r1, 16:32], ic[r0:r1, c, :], ALU.mult, sak + sbk + ["pp"], [("tiny", "all")])
                        tt(pb[r0:r1, c, 0:16], tq[r0:r1, :], v[r0:r1, 16:32], ALU.subtract, [("tiny", "all")] + vk, pbk)
            pb = pbv
            for c in range(2):
                for tb in range(NTB):
                    b = next_acc()
                    mm(ps[b][:, :], poolwS[:, l, c, :], pb[:, c, tb * TB:(tb + 1) * TB], True, True, [("Yr", 4 + c, tb), "poolw"], [("ps", b)])
                    ts(Yr[:, c, tb * TB:(tb + 1) * TB], ps[b][:, :], pcol(l, "pscale", c, c + 1), None, ALU.mult, None,
                       [("ps", b), "pp"], [("Yr", c, tb)])

        def mixer_sconv(l, seg):
            for c in range(2):
                GC, G, T1 = c * 6400, c * 6400 + 2048, c * 6400 + 4352
                gc = arf(GC, 2048)
                gck = ark(GC, 2048)
                wgc, wgck = load_win(l, 512 + c * 128, 128)
                proj_chunk(wgc, wgck, 0, 128,
                           lambda tb, p_, pk: act(gc[:, tb * TB:(tb + 1) * TB], p_, AF.Copy, [pk], gck))
                whh, whhk = load_win(l, 768 + c * 128, 128)
                g = arf(G, 2050)
                gk = ark(G, 2050)
                cp(g[:, 0:2], cSc[:, l, c, :], [("cSc", l, c)], gk)
                proj_chunk(whh, whhk, 0, 128,
                           lambda tb, p_, pk: tt(g[:, 2 + tb * TB:2 + (tb + 1) * TB], p_, gc[:, tb * TB:(tb + 1) * TB], ALU.mult,
                                                 [pk] + gck, gk))
                cp(cSc[:, l, c, :], g[:, 2048:2050], gk, [("cSc", l, c)])
                t1 = arf(T1, 2048)
                t1k = ark(T1, 2048)
                wv = pcol(l, "scw").rearrange("p (c k) -> p c k", c=2)
                ts(t1, g[:, 0:2048], wv[:, c, 0:1], None, ALU.mult, None, gk + ["pp"], t1k)
                stt(t1, g[:, 1:2049], wv[:, c, 1:2], t1, ALU.mult, ALU.add, gk + ["pp"] + t1k, t1k)
                stt(t1, g[:, 2:2050], wv[:, c, 2:3], t1, ALU.mult, ALU.add, gk + ["pp"] + t1k, t1k)
                wgb, wgbk = load_win(l, 256 + c * 128, 128)
                proj_chunk(wgb, wgbk, 0, 128,
                           lambda tb, p_, pk: tt(Yr[:, 2 + c, tb * TB:(tb + 1) * TB], p_, t1[:, tb * TB:(tb + 1) * TB], ALU.mult,
                                                 [pk] + t1k, [("Yr", 2 + c, tb)]))

        def mixer_ssd(l, seg):
            SZ, RAW, ACC, XBC = 0, 2048, 4352, 6400
            XTOK, BTOK = 2048, 4096
            sz = arb(SZ, 2048).rearrange("p (c t) -> p c t", c=2)
            szk = ark(SZ, 2048)
            for c in range(2):
                wz, wzk = load_win(l, 1024 + c * 128, 128)
                proj_chunk(wz, wzk, 0, 128,
                           lambda tb, p_, pk: act(sz[:, c, tb * TB:(tb + 1) * TB], p_, AF.Silu, [pk], szk))
            xbc = arb(XBC, 6144).rearrange("p (c t) -> p c t", c=6)
            kxb = [ark(XBC + j * 1024, 1024) for j in range(6)]
            cw = pcol(l, "cvw").rearrange("p (c k) -> p c k", c=6)
            for j in range(6):
                wx, wxk = load_win(l, 1280 + j * 128, 128)
                raw = arf(RAW, 2051)
                rawk = ark(RAW, 2051)
                cp(raw[:, 0:3], cCv[:, l, j, :], [("cCv", l, j)], rawk)
                proj_chunk(wx, wxk, 0, 128,
                           lambda tb, p_, pk: act(raw[:, 3 + tb * TB:3 + (tb + 1) * TB], p_, AF.Copy, [pk], rawk))
                cp(cCv[:, l, j, :], raw[:, 2048:2051], rawk, [("cCv", l, j)])
                acc = arf(ACC, 2048)
                acck = ark(ACC, 2048)
                ts(acc, raw[:, 0:2048], cw[:, j, 0:1], None, ALU.mult, None, rawk + ["pp"], acck)
                for kk in range(1, 4):
                    stt(acc, raw[:, kk:kk + 2048], cw[:, j, kk:kk + 1], acc, ALU.mult, ALU.add, rawk + acck + ["pp"], acck)
                act(xbc[:, j, :], acc, AF.Silu, acck + ["pp"], kxb[j], bias=pcol(l, "cvb", j, j + 1))
            print("  ssd: before dt", P.total)
            wd, wdk = load_win(l, 2048, 4)
            for c in range(NCH):
                for k in range(KC):
                    mm(ps[6][:, c * 4:(c + 1) * 4], uT[:, k, c * 128:(c + 1) * 128], wd[:, k, 0:4], k == 0, k == KC - 1,
                       wdk + [("uT", k, c // 4)], [("ps", 6)])
            print("  ssd: before small", P.total)
            tyk = [("tiny", "all")]
            def v3(a): return tiny[:, a:a + 64].rearrange("p (c h) -> p c h", h=4)
            dt_, adt, acs, tot, eacs, dte, cd, ddte = (v3(a) for a in (0, 64, 128, 192, 256, 320, 384, 448))
            xsp = v3(512)
            ex = v3(576)
            bc4 = lambda a, b: ssdp[:, l, a:b].unsqueeze(1).to_broadcast([128, NCH, 4])
            tt(xsp, ps[6][:, 0:64].rearrange("p (c h) -> p c h", h=4), bc4(4, 8), ALU.add, [("ps", 6), ("ssdp", l)], tyk)
            ts(ex, xsp, 30.0, None, ALU.min, None, tyk, tyk)
            act(ex, ex, AF.Exp, tyk, tyk)
            act(ex, ex, AF.Ln, tyk + ["eps"], tyk, bias=epsT[:, 1:2])
            tt(dt_, ex, xsp, ALU.max, tyk, tyk)
            tt(adt, dt_, bc4(0, 4), ALU.mult, tyk + [("ssdp", l)], tyk)
            ahi = tinyb[:, 0:64]
            alo = tinyb[:, 64:128]
            tbk = [("tinyb", "a")]
            adf = tiny[:, 64:128]
            cp(ahi, adf, tyk, tbk)
            tt(tiny[:, 640:704], adf, ahi, ALU.subtract, tyk + tbk, tyk)
            cp(alo, tiny[:, 640:704], tyk, tbk)
            cp(tiny[:, 768:832], ahi, tbk, tyk)
            mm(ps[6][:, 64:128], tri, ahi, True, False, tbk + ["cm"], [("ps", 6)])
            mm(ps[6][:, 64:128], tri, alo, False, True, tbk + ["cm"], [("ps", 6)])
            mm(ps[6][:, 128:192], ones, ahi, True, False, tbk + ["cm"], [("ps", 6)])
            mm(ps[6][:, 128:192], ones, alo, False, True, tbk + ["cm"], [("ps", 6)])
            cp(tiny[:, 128:256], ps[6][:, 64:192], [("ps", 6)], tyk)
            act(tiny[:, 256:320], tiny[:, 128:192], AF.Exp, tyk, tyk)
            tt(tiny[:, 320:384], tiny[:, 192:256], tiny[:, 128:192], ALU.subtract, tyk, tyk)
            act(tiny[:, 320:384], tiny[:, 320:384], AF.Exp, tyk, tyk)
            act(tiny[:, 384:448], tiny[:, 192:256], AF.Exp, tyk, tyk)
            tt(tiny[:, 448:512], tiny[:, 0:64], tiny[:, 320:384], ALU.mult, tyk, tyk)
            ts(tiny[:, 704:768], tiny[:, 128:192], -1.0, None, ALU.mult, None, tyk, tyk)
            nacs = v3(704)
            print("  ssd: before transposes", P.total)
            xtok = arb(XTOK, 2048).rearrange("p (c f) -> p c f", c=NCH)
            btok = arb(BTOK, 2048).rearrange("p (c f) -> p c f", c=NCH)
            xtk, btk = ark(XTOK, 2048), ark(BTOK, 2048)
            ti = 0
            for c in range(NCH):
                for j in range(4):
                    o = (ti % 4) * 128
                    ti += 1
                    bnk = 4 + (ti - 1) % 4
                    pk_ = ("ps", bnk)
                    mm(ps[bnk][:, 0:128], xbc[:, j, c * 128:(c + 1) * 128], ident, True, True, kxb[j] + ["cm"], [pk_])
                    dst = (xtok if j < 2 else btok)[:, c, (j % 2) * 128:(j % 2 + 1) * 128]
                    if ti % 2 == 0:
                        cp(dst, ps[bnk][:, 0:128], [pk_], xtk if j < 2 else btk)
                    else:
                        act(dst, ps[bnk][:, 0:128], AF.Copy, [pk_], xtk if j < 2 else btk)
            WT = XBC
            E_ = arb(WT, 256).rearrange("p (h s) -> p h s", h=4)
            MT = arb(WT + 256, 256).rearrange("p (h s) -> p h s", h=4)
            RH = arb(WT + 512, 512).rearrange("p (a h s) -> p a h s", a=2, h=4)
            XDT = arb(WT + 1024, 128)
            XDE = arb(WT + 1152, 128)
            YSB = arf(WT + 1280, 256)
            YTK = arb(WT + 1536, 128)
            SBF = arb(WT + 1664, 128)
            STMP = arf(WT + 1792, 256)
            kE, kMT, kRH, kXDT, kXDE, kYSB, kYTK, kSBF, kST = (ark(WT + a, n) for a, n in
                ((0, 256), (256, 256), (512, 512), (1024, 128), (1152, 128), (1280, 256), (1536, 128), (1664, 128), (1792, 256)))
            print("  ssd: before main loop", P.total)
            Sst = cS[:, l, :]
            kS = [("cS", l)]
            PKG, RBO, PTO = 12544, 12816, 13904
            pkg = arf(PKG, 272)
            pkgk = ark(PKG, 272)
            SL = pkg[:, 0:256]
            mset(SL, 0.0, pkgk)
            for c in range(NCH):
                x3 = xtok[:, c, :].rearrange("p (h d) -> p h d", h=4)
                tt(XDE.rearrange("p (h d) -> p h d", h=4), x3, ddte[:, c, :].unsqueeze(2).to_broadcast([128, 4, 64]), ALU.mult, xtk + tyk, kXDE)
                for g in range(2):
                    mm(ps[5][:, g * 128:(g + 1) * 128], btok[:, c, g * 128:(g + 1) * 128], XDE[:, g * 128:(g + 1) * 128], True, True,
                       btk + kXDE, [("ps", 5)])
                tt(STMP.rearrange("p (h d) -> p h d", h=4), SL.rearrange("p (h d) -> p h d", h=4),
                   cd[:, c, :].unsqueeze(2).to_broadcast([128, 4, 64]), ALU.mult, pkgk + tyk, kST)
                tt(SL, STMP, ps[5][:, 0:256], ALU.add, kST + [("ps", 5)], pkgk)
            tr_ = tiny[:, 832:864].rearrange("p (c h) -> p c h", h=4)
            tt(tr_, tot[:, 0:8, :], tot[:, 8:16, :], ALU.add, tyk, tyk)
            tt(tr_[:, 0:4, :], tr_[:, 0:4, :], tr_[:, 4:8, :], ALU.add, tyk, tyk)
            tt(tr_[:, 0:2, :], tr_[:, 0:2, :], tr_[:, 2:4, :], ALU.add, tyk, tyk)
            tt(tr_[:, 0:1, :], tr_[:, 0:1, :], tr_[:, 1:2, :], ALU.add, tyk, tyk)
            act(pkg[:, 256:260], tiny[:, 832:836], AF.Exp, tyk, pkgk)
            dma("sp", ccd[("s", l, "i")][:, 0:260], pkg[:, 0:260], pkgk, [("cci", "s", l)])
            allgather("s", l, [("cci", "s", l)], [("cco", "s", l)])
            RBs = arf(RBO, 1088).rearrange("p (r f) -> p r f", r=4)
            rbsk = ark(RBO, 1088)
            dma("sp", RBs, ccd[("s", l, "o")].rearrange("(r p) f -> p r f", p=128), [("cco", "s", l)], rbsk)
            PT = arf(PTO, 256)
            ptk = ark(PTO, 256)
            sel = pcol(l, "sel")
            mset(Sst, 0.0, kS)
            mset(PT, 0.0, ptk)
            for j in range(4):
                stt(Sst, PT, sel[:, 4 + j:5 + j], Sst, ALU.mult, ALU.add, ptk + kS + ["pp"], kS)
                if j < 3:
                    tt(PT.rearrange("p (h d) -> p h d", h=4), PT.rearrange("p (h d) -> p h d", h=4),
                       RBs[:, j, 256:260].unsqueeze(2).to_broadcast([128, 4, 64]), ALU.mult, ptk + rbsk, ptk)
                    tt(PT, PT, RBs[:, j, 0:256], ALU.add, ptk + rbsk, ptk)
            cp(SBF, Sst, kS, kSBF)
            for c in range(NCH):
                tsl = slice(c * 128, (c + 1) * 128)
                if c < 2:
                    print("  ssd: chunk", c, P.total)
                for h in range(4):
                    o = ps[0][:, h * 128:(h + 1) * 128]
                    mm(o, tinyb[:, c * 4 + h:c * 4 + h + 1].to_broadcast([128, 128]), tri, True, False, tbk + ["cm"], [("ps", 0)])
                    mm(o, tinyb[:, 64 + c * 4 + h:64 + c * 4 + h + 1].to_broadcast([128, 128]), tri, False, False, tbk + ["cm"], [("ps", 0)])
                    mm(o, ident, negm, False, True, ["cm"], [("ps", 0)])
                for h in range(4):
                    act(E_[:, h, :], ps[0][:, h * 128:(h + 1) * 128], AF.Exp, [("ps", 0)] + tyk, kE, bias=nacs[:, c, h:h + 1])
                for g in range(2):
                    mm(ps[1][:, g * 128:(g + 1) * 128], xbc[:, 2 + g, tsl], xbc[:, 4 + g, tsl], True, True,
                       kxb[2 + g] + kxb[4 + g], [("ps", 1)])
                for h in range(4):
                    g = h // 2
                    tt(MT[:, h, :], ps[1][:, g * 128:(g + 1) * 128], E_[:, h, :], ALU.mult, [("ps", 1)] + kE, kMT)
                x3 = xtok[:, c, :].rearrange("p (h d) -> p h d", h=4)
                tt(XDT.rearrange("p (h d) -> p h d", h=4), x3, dt_[:, c, :].unsqueeze(2).to_broadcast([128, 4, 64]), ALU.mult, xtk + tyk, kXDT, eng="pool")
                tt(XDE.rearrange("p (h d) -> p h d", h=4), x3, ddte[:, c, :].unsqueeze(2).to_broadcast([128, 4, 64]), ALU.mult, xtk + tyk, kXDE, eng="pool")
                for h in range(4):
                    o = ps[2][:, h * 64:(h + 1) * 64]
                    mm(o, MT[:, h, :], XDT[:, h * 64:(h + 1) * 64], True, True, kMT + kXDT, [("ps", 2, "y")])
                for h in range(4):
                    g = h // 2
                    mm(ps[3][:, h * 64:(h + 1) * 64], xbc[:, 4 + g, tsl], SBF[:, h * 64:(h + 1) * 64], True, True,
                       kxb[4 + g] + kSBF, [("ps", 3)])
                tt(YSB.rearrange("p (h d) -> p h d", h=4), x3, ssdp[:, l, 8:12].unsqueeze(2).to_broadcast([128, 4, 64]), ALU.mult,
                   xtk + [("ssdp", l)], kYSB)
                tt(YSB, YSB, ps[2][:, 0:256], ALU.add, kYSB + [("ps", 2, "y")], kYSB)
                tt(STMP.rearrange("p (h d) -> p h d", h=4), ps[3][:, 0:256].rearrange("p (h d) -> p h d", h=4),
                   eacs[:, c, :].unsqueeze(2).to_broadcast([128, 4, 64]), ALU.mult, [("ps", 3)] + tyk, kST)
                tt(YTK, STMP, YSB, ALU.add, kST + kYSB, kYTK)
                for j in range(2):
                    o = j * 128
                    mm(ps[4][:, o:o + 128], YTK[:, j * 128:(j + 1) * 128], ident, True, True, kYTK + ["cm"], [("ps", 4, o)])
                    tt(Yr[:, 4 + j, tsl], ps[4][:, o:o + 128], sz[:, j, tsl], ALU.mult, [("ps", 4, o)] + szk, [("Yr", 4 + j, c // 4)])
                for g in range(2):
                    mm(ps[5][:, g * 128:(g + 1) * 128], btok[:, c, g * 128:(g + 1) * 128], XDE[:, g * 128:(g + 1) * 128], True, True,
                       btk + kXDE, [("ps", 5)])
                tt(STMP.rearrange("p (h d) -> p h d", h=4), Sst.rearrange("p (h d) -> p h d", h=4),
                   cd[:, c, :].unsqueeze(2).to_broadcast([128, 4, 64]), ALU.mult, kS + tyk, kST)
                tt(Sst, STMP, ps[5][:, 0:256], ALU.add, kST + [("ps", 5)], kS)
                act(SBF, Sst, AF.Copy, kS, kSBF)

        def mixer_s5(l, seg):
            U, STG, BT, W, WB, PAT, INJ, TAB = 0, 2048, 4096, 6144, 8192, 10240, 11264, 13312
            tyk = [("tiny", "all")]
            pk = [("s5pw", l)]
            lk = [("s5lv", l)]
            u = arb(U, 2048).rearrange("p (c t) -> p c t", c=2)
            uk = ark(U, 2048)
            for c in range(2):
                wu, wuk = load_win(l, 2052 + c * 128, 128)
                proj_chunk(wu, wuk, 0, 128,
                           lambda tb, p_, pk_: act(u[:, c, tb * TB:(tb + 1) * TB], p_, AF.Copy, [pk_], uk))
            pat = arb(PAT, 1024)
            patk = ark(PAT, 1024)
            mset(pat, 1.0, patk)
            mset(pat.rearrange("p (c j) -> p c j", j=16)[:, :, 0:1], 0.0, patk)
            pw0 = tiny[:, 0:256].rearrange("p (r a j) -> p r a j", r=8, a=2)
            qq = tiny[:, 256:512].rearrange("p (r a j) -> p r a j", r=8, a=2)
            den = tiny[:, 512:640].rearrange("p (r j) -> p r j", r=8)
            tmp = tiny[:, 640:768].rearrange("p (r j) -> p r j", r=8)
            mset(pw0[:, :, 0, 0:1], 1.0, tyk)
            mset(pw0[:, :, 1, 0:1], 0.0, tyk)
            for a in range(2):
                cp(pw0[:, :, a, 1:16], s5pw[:, l, :, a, 0:15], pk, tyk)
            tt(den, pw0[:, :, 0, :], pw0[:, :, 0, :], ALU.mult, tyk, tyk)
            tt(tmp, pw0[:, :, 1, :], pw0[:, :, 1, :], ALU.mult, tyk, tyk)
            tt(den, den, tmp, ALU.add, tyk, tyk)
            P.op("dve", lambda e: e.reciprocal(out=den, in_=den), tyk, tyk)
            tt(qq[:, :, 0, :], pw0[:, :, 0, :], den, ALU.mult, tyk, tyk)
            tt(qq[:, :, 1, :], pw0[:, :, 1, :], den, ALU.mult, tyk, tyk)
            ts(qq[:, :, 1, :], qq[:, :, 1, :], -1.0, None, ALU.mult, None, tyk, tyk)
            bpad = arf(TAB, 512).rearrange("p (r a h) -> p r a h", r=8, a=2)
            cpad = arf(TAB + 512, 512).rearrange("p (r a h) -> p r a h", r=8, a=2)
            tabk = ark(TAB, 1024)
            mset(arf(TAB, 512), 0.0, tabk)
            for a in range(2):
                cp(bpad[0:64, :, a, 0:16], bbS[0:64, l, a, :, :], [("bb", l)], tabk)
                cp(bpad[64:128, :, a, 16:32], bbS[64:128, l, a, :, :], [("bb", l)], tabk)
            dma("sp", arf(TAB + 512, 512), s5c[l], [], tabk)
            Ef = Yr[:, 6:8, :].rearrange("p a t -> p (a t)").bitcast(F32)
            Eall = Ef.rearrange("p (r a c) -> p r a c", r=8, a=2)
            ek = [("Yr", 6 + a, tb) for a in range(2) for tb in range(NTB)]
            bufA = arf(W, 2048).rearrange("p (r a c) -> p r a c", r=8, a=2)
            bufB = arf(WB, 2048).rearrange("p (r a c) -> p r a c", r=8, a=2)
            kA, kB = ark(W, 2048), ark(WB, 2048)
            T1 = arf(STG, 2048)
            T2 = arf(BT, 2048)
            kT1, kT2 = ark(STG, 2048), ark(BT, 2048)

            def bc8(ap, n):
                return ap.unsqueeze(2).to_broadcast([128, 8, n])

            def lvl2(src, sk):
                cur, ck_ = src, sk
                nxt_list = [(bufA, kA), (bufB, kB)]
                if src is bufA:
                    nxt_list = [(bufB, kB), (bufA, kA)]
                for lev in range(7):
                    d = 1 << lev
                    n = 128 - d
                    dst, dk = nxt_list[lev % 2]
                    cr, ci = s5lv[:, l, :, 0, lev], s5lv[:, l, :, 1, lev]
                    t1 = T1[:, 0:8 * n].rearrange("p (r c) -> p r c", r=8)
                    t2 = T2[:, 0:8 * n].rearrange("p (r c) -> p r c", r=8)
                    cp(dst[:, :, :, 0:d], cur[:, :, :, 0:d], ck_, dk)
                    tt(t1, cur[:, :, 0, 0:n], bc8(cr, n), ALU.mult, ck_ + lk, kT1)
                    tt(t2, cur[:, :, 1, 0:n], bc8(ci, n), ALU.mult, ck_ + lk, kT2)
                    tt(t1, t1, t2, ALU.subtract, kT1 + kT2, kT1)
                    tt(dst[:, :, 0, d:128], cur[:, :, 0, d:128], t1, ALU.add, ck_ + kT1, dk)
                    tt(t1, cur[:, :, 1, 0:n], bc8(cr, n), ALU.mult, ck_ + lk, kT1)
                    tt(t2, cur[:, :, 0, 0:n], bc8(ci, n), ALU.mult, ck_ + lk, kT2)
                    tt(t1, t1, t2, ALU.add, kT1 + kT2, kT1)
                    tt(dst[:, :, 1, d:128], cur[:, :, 1, d:128], t1, ALU.add, ck_ + kT1, dk)
                    cur, ck_ = dst, dk
                return cur, ck_

            def run_pass(mode):
                inj = arf(INJ, 2048).rearrange("p (r a c) -> p r a c", r=8, a=2)
                injk = ark(INJ, 2048)
                if mode == "full":
                    cp(bufA, Eall, ek, kA)
                    mur, mui = s5lv[:, l, :, 0, 0], s5lv[:, l, :, 1, 0]
                    xr_, xi_ = cX[:, l, :, 0], cX[:, l, :, 1]
                    ckx = [("cX", l, r) for r in range(8)]
                    a1, a2 = tiny[:, 768:776], tiny[:, 776:784]
                    tt(a1, mur, xr_, ALU.mult, lk + ckx, tyk)
                    tt(a2, mui, xi_, ALU.mult, lk + ckx, tyk)
                    tt(a1, a1, a2, ALU.subtract, tyk, tyk)
                    tt(bufA[:, :, 0, 0], bufA[:, :, 0, 0], a1, ALU.add, kA + tyk, kA)
                    tt(a1, mur, xi_, ALU.mult, lk + ckx, tyk)
                    tt(a2, mui, xr_, ALU.mult, lk + ckx, tyk)
                    tt(a1, a1, a2, ALU.add, tyk, tyk)
                    tt(bufA[:, :, 1, 0], bufA[:, :, 1, 0], a1, ALU.add, kA + tyk, kA)
                    res, rk = lvl2(bufA, kA)
                    lr8, li8 = s5pw[:, l, :, 0, 0], s5pw[:, l, :, 1, 0]
                    n = 127
                    t1 = T1[:, 0:8 * n].rearrange("p (r c) -> p r c", r=8)
                    t2 = T2[:, 0:8 * n].rearrange("p (r c) -> p r c", r=8)
                    tt(t1, res[:, :, 0, 0:n], bc8(lr8, n), ALU.mult, rk + pk, kT1)
                    tt(t2, res[:, :, 1, 0:n], bc8(li8, n), ALU.mult, rk + pk, kT2)
                    tt(inj[:, :, 0, 1:128], t1, t2, ALU.subtract, kT1 + kT2, injk)
                    tt(t1, res[:, :, 1, 0:n], bc8(lr8, n), ALU.mult, rk + pk, kT1)
                    tt(t2, res[:, :, 0, 0:n], bc8(li8, n), ALU.mult, rk + pk, kT2)
                    tt(inj[:, :, 1, 1:128], t1, t2, ALU.add, kT1 + kT2, injk)
                    tt(a1, lr8, xr_, ALU.mult, pk + ckx, tyk)
                    tt(a2, li8, xi_, ALU.mult, pk + ckx, tyk)
                    tt(inj[:, :, 0, 0], a1, a2, ALU.subtract, tyk, injk)
                    tt(a1, lr8, xi_, ALU.mult, pk + ckx, tyk)
                    tt(a2, li8, xr_, ALU.mult, pk + ckx, tyk)
                    tt(inj[:, :, 1, 0], a1, a2, ALU.add, tyk, injk)

                stg = arb(STG, 2048).rearrange("p (a j c) -> p a j c", a=2, j=16)
                stgk = ark(STG, 2048)
                btv = arb(BT, 2048).rearrange("p (a j c) -> p a j c", a=2, j=16)
                btk_ = ark(BT, 2048)
                Wn = arf(W, 2048)
                wk_ = ark(W, 2048)
                Wv = Wn.rearrange("p (c j) -> p j c", j=16)
                wb = arb(WB, 2048).rearrange("p (a t) -> p a t", a=2)
                wbk = ark(WB, 2048)

                def b16(ap):
                    return ap.unsqueeze(2).to_broadcast([128, 16, 32])

                def h32(ap):
                    return ap.unsqueeze(1).to_broadcast([128, 16, 32])

                ct = Yr[:, 4:6, :].rearrange("p a (j c) -> p a j c", j=16)
                ctk = [("Yr", 4 + a_, t_) for a_ in range(2) for t_ in range(NTB)]
                if mode == "p1":
                    tmp_f, tmpk = arf(WB, 1024), ark(WB, 1024)
                    W1n, w1k_ = arf(INJ, 2048), ark(INJ, 2048)
                else:
                    tmp_f = Yr[:, 7, :].bitcast(F32)
                    tmpk = [("Yr", 7, t_) for t_ in range(NTB)]
                    W1n = Yr[:, 0:2, :].rearrange("p a t -> p (a t)").bitcast(F32)
                    w1k_ = [("Yr", a_, t_) for a_ in range(2) for t_ in range(NTB)]
                Wbufs = [(Wn, wk_), (W1n, w1k_)]
                t1 = tmp_f[:, 0:512].rearrange("p (j h) -> p j h", j=16)
                t2 = tmp_f[:, 512:1024].rearrange("p (j h) -> p j h", j=16)
                mset(arb(STG, 2048), 0.0, stgk)
                if mode == "full":
                    mset(Yr[:, 4:6, :], 0.0, ctk)

                ptf = Yr[:, 2, :].bitcast(F32)
                ptk = [("Yr", 2, t_) for t_ in range(NTB)]
                p1_ = ptf[:, 0:512].rearrange("p (j h) -> p j h", j=16)
                p2_ = ptf[:, 512:1024].rearrange("p (j h) -> p j h", j=16)

                def tables_b_dve(r):
                    q = r % 4
                    if r > 0:
                        pq = (r - 1) % 4
                        mset(stg[:, :, :, pq * 32:(pq + 1) * 32], 0.0, stgk, eng="pool")
                    Bre, Bim = h32(bpad[:, r, 0, :]), h32(bpad[:, r, 1, :])
                    qr_, qi_ = b16(qq[:, r, 0, :]), b16(qq[:, r, 1, :])
                    sv = stg[:, :, :, q * 32:(q + 1) * 32]
                    tt(p1_, Bre, qr_, ALU.mult, tabk + tyk, ptk, eng="pool")
                    tt(p2_, Bim, qi_, ALU.mult, tabk + tyk, ptk, eng="pool")
                    tt(sv[:, 0], p1_, p2_, ALU.subtract, ptk, stgk, eng="pool")
                    tt(p1_, Bim, qr_, ALU.mult, tabk + tyk, ptk, eng="pool")
                    tt(p2_, Bre, qi_, ALU.mult, tabk + tyk, ptk, eng="pool")
                    tt(sv[:, 1], p1_, p2_, ALU.add, ptk, stgk, eng="pool")

                def tables_b_pe(r):
                    for a in range(2):
                        for lg in range(4):
                            bnk = 4 + lg
                            for li_ in range(4):
                                mm(ps[bnk][:, li_ * 128:(li_ + 1) * 128], stg[:, a, lg * 4 + li_, :], ident, True, True,
                                   stgk + ["cm"], [("ps", bnk)])
                            act(btv[:, a, lg * 4:(lg + 1) * 4, :], ps[bnk][:, :].rearrange("p (j c) -> p j c", j=4), AF.Copy,
                                [("ps", bnk)], btk_)

                def ct_dve(r):
                    q = r % 4
                    if r > 0:
                        pq = (r - 1) % 4
                        mset(ct[:, :, :, pq * 32:(pq + 1) * 32], 0.0, ctk)
                    Cre, Cim = h32(cpad[:, r, 0, :]), h32(cpad[:, r, 1, :])
                    pr_, pi_ = b16(pw0[:, r, 0, :]), b16(pw0[:, r, 1, :])
                    cv = ct[:, :, :, q * 32:(q + 1) * 32]
                    tt(t1, Cre, pr_, ALU.mult, tabk + tyk, tmpk)
                    tt(t2, Cim, pi_, ALU.mult, tabk + tyk, tmpk)
                    tt(cv[:, 0], t1, t2, ALU.add, tmpk, ctk)
                    tt(t1, Cim, pr_, ALU.mult, tabk + tyk, tmpk)
                    tt(t2, Cre, pi_, ALU.mult, tabk + tyk, tmpk)
                    tt(cv[:, 1], t1, t2, ALU.subtract, tmpk, ctk)

                def bu_mm(r, a, wv_, wkk):
                    uv = u[:, r // 4, :].rearrange("p (c j) -> p j c", j=16)
                    for lg in range(4):
                        bnk = 4 + lg
                        for li_ in range(4):
                            j = lg * 4 + li_
                            mm(ps[bnk][:, li_ * 128:(li_ + 1) * 128], btv[:, a, j, :], uv[:, j, :], True, True, btk_ + uk, [("ps", bnk)])
                        act(wv_[:, lg * 4:(lg + 1) * 4, :], ps[bnk][:, :].rearrange("p (j c) -> p j c", j=4), AF.Copy, [("ps", bnk)], wkk)

                def scan(wn_, wkk):
                    P.op("dve", lambda e, wn_=wn_: e.tensor_tensor_scan(out=wn_, data0=pat, data1=wn_, initial=0.0, op0=ALU.mult, op1=ALU.add),
                         wkk + patk, wkk)

                tables_b_dve(0)
                tables_b_pe(0)
                for r in range(8):
                    oc = r // 4
                    q = r % 4
                    uv = u[:, oc, :].rearrange("p (c j) -> p j c", j=16)
                    wvs = [(wn_.rearrange("p (c j) -> p j c", j=16), wn_, kk_) for (wn_, kk_) in Wbufs]
                    if mode == "p1":
                        bu_mm(r, 0, wvs[0][0], wvs[0][2])
                        bu_mm(r, 1, wvs[1][0], wvs[1][2])
                        if r + 1 < 8:
                            tables_b_dve(r + 1)
                        for a in range(2):
                            scan(wvs[a][1], wvs[a][2])
                            cp(Eall[:, r, a, :], wvs[a][0][:, 15, :], wvs[a][2], ek)
                        if r + 1 < 8:
                            tables_b_pe(r + 1)
                        continue
                    bu_mm(r, 0, wvs[0][0], wvs[0][2])
                    bu_mm(r, 1, wvs[1][0], wvs[1][2])
                    ct_dve(r)
                    if r + 1 < 8:
                        tables_b_dve(r + 1)
                    for a in range(2):
                        wv_, wn_, wkk = wvs[a]
                        tt(wv_[:, 0, :], wv_[:, 0, :], inj[:, r, a, :], ALU.add, wkk + injk, wkk)
                        scan(wn_, wkk)
                        act(wb[:, a, :], wn_, AF.Copy, wkk, wbk)
                    if r + 1 < 8:
                        tables_b_pe(r + 1)
                    wbv = [wb[:, a, :].rearrange("p (c j) -> p j c", j=16) for a in range(2)]
                    for j in range(16):
                        o = ps[j // 4][:, (j % 4) * 128:(j % 4 + 1) * 128]
                        P.op("pe", lambda e, o=o, j=j, q=q, wbv=wbv: e.matmul(o, lhsT=ct[:, 0, j, :], rhs=wbv[0][:, j, :],
                                                                             start=(q == 0 and j % 4 == 0), stop=False, skip_group_check=True),
                             ctk + wbk, [("ps", j // 4)])
                        P.op("pe", lambda e, o=o, j=j, q=q, wbv=wbv: e.matmul(o, lhsT=ct[:, 1, j, :], rhs=wbv[1][:, j, :], start=False,
                                                                             stop=(q == 3), skip_group_check=True), ctk + wbk, [("ps", j // 4)])
                    if q == 3:
                        for tb in range(NTB):
                            yo = W + (tb % 2) * 512
                            yt = arf(yo, 512)
                            ytk = ark(yo, 512)
                            uv4 = uv[:, tb * 4:(tb + 1) * 4, :]
                            stt(yt.rearrange("p (j c) -> p j c", j=4), uv4, pcol(l, "s5d", oc, oc + 1),
                                ps[tb][:, :].rearrange("p (j c) -> p j c", j=4), ALU.mult, ALU.add, uk + ["pp", ("ps", tb)], ytk)
                            act(Yr[:, 6 + oc, :].rearrange("p (c j) -> p j c", j=16)[:, tb * 4:(tb + 1) * 4, :],
                                yt.rearrange("p (j c) -> p j c", j=4), AF.Gelu_apprx_tanh, ytk, [("Yr", 6 + oc, t_) for t_ in range(NTB)])
                if mode == "p1":
                    p15r, p15i = s5pw[:, l, :, 0, 14], s5pw[:, l, :, 1, 14]
                    n = 128
                    t1 = T1[:, 0:1024].rearrange("p (r c) -> p r c", r=8)
                    t2 = T1[:, 1024:2048].rearrange("p (r c) -> p r c", r=8)
                    t3 = T2[:, 0:1024].rearrange("p (r c) -> p r c", r=8)
                    t4 = T2[:, 1024:2048].rearrange("p (r c) -> p r c", r=8)
                    tt(t1, Eall[:, :, 0, :], bc8(p15r, n), ALU.mult, ek + pk, kT1)
                    tt(t2, Eall[:, :, 1, :], bc8(p15i, n), ALU.mult, ek + pk, kT1)
                    tt(t3, Eall[:, :, 1, :], bc8(p15r, n), ALU.mult, ek + pk, kT2)
                    tt(t4, Eall[:, :, 0, :], bc8(p15i, n), ALU.mult, ek + pk, kT2)
                    tt(Eall[:, :, 0, :], t1, t2, ALU.subtract, kT1, ek)
                    tt(Eall[:, :, 1, :], t3, t4, ALU.add, kT2, ek)
                    cur, ck_ = Eall, ek
                    bufs_ = [(bufA, kA), (bufB, kB)]
                    for lev in range(7):
                        n = 64 >> lev
                        dst, dk = bufs_[lev % 2]
                        cr, ci = s5lv[:, l, :, 0, lev], s5lv[:, l, :, 1, lev]
                        ev = [cur[:, :, a_, 0:2 * n].rearrange("p r (c two) -> p r c two", two=2)[:, :, :, 0] for a_ in range(2)]
                        od = [cur[:, :, a_, 0:2 * n].rearrange("p r (c two) -> p r c two", two=2)[:, :, :, 1] for a_ in range(2)]
                        t1 = T1[:, 0:8 * n].rearrange("p (r c) -> p r c", r=8)
                        t2 = T2[:, 0:8 * n].rearrange("p (r c) -> p r c", r=8)
                        tt(t1, ev[0], bc8(cr, n), ALU.mult, ck_ + lk, kT1)
                        tt(t2, ev[1], bc8(ci, n), ALU.mult, ck_ + lk, kT2)
                        tt(t1, t1, t2, ALU.subtract, kT1 + kT2, kT1)
                        tt(dst[:, :, 0, 0:n], t1, od[0], ALU.add, kT1 + ck_, dk)
                        tt(t1, ev[1], bc8(cr, n), ALU.mult, ck_ + lk, kT1)
                        tt(t2, ev[0], bc8(ci, n), ALU.mult, ck_ + lk, kT2)
                        tt(t1, t1, t2, ALU.add, kT1 + kT2, kT1)
                        tt(dst[:, :, 1, 0:n], t1, od[1], ALU.add, kT1 + ck_, dk)
                        cur, ck_ = dst, dk
                    cp(tiny[:, 800:816].rearrange("p (r a) -> p r a", a=2), cur[:, :, :, 0], ck_, tyk)
                    dma("sp", ccd[("x", l, "i")], tiny[:, 800:816], tyk, [("cci", "x", l)])
                    allgather("x", l, [("cci", "x", l)], [("cco", "x", l)])
                    return

            run_pass("p1")
            s5_combine(l)
            run_pass("full")
            SG = W
            for tb in range(NTB):
                sgs = []
                for m in range(2):
                    b = next_acc()
                    for k in range(2):
                        mm(ps[b][:, :], gluwS[:, l, k, m * 128:(m + 1) * 128], Yr[:, 6 + k, tb * TB:(tb + 1) * TB], k == 0, k == 1,
                           [("Yr", 6 + k, tb), "gluw"], [("ps", b)])
                    so = SG + ((tb * 2 + m) % 4) * 256
                    sg = arb(so, 256)
                    act(sg, ps[b][:, :], AF.Sigmoid, [("ps", b), "pp"], ark(so, 256), bias=pcol(l, "glub", m, m + 1))
                    sgs.append((sg, ark(so, 256)))
                for m in range(2):
                    sg, sgk = sgs[m]
                    tt(Yr[:, 6 + m, tb * TB:(tb + 1) * TB], Yr[:, 6 + m, tb * TB:(tb + 1) * TB], sg, ALU.mult,
                       [("Yr", 6 + m, tb)] + sgk, [("Yr", 6 + m, tb)])

        W1O = [7168, 9216]
        W2O = [11264, 13312]
        NG = 8

        def mlp_load(l, g):
            sl = g % 2
            w1g = arb(W1O[sl], 2048).rearrange("p (k c) -> p k c", k=8)
            w2g = arb(W2O[sl], 2048).rearrange("p (k c) -> p k c", k=4)
            w1k, w2k = ark(W1O[sl], 2048), ark(W2O[sl], 2048)
            dma("pool", w1g, w1[l, :, g * 512:(g + 1) * 512].rearrange("(k p) c -> p k c", p=128), [], w1k)
            dma("pool", w2g, w2[l, g * 512:(g + 1) * 512, :].rearrange("(k p) c -> p k c", p=128), [], w2k)

        def out_proj(l):
            WO, RSTD, SQ = 0, 4096, 6144
            wo = arb(WO, 4096).rearrange("p (k c) -> p k c", k=8)
            wok = ark(WO, 4096)
            dma("pool", wo, w_out[l, :, :].rearrange("(k p) c -> p k c", p=128), [], wok)
            mlp_load(l, 0)
            mlp_load(l, 1)
            for g in range(4):
                rms_stats(lambda k, tb: Yr[:, 2 * g + k, tb * TB:(tb + 1) * TB], 2, 256.0, RSTD, SQ,
                          lambda k, tb: [("Yr", 2 * g + k, tb)])
                for k in range(2):
                    ch = 2 * g + k
                    for tb in range(NTB):
                        stt(uT[:, ch, tb * TB:(tb + 1) * TB], Yr[:, ch, tb * TB:(tb + 1) * TB], pcol(l, "bnw", ch, ch + 1),
                            arf(RSTD + tb * TB, TB), ALU.mult, ALU.mult, [("Yr", ch, tb), "pp"] + ark(RSTD + tb * TB, TB), [("uT", ch, tb)])
            for m in range(KC):
                for tb in range(NTB):
                    b = next_acc()
                    for k in range(KC):
                        mm(ps[b][:, :], wo[:, k, m * 128:(m + 1) * 128], uT[:, k, tb * TB:(tb + 1) * TB], k == 0, k == KC - 1,
                           wok + [("uT", k, tb)], [("ps", b)])
                    stt(hT[:, m, tb * TB:(tb + 1) * TB], ps[b][:, :], G1(l, m), hT[:, m, tb * TB:(tb + 1) * TB], ALU.mult, ALU.add,
                        [("ps", b), ("der", l), ("hT", m, tb)], [("hT", m, tb)])

        def mlp(l):
            norm_mod(l, S2, B2)
            RS = 5120

            def up(g):
                sl = g % 2
                w1g = arb(W1O[sl], 2048).rearrange("p (k c) -> p k c", k=8)
                w1k = ark(W1O[sl], 2048)
                for jc in range(4):
                    yc = sl * 4 + jc
                    for tb in range(NTB):
                        b = next_acc()
                        for k in range(KC):
                            mm(ps[b][:, :], w1g[:, k, jc * 128:(jc + 1) * 128], uT[:, k, tb * TB:(tb + 1) * TB], k == 0, k == KC - 1,
                               w1k + [("uT", k, tb)], [("ps", b)])
                        ro = RS + ((jc * NTB + tb) % 4) * 256
                        rs = arb(ro, 256)
                        act(rs, ps[b][:, :], AF.Relu, [("ps", b)], ark(ro, 256))
                        tt(Yr[:, yc, tb * TB:(tb + 1) * TB], rs, rs, ALU.mult, ark(ro, 256), [("Yr", yc, tb)], eng="pool")

            def down(g):
                sl = g % 2
                w2g = arb(W2O[sl], 2048).rearrange("p (k c) -> p k c", k=4)
                w2k = ark(W2O[sl], 2048)
                for m in range(KC):
                    for tb in range(NTB):
                        b = next_acc()
                        for jc in range(4):
                            mm(ps[b][:, :], w2g[:, jc, m * 128:(m + 1) * 128], Yr[:, sl * 4 + jc, tb * TB:(tb + 1) * TB], jc == 0, jc == 3,
                               w2k + [("Yr", sl * 4 + jc, tb)], [("ps", b)])
                        stt(hT[:, m, tb * TB:(tb + 1) * TB], ps[b][:, :], G2(l, m), hT[:, m, tb * TB:(tb + 1) * TB], ALU.mult, ALU.add,
                            [("ps", b), ("der", l), ("hT", m, tb)], [("hT", m, tb)])

            up(0)
            for g in range(NG):
                if g + 1 < NG:
                    up(g + 1)
                down(g)
                if g + 2 < NG:
                    mlp_load(l, g + 2)

        def final_out(seg):
            RSTD, SQ, OUT = 0, 2048, 4096
            rms_stats(lambda k, tb: hT[:, k, tb * TB:(tb + 1) * TB], KC, float(DM), RSTD, SQ, lambda k, tb: [("hT", k, tb)])
            for k in range(KC):
                oo = OUT + (k % 2) * 2048
                o = arf(oo, 2048)
                for tb in range(NTB):
                    stt(o[:, tb * TB:(tb + 1) * TB], hT[:, k, tb * TB:(tb + 1) * TB], pcol(0, "fnw", k, k + 1), arf(RSTD + tb * TB, TB),
                        ALU.mult, ALU.mult, [("hT", k, tb), "pp"] + ark(RSTD + tb * TB, TB), ark(oo, 2048))
                dma("sp", yT[k * 128:(k + 1) * 128, :], o, ark(oo, 2048), [("yT", k, seg)])

        stopped = False
        for seg in range(nseg):
            if stopped:
                break
            for k in range(KC):
                dma("sp", hT[:, k, :], xT[k * 128:(k + 1) * 128, :], [], [("hT", k, tb) for tb in range(NTB)])
            for l in range(depth):
                norm_mod(l, S1, B1)
                if stop_after == (seg, l, "u"):
                    stopped = True
                    break
                print("ops before tail", P.total)
                tail_prepass(l)
                print("ops before s5 p1", P.total)
                mixer_s5(l, seg)
                print("ops before halo", P.total)
                halo_apply(l)
                mixer_pool(l, seg)
                mixer_sconv(l, seg)
                print("ops before ssd", P.total)
                mixer_ssd(l, seg)
                print("ops after ssd", P.total)
                if stop_after == (seg, l, "mix"):
                    stopped = True
                    break
                out_proj(l)
                if stop_after == (seg, l, "hmix"):
                    stopped = True
                    break
                mlp(l)
                if stop_after == (seg, l, "h"):
                    stopped = True
                    break
            if not stopped:
                final_out(seg)
        print("total ops recorded:", P.total)
        P.limit = None
        if dbg:
            if "uT" in dbg_out:
                cp(AR[:, 0:2048], uT[:, 0, :], all_keys("uT"), ark(0, 2048))
            for name in dbg_out:
                if name == "Yr":
                    for k in range(KC):
                        o = arf((k % 2) * 2048, 2048)
                        cp(o, Yr[:, k, :], [("Yr", k, tb) for tb in range(NTB)], ark((k % 2) * 2048, 2048))
                        dma("sp", dbg_out[name][k * 128:(k + 1) * 128, :], o, ark((k % 2) * 2048, 2048), [("dbg", name, k)])
                elif name == "uT":
                    for k in range(KC):
                        o = arf((k % 2) * 2048, 2048)
                        cp(o, uT[:, k, :], [("uT", k, tb) for tb in range(NTB)], ark((k % 2) * 2048, 2048))
                        dma("sp", dbg_out[name][k * 128:(k + 1) * 128, :], o, ark((k % 2) * 2048, 2048), [("dbg", name, k)])
                elif name == "hT":
                    for k in range(KC):
                        dma("sp", dbg_out[name][k * 128:(k + 1) * 128, :], hT[:, k, :], [("hT", k, tb) for tb in range(NTB)], [("dbg", name, k)])
                elif name == "mod":
                    dma("sp", dbg_out[name], modT[:, :, :].rearrange("p l c -> p (l c)"), [("mod", 0), ("mod", 1)], [("dbg", name)])
        P.wait_all("sp")

        with nc.Block() as block:
            def replay(e, name):
                for waits, fn, inc in P.q[name]:
                    for s, v in waits:
                        e.wait_ge(sems[s], v)
                    if fn is not None:
                        fn(e).then_inc(sems[inc[0]], inc[1])

            @block.tensor
            def _(e):
                replay(e, "pe")

            @block.scalar
            def _(e):
                replay(e, "act")

            @block.vector
            def _(e):
                replay(e, "dve")

            @block.gpsimd
            def _(e):
                replay(e, "pool")

            @block.sync
            def _(e):
                replay(e, "sp")
    return nc


def _fm(v):
    return np.ascontiguousarray(v.reshape(-1, 128).T)


def _pack_params(inp, b, sg):
    L = DEPTH
    pp = np.zeros((L, 128, NPCOL), np.float32)

    def put(l, name, arr):
        o, w = PCOL[name]
        arr = np.asarray(arr, np.float32).reshape(128, w)
        pp[l, :, o:o + w] = arr
    wins = (2, 4, 8, 16)
    for l in range(L):
        put(l, "nw1", _fm(inp["norm_mix_w"][l]))
        put(l, "nw2", _fm(inp["norm_mlp_w"][l]))
        put(l, "bnw", _fm(inp["branch_norm_w"][l]))
        put(l, "fnw", _fm(inp["final_norm_w"]))
        adab = np.zeros((128, 48), np.float32)
        adab[:, 0:12] = _fm(inp["ada_b"][l])[:, sg * 12:(sg + 1) * 12]
        put(l, "adab", adab)
        put(l, "pscale", _fm(inp["pool_scale"][l]))
        put(l, "scw", inp["sconv_w"][l].reshape(3, 2, 128).transpose(2, 1, 0))
        put(l, "cvw", inp["ssd_conv_w"][l].reshape(4, 6, 128).transpose(2, 1, 0))
        put(l, "cvb", _fm(inp["ssd_conv_b"][l]))
        put(l, "dtb", np.broadcast_to(inp["ssd_dt_bias"][l][None, :], (128, 4)))
        put(l, "alog", np.broadcast_to(inp["ssd_a_log"][l][None, :], (128, 4)))
        put(l, "dsk", np.broadcast_to(inp["ssd_d"][l][None, :], (128, 4)))
        def gp(a):
            return a.reshape(8, 2, 64).transpose(1, 2, 0).reshape(128, 8)
        put(l, "are", gp(inp["s5_a_re"][l]))
        put(l, "aim", gp(inp["s5_a_im"][l]))
        put(l, "lst", gp(np.broadcast_to(inp["s5_log_step"][l][:, None], (16, 64))))
        put(l, "s5d", _fm(inp["s5_d"][l]))
        put(l, "glub", _fm(inp["s5_glu_b"][l]))
        put(l, "cond", _fm(inp["c"][b]))
        invc = np.zeros((128, 2, 16), np.float32)
        for c in range(2):
            for half in range(2):
                win = wins[c * 2 + half]
                if sg == 0:
                    invc[half * 64:(half + 1) * 64, c, :] = 1.0 / np.minimum(np.arange(16) + 1, win)
                else:
                    invc[half * 64:(half + 1) * 64, c, :] = 1.0 / win
        put(l, "invc", invc)
        sel = np.zeros((128, 8), np.float32)
        if sg > 0:
            sel[:, sg - 1] = 1.0
        sel[:, 4 + sg] = 1.0
        put(l, "sel", sel)
    return pp


def _consts():
    cm = np.zeros((128, 4, 128), np.float32)
    i = np.arange(128)
    cm[:, 0, :] = (i[:, None] == i[None, :])
    cm[:, 1, :] = 1.0
    cm[:, 2, :] = (i[:, None] <= i[None, :])
    cm[:, 3, :] = np.where(i[None, :] < i[:, None], -30000.0, 0.0)
    return cm


def _host_inputs(inp, b, sg):
    f = lambda a: np.ascontiguousarray(np.asarray(a, np.float32))
    pw = np.zeros((DEPTH, 128, 2, 128), np.float32)
    for l in range(DEPTH):
        for g in range(4):
            c, half = g // 2, g % 2
            pw[l, half * 64:(half + 1) * 64, c, half * 64:(half + 1) * 64] = inp["pool_w"][l, g]
    gw = np.ascontiguousarray(np.asarray(inp["s5_glu_w"], np.float32).reshape(DEPTH, 2, 128, 256).transpose(0, 2, 1, 3))
    s5b = np.zeros((DEPTH, 128, 256), np.float32)
    s5c = np.zeros((DEPTH, 128, 8, 2, 32), np.float32)
    for l in range(DEPTH):
        s5b[l, :, 0:128] = np.asarray(inp["s5_b_re"][l]).reshape(8, 2, 64, 16).transpose(1, 2, 0, 3).reshape(128, 128)
        s5b[l, :, 128:256] = np.asarray(inp["s5_b_im"][l]).reshape(8, 2, 64, 16).transpose(1, 2, 0, 3).reshape(128, 128)
        for ri, nm in enumerate(("s5_c_re", "s5_c_im")):
            cc = np.asarray(inp[nm][l]).reshape(8, 2, 16, 64)
            for r in range(8):
                for gi in range(2):
                    s5c[l, gi * 64:(gi + 1) * 64, r, ri, gi * 16:gi * 16 + 16] = cc[r, gi].T
    return {
        "s5b": s5b, "s5c": s5c.reshape(DEPTH, 128, 512),
        "xT": f(np.asarray(inp["x"][b][sg * T:(sg + 1) * T]).T),
        "pp": _pack_params(inp, b, sg),
        "cmat": _consts(),
        "ada_w": f(np.asarray(inp["ada_w"])[:, :, sg * 1536:(sg + 1) * 1536]), "w_in": f(inp["w_in"]), "w_out": f(inp["w_out"]),
        "mlp_w1": f(inp["mlp_w1"]), "mlp_w2": f(inp["mlp_w2"]),
        "poolw": pw, "gluw": gw,
    }


_NC_CACHE = {}


def kernel(**inputs):
    inp = {k: np.asarray(v) for k, v in inputs.items()}
    if "full" not in _NC_CACHE:
        _NC_CACHE["full"] = build()
    nc = _NC_CACHE["full"]
    in_maps = [_host_inputs(inp, r // 4, r % 4) for r in range(8)]
    res = run_bass_kernel_spmd(nc, in_maps, core_ids=list(range(8)))
    out = np.empty((2, SEQ, DM), np.float32)
    for r in range(8):
        out[r // 4, (r % 4) * T:(r % 4 + 1) * T, :] = res.results[r]["yT"].T
    return out
```

```python
import numpy as np
import concourse.bass as bass
import concourse.mybir as mybir
from concourse.bass_utils import run_bass_kernel_spmd

F32, BF16 = mybir.dt.float32, mybir.dt.bfloat16
AF = mybir.ActivationFunctionType
ALU = mybir.AluOpType

T = 2048
TB = 512
NTB = 4
NCH = 16
DM = 1024
KC = 8
SEQ = 8192
DEPTH = 2
EPS = 1e-6
ENGS = ("pe", "act", "dve", "pool", "sp")
NDMA = 12

PCOL = {}
_off = 0
for _n, _w in [("nw1", 8), ("nw2", 8), ("bnw", 8), ("fnw", 8), ("adab", 48), ("pscale", 2), ("scw", 6),
               ("cvw", 24), ("cvb", 6), ("dtb", 4), ("alog", 4), ("dsk", 4), ("are", 8), ("aim", 8), ("lst", 8),
               ("s5d", 2), ("glub", 2), ("cond", 8),
               ("invc", 32), ("sel", 8)]:
    PCOL[_n] = (_off, _w)
    _off += _w
NPCOL = _off


class Prog:
    def __init__(self):
        self.q = {e: [] for e in ENGS}
        self.cnt = {e: 0 for e in ENGS}
        self.seen = {e: {} for e in ENGS}
        self.lastw = {}
        self.rd = {}
        self.dma_i = 0
        self.fam = {}
        import os as _os
        self.nosame = set(x for x in _os.environ.get("KNOSAME", "").split(",") if x)
        self.total = 0
        import os
        self.limit = int(os.environ.get("KLIMIT", "0")) or None

    def _skip(self):
        self.total += 1
        return self.limit is not None and self.total > self.limit

    def _deps(self, eng, reads, writes):
        need = {}

        def add(d):
            if d is None:
                return
            s, v = d
            if need.get(s, 0) < v:
                need[s] = v
        for k0 in reads:
            for k in self._rel(k0):
                add(self.lastw.get(k))
        for k0 in writes:
            for k in self._rel(k0):
                add(self.lastw.get(k))
                for s, v in self.rd.get(k, {}).items():
                    add((s, v))
        out = []
        for s, v in need.items():
            if s == eng and (eng == "pe" or eng in self.nosame):
                continue
            if self.seen[eng].get(s, 0) >= v:
                continue
            self.seen[eng][s] = v
            out.append((s, v))
        return out

    def _rel(self, k):
        if isinstance(k, tuple) and k[0] == "ps":
            fam = self.fam.setdefault(k[1], set())
            fam.add(k)
            if len(k) == 2:
                return list(fam)
            return [k, ("ps", k[1])]
        return [k]

    def _commit(self, reads, writes, tok):
        s, v = tok
        for k in reads:
            d = self.rd.setdefault(k, {})
            if d.get(s, 0) < v:
                d[s] = v
        for k in writes:
            self.lastw[k] = tok
            self.rd[k] = {}

    @staticmethod
    def _norm(reads, writes):
        r2, w2 = [], list(writes)
        for k in reads:
            if isinstance(k, tuple) and k[0] == "ps":
                w2.append(k)
            else:
                r2.append(k)
        w2 = [("ps", k[1]) if (isinstance(k, tuple) and k[0] == "ps") else k for k in w2]
        return r2, w2

    def op(self, eng, fn, reads=(), writes=()):
        if self._skip():
            return
        reads, writes = self._norm(reads, writes)
        waits = self._deps(eng, reads, writes)
        self.cnt[eng] += 1
        self.q[eng].append((waits, fn, (eng, 1)))
        self._commit(reads, writes, (eng, self.cnt[eng]))

    def dma(self, eng, fn, reads=(), writes=()):
        if self._skip():
            return None
        reads = list(reads)
        writes = list(writes)
        i = self.dma_i
        self.dma_i += 1
        sem = "dma%d" % (i % NDMA)
        val = 16 * (i // NDMA + 1)
        waits = self._deps(eng, reads, writes)
        if i >= NDMA:
            prev = 16 * (i // NDMA)
            if self.seen[eng].get(sem, 0) < prev:
                self.seen[eng][sem] = prev
                waits.append((sem, prev))
        self.q[eng].append((waits, fn, (sem, 16)))
        self._commit(reads, writes, (sem, val))
        return (sem, val)

    def cc(self, fn, sem, reads=(), writes=()):
        if self._skip():
            return
        reads, writes = self._norm(reads, writes)
        waits = self._deps("pool", reads, writes)
        self.q["pool"].append((waits, fn, (sem, 1)))
        self._commit(reads, writes, (sem, 1))

    def wait_all(self, eng):
        waits = []
        for e in ENGS:
            if e != eng and self.cnt[e] > self.seen[eng].get(e, 0):
                self.seen[eng][e] = self.cnt[e]
                waits.append((e, self.cnt[e]))
        for j in range(min(NDMA, self.dma_i)):
            sem = "dma%d" % j
            n = (self.dma_i - 1 - j) // NDMA + 1
            if self.seen[eng].get(sem, 0) < 16 * n:
                self.seen[eng][sem] = 16 * n
                waits.append((sem, 16 * n))
        self.q[eng].append((waits, None, None))


class Buf:
    def __init__(self, name, ap):
        self.name = name
        self.ap = ap

    def k(self, *idx):
        return (self.name,) + tuple(idx)


def build(depth=DEPTH, dbg=None, stop_after=None):
    nseg = 1
    nc = bass.Bass("TRN2", target_bir_lowering=False)
    P = Prog()
    dram = {}

    def din(name, shape):
        dram[name] = nc.dram_tensor(name, list(shape), F32, kind="ExternalInput").ap()
        return dram[name]

    xT = din("xT", [DM, T])
    pp = din("pp", [DEPTH, 128, NPCOL])
    cmat = din("cmat", [128, 4, 128])
    ada_w = din("ada_w", [DEPTH, DM, 1536])
    w_in = din("w_in", [DEPTH, DM, 2308])
    w_out = din("w_out", [DEPTH, DM, DM])
    w1 = din("mlp_w1", [DEPTH, DM, 4 * DM])
    w2 = din("mlp_w2", [DEPTH, 4 * DM, DM])
    poolw = din("poolw", [DEPTH, 128, 2, 128])
    gluw = din("gluw", [DEPTH, 128, 2, 256])
    s5b = din("s5b", [DEPTH, 128, 256])
    s5c = din("s5c", [DEPTH, 128, 512])
    yT = nc.dram_tensor("yT", [DM, T], F32, kind="ExternalOutput").ap()
    GRP = [[0, 1, 2, 3], [4, 5, 6, 7]]
    ccd = {}
    for l_ in range(DEPTH):
        for nm_, w_ in (("h", 64), ("x", 16), ("s", 272), ("m", 32)):
            ccd[(nm_, l_, "i")] = nc.dram_tensor("cc%s%di" % (nm_, l_), [128, w_], F32, kind="Internal").ap()
            ccd[(nm_, l_, "o")] = nc.dram_tensor("cc%s%do" % (nm_, l_), [4 * 128, w_], F32, kind="Internal").ap()
    dbg_out = {}
    if dbg:
        for name, shape in dbg.items():
            dbg_out[name] = nc.dram_tensor("dbg_" + name, list(shape), F32, kind="ExternalOutput").ap()

    import contextlib
    es = contextlib.ExitStack()
    with es:
        def sb(name, shape, dt):
            return es.enter_context(nc.sbuf_tensor(name, list(shape), dt))

        hT = sb("hT", [128, KC, T], F32)
        uT = sb("uT", [128, KC, T], BF16)
        Yr = sb("Yr", [128, KC, T], BF16)
        AR = sb("AR", [128, 15360], F32)
        cm = sb("cm", [128, 4, 128], BF16)
        ppS = sb("ppS", [128, DEPTH, NPCOL], F32)
        modT = sb("modT", [128, DEPTH, 48], F32)
        der = sb("der", [128, DEPTH, 64], F32)
        condb = sb("condb", [128, 8], BF16)
        epsT = sb("epsT", [128, 2], F32)
        poolwS = sb("poolwS", [128, DEPTH, 2, 128], BF16)
        gluwS = sb("gluwS", [128, DEPTH, 2, 256], BF16)
        cPool = sb("cPool", [128, DEPTH, 2, 16], F32)
        cSc = sb("cSc", [128, DEPTH, 2, 2], F32)
        cCv = sb("cCv", [128, DEPTH, 6, 3], F32)
        cS = sb("cS", [128, DEPTH, 256], F32)
        cX = sb("cX", [128, DEPTH, 8, 2], F32)
        ssdp = sb("ssdp", [128, DEPTH, 16], F32)
        s5pw = sb("s5pw", [128, DEPTH, 8, 3, 16], F32)
        s5lv = sb("s5lv", [128, DEPTH, 8, 3, 8], F32)
        bbS = sb("bbS", [128, DEPTH, 2, 8, 16], F32)
        tiny = sb("tiny", [128, 896], F32)
        tinyb = sb("tinyb", [128, 128], BF16)

        ps = [es.enter_context(nc.psum_tensor("ps%d" % i, [128, 512], F32)) for i in range(8)]
        sems = {}
        for e in ENGS:
            sems[e] = es.enter_context(nc.semaphore("s_" + e))
        for j in range(NDMA):
            sems["dma%d" % j] = es.enter_context(nc.semaphore("s_dma%d" % j))
        for l_ in range(DEPTH):
            for nm_ in ("h", "x", "s", "m"):
                sems["cc%s%d" % (nm_, l_)] = es.enter_context(nc.semaphore("s_cc%s%d" % (nm_, l_)))

        def allgather(nm_, l_, reads, writes):
            i_, o_ = ccd[(nm_, l_, "i")], ccd[(nm_, l_, "o")]
            P.cc(lambda e: e.collective_compute("AllGather", ALU.bypass, replica_groups=GRP, ins=[i_], outs=[o_]),
                 "cc%s%d" % (nm_, l_), reads, writes)

        ARb = AR[:, :].bitcast(BF16)

        def arf(off, n):
            return AR[:, off:off + n]

        def arb(off, n):
            return ARb[:, 2 * off:2 * (off + n)]

        def ark(off, n):
            return [("ar", b) for b in range(off // 256, (off + n + 255) // 256)]

        ident = cm[:, 0, :]
        ones = cm[:, 1, :]
        tri = cm[:, 2, :]
        negm = cm[:, 3, :]

        def pcol(l, name, a=0, b=None):
            o, w = PCOL[name]
            if b is None:
                b = w
            return ppS[:, l, o + a:o + b]

        def mm(out, lhsT, rhs, start, stop, reads, writes):
            P.op("pe", lambda e: e.matmul(out, lhsT=lhsT, rhs=rhs, start=start, stop=stop), reads, writes)

        def act(out, in_, func, reads, writes, bias=None, scale=None):
            kw = {}
            if bias is not None:
                kw["bias"] = bias
            if scale is not None:
                kw["scale"] = scale
            P.op("act", lambda e: e.activation(out=out, in_=in_, func=func, **kw), reads, writes)

        def tt(out, in0, in1, op, reads, writes, eng="dve"):
            P.op(eng, lambda e: e.tensor_tensor(out=out, in0=in0, in1=in1, op=op), reads, writes)

        def ts(out, in0, s1, s2, op0, op1, reads, writes, eng="dve"):
            if s2 is None:
                P.op(eng, lambda e: e.tensor_scalar(out=out, in0=in0, scalar1=s1, scalar2=None, op0=op0), reads, writes)
            else:
                P.op(eng, lambda e: e.tensor_scalar(out=out, in0=in0, scalar1=s1, scalar2=s2, op0=op0, op1=op1), reads, writes)

        def stt(out, in0, scalar, in1, op0, op1, reads, writes, eng="dve"):
            P.op(eng, lambda e: e.scalar_tensor_tensor(out=out, in0=in0, scalar=scalar, in1=in1, op0=op0, op1=op1), reads, writes)

        def cp(out, in_, reads, writes, eng="dve"):
            P.op(eng, lambda e: e.tensor_copy(out=out, in_=in_), reads, writes)

        def mset(ap, val, writes, eng="dve"):
            P.op(eng, lambda e: e.memset(ap, val), [], writes)

        def dma(eng, out, in_, reads, writes):
            return P.dma(eng, lambda e: e.dma_start(out=out, in_=in_), reads, writes)

        dma("pool", cm[:, :, :], cmat, [], ["cm"])
        dma("sp", ppS[:, :, :], pp.rearrange("l p c -> p l c"), [], ["pp"])
        dma("pool", poolwS[:, :, :, :], poolw.rearrange("l p c d -> p l c d"), [], ["poolw"])
        dma("pool", gluwS[:, :, :, :], gluw.rearrange("l p c d -> p l c d"), [], ["gluw"])
        mset(epsT[:, 0:1], EPS, ["eps"])
        mset(epsT[:, 1:2], 1.0, ["eps"])
        for t_, kk in ((cPool, "cPool"), (cSc, "cSc"), (cCv, "cCv"), (cS, "cS"), (cX, "cX")):
            mset(t_[:], 0.0, [kk])
        act(condb[:, :], pcol(0, "cond"), AF.Silu, ["pp"], ["condb"])

        WADA = 0
        bi = 0
        for l in range(DEPTH):
            for blk in range(3):
                slot = bi % 2
                bi += 1
                wsl = arb(WADA + slot * 2048, 2048).rearrange("p (k c) -> p k c", k=8)
                wk = ark(WADA + slot * 2048, 2048)
                dma("pool", wsl, ada_w[l, :, blk * 512:(blk + 1) * 512].rearrange("(k p) c -> p k c", p=128), [], wk)
                for jj in range(4):
                    j = l * 12 + blk * 4 + jj
                    for k in range(KC):
                        mm(ps[6][:, j:j + 1], wsl[:, k, jj * 128:(jj + 1) * 128], condb[:, k:k + 1], k == 0, k == KC - 1,
                           wk + ["condb"], [("ps", 6)])
        mpk = tiny[:, 0:24].rearrange("p (l c) -> p l c", l=2)
        tyk0 = [("tiny", "all")]
        for l in range(DEPTH):
            tt(mpk[:, l, :], ps[6][:, l * 12:(l + 1) * 12], pcol(l, "adab", 0, 12), ALU.add, [("ps", 6), "pp"], tyk0)
        dma("sp", ccd[("m", 0, "i")][:, 0:24], tiny[:, 0:24], tyk0, [("cci", "m", 0)])
        allgather("m", 0, [("cci", "m", 0)], [("cco", "m", 0)])
        mrb = tiny[:, 32:160].rearrange("p (r f) -> p r f", r=4)
        dma("sp", mrb, ccd[("m", 0, "o")].rearrange("(r p) f -> p r f", p=128), [("cco", "m", 0)], tyk0)
        for l in range(DEPTH):
            for sg_ in range(4):
                cp(modT[:, l, sg_ * 12:(sg_ + 1) * 12], mrb[:, sg_, l * 12:(l + 1) * 12], tyk0, [("mod", l)])
        for l in range(depth):
            stt(der[:, l, 0:8], modT[:, l, 8:16], 1.0, pcol(l, "nw1"), ALU.add, ALU.mult, [("mod", l), "pp"], [("der", l)])
            stt(der[:, l, 24:32], modT[:, l, 32:40], 1.0, pcol(l, "nw2"), ALU.add, ALU.mult, [("mod", l), "pp"], [("der", l)])
            cp(der[:, l, 8:16], modT[:, l, 0:8], [("mod", l)], [("der", l)])
            cp(der[:, l, 16:24], modT[:, l, 16:24], [("mod", l)], [("der", l)])
            cp(der[:, l, 32:40], modT[:, l, 24:32], [("mod", l)], [("der", l)])
            cp(der[:, l, 40:48], modT[:, l, 40:48], [("mod", l)], [("der", l)])

        def S1(l, k): return der[:, l, 0 + k:1 + k]
        def B1(l, k): return der[:, l, 8 + k:9 + k]
        def G1(l, k): return der[:, l, 16 + k:17 + k]
        def S2(l, k): return der[:, l, 24 + k:25 + k]
        def B2(l, k): return der[:, l, 32 + k:33 + k]
        def G2(l, k): return der[:, l, 40 + k:41 + k]

        for l in range(depth):
            act(ssdp[:, l, 0:4], pcol(l, "alog"), AF.Exp, ["pp"], [("ssdp", l)])
            ts(ssdp[:, l, 0:4], ssdp[:, l, 0:4], -1.0, None, ALU.mult, None, [("ssdp", l)], [("ssdp", l)])
            cp(ssdp[:, l, 4:8], pcol(l, "dtb"), ["pp"], [("ssdp", l)])
            cp(ssdp[:, l, 8:12], pcol(l, "dsk"), ["pp"], [("ssdp", l)])

        def tk(n): return [("tiny", n)]
        for l in range(depth):
            tyk = [("tiny", "all")]
            stp = tiny[:, 0:8]
            act(stp, pcol(l, "lst"), AF.Exp, ["pp"], tyk)
            mag = tiny[:, 8:16]
            tt(mag, pcol(l, "are"), stp, ALU.mult, ["pp"] + tyk, tyk)
            act(mag, mag, AF.Exp, tyk, tyk)
            th = tiny[:, 16:24]
            tt(th, pcol(l, "aim"), stp, ALU.mult, ["pp"] + tyk, tyk)
            sa = tiny[:, 24:32]
            ca = tiny[:, 32:40]
            act(sa, th, AF.Sin, tyk, tyk, scale=1.0 / 16.0)
            ts(ca, th, 1.0 / 16.0, float(np.pi / 2), ALU.mult, ALU.add, tyk, tyk)
            act(ca, ca, AF.Sin, tyk, tyk)
            for _ in range(4):
                t2a, t2b = tiny[:, 272:280], tiny[:, 280:288]
                tt(t2a, ca, ca, ALU.mult, tyk, tyk)
                tt(t2b, sa, sa, ALU.mult, tyk, tyk)
                tt(sa, sa, ca, ALU.mult, tyk, tyk)
                ts(sa, sa, 2.0, None, ALU.mult, None, tyk, tyk)
                tt(ca, t2a, t2b, ALU.subtract, tyk, tyk)
            lr = tiny[:, 40:48]
            li = tiny[:, 48:56]
            tt(lr, mag, ca, ALU.mult, tyk, tyk)
            tt(li, mag, sa, ALU.mult, tyk, tyk)
            den = tiny[:, 56:64]
            t0 = tiny[:, 64:72]
            tt(den, pcol(l, "are"), pcol(l, "are"), ALU.mult, ["pp"] + tyk, tyk)
            tt(t0, pcol(l, "aim"), pcol(l, "aim"), ALU.mult, ["pp"] + tyk, tyk)
            tt(den, den, t0, ALU.add, tyk, tyk)
            P.op("dve", lambda e, den=den: e.reciprocal(out=den, in_=den), tyk, tyk)
            nr = tiny[:, 72:80]
            ts(nr, lr, -1.0, None, ALU.add, None, tyk, tyk)
            fr = tiny[:, 80:88]
            fi = tiny[:, 88:96]
            t1 = tiny[:, 96:104]
            tt(fr, nr, pcol(l, "are"), ALU.mult, ["pp"] + tyk, tyk)
            tt(t1, li, pcol(l, "aim"), ALU.mult, ["pp"] + tyk, tyk)
            tt(fr, fr, t1, ALU.add, tyk, tyk)
            tt(fr, fr, den, ALU.mult, tyk, tyk)
            tt(fi, li, pcol(l, "are"), ALU.mult, ["pp"] + tyk, tyk)
            tt(t1, nr, pcol(l, "aim"), ALU.mult, ["pp"] + tyk, tyk)
            tt(fi, fi, t1, ALU.subtract, tyk, tyk)
            tt(fi, fi, den, ALU.mult, tyk, tyk)
            dma("sp", tiny[:, 512:768], s5b[l], [], tyk)
            bre = tiny[:, 512:640].rearrange("p (r h) -> p r h", r=8)
            bim = tiny[:, 640:768].rearrange("p (r h) -> p r h", r=8)
            frb = fr.unsqueeze(2).to_broadcast([128, 8, 16])
            fib = fi.unsqueeze(2).to_broadcast([128, 8, 16])
            tmpb = tiny[:, 128:256].rearrange("p (r h) -> p r h", r=8)
            tt(bbS[:, l, 0, :, :], bre, frb, ALU.mult, ["pp"] + tyk, [("bb", l)])
            tt(tmpb, bim, fib, ALU.mult, ["pp"] + tyk, tyk)
            tt(bbS[:, l, 0, :, :], bbS[:, l, 0, :, :], tmpb, ALU.subtract, tyk + [("bb", l)], [("bb", l)])
            tt(bbS[:, l, 1, :, :], bim, frb, ALU.mult, ["pp"] + tyk, [("bb", l)])
            tt(tmpb, bre, fib, ALU.mult, ["pp"] + tyk, tyk)
            tt(bbS[:, l, 1, :, :], bbS[:, l, 1, :, :], tmpb, ALU.add, tyk + [("bb", l)], [("bb", l)])
            ts(bbS[:, l, 1, :, :], bbS[:, l, 1, :, :], -1.0, None, ALU.mult, None, [("bb", l)], [("bb", l)])
            ts(li, li, -1.0, None, ALU.mult, None, tyk, tyk)
            pk = [("s5pw", l)]
            cp(s5pw[:, l, :, 0, 0], lr, tyk, pk)
            cp(s5pw[:, l, :, 1, 0], li, tyk, pk)
        tyk = [("tiny", "all")]
        pkA = [("s5pw", l_) for l_ in range(DEPTH)]
        lkA = [("s5lv", l_) for l_ in range(DEPTH)]
        lrA, liA = s5pw[:, :, :, 0, 0], s5pw[:, :, :, 1, 0]
        taA = tiny[:, 256:256 + 8 * DEPTH].rearrange("p (l r) -> p l r", l=DEPTH)
        tbA = tiny[:, 288:288 + 8 * DEPTH].rearrange("p (l r) -> p l r", l=DEPTH)
        for j in range(1, 16):
            pr_, pi_ = s5pw[:, :, :, 0, j - 1], s5pw[:, :, :, 1, j - 1]
            nr_, ni_ = s5pw[:, :, :, 0, j], s5pw[:, :, :, 1, j]
            tt(taA, pr_, lrA, ALU.mult, tyk + pkA, tyk)
            tt(tbA, pi_, liA, ALU.mult, tyk + pkA, tyk)
            tt(nr_, taA, tbA, ALU.subtract, tyk, pkA)
            tt(taA, pr_, liA, ALU.mult, tyk + pkA, tyk)
            tt(tbA, pi_, lrA, ALU.mult, tyk + pkA, tyk)
            tt(ni_, taA, tbA, ALU.add, tyk, pkA)
        for l in range(depth):
            ts(s5pw[:, l, :, 2, :], s5pw[:, l, :, 1, :], -1.0, None, ALU.mult, None, pkA, pkA)
        cp(s5lv[:, :, :, 0, 0], s5pw[:, :, :, 0, 15], pkA, lkA)
        cp(s5lv[:, :, :, 1, 0], s5pw[:, :, :, 1, 15], pkA, lkA)
        for k in range(1, 8):
            pr_, pi_ = s5lv[:, :, :, 0, k - 1], s5lv[:, :, :, 1, k - 1]
            tt(taA, pr_, pr_, ALU.mult, lkA + tyk, tyk)
            tt(tbA, pi_, pi_, ALU.mult, lkA + tyk, tyk)
            tt(s5lv[:, :, :, 0, k], taA, tbA, ALU.subtract, tyk, lkA)
            tt(taA, pr_, pi_, ALU.mult, lkA + tyk, tyk)
            ts(s5lv[:, :, :, 1, k], taA, 2.0, None, ALU.mult, None, tyk, lkA)
        for l in range(depth):
            ts(s5lv[:, l, :, 2, :], s5lv[:, l, :, 1, :], -1.0, None, ALU.mult, None, lkA, lkA)

        acc_i = [0]

        def next_acc():
            b = acc_i[0] % 4
            acc_i[0] += 1
            return b

        def rms_stats(src_fn, nchunks, denom, rstd_off, sq_off, src_keys_fn):
            for tb in range(NTB):
                bank = 4 + (tb % 2)
                for k in range(nchunks):
                    so = sq_off + ((tb * nchunks + k) % 4) * 256
                    sq = arb(so, 256)
                    src = src_fn(k, tb)
                    tt(sq, src, src, ALU.mult, src_keys_fn(k, tb), ark(so, 256), eng="pool")
                    mm(ps[bank][:, :], ones, sq, k == 0, k == nchunks - 1, ark(so, 256) + ["cm"], [("ps", bank)])
                r = arf(rstd_off + tb * TB, TB)
                rk = ark(rstd_off + tb * TB, TB)
                act(r, ps[bank][:, :], AF.Ln, [("ps", bank), "eps"], rk, bias=epsT[:, 0:1], scale=1.0 / denom)
                act(r, r, AF.Exp, rk, rk, scale=-0.5)

        def norm_mod(l, s_fn, b_fn):
            RSTD, SQ, TMP = 0, 2048, 3072
            rms_stats(lambda k, tb: hT[:, k, tb * TB:(tb + 1) * TB], KC, float(DM), RSTD, SQ,
                      lambda k, tb: [("hT", k, tb)])
            for k in range(KC):
                for tb in range(NTB):
                    to = TMP + ((k * NTB + tb) % 4) * TB
                    tmp = arf(to, TB)
                    stt(tmp, hT[:, k, tb * TB:(tb + 1) * TB], s_fn(l, k), arf(RSTD + tb * TB, TB), ALU.mult, ALU.mult,
                        [("hT", k, tb), ("der", l)] + ark(RSTD + tb * TB, TB), ark(to, TB))
                    act(uT[:, k, tb * TB:(tb + 1) * TB], tmp, AF.Identity, ark(to, TB) + [("der", l)], [("uT", k, tb)],
                        bias=b_fn(l, k))

        WIN = 14336
        win_i = [0]

        def load_win(l, c0, ncol):
            assert ncol <= 128
            slot = win_i[0] % 2
            win_i[0] += 1
            off = WIN + slot * 512
            w = arb(off, 512).rearrange("p (k c) -> p k c", k=8)
            dma("pool", w[:, :, 0:ncol], w_in[l, :, c0:c0 + ncol].rearrange("(k p) c -> p k c", p=128), [], ark(off, 512))
            return w, ark(off, 512)

        def proj_chunk(w, wk, cofs, ncol, evac):
            for tb in range(NTB):
                b = next_acc()
                for k in range(KC):
                    mm(ps[b][0:ncol, :], w[:, k, cofs:cofs + ncol], uT[:, k, tb * TB:(tb + 1) * TB], k == 0, k == KC - 1,
                       wk + [("uT", k, tb)], [("ps", b)])
                evac(tb, ps[b][0:ncol, :], ("ps", b))

        def dump(name, ap, keys):
            if dbg and name in dbg_out:
                dma("sp", dbg_out[name], ap, keys, [("dbg", name)])

        def all_keys_h():
            return [("hT", k, tb) for k in range(KC) for tb in range(NTB)]

        def all_keys(nm):
            return [(nm, k, tb) for k in range(KC) for tb in range(NTB)]

        def tail_prepass(l):
            PK = arf(0, 64)
            pkk = ark(0, 64)
            gct = tiny[:, 0:32].rearrange("p (c t) -> p c t", c=2)
            tyk = [("tiny", "all")]
            tail = slice(T - 16, T)

            def tproj(w, wk, cofs, evac):
                b = next_acc()
                for k in range(KC):
                    mm(ps[b][:, 0:16], w[:, k, cofs:cofs + 128], uT[:, k, tail], k == 0, k == KC - 1, wk + [("uT", k, 3)], [("ps", b)])
                evac(ps[b][:, 0:16], ("ps", b))
            for c in range(2):
                w, wk = load_win(l, c * 128, 128)
                tproj(w, wk, 0, lambda p_, pk, c=c: act(PK[:, c * 16:(c + 1) * 16], p_, AF.Copy, [pk], pkk))
            for c in range(2):
                w, wk = load_win(l, 512 + c * 128, 128)
                tproj(w, wk, 0, lambda p_, pk, c=c: act(gct[:, c, :], p_, AF.Copy, [pk], tyk))
            for c in range(2):
                w, wk = load_win(l, 768 + c * 128, 128)
                tproj(w, wk, 0, lambda p_, pk, c=c: tt(PK[:, 32 + 2 * c:34 + 2 * c], p_[:, 14:16], gct[:, c, 14:16], ALU.mult,
                                                       [pk] + tyk, pkk))
            for j in range(6):
                w, wk = load_win(l, 1280 + j * 128, 128)
                tproj(w, wk, 0, lambda p_, pk, j=j: act(PK[:, 36 + 3 * j:39 + 3 * j], p_[:, 13:16], AF.Copy, [pk], pkk))
            dma("sp", ccd[("h", l, "i")][:, 0:54], PK[:, 0:54], pkk, [("cci", "h", l)])
            allgather("h", l, [("cci", "h", l)], [("cco", "h", l)])

        def halo_apply(l):
            RB = arf(256, 256).rearrange("p (r f) -> p r f", r=4)
            rbk = ark(256, 256)
            tyk = [("tiny", "all")]
            dma("sp", RB, ccd[("h", l, "o")].rearrange("(r p) f -> p r f", p=128), [("cco", "h", l)], rbk)
            hal = tiny[:, 64:128]
            sel = pcol(l, "sel")
            ts(hal, RB[:, 0, :], sel[:, 0:1], None, ALU.mult, None, rbk + ["pp"], tyk)
            for j in range(1, 4):
                stt(hal, RB[:, j, :], sel[:, j:j + 1], hal, ALU.mult, ALU.add, rbk + ["pp"] + tyk, tyk)
            cp(cPool[:, l, :, :], hal[:, 0:32].rearrange("p (c t) -> p c t", c=2), tyk, [("cPool", l, 0), ("cPool", l, 1)])
            cp(cSc[:, l, :, :], hal[:, 32:36].rearrange("p (c t) -> p c t", c=2), tyk, [("cSc", l, 0), ("cSc", l, 1)])
            cp(cCv[:, l, :, :], hal[:, 36:54].rearrange("p (c t) -> p c t", c=6), tyk, [("cCv", l, j) for j in range(6)])

        def s5_combine(l):
            tyk = [("tiny", "all")]
            RB = tiny[:, 816:880].rearrange("p (r f) -> p r f", r=4)
            dma("sp", RB, ccd[("x", l, "o")].rearrange("(r p) f -> p r f", p=128), [("cco", "x", l)], tyk)
            sel = pcol(l, "sel")
            Lr, Li = s5lv[:, l, :, 0, 7], s5lv[:, l, :, 1, 7]
            lk = [("s5lv", l)]
            ar, ai, pr, pi, t1, t2 = (tiny[:, a:a + 8] for a in (768, 776, 784, 792, 880, 888))
            for t_ in (ar, ai, pr, pi):
                mset(t_, 0.0, tyk)
            for j in range(4):
                stt(ar, pr, sel[:, 4 + j:5 + j], ar, ALU.mult, ALU.add, tyk + ["pp"], tyk)
                stt(ai, pi, sel[:, 4 + j:5 + j], ai, ALU.mult, ALU.add, tyk + ["pp"], tyk)
                if j < 3:
                    F = RB[:, j, :].rearrange("p (r a) -> p r a", a=2)
                    tt(t1, pr, Lr, ALU.mult, tyk + lk, tyk)
                    tt(t2, pi, Li, ALU.mult, tyk + lk, tyk)
                    tt(t1, t1, t2, ALU.subtract, tyk, tyk)
                    tt(t2, pr, Li, ALU.mult, tyk + lk, tyk)
                    tt(pr, t1, F[:, :, 0], ALU.add, tyk, tyk)
                    tt(t1, pi, Lr, ALU.mult, tyk + lk, tyk)
                    tt(t1, t1, t2, ALU.add, tyk, tyk)
                    tt(pi, t1, F[:, :, 1], ALU.add, tyk, tyk)
            cp(cX[:, l, :, 0], ar, tyk, [("cX", l, r) for r in range(8)])
            cp(cX[:, l, :, 1], ai, tyk, [("cX", l, r) for r in range(8)])

        def mixer_pool(l, seg):
            pbv = Yr[:, 4:6, :]
            for c in range(2):
                V, SA, SB = c * 6912, c * 6912 + 2304, c * 6912 + 4608
                w, wk = load_win(l, c * 128, 128)
                v = arf(V, 2064)
                vk = ark(V, 2064)
                cp(v[:, 0:16], cPool[:, l, c, :], [("cPool", l, c)], vk)
                proj_chunk(w, wk, 0, 128,
                           lambda tb, p_, pk: act(v[:, 16 + tb * TB:16 + (tb + 1) * TB], p_, AF.Copy, [pk], vk))
                cp(cPool[:, l, c, :], v[:, 2048:2064], vk, [("cPool", l, c)])
                sa, sbb = arf(SA, 2064), arf(SB, 2064)
                sak, sbk = ark(SA, 2064), ark(SB, 2064)
                tt(sa[:, 1:2064], v[:, 1:2064], v[:, 0:2063], ALU.add, vk, sak)
                tt(sbb[:, 3:2064], sa[:, 3:2064], sa[:, 1:2062], ALU.add, sak, sbk)
                if c == 0:
                    lo_src, hi_src, lo_w, hi_w = sa, sbb, 2, 4
                else:
                    tt(sa[:, 7:2064], sbb[:, 7:2064], sbb[:, 3:2060], ALU.add, sbk, sak)
                    tt(sbb[:, 15:2064], sa[:, 15:2064], sa[:, 7:2056], ALU.add, sak, sbk)
                    lo_src, hi_src, lo_w, hi_w = sa, sbb, 8, 16
                pb = pbv
                pbk = [("Yr", 4 + c, t_) for t_ in range(NTB)]
                stt(pb[0:64, c, :], lo_src[0:64, 16:2064], 1.0 / lo_w, v[0:64, 16:2064], ALU.mult, ALU.subtract, sak + sbk + vk, pbk)
                stt(pb[64:128, c, :], hi_src[64:128, 16:2064], 1.0 / hi_w, v[64:128, 16:2064], ALU.mult, ALU.subtract, sak + sbk + vk, pbk)
                if True:
                    ic = pcol(l, "invc").rearrange("p (c t) -> p c t", c=2)
                    tq = tiny[:, 512:528]
                    for (r0, r1, src) in ((0, 64, lo_src), (64, 128, hi_src)):
                        tt(tq[r0:r1, :], src[r0:r1, 16:32], ic[r0:r1, c, :], ALU.mult, sak + sbk + ["pp"], [("tiny", "all")])
                        tt(pb[r0:r1, c, 0:16], tq[r0:r1, :], v[r0:r1, 16:32], ALU.subtract, [("tiny", "all")] + vk, pbk)
            pb = pbv
            for c in range(2):
                for tb in range(NTB):
                    b = next_acc()
                    mm(ps[b][:, :], poolwS[:, l, c, :], pb[:, c, tb * TB:(tb + 1) * TB], True, True, [("Yr", 4 + c, tb), "poolw"], [("ps", b)])
                    ts(Yr[:, c, tb * TB:(tb + 1) * TB], ps[b][:, :], pcol(l, "pscale", c, c + 1), None, ALU.mult, None,
                       [("ps", b), "pp"], [("Yr", c, tb)])

        def mixer_sconv(l, seg):
            for c in range(2):
                GC, G, T1 = c * 6400, c * 6400 + 2048, c * 6400 + 4352
                gc = arf(GC, 2048)
                gck = ark(GC, 2048)
                wgc, wgck = load_win(l, 512 + c * 128, 128)
                proj_chunk(wgc, wgck, 0, 128,
                           lambda tb, p_, pk: act(gc[:, tb * TB:(tb + 1) * TB], p_, AF.Copy, [pk], gck))
                whh, whhk = load_win(l, 768 + c * 128, 128)
                g = arf(G, 2050)
                gk = ark(G, 2050)
                cp(g[:, 0:2], cSc[:, l, c, :], [("cSc", l, c)], gk)
                proj_chunk(whh, whhk, 0, 128,
                           lambda tb, p_, pk: tt(g[:, 2 + tb * TB:2 + (tb + 1) * TB], p_, gc[:, tb * TB:(tb + 1) * TB], ALU.mult,
                                                 [pk] + gck, gk))
                cp(cSc[:, l, c, :], g[:, 2048:2050], gk, [("cSc", l, c)])
                t1 = arf(T1, 2048)
                t1k = ark(T1, 2048)
                wv = pcol(l, "scw").rearrange("p (c k) -> p c k", c=2)
                ts(t1, g[:, 0:2048], wv[:, c, 0:1], None, ALU.mult, None, gk + ["pp"], t1k)
                stt(t1, g[:, 1:2049], wv[:, c, 1:2], t1, ALU.mult, ALU.add, gk + ["pp"] + t1k, t1k)
                stt(t1, g[:, 2:2050], wv[:, c, 2:3], t1, ALU.mult, ALU.add, gk + ["pp"] + t1k, t1k)
                wgb, wgbk = load_win(l, 256 + c * 128, 128)
                proj_chunk(wgb, wgbk, 0, 128,
                           lambda tb, p_, pk: tt(Yr[:, 2 + c, tb * TB:(tb + 1) * TB], p_, t1[:, tb * TB:(tb + 1) * TB], ALU.mult,
                                                 [pk] + t1k, [("Yr", 2 + c, tb)]))

        def mixer_ssd(l, seg):
            SZ, RAW, ACC, XBC = 0, 2048, 4352, 6400
            XTOK, BTOK = 2048, 4096
            sz = arb(SZ, 2048).rearrange("p (c t) -> p c t", c=2)
            szk = ark(SZ, 2048)
            for c in range(2):
                wz, wzk = load_win(l, 1024 + c * 128, 128)
                proj_chunk(wz, wzk, 0, 128,
                           lambda tb, p_, pk: act(sz[:, c, tb * TB:(tb + 1) * TB], p_, AF.Silu, [pk], szk))
            xbc = arb(XBC, 6144).rearrange("p (c t) -> p c t", c=6)
            kxb = [ark(XBC + j * 1024, 1024) for j in range(6)]
            cw = pcol(l, "cvw").rearrange("p (c k) -> p c k", c=6)
            for j in range(6):
                wx, wxk = load_win(l, 1280 + j * 128, 128)
                raw = arf(RAW, 2051)
                rawk = ark(RAW, 2051)
                cp(raw[:, 0:3], cCv[:, l, j, :], [("cCv", l, j)], rawk)
                proj_chunk(wx, wxk, 0, 128,
                           lambda tb, p_, pk: act(raw[:, 3 + tb * TB:3 + (tb + 1) * TB], p_, AF.Copy, [pk], rawk))
                cp(cCv[:, l, j, :], raw[:, 2048:2051], rawk, [("cCv", l, j)])
                acc = arf(ACC, 2048)
                acck = ark(ACC, 2048)
                ts(acc, raw[:, 0:2048], cw[:, j, 0:1], None, ALU.mult, None, rawk + ["pp"], acck)
                for kk in range(1, 4):
                    stt(acc, raw[:, kk:kk + 2048], cw[:, j, kk:kk + 1], acc, ALU.mult, ALU.add, rawk + acck + ["pp"], acck)
                act(xbc[:, j, :], acc, AF.Silu, acck + ["pp"], kxb[j], bias=pcol(l, "cvb", j, j + 1))
            print("  ssd: before dt", P.total)
            wd, wdk = load_win(l, 2048, 4)
            for c in range(NCH):
                for k in range(KC):
                    mm(ps[6][:, c * 4:(c + 1) * 4], uT[:, k, c * 128:(c + 1) * 128], wd[:, k, 0:4], k == 0, k == KC - 1,
                       wdk + [("uT", k, c // 4)], [("ps", 6)])
            print("  ssd: before small", P.total)
            tyk = [("tiny", "all")]
            def v3(a): return tiny[:, a:a + 64].rearrange("p (c h) -> p c h", h=4)
            dt_, adt, acs, tot, eacs, dte, cd, ddte = (v3(a) for a in (0, 64, 128, 192, 256, 320, 384, 448))
            xsp = v3(512)
            ex = v3(576)
            bc4 = lambda a, b: ssdp[:, l, a:b].unsqueeze(1).to_broadcast([128, NCH, 4])
            tt(xsp, ps[6][:, 0:64].rearrange("p (c h) -> p c h", h=4), bc4(4, 8), ALU.add, [("ps", 6), ("ssdp", l)], tyk)
            ts(ex, xsp, 30.0, None, ALU.min, None, tyk, tyk)
            act(ex, ex, AF.Exp, tyk, tyk)
            act(ex, ex, AF.Ln, tyk + ["eps"], tyk, bias=epsT[:, 1:2])
            tt(dt_, ex, xsp, ALU.max, tyk, tyk)
            tt(adt, dt_, bc4(0, 4), ALU.mult, tyk + [("ssdp", l)], tyk)
            ahi = tinyb[:, 0:64]
            alo = tinyb[:, 64:128]
            tbk = [("tinyb", "a")]
            adf = tiny[:, 64:128]
            cp(ahi, adf, tyk, tbk)
            tt(tiny[:, 640:704], adf, ahi, ALU.subtract, tyk + tbk, tyk)
            cp(alo, tiny[:, 640:704], tyk, tbk)
            cp(tiny[:, 768:832], ahi, tbk, tyk)
            mm(ps[6][:, 64:128], tri, ahi, True, False, tbk + ["cm"], [("ps", 6)])
            mm(ps[6][:, 64:128], tri, alo, False, True, tbk + ["cm"], [("ps", 6)])
            mm(ps[6][:, 128:192], ones, ahi, True, False, tbk + ["cm"], [("ps", 6)])
            mm(ps[6][:, 128:192], ones, alo, False, True, tbk + ["cm"], [("ps", 6)])
            cp(tiny[:, 128:256], ps[6][:, 64:192], [("ps", 6)], tyk)
            act(tiny[:, 256:320], tiny[:, 128:192], AF.Exp, tyk, tyk)
            tt(tiny[:, 320:384], tiny[:, 192:256], tiny[:, 128:192], ALU.subtract, tyk, tyk)
            act(tiny[:, 320:384], tiny[:, 320:384], AF.Exp, tyk, tyk)
            act(tiny[:, 384:448], tiny[:, 192:256], AF.Exp, tyk, tyk)
            tt(tiny[:, 448:512], tiny[:, 0:64], tiny[:, 320:384], ALU.mult, tyk, tyk)
            ts(tiny[:, 704:768], tiny[:, 128:192], -1.0, None, ALU.mult, None, tyk, tyk)
            nacs = v3(704)
            print("  ssd: before transposes", P.total)
            xtok = arb(XTOK, 2048).rearrange("p (c f) -> p c f", c=NCH)
            btok = arb(BTOK, 2048).rearrange("p (c f) -> p c f", c=NCH)
            xtk, btk = ark(XTOK, 2048), ark(BTOK, 2048)
            ti = 0
            for c in range(NCH):
                for j in range(4):
                    o = (ti % 4) * 128
                    ti += 1
                    bnk = 4 + (ti - 1) % 4
                    pk_ = ("ps", bnk)
                    mm(ps[bnk][:, 0:128], xbc[:, j, c * 128:(c + 1) * 128], ident, True, True, kxb[j] + ["cm"], [pk_])
                    dst = (xtok if j < 2 else btok)[:, c, (j % 2) * 128:(j % 2 + 1) * 128]
                    if ti % 2 == 0:
                        cp(dst, ps[bnk][:, 0:128], [pk_], xtk if j < 2 else btk)
                    else:
                        act(dst, ps[bnk][:, 0:128], AF.Copy, [pk_], xtk if j < 2 else btk)
            WT = XBC
            E_ = arb(WT, 256).rearrange("p (h s) -> p h s", h=4)
            MT = arb(WT + 256, 256).rearrange("p (h s) -> p h s", h=4)
            RH = arb(WT + 512, 512).rearrange("p (a h s) -> p a h s", a=2, h=4)
            XDT = arb(WT + 1024, 128)
            XDE = arb(WT + 1152, 128)
            YSB = arf(WT + 1280, 256)
            YTK = arb(WT + 1536, 128)
            SBF = arb(WT + 1664, 128)
            STMP = arf(WT + 1792, 256)
            kE, kMT, kRH, kXDT, kXDE, kYSB, kYTK, kSBF, kST = (ark(WT + a, n) for a, n in
                ((0, 256), (256, 256), (512, 512), (1024, 128), (1152, 128), (1280, 256), (1536, 128), (1664, 128), (1792, 256)))
            print("  ssd: before main loop", P.total)
            Sst = cS[:, l, :]
            kS = [("cS", l)]
            PKG, RBO, PTO = 12544, 12816, 13904
            pkg = arf(PKG, 272)
            pkgk = ark(PKG, 272)
            SL = pkg[:, 0:256]
            mset(SL, 0.0, pkgk)
            for c in range(NCH):
                x3 = xtok[:, c, :].rearrange("p (h d) -> p h d", h=4)
                tt(XDE.rearrange("p (h d) -> p h d", h=4), x3, ddte[:, c, :].unsqueeze(2).to_broadcast([128, 4, 64]), ALU.mult, xtk + tyk, kXDE)
                for g in range(2):
                    mm(ps[5][:, g * 128:(g + 1) * 128], btok[:, c, g * 128:(g + 1) * 128], XDE[:, g * 128:(g + 1) * 128], True, True,
                       btk + kXDE, [("ps", 5)])
                tt(STMP.rearrange("p (h d) -> p h d", h=4), SL.rearrange("p (h d) -> p h d", h=4),
                   cd[:, c, :].unsqueeze(2).to_broadcast([128, 4, 64]), ALU.mult, pkgk + tyk, kST)
                tt(SL, STMP, ps[5][:, 0:256], ALU.add, kST + [("ps", 5)], pkgk)
            tr_ = tiny[:, 832:864].rearrange("p (c h) -> p c h", h=4)
            tt(tr_, tot[:, 0:8, :], tot[:, 8:16, :], ALU.add, tyk, tyk)
            tt(tr_[:, 0:4, :], tr_[:, 0:4, :], tr_[:, 4:8, :], ALU.add, tyk, tyk)
            tt(tr_[:, 0:2, :], tr_[:, 0:2, :], tr_[:, 2:4, :], ALU.add, tyk, tyk)
            tt(tr_[:, 0:1, :], tr_[:, 0:1, :], tr_[:, 1:2, :], ALU.add, tyk, tyk)
            act(pkg[:, 256:260], tiny[:, 832:836], AF.Exp, tyk, pkgk)
            dma("sp", ccd[("s", l, "i")][:, 0:260], pkg[:, 0:260], pkgk, [("cci", "s", l)])
            allgather("s", l, [("cci", "s", l)], [("cco", "s", l)])
            RBs = arf(RBO, 1088).rearrange("p (r f) -> p r f", r=4)
            rbsk = ark(RBO, 1088)
            dma("sp", RBs, ccd[("s", l, "o")].rearrange("(r p) f -> p r f", p=128), [("cco", "s", l)], rbsk)
            PT = arf(PTO, 256)
            ptk = ark(PTO, 256)
            sel = pcol(l, "sel")
            mset(Sst, 0.0, kS)
            mset(PT, 0.0, ptk)
            for j in range(4):
                stt(Sst, PT, sel[:, 4 + j:5 + j], Sst, ALU.mult, ALU.add, ptk + kS + ["pp"], kS)
                if j < 3:
                    tt(PT.rearrange("p (h d) -> p h d", h=4), PT.rearrange("p (h d) -> p h d", h=4),
                       RBs[:, j, 256:260].unsqueeze(2).to_broadcast([128, 4, 64]), ALU.mult, ptk + rbsk, ptk)
                    tt(PT, PT, RBs[:, j, 0:256], ALU.add, ptk + rbsk, ptk)
            cp(SBF, Sst, kS, kSBF)
            for c in range(NCH):
                tsl = slice(c * 128, (c + 1) * 128)
                if c < 2:
                    print("  ssd: chunk", c, P.total)
                for h in range(4):
                    o = ps[0][:, h * 128:(h + 1) * 128]
                    mm(o, tinyb[:, c * 4 + h:c * 4 + h + 1].to_broadcast([128, 128]), tri, True, False, tbk + ["cm"], [("ps", 0)])
                    mm(o, tinyb[:, 64 + c * 4 + h:64 + c * 4 + h + 1].to_broadcast([128, 128]), tri, False, False, tbk + ["cm"], [("ps", 0)])
                    mm(o, ident, negm, False, True, ["cm"], [("ps", 0)])
                for h in range(4):
                    act(E_[:, h, :], ps[0][:, h * 128:(h + 1) * 128], AF.Exp, [("ps", 0)] + tyk, kE, bias=nacs[:, c, h:h + 1])
                for g in range(2):
                    mm(ps[1][:, g * 128:(g + 1) * 128], xbc[:, 2 + g, tsl], xbc[:, 4 + g, tsl], True, True,
                       kxb[2 + g] + kxb[4 + g], [("ps", 1)])
                for h in range(4):
                    g = h // 2
                    tt(MT[:, h, :], ps[1][:, g * 128:(g + 1) * 128], E_[:, h, :], ALU.mult, [("ps", 1)] + kE, kMT)
                x3 = xtok[:, c, :].rearrange("p (h d) -> p h d", h=4)
                tt(XDT.rearrange("p (h d) -> p h d", h=4), x3, dt_[:, c, :].unsqueeze(2).to_broadcast([128, 4, 64]), ALU.mult, xtk + tyk, kXDT, eng="pool")
                tt(XDE.rearrange("p (h d) -> p h d", h=4), x3, ddte[:, c, :].unsqueeze(2).to_broadcast([128, 4, 64]), ALU.mult, xtk + tyk, kXDE, eng="pool")
                for h in range(4):
                    o = ps[2][:, h * 64:(h + 1) * 64]
                    mm(o, MT[:, h, :], XDT[:, h * 64:(h + 1) * 64], True, True, kMT + kXDT, [("ps", 2, "y")])
                for h in range(4):
                    g = h // 2
                    mm(ps[3][:, h * 64:(h + 1) * 64], xbc[:, 4 + g, tsl], SBF[:, h * 64:(h + 1) * 64], True, True,
                       kxb[4 + g] + kSBF, [("ps", 3)])
                tt(YSB.rearrange("p (h d) -> p h d", h=4), x3, ssdp[:, l, 8:12].unsqueeze(2).to_broadcast([128, 4, 64]), ALU.mult,
                   xtk + [("ssdp", l)], kYSB)
                tt(YSB, YSB, ps[2][:, 0:256], ALU.add, kYSB + [("ps", 2, "y")], kYSB)
                tt(STMP.rearrange("p (h d) -> p h d", h=4), ps[3][:, 0:256].rearrange("p (h d) -> p h d", h=4),
                   eacs[:, c, :].unsqueeze(2).to_broadcast([128, 4, 64]), ALU.mult, [("ps", 3)] + tyk, kST)
                tt(YTK, STMP, YSB, ALU.add, kST + kYSB, kYTK)
                for j in range(2):
                    o = j * 128
                    mm(ps[4][:, o:o + 128], YTK[:, j * 128:(j + 1) * 128], ident, True, True, kYTK + ["cm"], [("ps", 4, o)])
                    tt(Yr[:, 4 + j, tsl], ps[4][:, o:o + 128], sz[:, j, tsl], ALU.mult, [("ps", 4, o)] + szk, [("Yr", 4 + j, c // 4)])
                for g in range(2):
                    mm(ps[5][:, g * 128:(g + 1) * 128], btok[:, c, g * 128:(g + 1) * 128], XDE[:, g * 128:(g + 1) * 128], True, True,
                       btk + kXDE, [("ps", 5)])
                tt(STMP.rearrange("p (h d) -> p h d", h=4), Sst.rearrange("p (h d) -> p h d", h=4),
                   cd[:, c, :].unsqueeze(2).to_broadcast([128, 4, 64]), ALU.mult, kS + tyk, kST)
                tt(Sst, STMP, ps[5][:, 0:256], ALU.add, kST + [("ps", 5)], kS)
                act(SBF, Sst, AF.Copy, kS, kSBF)

        def mixer_s5(l, seg):
            U, STG, BT, W, WB, PAT, INJ, TAB = 0, 2048, 4096, 6144, 8192, 10240, 11264, 13312
            tyk = [("tiny", "all")]
            pk = [("s5pw", l)]
            lk = [("s5lv", l)]
            u = arb(U, 2048).rearrange("p (c t) -> p c t", c=2)
            uk = ark(U, 2048)
            for c in range(2):
                wu, wuk = load_win(l, 2052 + c * 128, 128)
                proj_chunk(wu, wuk, 0, 128,
                           lambda tb, p_, pk_: act(u[:, c, tb * TB:(tb + 1) * TB], p_, AF.Copy, [pk_], uk))
            pat = arb(PAT, 1024)
            patk = ark(PAT, 1024)
            mset(pat, 1.0, patk)
            mset(pat.rearrange("p (c j) -> p c j", j=16)[:, :, 0:1], 0.0, patk)
            pw0 = tiny[:, 0:256].rearrange("p (r a j) -> p r a j", r=8, a=2)
            qq = tiny[:, 256:512].rearrange("p (r a j) -> p r a j", r=8, a=2)
            den = tiny[:, 512:640].rearrange("p (r j) -> p r j", r=8)
            tmp = tiny[:, 640:768].rearrange("p (r j) -> p r j", r=8)
            mset(pw0[:, :, 0, 0:1], 1.0, tyk)
            mset(pw0[:, :, 1, 0:1], 0.0, tyk)
            for a in range(2):
                cp(pw0[:, :, a, 1:16], s5pw[:, l, :, a, 0:15], pk, tyk)
            tt(den, pw0[:, :, 0, :], pw0[:, :, 0, :], ALU.mult, tyk, tyk)
            tt(tmp, pw0[:, :, 1, :], pw0[:, :, 1, :], ALU.mult, tyk, tyk)
            tt(den, den, tmp, ALU.add, tyk, tyk)
            P.op("dve", lambda e: e.reciprocal(out=den, in_=den), tyk, tyk)
            tt(qq[:, :, 0, :], pw0[:, :, 0, :], den, ALU.mult, tyk, tyk)
            tt(qq[:, :, 1, :], pw0[:, :, 1, :], den, ALU.mult, tyk, tyk)
            ts(qq[:, :, 1, :], qq[:, :, 1, :], -1.0, None, ALU.mult, None, tyk, tyk)
            bpad = arf(TAB, 512).rearrange("p (r a h) -> p r a h", r=8, a=2)
            cpad = arf(TAB + 512, 512).rearrange("p (r a h) -> p r a h", r=8, a=2)
            tabk = ark(TAB, 1024)
            mset(arf(TAB, 512), 0.0, tabk)
            for a in range(2):
                cp(bpad[0:64, :, a, 0:16], bbS[0:64, l, a, :, :], [("bb", l)], tabk)
                cp(bpad[64:128, :, a, 16:32], bbS[64:128, l, a, :, :], [("bb", l)], tabk)
            dma("sp", arf(TAB + 512, 512), s5c[l], [], tabk)
            Ef = Yr[:, 6:8, :].rearrange("p a t -> p (a t)").bitcast(F32)
            Eall = Ef.rearrange("p (r a c) -> p r a c", r=8, a=2)
            ek = [("Yr", 6 + a, tb) for a in range(2) for tb in range(NTB)]
            bufA = arf(W, 2048).rearrange("p (r a c) -> p r a c", r=8, a=2)
            bufB = arf(WB, 2048).rearrange("p (r a c) -> p r a c", r=8, a=2)
            kA, kB = ark(W, 2048), ark(WB, 2048)
            T1 = arf(STG, 2048)
            T2 = arf(BT, 2048)
            kT1, kT2 = ark(STG, 2048), ark(BT, 2048)

            def bc8(ap, n):
                return ap.unsqueeze(2).to_broadcast([128, 8, n])

            def lvl2(src, sk):
                cur, ck_ = src, sk
                nxt_list = [(bufA, kA), (bufB, kB)]
                if src is bufA:
                    nxt_list = [(bufB, kB), (bufA, kA)]
                for lev in range(7):
                    d = 1 << lev
                    n = 128 - d
                    dst, dk = nxt_list[lev % 2]
                    cr, ci = s5lv[:, l, :, 0, lev], s5lv[:, l, :, 1, lev]
                    t1 = T1[:, 0:8 * n].rearrange("p (r c) -> p r c", r=8)
                    t2 = T2[:, 0:8 * n].rearrange("p (r c) -> p r c", r=8)
                    cp(dst[:, :, :, 0:d], cur[:, :, :, 0:d], ck_, dk)
                    tt(t1, cur[:, :, 0, 0:n], bc8(cr, n), ALU.mult, ck_ + lk, kT1)
                    tt(t2, cur[:, :, 1, 0:n], bc8(ci, n), ALU.mult, ck_ + lk, kT2)
                    tt(t1, t1, t2, ALU.subtract, kT1 + kT2, kT1)
                    tt(dst[:, :, 0, d:128], cur[:, :, 0, d:128], t1, ALU.add, ck_ + kT1, dk)
                    tt(t1, cur[:, :, 1, 0:n], bc8(cr, n), ALU.mult, ck_ + lk, kT1)
                    tt(t2, cur[:, :, 0, 0:n], bc8(ci, n), ALU.mult, ck_ + lk, kT2)
                    tt(t1, t1, t2, ALU.add, kT1 + kT2, kT1)
                    tt(dst[:, :, 1, d:128], cur[:, :, 1, d:128], t1, ALU.add, ck_ + kT1, dk)
                    cur, ck_ = dst, dk
                return cur, ck_

            def run_pass(mode):
                inj = arf(INJ, 2048).rearrange("p (r a c) -> p r a c", r=8, a=2)
                injk = ark(INJ, 2048)
                if mode == "full":
                    cp(bufA, Eall, ek, kA)
                    mur, mui = s5lv[:, l, :, 0, 0], s5lv[:, l, :, 1, 0]
                    xr_, xi_ = cX[:, l, :, 0], cX[:, l, :, 1]
                    ckx = [("cX", l, r) for r in range(8)]
                    a1, a2 = tiny[:, 768:776], tiny[:, 776:784]
                    tt(a1, mur, xr_, ALU.mult, lk + ckx, tyk)
                    tt(a2, mui, xi_, ALU.mult, lk + ckx, tyk)
                    tt(a1, a1, a2, ALU.subtract, tyk, tyk)
                    tt(bufA[:, :, 0, 0], bufA[:, :, 0, 0], a1, ALU.add, kA + tyk, kA)
                    tt(a1, mur, xi_, ALU.mult, lk + ckx, tyk)
                    tt(a2, mui, xr_, ALU.mult, lk + ckx, tyk)
                    tt(a1, a1, a2, ALU.add, tyk, tyk)
                    tt(bufA[:, :, 1, 0], bufA[:, :, 1, 0], a1, ALU.add, kA + tyk, kA)
                    res, rk = lvl2(bufA, kA)
                    lr8, li8 = s5pw[:, l, :, 0, 0], s5pw[:, l, :, 1, 0]
                    n = 127
                    t1 = T1[:, 0:8 * n].rearrange("p (r c) -> p r c", r=8)
                    t2 = T2[:, 0:8 * n].rearrange("p (r c) -> p r c", r=8)
                    tt(t1, res[:, :, 0, 0:n], bc8(lr8, n), ALU.mult, rk + pk, kT1)
                    tt(t2, res[:, :, 1, 0:n], bc8(li8, n), ALU.mult, rk + pk, kT2)
                    tt(inj[:, :, 0, 1:128], t1, t2, ALU.subtract, kT1 + kT2, injk)
                    tt(t1, res[:, :, 1, 0:n], bc8(lr8, n), ALU.mult, rk + pk, kT1)
                    tt(t2, res[:, :, 0, 0:n], bc8(li8, n), ALU.mult, rk + pk, kT2)
                    tt(inj[:, :, 1, 1:128], t1, t2, ALU.add, kT1 + kT2, injk)
                    tt(a1, lr8, xr_, ALU.mult, pk + ckx, tyk)
                    tt(a2, li8, xi_, ALU.mult, pk + ckx, tyk)
                    tt(inj[:, :, 0, 0], a1, a2, ALU.subtract, tyk, injk)
                    tt(a1, lr8, xi_, ALU.mult, pk + ckx, tyk)
                    tt(a2, li8, xr_, ALU.mult, pk + ckx, tyk)
                    tt(inj[:, :, 1, 0], a1, a2, ALU.add, tyk, injk)

                stg = arb(STG, 2048).rearrange("p (a j c) -> p a j c", a=2, j=16)
                stgk = ark(STG, 2048)
                btv = arb(BT, 2048).rearrange("p (a j c) -> p a j c", a=2, j=16)
                btk_ = ark(BT, 2048)
                Wn = arf(W, 2048)
                wk_ = ark(W, 2048)
                Wv = Wn.rearrange("p (c j) -> p j c", j=16)
                wb = arb(WB, 2048).rearrange("p (a t) -> p a t", a=2)
                wbk = ark(WB, 2048)

                def b16(ap):
                    return ap.unsqueeze(2).to_broadcast([128, 16, 32])

                def h32(ap):
                    return ap.unsqueeze(1).to_broadcast([128, 16, 32])

                ct = Yr[:, 4:6, :].rearrange("p a (j c) -> p a j c", j=16)
                ctk = [("Yr", 4 + a_, t_) for a_ in range(2) for t_ in range(NTB)]
                if mode == "p1":
                    tmp_f, tmpk = arf(WB, 1024), ark(WB, 1024)
                    W1n, w1k_ = arf(INJ, 2048), ark(INJ, 2048)
                else:
                    tmp_f = Yr[:, 7, :].bitcast(F32)
                    tmpk = [("Yr", 7, t_) for t_ in range(NTB)]
                    W1n = Yr[:, 0:2, :].rearrange("p a t -> p (a t)").bitcast(F32)
                    w1k_ = [("Yr", a_, t_) for a_ in range(2) for t_ in range(NTB)]
                Wbufs = [(Wn, wk_), (W1n, w1k_)]
                t1 = tmp_f[:, 0:512].rearrange("p (j h) -> p j h", j=16)
                t2 = tmp_f[:, 512:1024].rearrange("p (j h) -> p j h", j=16)
                mset(arb(STG, 2048), 0.0, stgk)
                if mode == "full":
                    mset(Yr[:, 4:6, :], 0.0, ctk)

                ptf = Yr[:, 2, :].bitcast(F32)
                ptk = [("Yr", 2, t_) for t_ in range(NTB)]
                p1_ = ptf[:, 0:512].rearrange("p (j h) -> p j h", j=16)
                p2_ = ptf[:, 512:1024].rearrange("p (j h) -> p j h", j=16)

                def tables_b_dve(r):
                    q = r % 4
                    if r > 0:
                        pq = (r - 1) % 4
                        mset(stg[:, :, :, pq * 32:(pq + 1) * 32], 0.0, stgk, eng="pool")
                    Bre, Bim = h32(bpad[:, r, 0, :]), h32(bpad[:, r, 1, :])
                    qr_, qi_ = b16(qq[:, r, 0, :]), b16(qq[:, r, 1, :])
                    sv = stg[:, :, :, q * 32:(q + 1) * 32]
                    tt(p1_, Bre, qr_, ALU.mult, tabk + tyk, ptk, eng="pool")
                    tt(p2_, Bim, qi_, ALU.mult, tabk + tyk, ptk, eng="pool")
                    tt(sv[:, 0], p1_, p2_, ALU.subtract, ptk, stgk, eng="pool")
                    tt(p1_, Bim, qr_, ALU.mult, tabk + tyk, ptk, eng="pool")
                    tt(p2_, Bre, qi_, ALU.mult, tabk + tyk, ptk, eng="pool")
                    tt(sv[:, 1], p1_, p2_, ALU.add, ptk, stgk, eng="pool")

                def tables_b_pe(r):
                    for a in range(2):
                        for lg in range(4):
                            bnk = 4 + lg
                            for li_ in range(4):
                                mm(ps[bnk][:, li_ * 128:(li_ + 1) * 128], stg[:, a, lg * 4 + li_, :], ident, True, True,
                                   stgk + ["cm"], [("ps", bnk)])
                            act(btv[:, a, lg * 4:(lg + 1) * 4, :], ps[bnk][:, :].rearrange("p (j c) -> p j c", j=4), AF.Copy,
                                [("ps", bnk)], btk_)

                def ct_dve(r):
                    q = r % 4
                    if r > 0:
                        pq = (r - 1) % 4
                        mset(ct[:, :, :, pq * 32:(pq + 1) * 32], 0.0, ctk)
                    Cre, Cim = h32(cpad[:, r, 0, :]), h32(cpad[:, r, 1, :])
                    pr_, pi_ = b16(pw0[:, r, 0, :]), b16(pw0[:, r, 1, :])
                    cv = ct[:, :, :, q * 32:(q + 1) * 32]
                    tt(t1, Cre, pr_, ALU.mult, tabk + tyk, tmpk)
                    tt(t2, Cim, pi_, ALU.mult, tabk + tyk, tmpk)
                    tt(cv[:, 0], t1, t2, ALU.add, tmpk, ctk)
                    tt(t1, Cim, pr_, ALU.mult, tabk + tyk, tmpk)
                    tt(t2, Cre, pi_, ALU.mult, tabk + tyk, tmpk)
                    tt(cv[:, 1], t1, t2, ALU.subtract, tmpk, ctk)

                def bu_mm(r, a, wv_, wkk):
                    uv = u[:, r // 4, :].rearrange("p (c j) -> p j c", j=16)
                    for lg in range(4):
                        bnk = 4 + lg
                        for li_ in range(4):
                            j = lg * 4 + li_
                            mm(ps[bnk][:, li_ * 128:(li_ + 1) * 128], btv[:, a, j, :], uv[:, j, :], True, True, btk_ + uk, [("ps", bnk)])
                        act(wv_[:, lg * 4:(lg + 1) * 4, :], ps[bnk][:, :].rearrange("p (j c) -> p j c", j=4), AF.Copy, [("ps", bnk)], wkk)

                def scan(wn_, wkk):
                    P.op("dve", lambda e, wn_=wn_: e.tensor_tensor_scan(out=wn_, data0=pat, data1=wn_, initial=0.0, op0=ALU.mult, op1=ALU.add),
                         wkk + patk, wkk)

                tables_b_dve(0)
                tables_b_pe(0)
                for r in range(8):
                    oc = r // 4
                    q = r % 4
                    uv = u[:, oc, :].rearrange("p (c j) -> p j c", j=16)
                    wvs = [(wn_.rearrange("p (c j) -> p j c", j=16), wn_, kk_) for (wn_, kk_) in Wbufs]
                    if mode == "p1":
                        bu_mm(r, 0, wvs[0][0], wvs[0][2])
                        bu_mm(r, 1, wvs[1][0], wvs[1][2])
                        if r + 1 < 8:
                            tables_b_dve(r + 1)
                        for a in range(2):
                            scan(wvs[a][1], wvs[a][2])
                            cp(Eall[:, r, a, :], wvs[a][0][:, 15, :], wvs[a][2], ek)
                        if r + 1 < 8:
                            tables_b_pe(r + 1)
                        continue
                    bu_mm(r, 0, wvs[0][0], wvs[0][2])
                    bu_mm(r, 1, wvs[1][0], wvs[1][2])
                    ct_dve(r)
                    if r + 1 < 8:
                        tables_b_dve(r + 1)
                    for a in range(2):
                        wv_, wn_, wkk = wvs[a]
                        tt(wv_[:, 0, :], wv_[:, 0, :], inj[:, r, a, :], ALU.add, wkk + injk, wkk)
                        scan(wn_, wkk)
                        act(wb[:, a, :], wn_, AF.Copy, wkk, wbk)
                    if r + 1 < 8:
                        tables_b_pe(r + 1)
                    wbv = [wb[:, a, :].rearrange("p (c j) -> p j c", j=16) for a in range(2)]
                    for j in range(16):
                        o = ps[j // 4][:, (j % 4) * 128:(j % 4 + 1) * 128]
                        P.op("pe", lambda e, o=o, j=j, q=q, wbv=wbv: e.matmul(o, lhsT=ct[:, 0, j, :], rhs=wbv[0][:, j, :],
                                                                             start=(q == 0 and j % 4 == 0), stop=False, skip_group_check=True),
                             ctk + wbk, [("ps", j // 4)])
                        P.op("pe", lambda e, o=o, j=j, q=q, wbv=wbv: e.matmul(o, lhsT=ct[:, 1, j, :], rhs=wbv[1][:, j, :], start=False,
                                                                             stop=(q == 3), skip_group_check=True), ctk + wbk, [("ps", j // 4)])
                    if q == 3:
                        for tb in range(NTB):
                            yo = W + (tb % 2) * 512
                            yt = arf(yo, 512)
                            ytk = ark(yo, 512)
                            uv4 = uv[:, tb * 4:(tb + 1) * 4, :]
                            stt(yt.rearrange("p (j c) -> p j c", j=4), uv4, pcol(l, "s5d", oc, oc + 1),
                                ps[tb][:, :].rearrange("p (j c) -> p j c", j=4), ALU.mult, ALU.add, uk + ["pp", ("ps", tb)], ytk)
                            act(Yr[:, 6 + oc, :].rearrange("p (c j) -> p j c", j=16)[:, tb * 4:(tb + 1) * 4, :],
                                yt.rearrange("p (j c) -> p j c", j=4), AF.Gelu_apprx_tanh, ytk, [("Yr", 6 + oc, t_) for t_ in range(NTB)])
                if mode == "p1":
                    p15r, p15i = s5pw[:, l, :, 0, 14], s5pw[:, l, :, 1, 14]
                    n = 128
                    t1 = T1[:, 0:1024].rearrange("p (r c) -> p r c", r=8)
                    t2 = T1[:, 1024:2048].rearrange("p (r c) -> p r c", r=8)
                    t3 = T2[:, 0:1024].rearrange("p (r c) -> p r c", r=8)
                    t4 = T2[:, 1024:2048].rearrange("p (r c) -> p r c", r=8)
                    tt(t1, Eall[:, :, 0, :], bc8(p15r, n), ALU.mult, ek + pk, kT1)
                    tt(t2, Eall[:, :, 1, :], bc8(p15i, n), ALU.mult, ek + pk, kT1)
                    tt(t3, Eall[:, :, 1, :], bc8(p15r, n), ALU.mult, ek + pk, kT2)
                    tt(t4, Eall[:, :, 0, :], bc8(p15i, n), ALU.mult, ek + pk, kT2)
                    tt(Eall[:, :, 0, :], t1, t2, ALU.subtract, kT1, ek)
                    tt(Eall[:, :, 1, :], t3, t4, ALU.add, kT2, ek)
                    cur, ck_ = Eall, ek
                    bufs_ = [(bufA, kA), (bufB, kB)]
                    for lev in range(7):
                        n = 64 >> lev
                        dst, dk = bufs_[lev % 2]
                        cr, ci = s5lv[:, l, :, 0, lev], s5lv[:, l, :, 1, lev]
                        ev = [cur[:, :, a_, 0:2 * n].rearrange("p r (c two) -> p r c two", two=2)[:, :, :, 0] for a_ in range(2)]
                        od = [cur[:, :, a_, 0:2 * n].rearrange("p r (c two) -> p r c two", two=2)[:, :, :, 1] for a_ in range(2)]
                        t1 = T1[:, 0:8 * n].rearrange("p (r c) -> p r c", r=8)
                        t2 = T2[:, 0:8 * n].rearrange("p (r c) -> p r c", r=8)
                        tt(t1, ev[0], bc8(cr, n), ALU.mult, ck_ + lk, kT1)
                        tt(t2, ev[1], bc8(ci, n), ALU.mult, ck_ + lk, kT2)
                        tt(t1, t1, t2, ALU.subtract, kT1 + kT2, kT1)
                        tt(dst[:, :, 0, 0:n], t1, od[0], ALU.add, kT1 + ck_, dk)
                        tt(t1, ev[1], bc8(cr, n), ALU.mult, ck_ + lk, kT1)
                        tt(t2, ev[0], bc8(ci, n), ALU.mult, ck_ + lk, kT2)
                        tt(t1, t1, t2, ALU.add, kT1 + kT2, kT1)
                        tt(dst[:, :, 1, 0:n], t1, od[1], ALU.add, kT1 + ck_, dk)
                        cur, ck_ = dst, dk
                    cp(tiny[:, 800:816].rearrange("p (r a) -> p r a", a=2), cur[:, :, :, 0], ck_, tyk)
                    dma("sp", ccd[("x", l, "i")], tiny[:, 800:816], tyk, [("cci", "x", l)])
                    allgather("x", l, [("cci", "x", l)], [("cco", "x", l)])
                    return

            run_pass("p1")
            s5_combine(l)
            run_pass("full")
            SG = W
            for tb in range(NTB):
                sgs = []
                for m in range(2):
                    b = next_acc()
                    for k in range(2):
                        mm(ps[b][:, :], gluwS[:, l, k, m * 128:(m + 1) * 128], Yr[:, 6 + k, tb * TB:(tb + 1) * TB], k == 0, k == 1,
                           [("Yr", 6 + k, tb), "gluw"], [("ps", b)])
                    so = SG + ((tb * 2 + m) % 4) * 256
                    sg = arb(so, 256)
                    act(sg, ps[b][:, :], AF.Sigmoid, [("ps", b), "pp"], ark(so, 256), bias=pcol(l, "glub", m, m + 1))
                    sgs.append((sg, ark(so, 256)))
                for m in range(2):
                    sg, sgk = sgs[m]
                    tt(Yr[:, 6 + m, tb * TB:(tb + 1) * TB], Yr[:, 6 + m, tb * TB:(tb + 1) * TB], sg, ALU.mult,
                       [("Yr", 6 + m, tb)] + sgk, [("Yr", 6 + m, tb)])

        W1O = [7168, 9216]
        W2O = [11264, 13312]
        NG = 8

        def mlp_load(l, g):
            sl = g % 2
            w1g = arb(W1O[sl], 2048).rearrange("p (k c) -> p k c", k=8)
            w2g = arb(W2O[sl], 2048).rearrange("p (k c) -> p k c", k=4)
            w1k, w2k = ark(W1O[sl], 2048), ark(W2O[sl], 2048)
            dma("pool", w1g, w1[l, :, g * 512:(g + 1) * 512].rearrange("(k p) c -> p k c", p=128), [], w1k)
            dma("pool", w2g, w2[l, g * 512:(g + 1) * 512, :].rearrange("(k p) c -> p k c", p=128), [], w2k)

        def out_proj(l):
            WO, RSTD, SQ = 0, 4096, 6144
            wo = arb(WO, 4096).rearrange("p (k c) -> p k c", k=8)
            wok = ark(WO, 4096)
            dma("pool", wo, w_out[l, :, :].rearrange("(k p) c -> p k c", p=128), [], wok)
            mlp_load(l, 0)
            mlp_load(l, 1)
            for g in range(4):
                rms_stats(lambda k, tb: Yr[:, 2 * g + k, tb * TB:(tb + 1) * TB], 2, 256.0, RSTD, SQ,
                          lambda k, tb: [("Yr", 2 * g + k, tb)])
                for k in range(2):
                    ch = 2 * g + k
                    for tb in range(NTB):
                        stt(uT[:, ch, tb * TB:(tb + 1) * TB], Yr[:, ch, tb * TB:(tb + 1) * TB], pcol(l, "bnw", ch, ch + 1),
                            arf(RSTD + tb * TB, TB), ALU.mult, ALU.mult, [("Yr", ch, tb), "pp"] + ark(RSTD + tb * TB, TB), [("uT", ch, tb)])
            for m in range(KC):
                for tb in range(NTB):
                    b = next_acc()
                    for k in range(KC):
                        mm(ps[b][:, :], wo[:, k, m * 128:(m + 1) * 128], uT[:, k, tb * TB:(tb + 1) * TB], k == 0, k == KC - 1,
                           wok + [("uT", k, tb)], [("ps", b)])
                    stt(hT[:, m, tb * TB:(tb + 1) * TB], ps[b][:, :], G1(l, m), hT[:, m, tb * TB:(tb + 1) * TB], ALU.mult, ALU.add,
                        [("ps", b), ("der", l), ("hT", m, tb)], [("hT", m, tb)])

        def mlp(l):
            norm_mod(l, S2, B2)
            RS = 5120

            def up(g):
                sl = g % 2
                w1g = arb(W1O[sl], 2048).rearrange("p (k c) -> p k c", k=8)
                w1k = ark(W1O[sl], 2048)
                for jc in range(4):
                    yc = sl * 4 + jc
                    for tb in range(NTB):
                        b = next_acc()
                        for k in range(KC):
                            mm(ps[b][:, :], w1g[:, k, jc * 128:(jc + 1) * 128], uT[:, k, tb * TB:(tb + 1) * TB], k == 0, k == KC - 1,
                               w1k + [("uT", k, tb)], [("ps", b)])
                        ro = RS + ((jc * NTB + tb) % 4) * 256
                        rs = arb(ro, 256)
                        act(rs, ps[b][:, :], AF.Relu, [("ps", b)], ark(ro, 256))
                        tt(Yr[:, yc, tb * TB:(tb + 1) * TB], rs, rs, ALU.mult, ark(ro, 256), [("Yr", yc, tb)], eng="pool")

            def down(g):
                sl = g % 2
                w2g = arb(W2O[sl], 2048).rearrange("p (k c) -> p k c", k=4)
                w2k = ark(W2O[sl], 2048)
                for m in range(KC):
                    for tb in range(NTB):
                        b = next_acc()
                        for jc in range(4):
                            mm(ps[b][:, :], w2g[:, jc, m * 128:(m + 1) * 128], Yr[:, sl * 4 + jc, tb * TB:(tb + 1) * TB], jc == 0, jc == 3,
                               w2k + [("Yr", sl * 4 + jc, tb)], [("ps", b)])
                        stt(hT[:, m, tb * TB:(tb + 1) * TB], ps[b][:, :], G2(l, m), hT[:, m, tb * TB:(tb + 1) * TB], ALU.mult, ALU.add,
                            [("ps", b), ("der", l), ("hT", m, tb)], [("hT", m, tb)])

            up(0)
            for g in range(NG):
                if g + 1 < NG:
                    up(g + 1)
                down(g)
                if g + 2 < NG:
                    mlp_load(l, g + 2)

        def final_out(seg):
            RSTD, SQ, OUT = 0, 2048, 4096
            rms_stats(lambda k, tb: hT[:, k, tb * TB:(tb + 1) * TB], KC, float(DM), RSTD, SQ, lambda k, tb: [("hT", k, tb)])
            for k in range(KC):
                oo = OUT + (k % 2) * 2048
                o = arf(oo, 2048)
                for tb in range(NTB):
                    stt(o[:, tb * TB:(tb + 1) * TB], hT[:, k, tb * TB:(tb + 1) * TB], pcol(0, "fnw", k, k + 1), arf(RSTD + tb * TB, TB),
                        ALU.mult, ALU.mult, [("hT", k, tb), "pp"] + ark(RSTD + tb * TB, TB), ark(oo, 2048))
                dma("sp", yT[k * 128:(k + 1) * 128, :], o, ark(oo, 2048), [("yT", k, seg)])

        stopped = False
        for seg in range(nseg):
            if stopped:
                break
            for k in range(KC):
                dma("sp", hT[:, k, :], xT[k * 128:(k + 1) * 128, :], [], [("hT", k, tb) for tb in range(NTB)])
            for l in range(depth):
                norm_mod(l, S1, B1)
                if stop_after == (seg, l, "u"):
                    stopped = True
                    break
                print("ops before tail", P.total)
                tail_prepass(l)
                print("ops before s5 p1", P.total)
                mixer_s5(l, seg)
                print("ops before halo", P.total)
                halo_apply(l)
                mixer_pool(l, seg)
                mixer_sconv(l, seg)
                print("ops before ssd", P.total)
                mixer_ssd(l, seg)
                print("ops after ssd", P.total)
                if stop_after == (seg, l, "mix"):
                    stopped = True
                    break
                out_proj(l)
                if stop_after == (seg, l, "hmix"):
                    stopped = True
                    break
                mlp(l)
                if stop_after == (seg, l, "h"):
                    stopped = True
                    break
            if not stopped:
                final_out(seg)
        print("total ops recorded:", P.total)
        P.limit = None
        if dbg:
            if "uT" in dbg_out:
                cp(AR[:, 0:2048], uT[:, 0, :], all_keys("uT"), ark(0, 2048))
            for name in dbg_out:
                if name == "Yr":
                    for k in range(KC):
                        o = arf((k % 2) * 2048, 2048)
                        cp(o, Yr[:, k, :], [("Yr", k, tb) for tb in range(NTB)], ark((k % 2) * 2048, 2048))
                        dma("sp", dbg_out[name][k * 128:(k + 1) * 128, :], o, ark((k % 2) * 2048, 2048), [("dbg", name, k)])
                elif name == "uT":
                    for k in range(KC):
                        o = arf((k % 2) * 2048, 2048)
                        cp(o, uT[:, k, :], [("uT", k, tb) for tb in range(NTB)], ark((k % 2) * 2048, 2048))
                        dma("sp", dbg_out[name][k * 128:(k + 1) * 128, :], o, ark((k % 2) * 2048, 2048), [("dbg", name, k)])
                elif name == "hT":
                    for k in range(KC):
                        dma("sp", dbg_out[name][k * 128:(k + 1) * 128, :], hT[:, k, :], [("hT", k, tb) for tb in range(NTB)], [("dbg", name, k)])
                elif name == "mod":
                    dma("sp", dbg_out[name], modT[:, :, :].rearrange("p l c -> p (l c)"), [("mod", 0), ("mod", 1)], [("dbg", name)])
        P.wait_all("sp")

        with nc.Block() as block:
            def replay(e, name):
                for waits, fn, inc in P.q[name]:
                    for s, v in waits:
                        e.wait_ge(sems[s], v)
                    if fn is not None:
                        fn(e).then_inc(sems[inc[0]], inc[1])

            @block.tensor
            def _(e):
                replay(e, "pe")

            @block.scalar
            def _(e):
                replay(e, "act")

            @block.vector
            def _(e):
                replay(e, "dve")

            @block.gpsimd
            def _(e):
                replay(e, "pool")

            @block.sync
            def _(e):
                replay(e, "sp")
    return nc


def _fm(v):
    return np.ascontiguousarray(v.reshape(-1, 128).T)


def _pack_params(inp, b, sg):
    L = DEPTH
    pp = np.zeros((L, 128, NPCOL), np.float32)

    def put(l, name, arr):
        o, w = PCOL[name]
        arr = np.asarray(arr, np.float32).reshape(128, w)
        pp[l, :, o:o + w] = arr
    wins = (2, 4, 8, 16)
    for l in range(L):
        put(l, "nw1", _fm(inp["norm_mix_w"][l]))
        put(l, "nw2", _fm(inp["norm_mlp_w"][l]))
        put(l, "bnw", _fm(inp["branch_norm_w"][l]))
        put(l, "fnw", _fm(inp["final_norm_w"]))
        adab = np.zeros((128, 48), np.float32)
        adab[:, 0:12] = _fm(inp["ada_b"][l])[:, sg * 12:(sg + 1) * 12]
        put(l, "adab", adab)
        put(l, "pscale", _fm(inp["pool_scale"][l]))
        put(l, "scw", inp["sconv_w"][l].reshape(3, 2, 128).transpose(2, 1, 0))
        put(l, "cvw", inp["ssd_conv_w"][l].reshape(4, 6, 128).transpose(2, 1, 0))
        put(l, "cvb", _fm(inp["ssd_conv_b"][l]))
        put(l, "dtb", np.broadcast_to(inp["ssd_dt_bias"][l][None, :], (128, 4)))
        put(l, "alog", np.broadcast_to(inp["ssd_a_log"][l][None, :], (128, 4)))
        put(l, "dsk", np.broadcast_to(inp["ssd_d"][l][None, :], (128, 4)))
        def gp(a):
            return a.reshape(8, 2, 64).transpose(1, 2, 0).reshape(128, 8)
        put(l, "are", gp(inp["s5_a_re"][l]))
        put(l, "aim", gp(inp["s5_a_im"][l]))
        put(l, "lst", gp(np.broadcast_to(inp["s5_log_step"][l][:, None], (16, 64))))
        put(l, "s5d", _fm(inp["s5_d"][l]))
        put(l, "glub", _fm(inp["s5_glu_b"][l]))
        put(l, "cond", _fm(inp["c"][b]))
        invc = np.zeros((128, 2, 16), np.float32)
        for c in range(2):
            for half in range(2):
                win = wins[c * 2 + half]
                if sg == 0:
                    invc[half * 64:(half + 1) * 64, c, :] = 1.0 / np.minimum(np.arange(16) + 1, win)
                else:
                    invc[half * 64:(half + 1) * 64, c, :] = 1.0 / win
        put(l, "invc", invc)
        sel = np.zeros((128, 8), np.float32)
        if sg > 0:
            sel[:, sg - 1] = 1.0
        sel[:, 4 + sg] = 1.0
        put(l, "sel", sel)
    return pp


def _consts():
    cm = np.zeros((128, 4, 128), np.float32)
    i = np.arange(128)
    cm[:, 0, :] = (i[:, None] == i[None, :])
    cm[:, 1, :] = 1.0
    cm[:, 2, :] = (i[:, None] <= i[None, :])
    cm[:, 3, :] = np.where(i[None, :] < i[:, None], -30000.0, 0.0)
    return cm


def _host_inputs(inp, b, sg):
    f = lambda a: np.ascontiguousarray(np.asarray(a, np.float32))
    pw = np.zeros((DEPTH, 128, 2, 128), np.float32)
    for l in range(DEPTH):
        for g in range(4):
            c, half = g // 2, g % 2
            pw[l, half * 64:(half + 1) * 64, c, half * 64:(half + 1) * 64] = inp["pool_w"][l, g]
    gw = np.ascontiguousarray(np.asarray(inp["s5_glu_w"], np.float32).reshape(DEPTH, 2, 128, 256).transpose(0, 2, 1, 3))
    s5b = np.zeros((DEPTH, 128, 256), np.float32)
    s5c = np.zeros((DEPTH, 128, 8, 2, 32), np.float32)
    for l in range(DEPTH):
        s5b[l, :, 0:128] = np.asarray(inp["s5_b_re"][l]).reshape(8, 2, 64, 16).transpose(1, 2, 0, 3).reshape(128, 128)
        s5b[l, :, 128:256] = np.asarray(inp["s5_b_im"][l]).reshape(8, 2, 64, 16).transpose(1, 2, 0, 3).reshape(128, 128)
        for ri, nm in enumerate(("s5_c_re", "s5_c_im")):
            cc = np.asarray(inp[nm][l]).reshape(8, 2, 16, 64)
            for r in range(8):
                for gi in range(2):
                    s5c[l, gi * 64:(gi + 1) * 64, r, ri, gi * 16:gi * 16 + 16] = cc[r, gi].T
    return {
        "s5b": s5b, "s5c": s5c.reshape(DEPTH, 128, 512),
        "xT": f(np.asarray(inp["x"][b][sg * T:(sg + 1) * T]).T),
        "pp": _pack_params(inp, b, sg),
        "cmat": _consts(),
        "ada_w": f(np.asarray(inp["ada_w"])[:, :, sg * 1536:(sg + 1) * 1536]), "w_in": f(inp["w_in"]), "w_out": f(inp["w_out"]),
        "mlp_w1": f(inp["mlp_w1"]), "mlp_w2": f(inp["mlp_w2"]),
        "poolw": pw, "gluw": gw,
    }


_NC_CACHE = {}


def kernel(**inputs):
    inp = {k: np.asarray(v) for k, v in inputs.items()}
    if "full" not in _NC_CACHE:
        _NC_CACHE["full"] = build()
    nc = _NC_CACHE["full"]
    in_maps = [_host_inputs(inp, r // 4, r % 4) for r in range(8)]
    res = run_bass_kernel_spmd(nc, in_maps, core_ids=list(range(8)))
    out = np.empty((2, SEQ, DM), np.float32)
    for r in range(8):
        out[r // 4, (r % 4) * T:(r % 4 + 1) * T, :] = res.results[r]["yT"].T
    return out
```

```python
import numpy as np
import concourse.bass as bass
import concourse.mybir as mybir
from concourse.bass_utils import run_bass_kernel_spmd

F32, BF16 = mybir.dt.float32, mybir.dt.bfloat16
AF = mybir.ActivationFunctionType
ALU = mybir.AluOpType

T = 2048
TB = 512
NTB = 4
NCH = 16
DM = 1024
KC = 8
SEQ = 8192
DEPTH = 2
EPS = 1e-6
ENGS = ("pe", "act", "dve", "pool", "sp")
NDMA = 12

PCOL = {}
_off = 0
for _n, _w in [("nw1", 8), ("nw2", 8), ("bnw", 8), ("fnw", 8), ("adab", 48), ("pscale", 2), ("scw", 6),
               ("cvw", 24), ("cvb", 6), ("dtb", 4), ("alog", 4), ("dsk", 4), ("are", 8), ("aim", 8), ("lst", 8),
               ("s5d", 2), ("glub", 2), ("cond", 8),
               ("invc", 32), ("sel", 8)]:
    PCOL[_n] = (_off, _w)
    _off += _w
NPCOL = _off


class Prog:
    def __init__(self):
        self.q = {e: [] for e in ENGS}
        self.cnt = {e: 0 for e in ENGS}
        self.seen = {e: {} for e in ENGS}
        self.lastw = {}
        self.rd = {}
        self.dma_i = 0
        self.fam = {}
        import os as _os
        self.nosame = set(x for x in _os.environ.get("KNOSAME", "").split(",") if x)
        self.total = 0
        import os
        self.limit = int(os.environ.get("KLIMIT", "0")) or None

    def _skip(self):
        self.total += 1
        return self.limit is not None and self.total > self.limit

    def _deps(self, eng, reads, writes):
        need = {}

        def add(d):
            if d is None:
                return
            s, v = d
            if need.get(s, 0) < v:
                need[s] = v
        for k0 in reads:
            for k in self._rel(k0):
                add(self.lastw.get(k))
        for k0 in writes:
            for k in self._rel(k0):
                add(self.lastw.get(k))
                for s, v in self.rd.get(k, {}).items():
                    add((s, v))
        out = []
        for s, v in need.items():
            if s == eng and (eng == "pe" or eng in self.nosame):
                continue
            if self.seen[eng].get(s, 0) >= v:
                continue
            self.seen[eng][s] = v
            out.append((s, v))
        return out

    def _rel(self, k):
        if isinstance(k, tuple) and k[0] == "ps":
            fam = self.fam.setdefault(k[1], set())
            fam.add(k)
            if len(k) == 2:
                return list(fam)
            return [k, ("ps", k[1])]
        return [k]

    def _commit(self, reads, writes, tok):
        s, v = tok
        for k in reads:
            d = self.rd.setdefault(k, {})
            if d.get(s, 0) < v:
                d[s] = v
        for k in writes:
            self.lastw[k] = tok
            self.rd[k] = {}

    @staticmethod
    def _norm(reads, writes):
        r2, w2 = [], list(writes)
        for k in reads:
            if isinstance(k, tuple) and k[0] == "ps":
                w2.append(k)
            else:
                r2.append(k)
        w2 = [("ps", k[1]) if (isinstance(k, tuple) and k[0] == "ps") else k for k in w2]
        return r2, w2

    def op(self, eng, fn, reads=(), writes=()):
        if self._skip():
            return
        reads, writes = self._norm(reads, writes)
        waits = self._deps(eng, reads, writes)
        self.cnt[eng] += 1
        self.q[eng].append((waits, fn, (eng, 1)))
        self._commit(reads, writes, (eng, self.cnt[eng]))

    def dma(self, eng, fn, reads=(), writes=()):
        if self._skip():
            return None
        reads = list(reads)
        writes = list(writes)
        i = self.dma_i
        self.dma_i += 1
        sem = "dma%d" % (i % NDMA)
        val = 16 * (i // NDMA + 1)
        waits = self._deps(eng, reads, writes)
        if i >= NDMA:
            prev = 16 * (i // NDMA)
            if self.seen[eng].get(sem, 0) < prev:
                self.seen[eng][sem] = prev
                waits.append((sem, prev))
        self.q[eng].append((waits, fn, (sem, 16)))
        self._commit(reads, writes, (sem, val))
        return (sem, val)

    def cc(self, fn, sem, reads=(), writes=()):
        if self._skip():
            return
        reads, writes = self._norm(reads, writes)
        waits = self._deps("pool", reads, writes)
        self.q["pool"].append((waits, fn, (sem, 1)))
        self._commit(reads, writes, (sem, 1))

    def wait_all(self, eng):
        waits = []
        for e in ENGS:
            if e != eng and self.cnt[e] > self.seen[eng].get(e, 0):
                self.seen[eng][e] = self.cnt[e]
                waits.append((e, self.cnt[e]))
        for j in range(min(NDMA, self.dma_i)):
            sem = "dma%d" % j
            n = (self.dma_i - 1 - j) // NDMA + 1
            if self.seen[eng].get(sem, 0) < 16 * n:
                self.seen[eng][sem] = 16 * n
                waits.append((sem, 16 * n))
        self.q[eng].append((waits, None, None))


class Buf:
    def __init__(self, name, ap):
        self.name = name
        self.ap = ap

    def k(self, *idx):
        return (self.name,) + tuple(idx)


def build(depth=DEPTH, dbg=None, stop_after=None):
    nseg = 1
    nc = bass.Bass("TRN2", target_bir_lowering=False)
    P = Prog()
    dram = {}

    def din(name, shape):
        dram[name] = nc.dram_tensor(name, list(shape), F32, kind="ExternalInput").ap()
        return dram[name]

    xT = din("xT", [DM, T])
    pp = din("pp", [DEPTH, 128, NPCOL])
    cmat = din("cmat", [128, 4, 128])
    ada_w = din("ada_w", [DEPTH, DM, 1536])
    w_in = din("w_in", [DEPTH, DM, 2308])
    w_out = din("w_out", [DEPTH, DM, DM])
    w1 = din("mlp_w1", [DEPTH, DM, 4 * DM])
    w2 = din("mlp_w2", [DEPTH, 4 * DM, DM])
    poolw = din("poolw", [DEPTH, 128, 2, 128])
    gluw = din("gluw", [DEPTH, 128, 2, 256])
    s5b = din("s5b", [DEPTH, 128, 256])
    s5c = din("s5c", [DEPTH, 128, 512])
    yT = nc.dram_tensor("yT", [DM, T], F32, kind="ExternalOutput").ap()
    GRP = [[0, 1, 2, 3], [4, 5, 6, 7]]
    ccd = {}
    for l_ in range(DEPTH):
        for nm_, w_ in (("h", 64), ("x", 16), ("s", 272), ("m", 32)):
            ccd[(nm_, l_, "i")] = nc.dram_tensor("cc%s%di" % (nm_, l_), [128, w_], F32, kind="Internal").ap()
            ccd[(nm_, l_, "o")] = nc.dram_tensor("cc%s%do" % (nm_, l_), [4 * 128, w_], F32, kind="Internal").ap()
    dbg_out = {}
    if dbg:
        for name, shape in dbg.items():
            dbg_out[name] = nc.dram_tensor("dbg_" + name, list(shape), F32, kind="ExternalOutput").ap()

    import contextlib
    es = contextlib.ExitStack()
    with es:
        def sb(name, shape, dt):
            return es.enter_context(nc.sbuf_tensor(name, list(shape), dt))

        hT = sb("hT", [128, KC, T], F32)
        uT = sb("uT", [128, KC, T], BF16)
        Yr = sb("Yr", [128, KC, T], BF16)
        AR = sb("AR", [128, 15360], F32)
        cm = sb("cm", [128, 4, 128], BF16)
        ppS = sb("ppS", [128, DEPTH, NPCOL], F32)
        modT = sb("modT", [128, DEPTH, 48], F32)
        der = sb("der", [128, DEPTH, 64], F32)
        condb = sb("condb", [128, 8], BF16)
        epsT = sb("epsT", [128, 2], F32)
        poolwS = sb("poolwS", [128, DEPTH, 2, 128], BF16)
        gluwS = sb("gluwS", [128, DEPTH, 2, 256], BF16)
        cPool = sb("cPool", [128, DEPTH, 2, 16], F32)
        cSc = sb("cSc", [128, DEPTH, 2, 2], F32)
        cCv = sb("cCv", [128, DEPTH, 6, 3], F32)
        cS = sb("cS", [128, DEPTH, 256], F32)
        cX = sb("cX", [128, DEPTH, 8, 2], F32)
        ssdp = sb("ssdp", [128, DEPTH, 16], F32)
        s5pw = sb("s5pw", [128, DEPTH, 8, 3, 16], F32)
        s5lv = sb("s5lv", [128, DEPTH, 8, 3, 8], F32)
        bbS = sb("bbS", [128, DEPTH, 2, 8, 16], F32)
        tiny = sb("tiny", [128, 896], F32)
        tinyb = sb("tinyb", [128, 128], BF16)

        ps = [es.enter_context(nc.psum_tensor("ps%d" % i, [128, 512], F32)) for i in range(8)]
        sems = {}
        for e in ENGS:
            sems[e] = es.enter_context(nc.semaphore("s_" + e))
        for j in range(NDMA):
            sems["dma%d" % j] = es.enter_context(nc.semaphore("s_dma%d" % j))
        for l_ in range(DEPTH):
            for nm_ in ("h", "x", "s", "m"):
                sems["cc%s%d" % (nm_, l_)] = es.enter_context(nc.semaphore("s_cc%s%d" % (nm_, l_)))

        def allgather(nm_, l_, reads, writes):
            i_, o_ = ccd[(nm_, l_, "i")], ccd[(nm_, l_, "o")]
            P.cc(lambda e: e.collective_compute("AllGather", ALU.bypass, replica_groups=GRP, ins=[i_], outs=[o_]),
                 "cc%s%d" % (nm_, l_), reads, writes)

        ARb = AR[:, :].bitcast(BF16)

        def arf(off, n):
            return AR[:, off:off + n]

        def arb(off, n):
            return ARb[:, 2 * off:2 * (off + n)]

        def ark(off, n):
            return [("ar", b) for b in range(off // 256, (off + n + 255) // 256)]

        ident = cm[:, 0, :]
        ones = cm[:, 1, :]
        tri = cm[:, 2, :]
        negm = cm[:, 3, :]

        def pcol(l, name, a=0, b=None):
            o, w = PCOL[name]
            if b is None:
                b = w
            return ppS[:, l, o + a:o + b]

        def mm(out, lhsT, rhs, start, stop, reads, writes):
            P.op("pe", lambda e: e.matmul(out, lhsT=lhsT, rhs=rhs, start=start, stop=stop), reads, writes)

        def act(out, in_, func, reads, writes, bias=None, scale=None):
            kw = {}
            if bias is not None:
                kw["bias"] = bias
            if scale is not None:
                kw["scale"] = scale
            P.op("act", lambda e: e.activation(out=out, in_=in_, func=func, **kw), reads, writes)

        def tt(out, in0, in1, op, reads, writes, eng="dve"):
            P.op(eng, lambda e: e.tensor_tensor(out=out, in0=in0, in1=in1, op=op), reads, writes)

        def ts(out, in0, s1, s2, op0, op1, reads, writes, eng="dve"):
            if s2 is None:
                P.op(eng, lambda e: e.tensor_scalar(out=out, in0=in0, scalar1=s1, scalar2=None, op0=op0), reads, writes)
            else:
                P.op(eng, lambda e: e.tensor_scalar(out=out, in0=in0, scalar1=s1, scalar2=s2, op0=op0, op1=op1), reads, writes)

        def stt(out, in0, scalar, in1, op0, op1, reads, writes, eng="dve"):
            P.op(eng, lambda e: e.scalar_tensor_tensor(out=out, in0=in0, scalar=scalar, in1=in1, op0=op0, op1=op1), reads, writes)

        def cp(out, in_, reads, writes, eng="dve"):
            P.op(eng, lambda e: e.tensor_copy(out=out, in_=in_), reads, writes)

        def mset(ap, val, writes, eng="dve"):
            P.op(eng, lambda e: e.memset(ap, val), [], writes)

        def dma(eng, out, in_, reads, writes):
            return P.dma(eng, lambda e: e.dma_start(out=out, in_=in_), reads, writes)

        dma("pool", cm[:, :, :], cmat, [], ["cm"])
        dma("sp", ppS[:, :, :], pp.rearrange("l p c -> p l c"), [], ["pp"])
        dma("pool", poolwS[:, :, :, :], poolw.rearrange("l p c d -> p l c d"), [], ["poolw"])
        dma("pool", gluwS[:, :, :, :], gluw.rearrange("l p c d -> p l c d"), [], ["gluw"])
        mset(epsT[:, 0:1], EPS, ["eps"])
        mset(epsT[:, 1:2], 1.0, ["eps"])
        for t_, kk in ((cPool, "cPool"), (cSc, "cSc"), (cCv, "cCv"), (cS, "cS"), (cX, "cX")):
            mset(t_[:], 0.0, [kk])
        act(condb[:, :], pcol(0, "cond"), AF.Silu, ["pp"], ["condb"])

        WADA = 0
        bi = 0
        for l in range(DEPTH):
            for blk in range(3):
                slot = bi % 2
                bi += 1
                wsl = arb(WADA + slot * 2048, 2048).rearrange("p (k c) -> p k c", k=8)
                wk = ark(WADA + slot * 2048, 2048)
                dma("pool", wsl, ada_w[l, :, blk * 512:(blk + 1) * 512].rearrange("(k p) c -> p k c", p=128), [], wk)
                for jj in range(4):
                    j = l * 12 + blk * 4 + jj
                    for k in range(KC):
                        mm(ps[6][:, j:j + 1], wsl[:, k, jj * 128:(jj + 1) * 128], condb[:, k:k + 1], k == 0, k == KC - 1,
                           wk + ["condb"], [("ps", 6)])
        mpk = tiny[:, 0:24].rearrange("p (l c) -> p l c", l=2)
        tyk0 = [("tiny", "all")]
        for l in range(DEPTH):
            tt(mpk[:, l, :], ps[6][:, l * 12:(l + 1) * 12], pcol(l, "adab", 0, 12), ALU.add, [("ps", 6), "pp"], tyk0)
        dma("sp", ccd[("m", 0, "i")][:, 0:24], tiny[:, 0:24], tyk0, [("cci", "m", 0)])
        allgather("m", 0, [("cci", "m", 0)], [("cco", "m", 0)])
        mrb = tiny[:, 32:160].rearrange("p (r f) -> p r f", r=4)
        dma("sp", mrb, ccd[("m", 0, "o")].rearrange("(r p) f -> p r f", p=128), [("cco", "m", 0)], tyk0)
        for l in range(DEPTH):
            for sg_ in range(4):
                cp(modT[:, l, sg_ * 12:(sg_ + 1) * 12], mrb[:, sg_, l * 12:(l + 1) * 12], tyk0, [("mod", l)])
        for l in range(depth):
            stt(der[:, l, 0:8], modT[:, l, 8:16], 1.0, pcol(l, "nw1"), ALU.add, ALU.mult, [("mod", l), "pp"], [("der", l)])
            stt(der[:, l, 24:32], modT[:, l, 32:40], 1.0, pcol(l, "nw2"), ALU.add, ALU.mult, [("mod", l), "pp"], [("der", l)])
            cp(der[:, l, 8:16], modT[:, l, 0:8], [("mod", l)], [("der", l)])
            cp(der[:, l, 16:24], modT[:, l, 16:24], [("mod", l)], [("der", l)])
            cp(der[:, l, 32:40], modT[:, l, 24:32], [("mod", l)], [("der", l)])
            cp(der[:, l, 40:48], modT[:, l, 40:48], [("mod", l)], [("der", l)])

        def S1(l, k): return der[:, l, 0 + k:1 + k]
        def B1(l, k): return der[:, l, 8 + k:9 + k]
        def G1(l, k): return der[:, l, 16 + k:17 + k]
        def S2(l, k): return der[:, l, 24 + k:25 + k]
        def B2(l, k): return der[:, l, 32 + k:33 + k]
        def G2(l, k): return der[:, l, 40 + k:41 + k]

        for l in range(depth):
            act(ssdp[:, l, 0:4], pcol(l, "alog"), AF.Exp, ["pp"], [("ssdp", l)])
            ts(ssdp[:, l, 0:4], ssdp[:, l, 0:4], -1.0, None, ALU.mult, None, [("ssdp", l)], [("ssdp", l)])
            cp(ssdp[:, l, 4:8], pcol(l, "dtb"), ["pp"], [("ssdp", l)])
            cp(ssdp[:, l, 8:12], pcol(l, "dsk"), ["pp"], [("ssdp", l)])

        def tk(n): return [("tiny", n)]
        for l in range(depth):
            tyk = [("tiny", "all")]
            stp = tiny[:, 0:8]
            act(stp, pcol(l, "lst"), AF.Exp, ["pp"], tyk)
            mag = tiny[:, 8:16]
            tt(mag, pcol(l, "are"), stp, ALU.mult, ["pp"] + tyk, tyk)
            act(mag, mag, AF.Exp, tyk, tyk)
            th = tiny[:, 16:24]
            tt(th, pcol(l, "aim"), stp, ALU.mult, ["pp"] + tyk, tyk)
            sa = tiny[:, 24:32]
            ca = tiny[:, 32:40]
            act(sa, th, AF.Sin, tyk, tyk, scale=1.0 / 16.0)
            ts(ca, th, 1.0 / 16.0, float(np.pi / 2), ALU.mult, ALU.add, tyk, tyk)
            act(ca, ca, AF.Sin, tyk, tyk)
            for _ in range(4):
                t2a, t2b = tiny[:, 272:280], tiny[:, 280:288]
                tt(t2a, ca, ca, ALU.mult, tyk, tyk)
                tt(t2b, sa, sa, ALU.mult, tyk, tyk)
                tt(sa, sa, ca, ALU.mult, tyk, tyk)
                ts(sa, sa, 2.0, None, ALU.mult, None, tyk, tyk)
                tt(ca, t2a, t2b, ALU.subtract, tyk, tyk)
            lr = tiny[:, 40:48]
            li = tiny[:, 48:56]
            tt(lr, mag, ca, ALU.mult, tyk, tyk)
            tt(li, mag, sa, ALU.mult, tyk, tyk)
            den = tiny[:, 56:64]
            t0 = tiny[:, 64:72]
            tt(den, pcol(l, "are"), pcol(l, "are"), ALU.mult, ["pp"] + tyk, tyk)
            tt(t0, pcol(l, "aim"), pcol(l, "aim"), ALU.mult, ["pp"] + tyk, tyk)
            tt(den, den, t0, ALU.add, tyk, tyk)
            P.op("dve", lambda e, den=den: e.reciprocal(out=den, in_=den), tyk, tyk)
            nr = tiny[:, 72:80]
            ts(nr, lr, -1.0, None, ALU.add, None, tyk, tyk)
            fr = tiny[:, 80:88]
            fi = tiny[:, 88:96]
            t1 = tiny[:, 96:104]
            tt(fr, nr, pcol(l, "are"), ALU.mult, ["pp"] + tyk, tyk)
            tt(t1, li, pcol(l, "aim"), ALU.mult, ["pp"] + tyk, tyk)
            tt(fr, fr, t1, ALU.add, tyk, tyk)
            tt(fr, fr, den, ALU.mult, tyk, tyk)
            tt(fi, li, pcol(l, "are"), ALU.mult, ["pp"] + tyk, tyk)
            tt(t1, nr, pcol(l, "aim"), ALU.mult, ["pp"] + tyk, tyk)
            tt(fi, fi, t1, ALU.subtract, tyk, tyk)
            tt(fi, fi, den, ALU.mult, tyk, tyk)
            dma("sp", tiny[:, 512:768], s5b[l], [], tyk)
            bre = tiny[:, 512:640].rearrange("p (r h) -> p r h", r=8)
            bim = tiny[:, 640:768].rearrange("p (r h) -> p r h", r=8)
            frb = fr.unsqueeze(2).to_broadcast([128, 8, 16])
            fib = fi.unsqueeze(2).to_broadcast([128, 8, 16])
            tmpb = tiny[:, 128:256].rearrange("p (r h) -> p r h", r=8)
            tt(bbS[:, l, 0, :, :], bre, frb, ALU.mult, ["pp"] + tyk, [("bb", l)])
            tt(tmpb, bim, fib, ALU.mult, ["pp"] + tyk, tyk)
            tt(bbS[:, l, 0, :, :], bbS[:, l, 0, :, :], tmpb, ALU.subtract, tyk + [("bb", l)], [("bb", l)])
            tt(bbS[:, l, 1, :, :], bim, frb, ALU.mult, ["pp"] + tyk, [("bb", l)])
            tt(tmpb, bre, fib, ALU.mult, ["pp"] + tyk, tyk)
            tt(bbS[:, l, 1, :, :], bbS[:, l, 1, :, :], tmpb, ALU.add, tyk + [("bb", l)], [("bb", l)])
            ts(bbS[:, l, 1, :, :], bbS[:, l, 1, :, :], -1.0, None, ALU.mult, None, [("bb", l)], [("bb", l)])
            ts(li, li, -1.0, None, ALU.mult, None, tyk, tyk)
            pk = [("s5pw", l)]
            cp(s5pw[:, l, :, 0, 0], lr, tyk, pk)
            cp(s5pw[:, l, :, 1, 0], li, tyk, pk)
        tyk = [("tiny", "all")]
        pkA = [("s5pw", l_) for l_ in range(DEPTH)]
        lkA = [("s5lv", l_) for l_ in range(DEPTH)]
        lrA, liA = s5pw[:, :, :, 0, 0], s5pw[:, :, :, 1, 0]
        taA = tiny[:, 256:256 + 8 * DEPTH].rearrange("p (l r) -> p l r", l=DEPTH)
        tbA = tiny[:, 288:288 + 8 * DEPTH].rearrange("p (l r) -> p l r", l=DEPTH)
        for j in range(1, 16):
            pr_, pi_ = s5pw[:, :, :, 0, j - 1], s5pw[:, :, :, 1, j - 1]
            nr_, ni_ = s5pw[:, :, :, 0, j], s5pw[:, :, :, 1, j]
            tt(taA, pr_, lrA, ALU.mult, tyk + pkA, tyk)
            tt(tbA, pi_, liA, ALU.mult, tyk + pkA, tyk)
            tt(nr_, taA, tbA, ALU.subtract, tyk, pkA)
            tt(taA, pr_, liA, ALU.mult, tyk + pkA, tyk)
            tt(tbA, pi_, lrA, ALU.mult, tyk + pkA, tyk)
            tt(ni_, taA, tbA, ALU.add, tyk, pkA)
        for l in range(depth):
            ts(s5pw[:, l, :, 2, :], s5pw[:, l, :, 1, :], -1.0, None, ALU.mult, None, pkA, pkA)
        cp(s5lv[:, :, :, 0, 0], s5pw[:, :, :, 0, 15], pkA, lkA)
        cp(s5lv[:, :, :, 1, 0], s5pw[:, :, :, 1, 15], pkA, lkA)
        for k in range(1, 8):
            pr_, pi_ = s5lv[:, :, :, 0, k - 1], s5lv[:, :, :, 1, k - 1]
            tt(taA, pr_, pr_, ALU.mult, lkA + tyk, tyk)
            tt(tbA, pi_, pi_, ALU.mult, lkA + tyk, tyk)
            tt(s5lv[:, :, :, 0, k], taA, tbA, ALU.subtract, tyk, lkA)
            tt(taA, pr_, pi_, ALU.mult, lkA + tyk, tyk)
            ts(s5lv[:, :, :, 1, k], taA, 2.0, None, ALU.mult, None, tyk, lkA)
        for l in range(depth):
            ts(s5lv[:, l, :, 2, :], s5lv[:, l, :, 1, :], -1.0, None, ALU.mult, None, lkA, lkA)

        acc_i = [0]

        def next_acc():
            b = acc_i[0] % 4
            acc_i[0] += 1
            return b

        def rms_stats(src_fn, nchunks, denom, rstd_off, sq_off, src_keys_fn):
            for tb in range(NTB):
                bank = 4 + (tb % 2)
                for k in range(nchunks):
                    so = sq_off + ((tb * nchunks + k) % 4) * 256
                    sq = arb(so, 256)
                    src = src_fn(k, tb)
                    tt(sq, src, src, ALU.mult, src_keys_fn(k, tb), ark(so, 256), eng="pool")
                    mm(ps[bank][:, :], ones, sq, k == 0, k == nchunks - 1, ark(so, 256) + ["cm"], [("ps", bank)])
                r = arf(rstd_off + tb * TB, TB)
                rk = ark(rstd_off + tb * TB, TB)
                act(r, ps[bank][:, :], AF.Ln, [("ps", bank), "eps"], rk, bias=epsT[:, 0:1], scale=1.0 / denom)
                act(r, r, AF.Exp, rk, rk, scale=-0.5)

        def norm_mod(l, s_fn, b_fn):
            RSTD, SQ, TMP = 0, 2048, 3072
            rms_stats(lambda k, tb: hT[:, k, tb * TB:(tb + 1) * TB], KC, float(DM), RSTD, SQ,
                      lambda k, tb: [("hT", k, tb)])
            for k in range(KC):
                for tb in range(NTB):
                    to = TMP + ((k * NTB + tb) % 4) * TB
                    tmp = arf(to, TB)
                    stt(tmp, hT[:, k, tb * TB:(tb + 1) * TB], s_fn(l, k), arf(RSTD + tb * TB, TB), ALU.mult, ALU.mult,
                        [("hT", k, tb), ("der", l)] + ark(RSTD + tb * TB, TB), ark(to, TB))
                    act(uT[:, k, tb * TB:(tb + 1) * TB], tmp, AF.Identity, ark(to, TB) + [("der", l)], [("uT", k, tb)],
                        bias=b_fn(l, k))

        WIN = 14336
        win_i = [0]

        def load_win(l, c0, ncol):
            assert ncol <= 128
            slot = win_i[0] % 2
            win_i[0] += 1
            off = WIN + slot * 512
            w = arb(off, 512).rearrange("p (k c) -> p k c", k=8)
            dma("pool", w[:, :, 0:ncol], w_in[l, :, c0:c0 + ncol].rearrange("(k p) c -> p k c", p=128), [], ark(off, 512))
            return w, ark(off, 512)

        def proj_chunk(w, wk, cofs, ncol, evac):
            for tb in range(NTB):
                b = next_acc()
                for k in range(KC):
                    mm(ps[b][0:ncol, :], w[:, k, cofs:cofs + ncol], uT[:, k, tb * TB:(tb + 1) * TB], k == 0, k == KC - 1,
                       wk + [("uT", k, tb)], [("ps", b)])
                evac(tb, ps[b][0:ncol, :], ("ps", b))

        def dump(name, ap, keys):
            if dbg and name in dbg_out:
                dma("sp", dbg_out[name], ap, keys, [("dbg", name)])

        def all_keys_h():
            return [("hT", k, tb) for k in range(KC) for tb in range(NTB)]

        def all_keys(nm):
            return [(nm, k, tb) for k in range(KC) for tb in range(NTB)]

        def tail_prepass(l):
            PK = arf(0, 64)
            pkk = ark(0, 64)
            gct = tiny[:, 0:32].rearrange("p (c t) -> p c t", c=2)
            tyk = [("tiny", "all")]
            tail = slice(T - 16, T)

            def tproj(w, wk, cofs, evac):
                b = next_acc()
                for k in range(KC):
                    mm(ps[b][:, 0:16], w[:, k, cofs:cofs + 128], uT[:, k, tail], k == 0, k == KC - 1, wk + [("uT", k, 3)], [("ps", b)])
                evac(ps[b][:, 0:16], ("ps", b))
            for c in range(2):
                w, wk = load_win(l, c * 128, 128)
                tproj(w, wk, 0, lambda p_, pk, c=c: act(PK[:, c * 16:(c + 1) * 16], p_, AF.Copy, [pk], pkk))
            for c in range(2):
                w, wk = load_win(l, 512 + c * 128, 128)
                tproj(w, wk, 0, lambda p_, pk, c=c: act(gct[:, c, :], p_, AF.Copy, [pk], tyk))
            for c in range(2):
                w, wk = load_win(l, 768 + c * 128, 128)
                tproj(w, wk, 0, lambda p_, pk, c=c: tt(PK[:, 32 + 2 * c:34 + 2 * c], p_[:, 14:16], gct[:, c, 14:16], ALU.mult,
                                                       [pk] + tyk, pkk))
            for j in range(6):
                w, wk = load_win(l, 1280 + j * 128, 128)
                tproj(w, wk, 0, lambda p_, pk, j=j: act(PK[:, 36 + 3 * j:39 + 3 * j], p_[:, 13:16], AF.Copy, [pk], pkk))
            dma("sp", ccd[("h", l, "i")][:, 0:54], PK[:, 0:54], pkk, [("cci", "h", l)])
            allgather("h", l, [("cci", "h", l)], [("cco", "h", l)])

        def halo_apply(l):
            RB = arf(256, 256).rearrange("p (r f) -> p r f", r=4)
            rbk = ark(256, 256)
            tyk = [("tiny", "all")]
            dma("sp", RB, ccd[("h", l, "o")].rearrange("(r p) f -> p r f", p=128), [("cco", "h", l)], rbk)
            hal = tiny[:, 64:128]
            sel = pcol(l, "sel")
            ts(hal, RB[:, 0, :], sel[:, 0:1], None, ALU.mult, None, rbk + ["pp"], tyk)
            for j in range(1, 4):
                stt(hal, RB[:, j, :], sel[:, j:j + 1], hal, ALU.mult, ALU.add, rbk + ["pp"] + tyk, tyk)
            cp(cPool[:, l, :, :], hal[:, 0:32].rearrange("p (c t) -> p c t", c=2), tyk, [("cPool", l, 0), ("cPool", l, 1)])
            cp(cSc[:, l, :, :], hal[:, 32:36].rearrange("p (c t) -> p c t", c=2), tyk, [("cSc", l, 0), ("cSc", l, 1)])
            cp(cCv[:, l, :, :], hal[:, 36:54].rearrange("p (c t) -> p c t", c=6), tyk, [("cCv", l, j) for j in range(6)])

        def s5_combine(l):
            tyk = [("tiny", "all")]
            RB = tiny[:, 816:880].rearrange("p (r f) -> p r f", r=4)
            dma("sp", RB, ccd[("x", l, "o")].rearrange("(r p) f -> p r f", p=128), [("cco", "x", l)], tyk)
            sel = pcol(l, "sel")
            Lr, Li = s5lv[:, l, :, 0, 7], s5lv[:, l, :, 1, 7]
            lk = [("s5lv", l)]
            ar, ai, pr, pi, t1, t2 = (tiny[:, a:a + 8] for a in (768, 776, 784, 792, 880, 888))
            for t_ in (ar, ai, pr, pi):
                mset(t_, 0.0, tyk)
            for j in range(4):
                stt(ar, pr, sel[:, 4 + j:5 + j], ar, ALU.mult, ALU.add, tyk + ["pp"], tyk)
                stt(ai, pi, sel[:, 4 + j:5 + j], ai, ALU.mult, ALU.add, tyk + ["pp"], tyk)
                if j < 3:
                    F = RB[:, j, :].rearrange("p (r a) -> p r a", a=2)
                    tt(t1, pr, Lr, ALU.mult, tyk + lk, tyk)
                    tt(t2, pi, Li, ALU.mult, tyk + lk, tyk)
                    tt(t1, t1, t2, ALU.subtract, tyk, tyk)
                    tt(t2, pr, Li, ALU.mult, tyk + lk, tyk)
                    tt(pr, t1, F[:, :, 0], ALU.add, tyk, tyk)
                    tt(t1, pi, Lr, ALU.mult, tyk + lk, tyk)
                    tt(t1, t1, t2, ALU.add, tyk, tyk)
                    tt(pi, t1, F[:, :, 1], ALU.add, tyk, tyk)
            cp(cX[:, l, :, 0], ar, tyk, [("cX", l, r) for r in range(8)])
            cp(cX[:, l, :, 1], ai, tyk, [("cX", l, r) for r in range(8)])

        def mixer_pool(l, seg):
            pbv = Yr[:, 4:6, :]
            for c in range(2):
                V, SA, SB = c * 6912, c * 6912 + 2304, c * 6912 + 4608
                w, wk = load_win(l, c * 128, 128)
                v = arf(V, 2064)
                vk = ark(V, 2064)
                cp(v[:, 0:16], cPool[:, l, c, :], [("cPool", l, c)], vk)
                proj_chunk(w, wk, 0, 128,
                           lambda tb, p_, pk: act(v[:, 16 + tb * TB:16 + (tb + 1) * TB], p_, AF.Copy, [pk], vk))
                cp(cPool[:, l, c, :], v[:, 2048:2064], vk, [("cPool", l, c)])
                sa, sbb = arf(SA, 2064), arf(SB, 2064)
                sak, sbk = ark(SA, 2064), ark(SB, 2064)
                tt(sa[:, 1:2064], v[:, 1:2064], v[:, 0:2063], ALU.add, vk, sak)
                tt(sbb[:, 3:2064], sa[:, 3:2064], sa[:, 1:2062], ALU.add, sak, sbk)
                if c == 0:
                    lo_src, hi_src, lo_w, hi_w = sa, sbb, 2, 4
                else:
                    tt(sa[:, 7:2064], sbb[:, 7:2064], sbb[:, 3:2060], ALU.add, sbk, sak)
                    tt(sbb[:, 15:2064], sa[:, 15:2064], sa[:, 7:2056], ALU.add, sak, sbk)
                    lo_src, hi_src, lo_w, hi_w = sa, sbb, 8, 16
                pb = pbv
                pbk = [("Yr", 4 + c, t_) for t_ in range(NTB)]
                stt(pb[0:64, c, :], lo_src[0:64, 16:2064], 1.0 / lo_w, v[0:64, 16:2064], ALU.mult, ALU.subtract, sak + sbk + vk, pbk)
                stt(pb[64:128, c, :], hi_src[64:128, 16:2064], 1.0 / hi_w, v[64:128, 16:2064], ALU.mult, ALU.subtract, sak + sbk + vk, pbk)
                if True:
                    ic = pcol(l, "invc").rearrange("p (c t) -> p c t", c=2)
                    tq = tiny[:, 512:528]
                    for (r0, r1, src) in ((0, 64, lo_src), (64, 128, hi_src)):
                        tt(tq[r0:r1, :], src[r0:r1, 16:32], ic[r0:r1, c, :], ALU.mult, sak + sbk + ["pp"], [("tiny", "all")])
                        tt(pb[r0:r1, c, 0:16], tq[r0:r1, :], v[r0:r1, 16:32], ALU.subtract, [("tiny", "all")] + vk, pbk)
            pb = pbv
            for c in range(2):
                for tb in range(NTB):
                    b = next_acc()
                    mm(ps[b][:, :], poolwS[:, l, c, :], pb[:, c, tb * TB:(tb + 1) * TB], True, True, [("Yr", 4 + c, tb), "poolw"], [("ps", b)])
                    ts(Yr[:, c, tb * TB:(tb + 1) * TB], ps[b][:, :], pcol(l, "pscale", c, c + 1), None, ALU.mult, None,
                       [("ps", b), "pp"], [("Yr", c, tb)])

        def mixer_sconv(l, seg):
            for c in range(2):
                GC, G, T1 = c * 6400, c * 6400 + 2048, c * 6400 + 4352
                gc = arf(GC, 2048)
                gck = ark(GC, 2048)
                wgc, wgck = load_win(l, 512 + c * 128, 128)
                proj_chunk(wgc, wgck, 0, 128,
                           lambda tb, p_, pk: act(gc[:, tb * TB:(tb + 1) * TB], p_, AF.Copy, [pk], gck))
                whh, whhk = load_win(l, 768 + c * 128, 128)
                g = arf(G, 2050)
                gk = ark(G, 2050)
                cp(g[:, 0:2], cSc[:, l, c, :], [("cSc", l, c)], gk)
                proj_chunk(whh, whhk, 0, 128,
                           lambda tb, p_, pk: tt(g[:, 2 + tb * TB:2 + (tb + 1) * TB], p_, gc[:, tb * TB:(tb + 1) * TB], ALU.mult,
                                                 [pk] + gck, gk))
                cp(cSc[:, l, c, :], g[:, 2048:2050], gk, [("cSc", l, c)])
                t1 = arf(T1, 2048)
                t1k = ark(T1, 2048)
                wv = pcol(l, "scw").rearrange("p (c k) -> p c k", c=2)
                ts(t1, g[:, 0:2048], wv[:, c, 0:1], None, ALU.mult, None, gk + ["pp"], t1k)
                stt(t1, g[:, 1:2049], wv[:, c, 1:2], t1, ALU.mult, ALU.add, gk + ["pp"] + t1k, t1k)
                stt(t1, g[:, 2:2050], wv[:, c, 2:3], t1, ALU.mult, ALU.add, gk + ["pp"] + t1k, t1k)
                wgb, wgbk = load_win(l, 256 + c * 128, 128)
                proj_chunk(wgb, wgbk, 0, 128,
                           lambda tb, p_, pk: tt(Yr[:, 2 + c, tb * TB:(tb + 1) * TB], p_, t1[:, tb * TB:(tb + 1) * TB], ALU.mult,
                                                 [pk] + t1k, [("Yr", 2 + c, tb)]))

        def mixer_ssd(l, seg):
            SZ, RAW, ACC, XBC = 0, 2048, 4352, 6400
            XTOK, BTOK = 2048, 4096
            sz = arb(SZ, 2048).rearrange("p (c t) -> p c t", c=2)
            szk = ark(SZ, 2048)
            for c in range(2):
                wz, wzk = load_win(l, 1024 + c * 128, 128)
                proj_chunk(wz, wzk, 0, 128,
                           lambda tb, p_, pk: act(sz[:, c, tb * TB:(tb + 1) * TB], p_, AF.Silu, [pk], szk))
            xbc = arb(XBC, 6144).rearrange("p (c t) -> p c t", c=6)
            kxb = [ark(XBC + j * 1024, 1024) for j in range(6)]
            cw = pcol(l, "cvw").rearrange("p (c k) -> p c k", c=6)
            for j in range(6):
                wx, wxk = load_win(l, 1280 + j * 128, 128)
                raw = arf(RAW, 2051)
                rawk = ark(RAW, 2051)
                cp(raw[:, 0:3], cCv[:, l, j, :], [("cCv", l, j)], rawk)
                proj_chunk(wx, wxk, 0, 128,
                           lambda tb, p_, pk: act(raw[:, 3 + tb * TB:3 + (tb + 1) * TB], p_, AF.Copy, [pk], rawk))
                cp(cCv[:, l, j, :], raw[:, 2048:2051], rawk, [("cCv", l, j)])
                acc = arf(ACC, 2048)
                acck = ark(ACC, 2048)
                ts(acc, raw[:, 0:2048], cw[:, j, 0:1], None, ALU.mult, None, rawk + ["pp"], acck)
                for kk in range(1, 4):
                    stt(acc, raw[:, kk:kk + 2048], cw[:, j, kk:kk + 1], acc, ALU.mult, ALU.add, rawk + acck + ["pp"], acck)
                act(xbc[:, j, :], acc, AF.Silu, acck + ["pp"], kxb[j], bias=pcol(l, "cvb", j, j + 1))
            print("  ssd: before dt", P.total)
            wd, wdk = load_win(l, 2048, 4)
            for c in range(NCH):
                for k in range(KC):
                    mm(ps[6][:, c * 4:(c + 1) * 4], uT[:, k, c * 128:(c + 1) * 128], wd[:, k, 0:4], k == 0, k == KC - 1,
                       wdk + [("uT", k, c // 4)], [("ps", 6)])
            print("  ssd: before small", P.total)
            tyk = [("tiny", "all")]
            def v3(a): return tiny[:, a:a + 64].rearrange("p (c h) -> p c h", h=4)
            dt_, adt, acs, tot, eacs, dte, cd, ddte = (v3(a) for a in (0, 64, 128, 192, 256, 320, 384, 448))
            xsp = v3(512)
            ex = v3(576)
            bc4 = lambda a, b: ssdp[:, l, a:b].unsqueeze(1).to_broadcast([128, NCH, 4])
            tt(xsp, ps[6][:, 0:64].rearrange("p (c h) -> p c h", h=4), bc4(4, 8), ALU.add, [("ps", 6), ("ssdp", l)], tyk)
            ts(ex, xsp, 30.0, None, ALU.min, None, tyk, tyk)
            act(ex, ex, AF.Exp, tyk, tyk)
            act(ex, ex, AF.Ln, tyk + ["eps"], tyk, bias=epsT[:, 1:2])
            tt(dt_, ex, xsp, ALU.max, tyk, tyk)
            tt(adt, dt_, bc4(0, 4), ALU.mult, tyk + [("ssdp", l)], tyk)
            ahi = tinyb[:, 0:64]
            alo = tinyb[:, 64:128]
            tbk = [("tinyb", "a")]
            adf = tiny[:, 64:128]
            cp(ahi, adf, tyk, tbk)
            tt(tiny[:, 640:704], adf, ahi, ALU.subtract, tyk + tbk, tyk)
            cp(alo, tiny[:, 640:704], tyk, tbk)
            cp(tiny[:, 768:832], ahi, tbk, tyk)
            mm(ps[6][:, 64:128], tri, ahi, True, False, tbk + ["cm"], [("ps", 6)])
            mm(ps[6][:, 64:128], tri, alo, False, True, tbk + ["cm"], [("ps", 6)])
            mm(ps[6][:, 128:192], ones, ahi, True, False, tbk + ["cm"], [("ps", 6)])
            mm(ps[6][:, 128:192], ones, alo, False, True, tbk + ["cm"], [("ps", 6)])
            cp(tiny[:, 128:256], ps[6][:, 64:192], [("ps", 6)], tyk)
            act(tiny[:, 256:320], tiny[:, 128:192], AF.Exp, tyk, tyk)
            tt(tiny[:, 320:384], tiny[:, 192:256], tiny[:, 128:192], ALU.subtract, tyk, tyk)
            act(tiny[:, 320:384], tiny[:, 320:384], AF.Exp, tyk, tyk)
            act(tiny[:, 384:448], tiny[:, 192:256], AF.Exp, tyk, tyk)
            tt(tiny[:, 448:512], tiny[:, 0:64], tiny[:, 320:384], ALU.mult, tyk, tyk)
            ts(tiny[:, 704:768], tiny[:, 128:192], -1.0, None, ALU.mult, None, tyk, tyk)
            nacs = v3(704)
            print("  ssd: before transposes", P.total)
            xtok = arb(XTOK, 2048).rearrange("p (c f) -> p c f", c=NCH)
            btok = arb(BTOK, 2048).rearrange("p (c f) -> p c f", c=NCH)
            xtk, btk = ark(XTOK, 2048), ark(BTOK, 2048)
            ti = 0
            for c in range(NCH):
                for j in range(4):
                    o = (ti % 4) * 128
                    ti += 1
                    bnk = 4 + (ti - 1) % 4
                    pk_ = ("ps", bnk)
                    mm(ps[bnk][:, 0:128], xbc[:, j, c * 128:(c + 1) * 128], ident, True, True, kxb[j] + ["cm"], [pk_])
                    dst = (xtok if j < 2 else btok)[:, c, (j % 2) * 128:(j % 2 + 1) * 128]
                    if ti % 2 == 0:
                        cp(dst, ps[bnk][:, 0:128], [pk_], xtk if j < 2 else btk)
                    else:
                        act(dst, ps[bnk][:, 0:128], AF.Copy, [pk_], xtk if j < 2 else btk)
            WT = XBC
            E_ = arb(WT, 256).rearrange("p (h s) -> p h s", h=4)
            MT = arb(WT + 256, 256).rearrange("p (h s) -> p h s", h=4)
            RH = arb(WT + 512, 512).rearrange("p (a h s) -> p a h s", a=2, h=4)
            XDT = arb(WT + 1024, 128)
            XDE = arb(WT + 1152, 128)
            YSB = arf(WT + 1280, 256)
            YTK = arb(WT + 1536, 128)
            SBF = arb(WT + 1664, 128)
            STMP = arf(WT + 1792, 256)
            kE, kMT, kRH, kXDT, kXDE, kYSB, kYTK, kSBF, kST = (ark(WT + a, n) for a, n in
                ((0, 256), (256, 256), (512, 512), (1024, 128), (1152, 128), (1280, 256), (1536, 128), (1664, 128), (1792, 256)))
            print("  ssd: before main loop", P.total)
            Sst = cS[:, l, :]
            kS = [("cS", l)]
            PKG, RBO, PTO = 12544, 12816, 13904
            pkg = arf(PKG, 272)
            pkgk = ark(PKG, 272)
            SL = pkg[:, 0:256]
            mset(SL, 0.0, pkgk)
            for c in range(NCH):
                x3 = xtok[:, c, :].rearrange("p (h d) -> p h d", h=4)
                tt(XDE.rearrange("p (h d) -> p h d", h=4), x3, ddte[:, c, :].unsqueeze(2).to_broadcast([128, 4, 64]), ALU.mult, xtk + tyk, kXDE)
                for g in range(2):
                    mm(ps[5][:, g * 128:(g + 1) * 128], btok[:, c, g * 128:(g + 1) * 128], XDE[:, g * 128:(g + 1) * 128], True, True,
                       btk + kXDE, [("ps", 5)])
                tt(STMP.rearrange("p (h d) -> p h d", h=4), SL.rearrange("p (h d) -> p h d", h=4),
                   cd[:, c, :].unsqueeze(2).to_broadcast([128, 4, 64]), ALU.mult, pkgk + tyk, kST)
                tt(SL, STMP, ps[5][:, 0:256], ALU.add, kST + [("ps", 5)], pkgk)
            tr_ = tiny[:, 832:864].rearrange("p (c h) -> p c h", h=4)
            tt(tr_, tot[:, 0:8, :], tot[:, 8:16, :], ALU.add, tyk, tyk)
            tt(tr_[:, 0:4, :], tr_[:, 0:4, :], tr_[:, 4:8, :], ALU.add, tyk, tyk)
            tt(tr_[:, 0:2, :], tr_[:, 0:2, :], tr_[:, 2:4, :], ALU.add, tyk, tyk)
            tt(tr_[:, 0:1, :], tr_[:, 0:1, :], tr_[:, 1:2, :], ALU.add, tyk, tyk)
            act(pkg[:, 256:260], tiny[:, 832:836], AF.Exp, tyk, pkgk)
            dma("sp", ccd[("s", l, "i")][:, 0:260], pkg[:, 0:260], pkgk, [("cci", "s", l)])
            allgather("s", l, [("cci", "s", l)], [("cco", "s", l)])
            RBs = arf(RBO, 1088).rearrange("p (r f) -> p r f", r=4)
            rbsk = ark(RBO, 1088)
            dma("sp", RBs, ccd[("s", l, "o")].rearrange("(r p) f -> p r f", p=128), [("cco", "s", l)], rbsk)
            PT = arf(PTO, 256)
            ptk = ark(PTO, 256)
            sel = pcol(l, "sel")
            mset(Sst, 0.0, kS)
            mset(PT, 0.0, ptk)
            for j in range(4):
                stt(Sst, PT, sel[:, 4 + j:5 + j], Sst, ALU.mult, ALU.add, ptk + kS + ["pp"], kS)
                if j < 3:
                    tt(PT.rearrange("p (h d) -> p h d", h=4), PT.rearrange("p (h d) -> p h d", h=4),
                       RBs[:, j, 256:260].unsqueeze(2).to_broadcast([128, 4, 64]), ALU.mult, ptk + rbsk, ptk)
                    tt(PT, PT, RBs[:, j, 0:256], ALU.add, ptk + rbsk, ptk)
            cp(SBF, Sst, kS, kSBF)
            for c in range(NCH):
                tsl = slice(c * 128, (c + 1) * 128)
                if c < 2:
                    print("  ssd: chunk", c, P.total)
                for h in range(4):
                    o = ps[0][:, h * 128:(h + 1) * 128]
                    mm(o, tinyb[:, c * 4 + h:c * 4 + h + 1].to_broadcast([128, 128]), tri, True, False, tbk + ["cm"], [("ps", 0)])
                    mm(o, tinyb[:, 64 + c * 4 + h:64 + c * 4 + h + 1].to_broadcast([128, 128]), tri, False, False, tbk + ["cm"], [("ps", 0)])
                    mm(o, ident, negm, False, True, ["cm"], [("ps", 0)])
                for h in range(4):
                    act(E_[:, h, :], ps[0][:, h * 128:(h + 1) * 128], AF.Exp, [("ps", 0)] + tyk, kE, bias=nacs[:, c, h:h + 1])
                for g in range(2):
                    mm(ps[1][:, g * 128:(g + 1) * 128], xbc[:, 2 + g, tsl], xbc[:, 4 + g, tsl], True, True,
                       kxb[2 + g] + kxb[4 + g], [("ps", 1)])
                for h in range(4):
                    g = h // 2
                    tt(MT[:, h, :], ps[1][:, g * 128:(g + 1) * 128], E_[:, h, :], ALU.mult, [("ps", 1)] + kE, kMT)
                x3 = xtok[:, c, :].rearrange("p (h d) -> p h d", h=4)
                tt(XDT.rearrange("p (h d) -> p h d", h=4), x3, dt_[:, c, :].unsqueeze(2).to_broadcast([128, 4, 64]), ALU.mult, xtk + tyk, kXDT, eng="pool")
                tt(XDE.rearrange("p (h d) -> p h d", h=4), x3, ddte[:, c, :].unsqueeze(2).to_broadcast([128, 4, 64]), ALU.mult, xtk + tyk, kXDE, eng="pool")
                for h in range(4):
                    o = ps[2][:, h * 64:(h + 1) * 64]
                    mm(o, MT[:, h, :], XDT[:, h * 64:(h + 1) * 64], True, True, kMT + kXDT, [("ps", 2, "y")])
                for h in range(4):
                    g = h // 2
                    mm(ps[3][:, h * 64:(h + 1) * 64], xbc[:, 4 + g, tsl], SBF[:, h * 64:(h + 1) * 64], True, True,
                       kxb[4 + g] + kSBF, [("ps", 3)])
                tt(YSB.rearrange("p (h d) -> p h d", h=4), x3, ssdp[:, l, 8:12].unsqueeze(2).to_broadcast([128, 4, 64]), ALU.mult,
                   xtk + [("ssdp", l)], kYSB)
                tt(YSB, YSB, ps[2][:, 0:256], ALU.add, kYSB + [("ps", 2, "y")], kYSB)
                tt(STMP.rearrange("p (h d) -> p h d", h=4), ps[3][:, 0:256].rearrange("p (h d) -> p h d", h=4),
                   eacs[:, c, :].unsqueeze(2).to_broadcast([128, 4, 64]), ALU.mult, [("ps", 3)] + tyk, kST)
                tt(YTK, STMP, YSB, ALU.add, kST + kYSB, kYTK)
                for j in range(2):
                    o = j * 128
                    mm(ps[4][:, o:o + 128], YTK[:, j * 128:(j + 1) * 128], ident, True, True, kYTK + ["cm"], [("ps", 4, o)])
                    tt(Yr[:, 4 + j, tsl], ps[4][:, o:o + 128], sz[:, j, tsl], ALU.mult, [("ps", 4, o)] + szk, [("Yr", 4 + j, c // 4)])
                for g in range(2):
                    mm(ps[5][:, g * 128:(g + 1) * 128], btok[:, c, g * 128:(g + 1) * 128], XDE[:, g * 128:(g + 1) * 128], True, True,
                       btk + kXDE, [("ps", 5)])
                tt(STMP.rearrange("p (h d) -> p h d", h=4), Sst.rearrange("p (h d) -> p h d", h=4),
                   cd[:, c, :].unsqueeze(2).to_broadcast([128, 4, 64]), ALU.mult, kS + tyk, kST)
                tt(Sst, STMP, ps[5][:, 0:256], ALU.add, kST + [("ps", 5)], kS)
                act(SBF, Sst, AF.Copy, kS, kSBF)

        def mixer_s5(l, seg):
            U, STG, BT, W, WB, PAT, INJ, TAB = 0, 2048, 4096, 6144, 8192, 10240, 11264, 13312
            tyk = [("tiny", "all")]
            pk = [("s5pw", l)]
            lk = [("s5lv", l)]
            u = arb(U, 2048).rearrange("p (c t) -> p c t", c=2)
            uk = ark(U, 2048)
            for c in range(2):
                wu, wuk = load_win(l, 2052 + c * 128, 128)
                proj_chunk(wu, wuk, 0, 128,
                           lambda tb, p_, pk_: act(u[:, c, tb * TB:(tb + 1) * TB], p_, AF.Copy, [pk_], uk))
            pat = arb(PAT, 1024)
            patk = ark(PAT, 1024)
            mset(pat, 1.0, patk)
            mset(pat.rearrange("p (c j) -> p c j", j=16)[:, :, 0:1], 0.0, patk)
            pw0 = tiny[:, 0:256].rearrange("p (r a j) -> p r a j", r=8, a=2)
            qq = tiny[:, 256:512].rearrange("p (r a j) -> p r a j", r=8, a=2)
            den = tiny[:, 512:640].rearrange("p (r j) -> p r j", r=8)
            tmp = tiny[:, 640:768].rearrange("p (r j) -> p r j", r=8)
            mset(pw0[:, :, 0, 0:1], 1.0, tyk)
            mset(pw0[:, :, 1, 0:1], 0.0, tyk)
            for a in range(2):
                cp(pw0[:, :, a, 1:16], s5pw[:, l, :, a, 0:15], pk, tyk)
            tt(den, pw0[:, :, 0, :], pw0[:, :, 0, :], ALU.mult, tyk, tyk)
            tt(tmp, pw0[:, :, 1, :], pw0[:, :, 1, :], ALU.mult, tyk, tyk)
            tt(den, den, tmp, ALU.add, tyk, tyk)
            P.op("dve", lambda e: e.reciprocal(out=den, in_=den), tyk, tyk)
            tt(qq[:, :, 0, :], pw0[:, :, 0, :], den, ALU.mult, tyk, tyk)
            tt(qq[:, :, 1, :], pw0[:, :, 1, :], den, ALU.mult, tyk, tyk)
            ts(qq[:, :, 1, :], qq[:, :, 1, :], -1.0, None, ALU.mult, None, tyk, tyk)
            bpad = arf(TAB, 512).rearrange("p (r a h) -> p r a h", r=8, a=2)
            cpad = arf(TAB + 512, 512).rearrange("p (r a h) -> p r a h", r=8, a=2)
            tabk = ark(TAB, 1024)
            mset(arf(TAB, 512), 0.0, tabk)
            for a in range(2):
                cp(bpad[0:64, :, a, 0:16], bbS[0:64, l, a, :, :], [("bb", l)], tabk)
                cp(bpad[64:128, :, a, 16:32], bbS[64:128, l, a, :, :], [("bb", l)], tabk)
            dma("sp", arf(TAB + 512, 512), s5c[l], [], tabk)
            Ef = Yr[:, 6:8, :].rearrange("p a t -> p (a t)").bitcast(F32)
            Eall = Ef.rearrange("p (r a c) -> p r a c", r=8, a=2)
            ek = [("Yr", 6 + a, tb) for a in range(2) for tb in range(NTB)]
            bufA = arf(W, 2048).rearrange("p (r a c) -> p r a c", r=8, a=2)
            bufB = arf(WB, 2048).rearrange("p (r a c) -> p r a c", r=8, a=2)
            kA, kB = ark(W, 2048), ark(WB, 2048)
            T1 = arf(STG, 2048)
            T2 = arf(BT, 2048)
            kT1, kT2 = ark(STG, 2048), ark(BT, 2048)

            TOFF = [0, 64, 96, 112, 120, 124, 126]

            def bc8(ap, n):
                return ap.unsqueeze(2).to_broadcast([128, 8, n])

            def lvl2(src, sk):
                cur, ck_ = src, sk
                nxt_list = [(bufA, kA), (bufB, kB)]
                if src is bufA:
                    nxt_list = [(bufB, kB), (bufA, kA)]
                for lev in range(7):
                    d = 1 << lev
                    n = 128 - d
                    dst, dk = nxt_list[lev % 2]
                    cr, ci = s5lv[:, l, :, 0, lev], s5lv[:, l, :, 1, lev]
                    t1 = T1[:, 0:8 * n].rearrange("p (r c) -> p r c", r=8)
                    t2 = T2[:, 0:8 * n].rearrange("p (r c) -> p r c", r=8)
                    cp(dst[:, :, :, 0:d], cur[:, :, :, 0:d], ck_, dk)
                    tt(t1, cur[:, :, 0, 0:n], bc8(cr, n), ALU.mult, ck_ + lk, kT1)
                    tt(t2, cur[:, :, 1, 0:n], bc8(ci, n), ALU.mult, ck_ + lk, kT2)
                    tt(t1, t1, t2, ALU.subtract, kT1 + kT2, kT1)
                    tt(dst[:, :, 0, d:128], cur[:, :, 0, d:128], t1, ALU.add, ck_ + kT1, dk)
                    tt(t1, cur[:, :, 1, 0:n], bc8(cr, n), ALU.mult, ck_ + lk, kT1)
                    tt(t2, cur[:, :, 0, 0:n], bc8(ci, n), ALU.mult, ck_ + lk, kT2)
                    tt(t1, t1, t2, ALU.add, kT1 + kT2, kT1)
                    tt(dst[:, :, 1, d:128], cur[:, :, 1, d:128], t1, ALU.add, ck_ + kT1, dk)
                    cur, ck_ = dst, dk
                return cur, ck_

            def run_pass(mode):
                inj = arf(INJ, 2048).rearrange("p (r a c) -> p r a c", r=8, a=2)
                injk = ark(INJ, 2048)
                if mode == "full":
                    xr_, xi_ = cX[:, l, :, 0], cX[:, l, :, 1]
                    ckx = [("cX", l, r) for r in range(8)]
                    PA = arf(STG, 2048).rearrange("p (r a c) -> p r a c", r=8, a=2)
                    PB_ = bufA
                    kPA, kPB = ark(STG, 2048), kA
                    tq1 = T2[:, 0:512].rearrange("p (r c) -> p r c", r=8)
                    tq2 = T2[:, 512:1024].rearrange("p (r c) -> p r c", r=8)
                    kq = ark(BT, 1024)
                    cp(PA[:, :, 0, 0], xr_, ckx, kPA)
                    cp(PA[:, :, 1, 0], xi_, ckx, kPA)
                    curP, kcur, nxtP, knxt = PA, kPA, PB_, kPB
                    for lev in range(6, -1, -1):
                        n = 64 >> lev
                        cr, ci = s5lv[:, l, :, 0, lev], s5lv[:, l, :, 1, lev]
                        if lev > 0:
                            usrc, uk_, uo = bufB, kB, TOFF[lev - 1]
                        else:
                            usrc, uk_, uo = Eall, ek, 0
                        uev = [usrc[:, :, a_, uo:uo + 2 * n].rearrange("p r (c two) -> p r c two", two=2)[:, :, :, 0] for a_ in range(2)]
                        oev = [nxtP[:, :, a_, 0:2 * n].rearrange("p r (c two) -> p r c two", two=2)[:, :, :, 0] for a_ in range(2)]
                        ood = [nxtP[:, :, a_, 0:2 * n].rearrange("p r (c two) -> p r c two", two=2)[:, :, :, 1] for a_ in range(2)]
                        pr_, pi_ = curP[:, :, 0, 0:n], curP[:, :, 1, 0:n]
                        t1 = tq1[:, :, 0:n]
                        t2 = tq2[:, :, 0:n]
                        for a_ in range(2):
                            cp(oev[a_], curP[:, :, a_, 0:n], kcur, knxt)
                        tt(t1, pr_, bc8(cr, n), ALU.mult, kcur + lk, kq)
                        tt(t2, pi_, bc8(ci, n), ALU.mult, kcur + lk, kq)
                        tt(t1, t1, t2, ALU.subtract, kq, kq)
                        tt(ood[0], t1, uev[0], ALU.add, kq + uk_, knxt)
                        tt(t1, pi_, bc8(cr, n), ALU.mult, kcur + lk, kq)
                        tt(t2, pr_, bc8(ci, n), ALU.mult, kcur + lk, kq)
                        tt(t1, t1, t2, ALU.add, kq, kq)
                        tt(ood[1], t1, uev[1], ALU.add, kq + uk_, knxt)
                        curP, kcur, nxtP, knxt = nxtP, knxt, curP, kcur
                    z0, z0k = curP, kcur
                    lr8, li8 = s5pw[:, l, :, 0, 0], s5pw[:, l, :, 1, 0]
                    n = 128
                    t1 = T2[:, 0:1024].rearrange("p (r c) -> p r c", r=8)
                    t2 = T2[:, 1024:2048].rearrange("p (r c) -> p r c", r=8)
                    tt(t1, z0[:, :, 0, :], bc8(lr8, n), ALU.mult, z0k + pk, kT2)
                    tt(t2, z0[:, :, 1, :], bc8(li8, n), ALU.mult, z0k + pk, kT2)
                    tt(inj[:, :, 0, :], t1, t2, ALU.subtract, kT2, injk)
                    tt(t1, z0[:, :, 1, :], bc8(lr8, n), ALU.mult, z0k + pk, kT2)
                    tt(t2, z0[:, :, 0, :], bc8(li8, n), ALU.mult, z0k + pk, kT2)
                    tt(inj[:, :, 1, :], t1, t2, ALU.add, kT2, injk)

                stg = arb(STG, 2048).rearrange("p (a j c) -> p a j c", a=2, j=16)
                stgk = ark(STG, 2048)
                btv = arb(BT, 2048).rearrange("p (a j c) -> p a j c", a=2, j=16)
                btk_ = ark(BT, 2048)
                Wn = arf(W, 2048)
                wk_ = ark(W, 2048)
                Wv = Wn.rearrange("p (c j) -> p j c", j=16)
                wb = arb(WB, 2048).rearrange("p (a t) -> p a t", a=2)
                wbk = ark(WB, 2048)

                def b16(ap):
                    return ap.unsqueeze(2).to_broadcast([128, 16, 32])

                def h32(ap):
                    return ap.unsqueeze(1).to_broadcast([128, 16, 32])

                ct = Yr[:, 4:6, :].rearrange("p a (j c) -> p a j c", j=16)
                ctk = [("Yr", 4 + a_, t_) for a_ in range(2) for t_ in range(NTB)]
                if mode == "p1":
                    tmp_f, tmpk = arf(WB, 1024), ark(WB, 1024)
                    W1n, w1k_ = arf(INJ, 2048), ark(INJ, 2048)
                else:
                    tmp_f = Yr[:, 7, :].bitcast(F32)
                    tmpk = [("Yr", 7, t_) for t_ in range(NTB)]
                    W1n = Yr[:, 0:2, :].rearrange("p a t -> p (a t)").bitcast(F32)
                    w1k_ = [("Yr", a_, t_) for a_ in range(2) for t_ in range(NTB)]
                Wbufs = [(Wn, wk_), (W1n, w1k_)]
                t1 = tmp_f[:, 0:512].rearrange("p (j h) -> p j h", j=16)
                t2 = tmp_f[:, 512:1024].rearrange("p (j h) -> p j h", j=16)
                mset(arb(STG, 2048), 0.0, stgk)
                if mode == "full":
                    mset(Yr[:, 4:6, :], 0.0, ctk)

                ptf = Yr[:, 2, :].bitcast(F32)
                ptk = [("Yr", 2, t_) for t_ in range(NTB)]
                p1_ = ptf[:, 0:512].rearrange("p (j h) -> p j h", j=16)
                p2_ = ptf[:, 512:1024].rearrange("p (j h) -> p j h", j=16)

                def tables_b_dve(r):
                    q = r % 4
                    if r > 0:
                        pq = (r - 1) % 4
                        mset(stg[:, :, :, pq * 32:(pq + 1) * 32], 0.0, stgk, eng="pool")
                    Bre, Bim = h32(bpad[:, r, 0, :]), h32(bpad[:, r, 1, :])
                    qr_, qi_ = b16(qq[:, r, 0, :]), b16(qq[:, r, 1, :])
                    sv = stg[:, :, :, q * 32:(q + 1) * 32]
                    tt(p1_, Bre, qr_, ALU.mult, tabk + tyk, ptk, eng="pool")
                    tt(p2_, Bim, qi_, ALU.mult, tabk + tyk, ptk, eng="pool")
                    tt(sv[:, 0], p1_, p2_, ALU.subtract, ptk, stgk, eng="pool")
                    tt(p1_, Bim, qr_, ALU.mult, tabk + tyk, ptk, eng="pool")
                    tt(p2_, Bre, qi_, ALU.mult, tabk + tyk, ptk, eng="pool")
                    tt(sv[:, 1], p1_, p2_, ALU.add, ptk, stgk, eng="pool")

                def tables_b_pe(r):
                    for a in range(2):
                        for lg in range(4):
                            bnk = 4 + lg
                            for li_ in range(4):
                                mm(ps[bnk][:, li_ * 128:(li_ + 1) * 128], stg[:, a, lg * 4 + li_, :], ident, True, True,
                                   stgk + ["cm"], [("ps", bnk)])
                            act(btv[:, a, lg * 4:(lg + 1) * 4, :], ps[bnk][:, :].rearrange("p (j c) -> p j c", j=4), AF.Copy,
                                [("ps", bnk)], btk_)

                def ct_dve(r):
                    q = r % 4
                    if r > 0:
                        pq = (r - 1) % 4
                        mset(ct[:, :, :, pq * 32:(pq + 1) * 32], 0.0, ctk)
                    Cre, Cim = h32(cpad[:, r, 0, :]), h32(cpad[:, r, 1, :])
                    pr_, pi_ = b16(pw0[:, r, 0, :]), b16(pw0[:, r, 1, :])
                    cv = ct[:, :, :, q * 32:(q + 1) * 32]
                    tt(t1, Cre, pr_, ALU.mult, tabk + tyk, tmpk)
                    tt(t2, Cim, pi_, ALU.mult, tabk + tyk, tmpk)
                    tt(cv[:, 0], t1, t2, ALU.add, tmpk, ctk)
                    tt(t1, Cim, pr_, ALU.mult, tabk + tyk, tmpk)
                    tt(t2, Cre, pi_, ALU.mult, tabk + tyk, tmpk)
                    tt(cv[:, 1], t1, t2, ALU.subtract, tmpk, ctk)

                def bu_mm(r, a, wv_, wkk):
                    uv = u[:, r // 4, :].rearrange("p (c j) -> p j c", j=16)
                    for lg in range(4):
                        bnk = 4 + lg
                        for li_ in range(4):
                            j = lg * 4 + li_
                            mm(ps[bnk][:, li_ * 128:(li_ + 1) * 128], btv[:, a, j, :], uv[:, j, :], True, True, btk_ + uk, [("ps", bnk)])
                        act(wv_[:, lg * 4:(lg + 1) * 4, :], ps[bnk][:, :].rearrange("p (j c) -> p j c", j=4), AF.Copy, [("ps", bnk)], wkk)

                def scan(wn_, wkk):
                    P.op("dve", lambda e, wn_=wn_: e.tensor_tensor_scan(out=wn_, data0=pat, data1=wn_, initial=0.0, op0=ALU.mult, op1=ALU.add),
                         wkk + patk, wkk)

                tables_b_dve(0)
                tables_b_pe(0)
                for r in range(8):
                    oc = r // 4
                    q = r % 4
                    uv = u[:, oc, :].rearrange("p (c j) -> p j c", j=16)
                    wvs = [(wn_.rearrange("p (c j) -> p j c", j=16), wn_, kk_) for (wn_, kk_) in Wbufs]
                    if mode == "p1":
                        bu_mm(r, 0, wvs[0][0], wvs[0][2])
                        bu_mm(r, 1, wvs[1][0], wvs[1][2])
                        if r + 1 < 8:
                            tables_b_dve(r + 1)
                        for a in range(2):
                            scan(wvs[a][1], wvs[a][2])
                            cp(Eall[:, r, a, :], wvs[a][0][:, 15, :], wvs[a][2], ek)
                        if r + 1 < 8:
                            tables_b_pe(r + 1)
                        continue
                    bu_mm(r, 0, wvs[0][0], wvs[0][2])
                    bu_mm(r, 1, wvs[1][0], wvs[1][2])
                    ct_dve(r)
                    if r + 1 < 8:
                        tables_b_dve(r + 1)
                    for a in range(2):
                        wv_, wn_, wkk = wvs[a]
                        tt(wv_[:, 0, :], wv_[:, 0, :], inj[:, r, a, :], ALU.add, wkk + injk, wkk)
                        scan(wn_, wkk)
                        act(wb[:, a, :], wn_, AF.Copy, wkk, wbk)
                    if r + 1 < 8:
                        tables_b_pe(r + 1)
                    wbv = [wb[:, a, :].rearrange("p (c j) -> p j c", j=16) for a in range(2)]
                    for j in range(16):
                        o = ps[j // 4][:, (j % 4) * 128:(j % 4 + 1) * 128]
                        P.op("pe", lambda e, o=o, j=j, q=q, wbv=wbv: e.matmul(o, lhsT=ct[:, 0, j, :], rhs=wbv[0][:, j, :],
                                                                             start=(q == 0 and j % 4 == 0), stop=False, skip_group_check=True),
                             ctk + wbk, [("ps", j // 4)])
                        P.op("pe", lambda e, o=o, j=j, q=q, wbv=wbv: e.matmul(o, lhsT=ct[:, 1, j, :], rhs=wbv[1][:, j, :], start=False,
                                                                             stop=(q == 3), skip_group_check=True), ctk + wbk, [("ps", j // 4)])
                    if q == 3:
                        for tb in range(NTB):
                            yo = W + (tb % 2) * 512
                            yt = arf(yo, 512)
                            ytk = ark(yo, 512)
                            uv4 = uv[:, tb * 4:(tb + 1) * 4, :]
                            stt(yt.rearrange("p (j c) -> p j c", j=4), uv4, pcol(l, "s5d", oc, oc + 1),
                                ps[tb][:, :].rearrange("p (j c) -> p j c", j=4), ALU.mult, ALU.add, uk + ["pp", ("ps", tb)], ytk)
                            act(Yr[:, 6 + oc, :].rearrange("p (c j) -> p j c", j=16)[:, tb * 4:(tb + 1) * 4, :],
                                yt.rearrange("p (j c) -> p j c", j=4), AF.Gelu_apprx_tanh, ytk, [("Yr", 6 + oc, t_) for t_ in range(NTB)])
                if mode == "p1":
                    p15r, p15i = s5pw[:, l, :, 0, 14], s5pw[:, l, :, 1, 14]
                    n = 128
                    t1 = T1[:, 0:1024].rearrange("p (r c) -> p r c", r=8)
                    t2 = T1[:, 1024:2048].rearrange("p (r c) -> p r c", r=8)
                    t3 = T2[:, 0:1024].rearrange("p (r c) -> p r c", r=8)
                    t4 = T2[:, 1024:2048].rearrange("p (r c) -> p r c", r=8)
                    tt(t1, Eall[:, :, 0, :], bc8(p15r, n), ALU.mult, ek + pk, kT1)
                    tt(t2, Eall[:, :, 1, :], bc8(p15i, n), ALU.mult, ek + pk, kT1)
                    tt(t3, Eall[:, :, 1, :], bc8(p15r, n), ALU.mult, ek + pk, kT2)
                    tt(t4, Eall[:, :, 0, :], bc8(p15i, n), ALU.mult, ek + pk, kT2)
                    tt(Eall[:, :, 0, :], t1, t2, ALU.subtract, kT1, ek)
                    tt(Eall[:, :, 1, :], t3, t4, ALU.add, kT2, ek)
                    cur, ck_, co = Eall, ek, 0
                    for lev in range(7):
                        n = 64 >> lev
                        dst, dk = bufB[:, :, :, TOFF[lev]:TOFF[lev] + n], kB
                        cr, ci = s5lv[:, l, :, 0, lev], s5lv[:, l, :, 1, lev]
                        ev = [cur[:, :, a_, co:co + 2 * n].rearrange("p r (c two) -> p r c two", two=2)[:, :, :, 0] for a_ in range(2)]
                        od = [cur[:, :, a_, co:co + 2 * n].rearrange("p r (c two) -> p r c two", two=2)[:, :, :, 1] for a_ in range(2)]
                        t1 = T1[:, 0:8 * n].rearrange("p (r c) -> p r c", r=8)
                        t2 = T2[:, 0:8 * n].rearrange("p (r c) -> p r c", r=8)
                        tt(t1, ev[0], bc8(cr, n), ALU.mult, ck_ + lk, kT1)
                        tt(t2, ev[1], bc8(ci, n), ALU.mult, ck_ + lk, kT2)
                        tt(t1, t1, t2, ALU.subtract, kT1 + kT2, kT1)
                        tt(dst[:, :, 0, :], t1, od[0], ALU.add, kT1 + ck_, dk)
                        tt(t1, ev[1], bc8(cr, n), ALU.mult, ck_ + lk, kT1)
                        tt(t2, ev[0], bc8(ci, n), ALU.mult, ck_ + lk, kT2)
                        tt(t1, t1, t2, ALU.add, kT1 + kT2, kT1)
                        tt(dst[:, :, 1, :], t1, od[1], ALU.add, kT1 + ck_, dk)
                        cur, ck_, co = bufB, kB, TOFF[lev]
                    cp(tiny[:, 800:816].rearrange("p (r a) -> p r a", a=2), bufB[:, :, :, TOFF[6]], kB, tyk)
                    dma("sp", ccd[("x", l, "i")], tiny[:, 800:816], tyk, [("cci", "x", l)])
                    allgather("x", l, [("cci", "x", l)], [("cco", "x", l)])
                    return

            run_pass("p1")
            s5_combine(l)
            run_pass("full")
            SG = W
            for tb in range(NTB):
                sgs = []
                for m in range(2):
                    b = next_acc()
                    for k in range(2):
                        mm(ps[b][:, :], gluwS[:, l, k, m * 128:(m + 1) * 128], Yr[:, 6 + k, tb * TB:(tb + 1) * TB], k == 0, k == 1,
                           [("Yr", 6 + k, tb), "gluw"], [("ps", b)])
                    so = SG + ((tb * 2 + m) % 4) * 256
                    sg = arb(so, 256)
                    act(sg, ps[b][:, :], AF.Sigmoid, [("ps", b), "pp"], ark(so, 256), bias=pcol(l, "glub", m, m + 1))
                    sgs.append((sg, ark(so, 256)))
                for m in range(2):
                    sg, sgk = sgs[m]
                    tt(Yr[:, 6 + m, tb * TB:(tb + 1) * TB], Yr[:, 6 + m, tb * TB:(tb + 1) * TB], sg, ALU.mult,
                       [("Yr", 6 + m, tb)] + sgk, [("Yr", 6 + m, tb)])

        W1O = [7168, 9216]
        W2O = [11264, 13312]
        NG = 8

        def mlp_load(l, g):
            sl = g % 2
            w1g = arb(W1O[sl], 2048).rearrange("p (k c) -> p k c", k=8)
            w2g = arb(W2O[sl], 2048).rearrange("p (k c) -> p k c", k=4)
            w1k, w2k = ark(W1O[sl], 2048), ark(W2O[sl], 2048)
            dma("pool", w1g, w1[l, :, g * 512:(g + 1) * 512].rearrange("(k p) c -> p k c", p=128), [], w1k)
            dma("pool", w2g, w2[l, g * 512:(g + 1) * 512, :].rearrange("(k p) c -> p k c", p=128), [], w2k)

        def out_proj(l):
            WO, RSTD, SQ = 0, 4096, 6144
            wo = arb(WO, 4096).rearrange("p (k c) -> p k c", k=8)
            wok = ark(WO, 4096)
            dma("pool", wo, w_out[l, :, :].rearrange("(k p) c -> p k c", p=128), [], wok)
            mlp_load(l, 0)
            mlp_load(l, 1)
            for g in range(4):
                rms_stats(lambda k, tb: Yr[:, 2 * g + k, tb * TB:(tb + 1) * TB], 2, 256.0, RSTD, SQ,
                          lambda k, tb: [("Yr", 2 * g + k, tb)])
                for k in range(2):
                    ch = 2 * g + k
                    for tb in range(NTB):
                        stt(uT[:, ch, tb * TB:(tb + 1) * TB], Yr[:, ch, tb * TB:(tb + 1) * TB], pcol(l, "bnw", ch, ch + 1),
                            arf(RSTD + tb * TB, TB), ALU.mult, ALU.mult, [("Yr", ch, tb), "pp"] + ark(RSTD + tb * TB, TB), [("uT", ch, tb)])
            for m in range(KC):
                for tb in range(NTB):
                    b = next_acc()
                    for k in range(KC):
                        mm(ps[b][:, :], wo[:, k, m * 128:(m + 1) * 128], uT[:, k, tb * TB:(tb + 1) * TB], k == 0, k == KC - 1,
                           wok + [("uT", k, tb)], [("ps", b)])
                    stt(hT[:, m, tb * TB:(tb + 1) * TB], ps[b][:, :], G1(l, m), hT[:, m, tb * TB:(tb + 1) * TB], ALU.mult, ALU.add,
                        [("ps", b), ("der", l), ("hT", m, tb)], [("hT", m, tb)])

        def mlp(l):
            norm_mod(l, S2, B2)
            RS = 5120

            def up(g):
                sl = g % 2
                w1g = arb(W1O[sl], 2048).rearrange("p (k c) -> p k c", k=8)
                w1k = ark(W1O[sl], 2048)
                for jc in range(4):
                    yc = sl * 4 + jc
                    for tb in range(NTB):
                        b = next_acc()
                        for k in range(KC):
                            mm(ps[b][:, :], w1g[:, k, jc * 128:(jc + 1) * 128], uT[:, k, tb * TB:(tb + 1) * TB], k == 0, k == KC - 1,
                               w1k + [("uT", k, tb)], [("ps", b)])
                        ro = RS + ((jc * NTB + tb) % 4) * 256
                        rs = arb(ro, 256)
                        act(rs, ps[b][:, :], AF.Relu, [("ps", b)], ark(ro, 256))
                        tt(Yr[:, yc, tb * TB:(tb + 1) * TB], rs, rs, ALU.mult, ark(ro, 256), [("Yr", yc, tb)], eng="pool")

            def down(g):
                sl = g % 2
                w2g = arb(W2O[sl], 2048).rearrange("p (k c) -> p k c", k=4)
                w2k = ark(W2O[sl], 2048)
                for m in range(KC):
                    for tb in range(NTB):
                        b = next_acc()
                        for jc in range(4):
                            mm(ps[b][:, :], w2g[:, jc, m * 128:(m + 1) * 128], Yr[:, sl * 4 + jc, tb * TB:(tb + 1) * TB], jc == 0, jc == 3,
                               w2k + [("Yr", sl * 4 + jc, tb)], [("ps", b)])
                        stt(hT[:, m, tb * TB:(tb + 1) * TB], ps[b][:, :], G2(l, m), hT[:, m, tb * TB:(tb + 1) * TB], ALU.mult, ALU.add,
                            [("ps", b), ("der", l), ("hT", m, tb)], [("hT", m, tb)])

            up(0)
            for g in range(NG):
                if g + 1 < NG:
                    up(g + 1)
                down(g)
                if g + 2 < NG:
                    mlp_load(l, g + 2)

        def final_out(seg):
            RSTD, SQ, OUT = 0, 2048, 4096
            rms_stats(lambda k, tb: hT[:, k, tb * TB:(tb + 1) * TB], KC, float(DM), RSTD, SQ, lambda k, tb: [("hT", k, tb)])
            for k in range(KC):
                oo = OUT + (k % 2) * 2048
                o = arf(oo, 2048)
                for tb in range(NTB):
                    stt(o[:, tb * TB:(tb + 1) * TB], hT[:, k, tb * TB:(tb + 1) * TB], pcol(0, "fnw", k, k + 1), arf(RSTD + tb * TB, TB),
                        ALU.mult, ALU.mult, [("hT", k, tb), "pp"] + ark(RSTD + tb * TB, TB), ark(oo, 2048))
                dma("sp", yT[k * 128:(k + 1) * 128, :], o, ark(oo, 2048), [("yT", k, seg)])

        stopped = False
        for seg in range(nseg):
            if stopped:
                break
            for k in range(KC):
                dma("sp", hT[:, k, :], xT[k * 128:(k + 1) * 128, :], [], [("hT", k, tb) for tb in range(NTB)])
            for l in range(depth):
                norm_mod(l, S1, B1)
                if stop_after == (seg, l, "u"):
                    stopped = True
                    break
                print("ops before tail", P.total)
                tail_prepass(l)
                print("ops before s5 p1", P.total)
                mixer_s5(l, seg)
                print("ops before halo", P.total)
                halo_apply(l)
                mixer_pool(l, seg)
                mixer_sconv(l, seg)
                print("ops before ssd", P.total)
                mixer_ssd(l, seg)
                print("ops after ssd", P.total)
                if stop_after == (seg, l, "mix"):
                    stopped = True
                    break
                out_proj(l)
                if stop_after == (seg, l, "hmix"):
                    stopped = True
                    break
                mlp(l)
                if stop_after == (seg, l, "h"):
                    stopped = True
                    break
            if not stopped:
                final_out(seg)
        print("total ops recorded:", P.total)
        P.limit = None
        if dbg:
            if "uT" in dbg_out:
                cp(AR[:, 0:2048], uT[:, 0, :], all_keys("uT"), ark(0, 2048))
            for name in dbg_out:
                if name == "Yr":
                    for k in range(KC):
                        o = arf((k % 2) * 2048, 2048)
                        cp(o, Yr[:, k, :], [("Yr", k, tb) for tb in range(NTB)], ark((k % 2) * 2048, 2048))
                        dma("sp", dbg_out[name][k * 128:(k + 1) * 128, :], o, ark((k % 2) * 2048, 2048), [("dbg", name, k)])
                elif name == "uT":
                    for k in range(KC):
                        o = arf((k % 2) * 2048, 2048)
                        cp(o, uT[:, k, :], [("uT", k, tb) for tb in range(NTB)], ark((k % 2) * 2048, 2048))
                        dma("sp", dbg_out[name][k * 128:(k + 1) * 128, :], o, ark((k % 2) * 2048, 2048), [("dbg", name, k)])
                elif name == "hT":
                    for k in range(KC):
                        dma("sp", dbg_out[name][k * 128:(k + 1) * 128, :], hT[:, k, :], [("hT", k, tb) for tb in range(NTB)], [("dbg", name, k)])
                elif name == "mod":
                    dma("sp", dbg_out[name], modT[:, :, :].rearrange("p l c -> p (l c)"), [("mod", 0), ("mod", 1)], [("dbg", name)])
        P.wait_all("sp")

        with nc.Block() as block:
            def replay(e, name):
                for waits, fn, inc in P.q[name]:
                    for s, v in waits:
                        e.wait_ge(sems[s], v)
                    if fn is not None:
                        fn(e).then_inc(sems[inc[0]], inc[1])

            @block.tensor
            def _(e):
                replay(e, "pe")

            @block.scalar
            def _(e):
                replay(e, "act")

            @block.vector
            def _(e):
                replay(e, "dve")

            @block.gpsimd
            def _(e):
                replay(e, "pool")

            @block.sync
            def _(e):
                replay(e, "sp")
    return nc


def _fm(v):
    return np.ascontiguousarray(v.reshape(-1, 128).T)


def _pack_params(inp, b, sg):
    L = DEPTH
    pp = np.zeros((L, 128, NPCOL), np.float32)

    def put(l, name, arr):
        o, w = PCOL[name]
        arr = np.asarray(arr, np.float32).reshape(128, w)
        pp[l, :, o:o + w] = arr
    wins = (2, 4, 8, 16)
    for l in range(L):
        put(l, "nw1", _fm(inp["norm_mix_w"][l]))
        put(l, "nw2", _fm(inp["norm_mlp_w"][l]))
        put(l, "bnw", _fm(inp["branch_norm_w"][l]))
        put(l, "fnw", _fm(inp["final_norm_w"]))
        adab = np.zeros((128, 48), np.float32)
        adab[:, 0:12] = _fm(inp["ada_b"][l])[:, sg * 12:(sg + 1) * 12]
        put(l, "adab", adab)
        put(l, "pscale", _fm(inp["pool_scale"][l]))
        put(l, "scw", inp["sconv_w"][l].reshape(3, 2, 128).transpose(2, 1, 0))
        put(l, "cvw", inp["ssd_conv_w"][l].reshape(4, 6, 128).transpose(2, 1, 0))
        put(l, "cvb", _fm(inp["ssd_conv_b"][l]))
        put(l, "dtb", np.broadcast_to(inp["ssd_dt_bias"][l][None, :], (128, 4)))
        put(l, "alog", np.broadcast_to(inp["ssd_a_log"][l][None, :], (128, 4)))
        put(l, "dsk", np.broadcast_to(inp["ssd_d"][l][None, :], (128, 4)))
        def gp(a):
            return a.reshape(8, 2, 64).transpose(1, 2, 0).reshape(128, 8)
        put(l, "are", gp(inp["s5_a_re"][l]))
        put(l, "aim", gp(inp["s5_a_im"][l]))
        put(l, "lst", gp(np.broadcast_to(inp["s5_log_step"][l][:, None], (16, 64))))
        put(l, "s5d", _fm(inp["s5_d"][l]))
        put(l, "glub", _fm(inp["s5_glu_b"][l]))
        put(l, "cond", _fm(inp["c"][b]))
        invc = np.zeros((128, 2, 16), np.float32)
        for c in range(2):
            for half in range(2):
                win = wins[c * 2 + half]
                if sg == 0:
                    invc[half * 64:(half + 1) * 64, c, :] = 1.0 / np.minimum(np.arange(16) + 1, win)
                else:
                    invc[half * 64:(half + 1) * 64, c, :] = 1.0 / win
        put(l, "invc", invc)
        sel = np.zeros((128, 8), np.float32)
        if sg > 0:
            sel[:, sg - 1] = 1.0
        sel[:, 4 + sg] = 1.0
        put(l, "sel", sel)
    return pp


def _consts():
    cm = np.zeros((128, 4, 128), np.float32)
    i = np.arange(128)
    cm[:, 0, :] = (i[:, None] == i[None, :])
    cm[:, 1, :] = 1.0
    cm[:, 2, :] = (i[:, None] <= i[None, :])
    cm[:, 3, :] = np.where(i[None, :] < i[:, None], -30000.0, 0.0)
    return cm


def _host_inputs(inp, b, sg):
    f = lambda a: np.ascontiguousarray(np.asarray(a, np.float32))
    pw = np.zeros((DEPTH, 128, 2, 128), np.float32)
    for l in range(DEPTH):
        for g in range(4):
            c, half = g // 2, g % 2
            pw[l, half * 64:(half + 1) * 64, c, half * 64:(half + 1) * 64] = inp["pool_w"][l, g]
    gw = np.ascontiguousarray(np.asarray(inp["s5_glu_w"], np.float32).reshape(DEPTH, 2, 128, 256).transpose(0, 2, 1, 3))
    s5b = np.zeros((DEPTH, 128, 256), np.float32)
    s5c = np.zeros((DEPTH, 128, 8, 2, 32), np.float32)
    for l in range(DEPTH):
        s5b[l, :, 0:128] = np.asarray(inp["s5_b_re"][l]).reshape(8, 2, 64, 16).transpose(1, 2, 0, 3).reshape(128, 128)
        s5b[l, :, 128:256] = np.asarray(inp["s5_b_im"][l]).reshape(8, 2, 64, 16).transpose(1, 2, 0, 3).reshape(128, 128)
        for ri, nm in enumerate(("s5_c_re", "s5_c_im")):
            cc = np.asarray(inp[nm][l]).reshape(8, 2, 16, 64)
            for r in range(8):
                for gi in range(2):
                    s5c[l, gi * 64:(gi + 1) * 64, r, ri, gi * 16:gi * 16 + 16] = cc[r, gi].T
    return {
        "s5b": s5b, "s5c": s5c.reshape(DEPTH, 128, 512),
        "xT": f(np.asarray(inp["x"][b][sg * T:(sg + 1) * T]).T),
        "pp": _pack_params(inp, b, sg),
        "cmat": _consts(),
        "ada_w": f(np.asarray(inp["ada_w"])[:, :, sg * 1536:(sg + 1) * 1536]), "w_in": f(inp["w_in"]), "w_out": f(inp["w_out"]),
        "mlp_w1": f(inp["mlp_w1"]), "mlp_w2": f(inp["mlp_w2"]),
        "poolw": pw, "gluw": gw,
    }


_NC_CACHE = {}


def kernel(**inputs):
    inp = {k: np.asarray(v) for k, v in inputs.items()}
    if "full" not in _NC_CACHE:
        _NC_CACHE["full"] = build()
    nc = _NC_CACHE["full"]
    in_maps = [_host_inputs(inp, r // 4, r % 4) for r in range(8)]
    res = run_bass_kernel_spmd(nc, in_maps, core_ids=list(range(8)))
    out = np.empty((2, SEQ, DM), np.float32)
    for r in range(8):
        out[r // 4, (r % 4) * T:(r % 4 + 1) * T, :] = res.results[r]["yT"].T
    return out
```
